# Optimizing a Trainium2 kernel written in Bass

```python
import math
import jax, jax.numpy as jnp
from jax import lax
import numpy as np

D_MODEL = 1024
BATCH = 4
SEQ = 4096
DEPTH = 2

CHUNK = 64
D_CONV = D_MODEL // 4
CONV_WIDTH = 31
REC_DK = 128
REC_DV = 128
D_REC = D_MODEL // 2
REC_HEADS = D_REC // REC_DV
ATT_HEAD_DIM = 64
D_ATT = D_MODEL // 4
ATT_HEADS = D_ATT // ATT_HEAD_DIM
ATT_LEFT_CHUNKS = 8
MAX_REL = 128
D_MIX = D_CONV + D_REC + D_ATT
D_IN = 2 * D_CONV + 4 * D_REC + 3 * D_ATT
D_FF = ((8 * D_MODEL // 3 + 255) // 256) * 256
ALPHA = (2 * DEPTH) ** 0.25
BETA = (8 * DEPTH) ** -0.25
LN_EPS = 1e-5
NEG_BIG = -1e30
TINY = 1e-30

kernel_name = 'hybrid_conformer_hgrn2_chunkattn_deepnorm'


def layer_norm(x, g, b):
    xf = x.astype(jnp.float32)
    mu = jnp.mean(xf, axis=-1, keepdims=True)
    var = jnp.mean(jnp.square(xf - mu), axis=-1, keepdims=True)
    y = (xf - mu) * lax.rsqrt(var + LN_EPS) * g.astype(jnp.float32) + b.astype(jnp.float32)
    return y.astype(x.dtype)


def conv_module(a_val, a_gate, w, bias, ln_g, ln_b):
    u = a_val * jax.nn.sigmoid(a_gate)
    u = lax.conv_general_dilated(
        u, w[:, None, :].astype(u.dtype), window_strides=(1,),
        padding=[(CONV_WIDTH - 1, 0)],
        dimension_numbers=('NWC', 'WIO', 'NWC'),
        feature_group_count=D_CONV) + bias
    return jax.nn.silu(layer_norm(u, ln_g, ln_b))


def hgrn2_mixer(q, f_logit, i, g, lb, norm_g):
    B, T, _ = q.shape
    nc = T // CHUNK
    f32 = jnp.float32
    zf = f_logit.astype(f32)
    lb = lb.astype(f32)
    log_f = jnp.logaddexp(jnp.log(jnp.maximum(lb, TINY)),
                          jnp.log1p(-lb) + jax.nn.log_sigmoid(zf))
    k = (1.0 - lb) * jax.nn.sigmoid(-zf)

    def to_chunks(t, d):
        return t.astype(f32).reshape(B, nc, CHUNK, REC_HEADS, d).transpose(1, 0, 3, 2, 4)

    qc, kc, lfc = to_chunks(q, REC_DK), to_chunks(k, REC_DK), to_chunks(log_f, REC_DK)
    vc = to_chunks(i, REC_DV)
    causal = jnp.tril(jnp.ones((CHUNK, CHUNK), dtype=bool))

    def step(S, inp):
        qq, kk, vv, lf = inp
        bcum = jnp.cumsum(lf, axis=2)
        o_inter = jnp.einsum('bhtd,bhde->bhte', qq * jnp.exp(bcum), S)
        diff = bcum[:, :, :, None, :] - bcum[:, :, None, :, :]
        decay = jnp.exp(jnp.where(causal[:, :, None], diff, NEG_BIG))
        attn = jnp.einsum('bhtsd,bhsd->bhts', qq[:, :, :, None, :] * decay, kk)
        o_intra = jnp.einsum('bhts,bhse->bhte', attn, vv)
        b_last = bcum[:, :, -1:, :]
        S_new = jnp.exp(b_last[:, :, 0, :])[..., None] * S + jnp.einsum(
            'bhsd,bhse->bhde', kk * jnp.exp(b_last - bcum), vv)
        return S_new, o_inter + o_intra

    S0 = jnp.zeros((B, REC_HEADS, REC_DK, REC_DV), f32)
    _, o = lax.scan(step, S0, (qc, kc, vc, lfc))
    o = o.transpose(1, 0, 3, 2, 4).reshape(B, T, REC_HEADS, REC_DV)
    o = o * lax.rsqrt(jnp.mean(jnp.square(o), axis=-1, keepdims=True) + LN_EPS) * norm_g.astype(f32)
    o = o.reshape(B, T, D_REC) * jax.nn.silu(g.astype(f32))
    return o.astype(q.dtype)


def chunk_attention(q, k, v, rel_table):
    B, T, _ = q.shape
    nc = T // CHUNK
    W = ATT_LEFT_CHUNKS + 1
    f32 = jnp.float32
    shp = (B, nc, CHUNK, ATT_HEADS, ATT_HEAD_DIM)
    qc = q.astype(f32).reshape(shp)
    pad = ((0, 0), (ATT_LEFT_CHUNKS, 0), (0, 0), (0, 0), (0, 0))
    kp = jnp.pad(k.astype(f32).reshape(shp), pad)
    vp = jnp.pad(v.astype(f32).reshape(shp), pad)
    kb = jnp.concatenate([kp[:, j:j + nc] for j in range(W)], axis=2)
    vb = jnp.concatenate([vp[:, j:j + nc] for j in range(W)], axis=2)
    s = jnp.einsum('bnqhd,bnkhd->bhnqk', qc, kb) * (ATT_HEAD_DIM ** -0.5)
    rel = (jnp.arange(W * CHUNK)[None, :] - ATT_LEFT_CHUNKS * CHUNK
           - jnp.arange(CHUNK)[:, None])
    idx = jnp.clip(rel, -MAX_REL, MAX_REL) + MAX_REL
    bias = rel_table.astype(f32)[:, idx]
    key_chunk = (jnp.arange(nc)[:, None] + jnp.repeat(jnp.arange(W), CHUNK)[None, :]
                 - ATT_LEFT_CHUNKS)
    valid = key_chunk >= 0
    s = s + bias[:, None]
    s = jnp.where(valid[:, None, :], s, NEG_BIG)
    p = jax.nn.softmax(s, axis=-1)
    o = jnp.einsum('bhnqk,bnkhd->bnqhd', p, vb).reshape(B, T, D_ATT)
    return o.astype(q.dtype)


def setup_inputs(seed: int = 0) -> dict:
    key = jax.random.key(seed)
    ks = jax.random.split(key, 24)
    f32 = jnp.float32
    nrm = lambda k, shape, s: (jax.random.normal(k, shape, f32) * s)
    x = jax.random.normal(ks[0], (BATCH, SEQ, D_MODEL), f32)
    c = jax.random.normal(ks[1], (BATCH, D_MODEL), f32)
    w_ada = nrm(ks[2], (DEPTH, D_MODEL, 6 * D_MODEL), 0.1 * D_MODEL ** -0.5)
    b_ada = nrm(ks[3], (DEPTH, 6 * D_MODEL), 0.01)
    s_in = D_MODEL ** -0.5
    w_in = jnp.concatenate([
        nrm(ks[4], (DEPTH, D_MODEL, D_CONV), s_in * BETA),
        nrm(ks[5], (DEPTH, D_MODEL, D_CONV), s_in),
        nrm(ks[6], (DEPTH, D_MODEL, D_REC), s_in),
        nrm(ks[7], (DEPTH, D_MODEL, D_REC), s_in),
        nrm(ks[8], (DEPTH, D_MODEL, D_REC), s_in * BETA),
        nrm(ks[9], (DEPTH, D_MODEL, D_REC), s_in),
        nrm(ks[10], (DEPTH, D_MODEL, D_ATT), s_in),
        nrm(ks[11], (DEPTH, D_MODEL, D_ATT), s_in),
        nrm(ks[12], (DEPTH, D_MODEL, D_ATT), s_in * BETA)], axis=-1)
    conv_w = nrm(ks[13], (DEPTH, CONV_WIDTH, D_CONV), CONV_WIDTH ** -0.5)
    conv_b = nrm(ks[14], (DEPTH, D_CONV), 0.01)
    conv_ln_g = 1.0 + nrm(ks[15], (DEPTH, D_CONV), 0.01)
    conv_ln_b = nrm(ks[16], (DEPTH, D_CONV), 0.01)
    rec_lower_bound = 1.0 + nrm(ks[17], (DEPTH, D_REC), 0.1)
    rec_norm_g = 1.0 + nrm(ks[18], (DEPTH, REC_DV), 0.01)
    rel_bias = nrm(ks[19], (DEPTH, ATT_HEADS, 2 * MAX_REL + 1), 0.1)
    w_out = nrm(ks[20], (DEPTH, D_MIX, D_MODEL), D_MIX ** -0.5 * BETA)
    kln = jax.random.split(ks[21], 4)
    ln1_g = 1.0 + nrm(kln[0], (DEPTH, D_MODEL), 0.01)
    ln1_b = nrm(kln[1], (DEPTH, D_MODEL), 0.01)
    ln2_g = 1.0 + nrm(kln[2], (DEPTH, D_MODEL), 0.01)
    ln2_b = nrm(kln[3], (DEPTH, D_MODEL), 0.01)
    w_ffn_in = nrm(ks[22], (DEPTH, D_MODEL, 2 * D_FF), s_in * BETA)
    w_ffn_out = nrm(ks[23], (DEPTH, D_FF, D_MODEL), D_FF ** -0.5 * BETA)
    return {'x': x, 'c': c, 'w_ada': w_ada, 'b_ada': b_ada, 'w_in': w_in,
            'conv_w': conv_w, 'conv_b': conv_b, 'conv_ln_g': conv_ln_g, 'conv_ln_b': conv_ln_b,
            'rec_lower_bound': rec_lower_bound, 'rec_norm_g': rec_norm_g, 'rel_bias': rel_bias,
            'w_out': w_out, 'ln1_g': ln1_g, 'ln1_b': ln1_b, 'ln2_g': ln2_g, 'ln2_b': ln2_b,
            'w_ffn_in': w_ffn_in, 'w_ffn_out': w_ffn_out}


def reference(x, c, w_ada, b_ada, w_in, conv_w, conv_b, conv_ln_g, conv_ln_b,
              rec_lower_bound, rec_norm_g, rel_bias, w_out, ln1_g, ln1_b, ln2_g, ln2_b,
              w_ffn_in, w_ffn_out):
    lbs = jax.nn.softmax(rec_lower_bound.astype(jnp.float32), axis=0)
    lbs = jnp.cumsum(lbs, axis=0) - lbs[0]
    split_pts = [D_CONV, 2 * D_CONV,
                 2 * D_CONV + D_REC, 2 * D_CONV + 2 * D_REC,
                 2 * D_CONV + 3 * D_REC, 2 * D_CONV + 4 * D_REC,
                 2 * D_CONV + 4 * D_REC + D_ATT, 2 * D_CONV + 4 * D_REC + 2 * D_ATT]
    c_act = jax.nn.silu(c)
    for l in range(DEPTH):
        mod = (c_act @ w_ada[l] + b_ada[l])[:, None, :]
        sh1, sc1, g1, sh2, sc2, g2 = jnp.split(mod, 6, axis=-1)
        h = x * (1.0 + sc1) + sh1
        p = h @ w_in[l]
        a_val, a_gate, rq, rf, ri, rg, aq, ak, av = jnp.split(p, split_pts, axis=-1)
        y_conv = conv_module(a_val, a_gate, conv_w[l], conv_b[l], conv_ln_g[l], conv_ln_b[l])
        y_rec = hgrn2_mixer(rq, rf, ri, rg, lbs[l], rec_norm_g[l])
        y_att = chunk_attention(aq, ak, av, rel_bias[l])
        y = jnp.concatenate([y_conv, y_rec, y_att], axis=-1) @ w_out[l]
        x = layer_norm(ALPHA * x + (1.0 + g1) * y, ln1_g[l], ln1_b[l])
        h = x * (1.0 + sc2) + sh2
        gt, up = jnp.split(h @ w_ffn_in[l], 2, axis=-1)
        y = (jax.nn.silu(gt) * up) @ w_ffn_out[l]
        x = layer_norm(ALPHA * x + (1.0 + g2) * y, ln2_g[l], ln2_b[l])
    return x
```

```python
import contextlib
import numpy as np
import concourse.bass as bass
import concourse.mybir as mybir
from concourse.bass_utils import run_bass_kernel_spmd

F32 = mybir.dt.float32
BF16 = mybir.dt.bfloat16
AF = mybir.ActivationFunctionType
ALU = mybir.AluOpType

D = 1024
DIN = 3328
DFF = 2816
DEPTH_FULL = 2
ALPHA = (2 * DEPTH_FULL) ** 0.25
LN_EPS = 1e-5
NEG_BIG = -1e30
TL = 512
PL = 149
NPAR = 2 * PL + 8
SEM_CAP = 30000


class Prog:
    ENGS = ("pe", "act", "dve", "pool", "sp")

    def __init__(self, nc, same_engine_sync=True):
        self.nc = nc
        self.eng = {"pe": nc.tensor, "act": nc.scalar, "dve": nc.vector,
                    "pool": nc.gpsimd, "sp": nc.sync}
        self.plan = {e: [] for e in self.ENGS}
        self.seq = {e: 0 for e in self.ENGS}
        self.sems = {}
        self.waited = {e: {} for e in self.ENGS}
        self.res = {}
        self.same = same_engine_sync
        self.ctx = []
        self.dma_sems = {}
        self.ninst = 0
        self.stopped = False

    def _sem(self, name):
        cm = self.nc.semaphore(name)
        s = cm.__enter__()
        self.ctx.append(cm)
        return s

    def eng_sem(self, e, epoch):
        k = (e, epoch)
        if k not in self.sems:
            self.sems[k] = self._sem(f"s_{e}_{epoch}")
        return self.sems[k]

    def _need_wait(self, E, tok):
        if tok is None:
            return None
        if tok[0] == "eng":
            _, F, s = tok
            if F == E and (E in ("pe", "sp") or not self.same):
                return None
            key = ("eng", F)
            if self.waited[E].get(key, 0) >= s:
                return None
            self.waited[E][key] = s
            epoch, val = (s - 1) // SEM_CAP, (s - 1) % SEM_CAP + 1
            return (self.eng_sem(F, epoch), val)
        _, name, val = tok
        key = ("dma", name)
        if self.waited[E].get(key, 0) >= val:
            return None
        self.waited[E][key] = val
        return (self.dma_sems[name][0], val)

    def _deps(self, E, reads, writes, own_dma=None):
        toks = []
        for k in reads:
            r = self.res.get(k)
            if r and r["w"] is not None:
                toks.append(r["w"])
        for k in writes:
            r = self.res.get(k)
            if r:
                if r["w"] is not None:
                    toks.append(r["w"])
                toks.extend(r["r"])
        best = {}
        for t in toks:
            if t[0] == "dma" and own_dma is not None and t[1] == own_dma:
                continue
            key = (t[0], t[1])
            if key not in best or t[2] > best[key][2]:
                best[key] = t
        waits = []
        for t in best.values():
            w = self._need_wait(E, t)
            if w:
                waits.append(w)
        return waits

    def _update(self, tok, reads, writes):
        for k in reads:
            r = self.res.setdefault(k, {"w": None, "r": []})
            r["r"] = [t for t in r["r"] if not (t[0] == tok[0] and t[1] == tok[1])] + [tok]
        for k in writes:
            self.res[k] = {"w": tok, "r": []}

    def emit(self, E, fn, reads=(), writes=()):
        if self.stopped:
            return
        waits = self._deps(E, reads, writes)
        self.seq[E] += 1
        s = self.seq[E]
        sem = self.eng_sem(E, (s - 1) // SEM_CAP)
        eng = self.eng[E]

        ops = fn if isinstance(fn, list) else [fn]

        def run(waits=waits, ops=ops, sem=sem, eng=eng):
            for (ws, wv) in waits:
                eng.wait_ge(ws, wv)
            for (m_, a_, k_) in ops:
                inst = m_(*a_, **k_)
            inst.then_inc(sem, 1)
        self.plan[E].append(run)
        self._update(("eng", E, s), reads, writes)
        self.ninst += 1

    def dma(self, Q, name, fn, reads=(), writes=(), inc=16):
        if self.stopped:
            return
        if name not in self.dma_sems:
            self.dma_sems[name] = [self._sem(f"d_{name}"), 0]
        waits = self._deps(Q, reads, writes, own_dma=name)
        self.dma_sems[name][1] += inc
        val = self.dma_sems[name][1]
        sem = self.dma_sems[name][0]
        eng = self.eng[Q]

        def run(waits=waits, fn=fn, sem=sem, eng=eng, inc=inc):
            for (ws, wv) in waits:
                eng.wait_ge(ws, wv)
            m_, a_, k_ = fn
            if inc == 16:
                m_(*a_, **k_).then_inc(sem, 16)
            else:
                m_(*a_, **k_).then_inc(sem)
        self.plan[Q].append(run)
        self._update(("dma", name, val), reads, writes)
        self.ninst += 1

    def barrier(self):
        if self.stopped:
            return
        for E in self.ENGS:
            waits = []
            for F in self.ENGS:
                if F == E or F == "sp" or self.seq[F] == 0:
                    continue
                w = self._need_wait(E, ("eng", F, self.seq[F]))
                if w:
                    waits.append(w)
            for name, (sem, val) in self.dma_sems.items():
                if val > 0:
                    w = self._need_wait(E, ("dma", name, val))
                    if w:
                        waits.append(w)
            eng = self.eng[E]

            def run(waits=waits, eng=eng):
                for (ws, wv) in waits:
                    eng.wait_ge(ws, wv)
            self.plan[E].append(run)

    def final_wait(self, Q, keys):
        waits = self._deps(Q, keys, keys)
        eng = self.eng[Q]

        def run(waits=waits, eng=eng):
            for (ws, wv) in waits:
                eng.wait_ge(ws, wv)
        self.plan[Q].append(run)

    def run(self):
        nc = self.nc
        with nc.Block() as block:
            @block.tensor
            def _(e):
                for f in self.plan["pe"]:
                    f()

            @block.scalar
            def _(e):
                for f in self.plan["act"]:
                    f()

            @block.vector
            def _(e):
                for f in self.plan["dve"]:
                    f()

            @block.gpsimd
            def _(e):
                for f in self.plan["pool"]:
                    f()

            @block.sync
            def _(e):
                for f in self.plan["sp"]:
                    f()
        for cm in reversed(self.ctx):
            cm.__exit__(None, None, None)


def I(m, *a, **k):
    return (m, a, k)


class _Stop(Exception):
    pass


def build(NT, DEPTH=2, dbg=None, same_engine_sync=True, stop=None, n_cores=8):
    assert NT % TL == 0
    NTILE = NT // TL
    FB = min(NT, 2048)
    NFB = NT // FB
    FT = FB // TL
    nc = bass.Bass("TRN2", target_bir_lowering=False)
    dr = lambda n, s, kind="ExternalInput", d=F32: nc.dram_tensor(n, s, d, kind=kind).ap()
    xT = dr("xT", [D, NT])
    cT = dr("cT", [128, 8])
    par = dr("par", [128, NPAR])
    cst = dr("cst", [128, 128 + 512 + 512])
    wada = [dr(f"wada{l}", [D, 6 * D]) for l in range(DEPTH)]
    win_d = [dr(f"win{l}", [D, DIN]) for l in range(DEPTH)]
    wout_d = [dr(f"wout{l}", [D, D]) for l in range(DEPTH)]
    wf1_d = [dr(f"wf1{l}", [D, 2 * DFF]) for l in range(DEPTH)]
    wf2_d = [dr(f"wf2{l}", [DFF, D]) for l in range(DEPTH)]
    bias_d = [dr(f"bias{l}", [128, 4 * 5 * 128]) for l in range(DEPTH)]
    flg_d = dr("flg", [128, 4])
    PAYW = 512 + 1024 + 1024 + 60
    bounce = [nc.dram_tensor(f"bounce{l}", [128, PAYW], F32, kind="Internal") for l in range(DEPTH)]
    sEp = [dr(f"sEp{t}", [128, 2048], kind="Internal") for t in range(NT // TL)]
    sKh = [dr(f"sKh{t}", [128, 2048], kind="Internal", d=BF16) for t in range(NT // TL)]
    sKt = [dr(f"sKt{t}", [128, 2048], kind="Internal", d=BF16) for t in range(NT // TL)]
    sAd = [dr(f"sAd{t}", [128, 32], kind="Internal") for t in range(NT // TL)]
    sVt = [dr(f"sVt{t}", [128, 2048], kind="Internal", d=BF16) for t in range(NT // TL)]
    gath = [nc.dram_tensor(f"gath{l}", [128, PAYW], F32, kind="Internal") for l in range(DEPTH)]
    outT = dr("outT", [D, NT], kind="ExternalOutput")
    scrA = dr("scrA", [D, NT], kind="Internal")
    scrB = dr("scrB", [D, NT], kind="Internal")
    dbg_out = {}
    if dbg:
        for n, s in dbg.items():
            dbg_out[n] = dr("dbg_" + n, list(s), kind="ExternalOutput")

    P = Prog(nc, same_engine_sync)
    es = contextlib.ExitStack()
    sb = lambda n, s, d=F32: es.enter_context(nc.sbuf_tensor(n, s, d))
    ps = lambda n, s, d=F32: es.enter_context(nc.psum_tensor(n, s, d))
    fm = lambda ap: ap.rearrange("(kc p) n -> p kc n", p=128)
    V, A, G, T = nc.vector, nc.scalar, nc.gpsimd, nc.tensor
    MM = T.matmul
    ACT = A.activation

    with es:
        pb = [ps(f"pb{i}", [128, 512]) for i in range(7)]
        pT = ps("pT", [128, 1024], BF16)
        part = sb("part", [128, NPAR])
        cstt = sb("cstt", [128, 1152])
        ident = sb("ident", [128, 128], BF16)
        ones_d = sb("ones_d", [128, 128])
        ones_c = sb("ones_c", [128, 128])
        ones_r = sb("ones_r", [128, 128])
        ones_b = sb("ones_b", [128, 64], BF16)
        epsT = sb("epsT", [128, 4])
        cact = sb("cact", [128, 8], BF16)
        cf = sb("cf", [128, 8])
        modv2 = sb("modv", [128, 2, 48])
        cactf = sb("cactf", [128, 8])
        vec2 = sb("vec", [128, 2, 8, 8])
        lbt = sb("lbt", [128, 8, 4])
        omlb = sb("omlb", [128, 2, 4])
        W = [sb(f"W{i}", [128, 512]) for i in range(8)]
        wpre = sb("wpre", [128, 8, 1024], BF16)

        def load_wpre(lw):
            for k in range(8):
                P.dma("pool", "wpre", I(G.dma_start, out=wpre[:, k, :], in_=win_d[lw][k * 128:(k + 1) * 128, 1024:2048]), writes=[f"wpre{k}"])
        WPK = [f"wpre{k}" for k in range(8)]
        rstdT = sb("rstdT", [128, 512])
        nmrT = sb("nmrT", [128, 512])
        wctr = [0]

        WROT = [0, 1, 3, 4, 5, 7]

        def wtmp():
            i = WROT[wctr[0] % len(WROT)]
            wctr[0] += 1
            return W[i], f"W{i}"

        gctr = [0]

        def gbank(n=2):
            i = gctr[0] % n
            gctr[0] += 1
            return pb[i], f"pb{i}"

        recmask = cstt[:, 128:640]
        resetm = cstt[:, 640:1152]
        b3 = lambda a: a[:].rearrange("p (c t) -> p c t", t=64)

        P.dma("sp", "ld0", I(nc.sync.dma_start, out=part[:], in_=par), writes=["part"])
        P.dma("sp", "ld1", I(nc.sync.dma_start, out=cstt[:], in_=cst), writes=["cstt"])
        P.dma("sp", "ld2", I(nc.sync.dma_start, out=cf[:], in_=cT), writes=["cf"])
        flg = sb("flgs", [128, 4])
        P.dma("sp", "ld3", I(nc.sync.dma_start, out=flg[:], in_=flg_d), writes=["flg"])
        isA, isB, hbias = flg[:, 0:1], flg[:, 1:2], flg[:, 2:3]
        P.emit("pool", I(G.memset, ones_d[:], 1.0 / D), writes=["ones_d"])
        P.emit("pool", I(G.memset, ones_c[:], 1.0 / 256), writes=["ones_c"])
        P.emit("pool", I(G.memset, ones_r[:], 1.0 / 128), writes=["ones_r"])
        P.emit("pool", I(G.memset, ones_b[:], 1.0), writes=["ones_b"])
        P.emit("pool", I(G.memset, epsT[:, 0:1], LN_EPS), writes=["epsT"])
        P.emit("pool", I(G.memset, epsT[:, 1:2], LN_EPS / (ALPHA * ALPHA)), writes=["epsT"])
        P.emit("pool", I(G.memset, epsT[:, 2:3], 1.0), writes=["epsT"])
        P.emit("dve", I(V.tensor_copy, out=ident[:], in_=cstt[:, 0:128]), reads=["cstt"], writes=["ident"])
        P.emit("act", I(ACT, out=cact[:], in_=cf[:], func=AF.Silu), reads=["cf"], writes=["cact"])
        P.emit("act", I(ACT, out=cactf[:], in_=cf[:], func=AF.Silu), reads=["cf"], writes=["cactf"])
        rl = part[:, 2 * PL:2 * PL + 8].rearrange("p (l c) -> p l c", c=4)
        r0, r1 = rl[:, 0, :], rl[:, 1, :]
        mx, e0, e1, ssum, rs, s0, s1, c1 = [lbt[:, i, :] for i in range(8)]
        TT = V.tensor_tensor
        P.emit("dve", I(V.tensor_max, out=mx, in0=r0, in1=r1), reads=["part"], writes=["lb_mx"])
        P.emit("dve", I(TT, out=e0, in0=r0, in1=mx, op=ALU.subtract), reads=["part", "lb_mx"], writes=["lb_e0"])
        P.emit("dve", I(TT, out=e1, in0=r1, in1=mx, op=ALU.subtract), reads=["part", "lb_mx"], writes=["lb_e1"])
        P.emit("act", I(ACT, out=e0, in_=e0, func=AF.Exp), reads=["lb_e0"], writes=["lb_e0"])
        P.emit("act", I(ACT, out=e1, in_=e1, func=AF.Exp), reads=["lb_e1"], writes=["lb_e1"])
        P.emit("dve", I(TT, out=ssum, in0=e0, in1=e1, op=ALU.add), reads=["lb_e0", "lb_e1"], writes=["lb_s"])
        P.emit("dve", I(V.reciprocal, out=rs, in_=ssum), reads=["lb_s"], writes=["lb_rs"])
        P.emit("dve", I(TT, out=s0, in0=e0, in1=rs, op=ALU.mult), reads=["lb_e0", "lb_rs"], writes=["lb_s0"])
        P.emit("dve", I(TT, out=s1, in0=e1, in1=rs, op=ALU.mult), reads=["lb_e1", "lb_rs"], writes=["lb_s1"])
        P.emit("dve", I(TT, out=c1, in0=s0, in1=s1, op=ALU.add), reads=["lb_s0", "lb_s1"], writes=["lb_c1"])
        P.emit("dve", I(TT, out=mx, in0=s0, in1=s0, op=ALU.subtract), reads=["lb_s0"], writes=["lb_mx"])
        P.emit("dve", I(TT, out=c1, in0=c1, in1=s0, op=ALU.subtract), reads=["lb_c1", "lb_s0"], writes=["lb_c1"])
        P.emit("dve", I(V.tensor_scalar, out=omlb[:, 0, :], in0=mx, scalar1=-1.0, scalar2=1.0, op0=ALU.mult, op1=ALU.add),
               reads=["lb_mx"], writes=["omlb"])
        P.emit("dve", I(V.tensor_scalar, out=omlb[:, 1, :], in0=c1, scalar1=-1.0, scalar2=1.0, op0=ALU.mult, op1=ALU.add),
               reads=["lb_c1", "omlb"], writes=["omlb"])
        nomlb = sb("nomlb", [128, 2, 4])
        P.emit("dve", I(V.tensor_scalar, out=nomlb[:], in0=omlb[:], scalar1=-1.0, scalar2=None, op0=ALU.mult), reads=["omlb"], writes=["nomlb"])

        def tap(name, src_ap, keys):
            if name in dbg_out:
                P.dma("pool", "dbg_" + name, I(G.dma_start, out=dbg_out[name], in_=src_ap), reads=keys, writes=["dbgo_" + name])

        def ln_stats(epscol, bk=(3, 4), outs=None):
            pm_, pq_ = pb[bk[0]], pb[bk[1]]
            km_, kq_ = f"pb{bk[0]}", f"pb{bk[1]}"
            m2, m2k = wtmp()
            P.emit("act", I(ACT, out=m2[:], in_=pm_[:], func=AF.Square), reads=[km_], writes=[m2k])
            var, vark = wtmp()
            P.emit("dve", I(TT, out=var[:], in0=pq_[:], in1=m2[:], op=ALU.subtract), reads=[kq_, m2k], writes=[vark])
            sd, sdk = wtmp()
            P.emit("act", I(ACT, out=sd[:], in_=var[:], func=AF.Ln, bias=epsT[:, epscol:epscol + 1]), reads=[vark, "epsT"], writes=[sdk])
            (rstd, rsk), (nmr, nmk) = outs if outs else ((rstdT, "rstdT"), (nmrT, "nmrT"))
            P.emit("act", I(ACT, out=rstd[:], in_=sd[:], func=AF.Exp, scale=-0.5), reads=[sdk], writes=[rsk])
            P.emit("dve", I(V.scalar_tensor_tensor, out=nmr[:], in0=pm_[:], scalar=-1.0, in1=rstd[:], op0=ALU.mult, op1=ALU.mult),
                   reads=[km_, rsk], writes=[nmk])
            return rstd, rsk, nmr, nmk

        def ln_stats_part(xt, xkey, csl=slice(None), bk=(3, 4), outs=None):
            for m in range(8):
                q_, qk_ = wtmp()
                P.emit("act", I(ACT, out=q_[:], in_=xt[:, m, csl], func=AF.Square), reads=[xkey], writes=[qk_])
                P.emit("pe", [I(MM, pb[bk[0]][:], lhsT=ones_d[:], rhs=xt[:, m, csl], start=(m == 0), stop=(m == 7)),
                              I(MM, pb[bk[1]][:], lhsT=ones_d[:], rhs=q_[:], start=(m == 0), stop=(m == 7))],
                       reads=[xkey, qk_, "ones_d"], writes=[f"pb{bk[0]}", f"pb{bk[1]}"])
            return ln_stats(1, bk, outs)

        def ln_apply_gen(xt, xkey, lng, lnb, st, csl=slice(None), bufs=None):
            rstd, rsk, nmr, nmk = st
            for m in range(8):
                ta, tak = bufs[0] if bufs else wtmp()
                P.emit("dve", I(TT, out=ta[:], in0=xt[:, m, csl], in1=rstd[:], op=ALU.mult), reads=[xkey, rsk], writes=[tak])
                yield
                tb, tbk = bufs[1] if bufs else wtmp()
                P.emit("pool", I(G.tensor_tensor, out=tb[:], in0=ta[:], in1=nmr[:], op=ALU.add), reads=[tak, nmk], writes=[tbk])
                yield
                P.emit("act", I(ACT, out=xt[:, m, csl], in_=tb[:], func=AF.Identity, scale=lng[:, m:m + 1], bias=lnb[:, m:m + 1]),
                       reads=[tbk, "part", xkey], writes=[xkey])
                yield

        def ln_apply(xt, xkey, lng, lnb, csl=slice(None)):
            st = ln_stats_part(xt, xkey, csl)
            for _ in ln_apply_gen(xt, xkey, lng, lnb, st, csl):
                pass

        def ck(name):
            if stop == name:
                P.stopped = True

        def interleave(gens):
            gens = list(gens)
            while gens:
                for g in list(gens):
                    try:
                        next(g)
                    except StopIteration:
                        gens.remove(g)

        def mod_gen(lm, stg):
            pm = lm % 2
            po_ = lm * PL
            NG = 24
            stgw, rowb = stg
            def ld(g):
                P.dma("pool", f"wa{g % 2}", I(G.dma_start, out=stgw[g % 2][:], in_=fm(wada[lm][:, g * 256:(g + 1) * 256])), writes=[f"stg{g % 2}"])
            ld(0)
            for g in range(NG):
                if g + 1 < NG:
                    ld(g + 1)
                yield
                P.emit("pe", [I(MM, pb[6][0:1, 0:256], lhsT=cact[:, k:k + 1], rhs=stgw[g % 2][:, k, :], start=(k == 0), stop=(k == 7)) for k in range(8)],
                       reads=[f"stg{g % 2}", "cact"], writes=["pb6"])
                P.emit("act", I(A.copy, out=rowb[g % 2], in_=pb[6][0:1, 0:256]), reads=["pb6"], writes=[f"rowb{g % 2}", f"lnb2_{g % 2}"])
                yield
                P.emit("pe", [I(MM, pb[5][:, g * 2 + j:g * 2 + j + 1], lhsT=rowb[g % 2][:, j * 128:(j + 1) * 128], rhs=epsT[0:1, 2:3], start=True, stop=True)
                              for j in range(2)], reads=[f"rowb{g % 2}", f"lnb2_{g % 2}", "epsT"], writes=["pb5"])
                yield
            mv = modv2[:, pm, :]
            mk = f"modv{pm}"
            P.emit("dve", I(TT, out=mv, in0=pb[5][:, 0:48], in1=part[:, po_:po_ + 48], op=ALU.add), reads=["pb5", "part"], writes=[mk])
            sh1, sc1, g1, sh2, sc2, g2 = [modv2[:, pm, i * 8:(i + 1) * 8] for i in range(6)]
            A1_, B1_, G1_, G2_, A2_ = [vec2[:, pm, i, :] for i in range(5)]
            sfx = f"_{pm}"
            P.emit("dve", I(V.tensor_scalar, out=A1_, in0=sc1, scalar1=1.0, scalar2=None, op0=ALU.add), reads=[mk], writes=["vA1" + sfx])
            P.emit("dve", I(V.tensor_copy, out=B1_, in_=sh1), reads=[mk], writes=["vB1" + sfx])
            P.emit("dve", I(V.tensor_scalar, out=G1_, in0=g1, scalar1=1.0, scalar2=1.0 / ALPHA, op0=ALU.add, op1=ALU.mult), reads=[mk], writes=["vG1" + sfx])
            P.emit("dve", I(V.tensor_scalar, out=G2_, in0=g2, scalar1=1.0, scalar2=1.0 / ALPHA, op0=ALU.add, op1=ALU.mult), reads=[mk], writes=["vG2" + sfx])
            P.emit("dve", I(V.tensor_scalar, out=A2_, in0=sc2, scalar1=1.0, scalar2=None, op0=ALU.add), reads=[mk], writes=["vA2" + sfx])
            yield

        def layer(l):
            po = l * PL
            bada = part[:, po:po + 48]
            ln1g, ln1b = part[:, po + 48:po + 56], part[:, po + 56:po + 64]
            ln2g, ln2b = part[:, po + 64:po + 72], part[:, po + 72:po + 80]
            convw = part[:, po + 80:po + 142]
            convb, convg, convlb = part[:, po + 142:po + 144], part[:, po + 144:po + 146], part[:, po + 146:po + 148]
            normg = part[:, po + 148:po + 149]
            xsrc, xsk = (xT, "xT") if l == 0 else (scrB, "scrB")
            xdst, xdk = (outT, "outT") if l == DEPTH - 1 else (scrB, "scrB")

            pm = l % 2
            if l == 0:
                load_wpre(0)
            if l == 0:
                with contextlib.ExitStack() as ls:
                    stg = ([ls.enter_context(nc.sbuf_tensor(f"stgA{i}", [128, 8, 256], BF16)) for i in range(2)],
                           [ls.enter_context(nc.sbuf_tensor(f"rowA{i}", [1, 256], F32))[0:1, :] for i in range(2)])
                    for _ in mod_gen(0, stg):
                        pass
                P.barrier()
            modv = modv2[:, pm, :]
            MK = f"modv{pm}"
            sh1, sc1, g1, sh2, sc2, g2 = [modv2[:, pm, i * 8:(i + 1) * 8] for i in range(6)]
            A1, B1, G1, G2, A2 = [vec2[:, pm, i, :] for i in range(5)]
            if l == 0:
                tap("modv", modv, [MK])
            ck("mod")
            with contextlib.ExitStack() as ms:
                msb = lambda n, s, d=F32: ms.enter_context(nc.sbuf_tensor(f"{n}_{l}", s, d))
                winb = msb("winb", [128, 8, DIN - 1024], BF16)
                woutb = msb("woutb", [128, 8, D], BF16)
                biasb = msb("biasb", [128, 4, 5, 128])
                diag = msb("diag", [128, 62, 128], BF16)
                xt = msb("xt", [128, 8, TL])
                hbf = msb("hbf", [128, 8, TL], BF16)
                qt_bf = msb("qt_bf", [128, 4, TL], BF16)
                kh_bf = msb("kh_bf", [128, 4, TL], BF16)
                kh_tm = msb("kh_tm", [128, 4, 512], BF16)
                v_tm = msb("v_tm", [128, 4, 512], BF16)
                attnT = msb("attnT", [128, 2, 512], BF16)
                adec = msb("adec", [128, 4, 8])
                Sst = msb("Sst", [128, 4, 2, 128])
                Sbf = msb("Sbf", [128, 2, 8, 128], BF16)
                ymix = msb("ymix", [128, 8, TL], BF16)
                ubuf = msb("ubuf", [128, 2, 30 + TL], BF16)
                uc = msb("uc", [128, 2, TL])
                qTa = msb("qTa", [128, 2, TL], BF16)
                kTa = msb("kTa", [128, 2, 2 * TL], BF16)
                vat = msb("vat", [128, 8, 256], BF16)
                tS2 = msb("tS", [128, 2, 5, 128])
                PT2 = msb("PT", [128, 2, 5, 128], BF16)
                rrec = msb("rrec", [128, 128])
                WOUTK = [f"woutb{k}" for k in range(8)]

                WBLK = [(512, 1024), (2048, 2560), (0, 512), (2560, 3328)]

                def wsrc(c0, c1_):
                    if 1024 <= c0 and c1_ <= 2048:
                        return wpre, c0 - 1024, WPK
                    loc = c0 if c0 < 1024 else c0 - 1024
                    keys = [f"winb{k}_{b}" for k in range(8) for b, (a0, a1) in enumerate(WBLK) if a0 < c1_ and c0 < a1]
                    return winb, loc, keys
                for b, (c0, c1_) in enumerate(WBLK):
                    loc = c0 if c0 < 1024 else c0 - 1024
                    for k in range(8):
                        P.dma("pool", f"winb{b}", I(G.dma_start, out=winb[:, k, loc:loc + (c1_ - c0)], in_=win_d[l][k * 128:(k + 1) * 128, c0:c1_]), writes=[f"winb{k}_{b}"])
                for k in range(8):
                    P.dma("pool", "woutb", I(G.dma_start, out=woutb[:, k, :], in_=wout_d[l][k * 128:(k + 1) * 128, :]), writes=[f"woutb{k}"])
                P.dma("sp", "biasb", I(nc.sync.dma_start, out=biasb[:].rearrange("p h r j -> p (h r j)"), in_=bias_d[l]), writes=["biasb"])
                for cj in range(62):
                    P.emit("pool", I(G.tensor_scalar, out=diag[:, cj, :], in0=cstt[:, 0:128], scalar1=convw[:, cj:cj + 1], scalar2=None, op0=ALU.mult),
                           reads=["cstt", "part"], writes=["diag"])
                P.emit("dve", I(V.memset, Sst[:], 0.0), writes=[f"S{h}_{c}" for h in range(4) for c in range(2)])
                P.emit("dve", I(V.memset, ubuf[:], 0.0), writes=["ubuf"])
                P.emit("dve", I(V.memset, kTa[:], 0.0), writes=["kTa"])
                P.emit("dve", I(V.memset, vat[:], 0.0), writes=["vat"])
                scur = [0, 0, 0, 0]
                ck("wload")

                def proj_fm(col):
                    bank, bkey = gbank()
                    wt_, lc_, wk_ = wsrc(col, col + 128)
                    P.emit("pe", [I(MM, bank[:], lhsT=wt_[:, k, lc_:lc_ + 128], rhs=hbf[:, k, :], start=(k == 0), stop=(k == 7)) for k in range(8)],
                           reads=wk_ + ["hbf"], writes=[bkey])
                    return bank, bkey

                TB = [[(W[i], f"W{i}") for i in range(0, 4)], [(W[i], f"W{i}") for i in range(4, 8)]]
                tsf = tS2[:].rearrange("p a r q -> p (a r q)")
                phys = [(W[i], f"W{i}") for i in range(8)] + [(rstdT, "rstdT"), (nmrT, "nmrT"), (tsf[:, 0:512], "tS0"), (tsf[:, 640:1152], "tS1")]
                TBA = [[phys[3 * i], phys[3 * i + 1], phys[3 * i + 2], phys[3 * i + 2]] for i in range(4)]

                def load_tile(t):
                    tsl = slice(t * TL, (t + 1) * TL)
                    P.dma("sp", "xld", I(nc.sync.dma_start, out=xt[:], in_=fm(xsrc[:, tsl])), reads=[xsk], writes=["xt"])

                def prefetch_h(t):
                    for k in range(8):
                        P.dma("sp", f"xpf{k % 2}", I(nc.sync.dma_start, out=uc[:, k % 2, :], in_=xsrc[k * 128:(k + 1) * 128, t * TL:(t + 1) * TL]),
                              reads=[xsk], writes=[f"uc{k % 2}"])
                        P.emit("act", I(ACT, out=hbf[:, k, :], in_=uc[:, k % 2, :], func=AF.Identity, scale=A1[:, k:k + 1], bias=B1[:, k:k + 1]),
                               reads=[f"uc{k % 2}", f"vA1_{pm}", f"vB1_{pm}"], writes=["hbf"])
                        yield

                def rec_v(t):
                    for s in range(4):
                        bank, bkey = gbank()
                        P.emit("pe", [I(MM, bank[:], lhsT=hbf[:, k, s * 128:(s + 1) * 128], rhs=wpre[:, k, 512:1024], start=(k == 0), stop=(k == 7)) for k in range(8)],
                               reads=WPK + ["hbf"], writes=[bkey])
                        P.emit("act", I(A.copy, out=v_tm[:, s, :], in_=bank[:]), reads=[bkey], writes=["v_tm"])
                    P.dma("sp", "sVt", I(nc.sync.dma_start, out=sVt[t], in_=v_tm[:].rearrange("p s n -> p (s n)")), reads=["v_tm"], writes=[f"sVt{t}"])

                def rec_pre(hd, th, t):
                    (B0, B0k), (B1, B1k), _, (B3, B3k) = TBA[th]
                    pnA, pnB = (5, 6) if th % 2 == 0 else (3, 4)
                    hs = slice(hd * 128, (hd + 1) * 128)
                    h5 = slice(hd * 512, (hd + 1) * 512)
                    bank, bkey = proj_fm(1024 + hd * 128)
                    sg, sgk = B0, B0k
                    P.emit("act", I(ACT, out=sg[:], in_=bank[:], func=AF.Sigmoid, scale=-1.0), reads=[bkey], writes=[sgk])
                    yield
                    kk, kkk = B1, B1k
                    P.emit("dve", I(V.tensor_scalar, out=kk[:], in0=sg[:], scalar1=omlb[:, l, hd:hd + 1], scalar2=None, op0=ALU.mult),
                           reads=[sgk, "omlb"], writes=[kkk])
                    yield
                    lf, lfk = B3, B3k
                    P.emit("act", I(ACT, out=lf[:], in_=sg[:], func=AF.Ln, scale=nomlb[:, l, hd:hd + 1], bias=epsT[:, 2:3]), reads=[sgk, "nomlb", "epsT"], writes=[lfk])
                    yield
                    bcum, bck = B0, B0k
                    P.emit("dve", I(V.tensor_tensor_scan, out=bcum[:], data0=resetm, data1=lf[:], initial=0.0, op0=ALU.mult, op1=ALU.add),
                           reads=[lfk, "cstt"], writes=[bck])
                    yield
                    bc, bcK = B3, B3k
                    P.emit("dve", I(TT, out=b3(bc), in0=b3(bcum), in1=b3(bcum)[:, :, 63:64].broadcast_to([128, 8, 64]), op=ALU.subtract),
                           reads=[bck], writes=[bcK])
                    yield
                    P.emit("act", I(ACT, out=adec[:, hd, :], in_=b3(bcum)[:, :, 63], func=AF.Exp), reads=[bck], writes=[f"adec{hd}"])
                    P.dma("sp", f"sAd{hd}", I(nc.sync.dma_start, out=sAd[t][:, hd * 8:(hd + 1) * 8], in_=adec[:, hd, :]), reads=[f"adec{hd}"], writes=[f"sAd{t}_{hd}"])
                    yield
                    Ep, Epk = B0, B0k
                    P.emit("act", I(ACT, out=Ep[:], in_=bc[:], func=AF.Exp), reads=[bcK, bck], writes=[Epk])
                    P.dma("sp", f"sEp{th}", I(nc.sync.dma_start, out=sEp[t][:, h5], in_=Ep[:]), reads=[Epk], writes=[f"sEp{t}_{hd}"])
                    yield
                    Em, Emk = B3, B3k
                    P.emit("act", I(ACT, out=Em[:], in_=bc[:], func=AF.Exp, scale=-1.0), reads=[bcK], writes=[Emk])
                    yield
                    P.emit("dve", I(TT, out=kh_bf[:, hd, :], in0=kk[:], in1=Em[:], op=ALU.mult), reads=[kkk, Emk], writes=[f"kh{hd}"])
                    P.dma("sp", f"sKh{hd}", I(nc.sync.dma_start, out=sKh[t][:, h5], in_=kh_bf[:, hd, :]), reads=[f"kh{hd}"], writes=[f"sKh{t}_{hd}"])
                    yield
                    P.emit("pe", [I(T.transpose, pT[:, s * 128:(s + 1) * 128], kh_bf[:, hd, s * 128:(s + 1) * 128], ident[:]) for s in range(4)],
                           reads=[f"kh{hd}", "ident"], writes=["pT"])
                    P.emit("act", I(A.copy, out=kh_tm[:, :, hs], in_=pT[:, 0:512].rearrange("p (s d) -> p s d", d=128)),
                           reads=["pT"], writes=[f"khtm{hd}"])
                    P.dma("sp", f"sKt{hd}", I(nc.sync.dma_start, out=sKt[t].rearrange("p (s d) -> p s d", d=512)[:, :, hs], in_=kh_tm[:, :, hs]),
                          reads=[f"khtm{hd}"], writes=[f"sKt{t}_{hd}"])
                    yield
                    ops = []
                    for n in range(8):
                        pr, base = n // 2, (n % 2) * 64
                        bankp = pb[pnA if n % 2 == 0 else pnB]
                        ops.append(I(MM, bankp[:, pr * 128:(pr + 1) * 128], lhsT=kh_tm[base:base + 64, pr, hs],
                                     rhs=v_tm[base:base + 64, pr, hs], start=True, stop=True))
                    P.emit("pe", ops, reads=[f"khtm{hd}", "v_tm"], writes=[f"pb{pnA}", f"pb{pnB}"])
                    for n in range(8):
                        cur = scur[hd]
                        bankp = pb[pnA if n % 2 == 0 else pnB]
                        P.emit("dve", I(V.scalar_tensor_tensor, out=Sst[:, hd, 1 - cur, :], in0=Sst[:, hd, cur, :], scalar=adec[:, hd, n:n + 1],
                                        in1=bankp[:, (n // 2) * 128:(n // 2 + 1) * 128], op0=ALU.mult, op1=ALU.add),
                               reads=[f"S{hd}_{cur}", f"adec{hd}", f"pb{pnA}", f"pb{pnB}"], writes=[f"S{hd}_{1 - cur}"])
                        scur[hd] = 1 - cur
                    yield

                def load_rec(t):
                    P.dma("sp", "lKh", I(nc.sync.dma_start, out=kh_bf[:].rearrange("p h n -> p (h n)"), in_=sKh[t]),
                          reads=[f"sKh{t}_{h}" for h in range(4)], writes=[f"kh{h}" for h in range(4)])
                    P.dma("sp", "lKt", I(nc.sync.dma_start, out=kh_tm[:].rearrange("p s n -> p (s n)"), in_=sKt[t]),
                          reads=[f"sKt{t}_{h}" for h in range(4)], writes=[f"khtm{h}" for h in range(4)])
                    P.dma("sp", "lVt", I(nc.sync.dma_start, out=v_tm[:].rearrange("p s n -> p (s n)"), in_=sVt[t]), reads=[f"sVt{t}"], writes=["v_tm"])
                    P.dma("sp", "lAd", I(nc.sync.dma_start, out=adec[:].rearrange("p h n -> p (h n)"), in_=sAd[t]),
                          reads=[f"sAd{t}_{h}" for h in range(4)], writes=[f"adec{h}" for h in range(4)])

                def load_ep(hd, th, t):
                    B2, B2k = TB[th][2]
                    P.dma("sp", f"lEp{th}", I(nc.sync.dma_start, out=B2[:], in_=sEp[t][:, hd * 512:(hd + 1) * 512]), reads=[f"sEp{t}_{hd}"], writes=[B2k])

                def rec_main(hd, th, t, preloaded=False):
                    (B0, B0k), (B1, B1k), (B2, B2k), (B3, B3k) = TB[th]
                    pO, pOk = (pb[4], "pb4") if th == 0 else (pb[2], "pb2")
                    aT, aTk = attnT[:, th, :], f"attnT{th}"
                    hs = slice(hd * 128, (hd + 1) * 128)
                    Ep, Epk = B2, B2k
                    if not preloaded:
                        load_ep(hd, th, t)
                        yield
                    bankq, bqk = proj_fm(512 + hd * 128)
                    P.emit("dve", I(TT, out=qt_bf[:, hd, :], in0=bankq[:], in1=Ep[:], op=ALU.mult), reads=[bqk, Epk], writes=[f"qt{hd}"])
                    yield
                    P.emit("pe", [I(MM, pb[3][:, pr * 128:(pr + 1) * 128], lhsT=kh_bf[:, hd, pr * 128:(pr + 1) * 128],
                                    rhs=qt_bf[:, hd, pr * 128:(pr + 1) * 128], start=True, stop=True) for pr in range(4)],
                           reads=[f"kh{hd}", f"qt{hd}"], writes=["pb3"])
                    P.emit("dve", I(TT, out=aT, in0=pb[3][:], in1=recmask, op=ALU.mult), reads=["pb3", "cstt"], writes=[aTk])
                    yield
                    ops = []
                    for n in range(8):
                        pr, base = n // 2, (n % 2) * 64
                        bankp = pb[5 + n % 2]
                        ops.append(I(MM, bankp[:, pr * 128:(pr + 1) * 128], lhsT=kh_tm[base:base + 64, pr, hs],
                                     rhs=v_tm[base:base + 64, pr, hs], start=True, stop=True))
                    P.emit("pe", ops, reads=[f"khtm{hd}", "v_tm"], writes=["pb5", "pb6"])
                    for n in range(8):
                        cur = scur[hd]
                        bankp = pb[5 + n % 2]
                        P.emit("act", I(ACT, out=Sbf[:, th, n, :], in_=Sst[:, hd, cur, :], func=AF.Identity, scale=adec[:, hd, n:n + 1]),
                               reads=[f"S{hd}_{cur}", f"adec{hd}"], writes=[f"Sbf{th}_{n}"])
                        P.emit("dve", I(V.scalar_tensor_tensor, out=Sst[:, hd, 1 - cur, :], in0=Sst[:, hd, cur, :], scalar=adec[:, hd, n:n + 1],
                                        in1=bankp[:, (n // 2) * 128:(n // 2 + 1) * 128], op0=ALU.mult, op1=ALU.add),
                               reads=[f"S{hd}_{cur}", f"adec{hd}", "pb5", "pb6"], writes=[f"S{hd}_{1 - cur}"])
                        scur[hd] = 1 - cur
                    yield
                    ops = []
                    for pr in range(4):
                        ops.append(I(MM, pO[:, pr * 128:(pr + 1) * 128], lhsT=v_tm[:, pr, hs], rhs=aT[:, pr * 128:(pr + 1) * 128], start=True, stop=False))
                        for n in (2 * pr, 2 * pr + 1):
                            ops.append(I(MM, pO[:, n * 64:(n + 1) * 64], lhsT=Sbf[:, th, n, :], rhs=qt_bf[:, hd, n * 64:(n + 1) * 64],
                                         start=False, stop=(n % 2 == 1)))
                    P.emit("pe", ops, reads=["v_tm", aTk, f"qt{hd}"] + [f"Sbf{th}_{n}" for n in range(8)], writes=[pOk])
                    yield
                    osq, osqk = B2, B2k
                    P.emit("act", I(ACT, out=osq[:], in_=pO[:], func=AF.Square), reads=[pOk], writes=[osqk])
                    yield
                    P.emit("pe", I(MM, pb[3][:], lhsT=ones_r[:], rhs=osq[:], start=True, stop=True), reads=[osqk, "ones_r"], writes=["pb3"])
                    sd, sdk = B0, B0k
                    P.emit("act", I(ACT, out=sd[:], in_=pb[3][:], func=AF.Ln, bias=epsT[:, 0:1]), reads=["pb3", "epsT"], writes=[sdk])
                    yield
                    rstd, rsk = B0, B0k
                    P.emit("act", I(ACT, out=rstd[:], in_=sd[:], func=AF.Exp, scale=-0.5), reads=[sdk], writes=[rsk])
                    yield
                    t1, t1k = B3, B3k
                    P.emit("dve", I(TT, out=t1[:], in0=pO[:], in1=rstd[:], op=ALU.mult), reads=[pOk, rsk], writes=[t1k])
                    yield
                    bankg, bgk = proj_fm(2048 + hd * 128)
                    sgt, sgtk = B1, B1k
                    P.emit("act", I(ACT, out=sgt[:], in_=bankg[:], func=AF.Silu), reads=[bgk], writes=[sgtk])
                    yield
                    P.emit("dve", I(V.scalar_tensor_tensor, out=ymix[:, 2 + hd, :], in0=t1[:], scalar=normg, in1=sgt[:], op0=ALU.mult, op1=ALU.mult),
                           reads=[t1k, sgtk, "part"], writes=[f"ymix{2 + hd}"])
                    yield

                def conv_glu():
                    for c in range(2):
                        bankg, bgk = proj_fm(256 + c * 128)
                        sg, sgk = wtmp()
                        P.emit("act", I(ACT, out=sg[:], in_=bankg[:], func=AF.Sigmoid), reads=[bgk], writes=[sgk])
                        bankv, bvk = proj_fm(c * 128)
                        P.emit("dve", I(TT, out=ubuf[:, c, 30:30 + TL], in0=bankv[:], in1=sg[:], op=ALU.mult), reads=[bvk, sgk, "ubuf"], writes=["ubuf"])

                def conv_rest():
                    usq = []
                    for c in range(2):
                        P.emit("pe", [I(MM, pb[5 + c][:], lhsT=diag[:, c * 31 + j, :], rhs=ubuf[:, c, j:j + TL], start=(j == 0), stop=(j == 30)) for j in range(31)],
                               reads=["diag", "ubuf"], writes=[f"pb{5 + c}"])
                        P.emit("act", I(ACT, out=uc[:, c, :], in_=pb[5 + c][:], func=AF.Identity, bias=convb[:, c:c + 1]), reads=[f"pb{5 + c}", "part"], writes=[f"uc{c}"])
                        q_, qk_ = wtmp()
                        P.emit("act", I(ACT, out=q_[:], in_=uc[:, c, :], func=AF.Square), reads=[f"uc{c}"], writes=[qk_])
                        usq.append((q_, qk_))
                    P.emit("pe", [I(MM, pb[3][:], lhsT=ones_c[:], rhs=uc[:, c, :], start=(c == 0), stop=(c == 1)) for c in range(2)]
                           + [I(MM, pb[4][:], lhsT=ones_c[:], rhs=usq[c][0][:], start=(c == 0), stop=(c == 1)) for c in range(2)],
                           reads=["uc0", "uc1", usq[0][1], usq[1][1], "ones_c"], writes=["pb3", "pb4"])
                    rstd, rsk, nmr, nmk = ln_stats(0)
                    for c in range(2):
                        ta, tak = wtmp()
                        P.emit("dve", I(TT, out=ta[:], in0=uc[:, c, :], in1=rstd[:], op=ALU.mult), reads=[f"uc{c}", rsk], writes=[tak])
                        tb, tbk = wtmp()
                        P.emit("pool", I(G.tensor_tensor, out=tb[:], in0=ta[:], in1=nmr[:], op=ALU.add), reads=[tak, nmk], writes=[tbk])
                        P.emit("act", I(ACT, out=ymix[:, c, :], in_=tb[:], func=AF.Silu, scale=convg[:, c:c + 1], bias=convlb[:, c:c + 1]),
                               reads=[tbk, "part"], writes=[f"ymix{c}"])
                    P.emit("pool", I(G.tensor_copy, out=ubuf[:, :, 0:30], in_=ubuf[:, :, TL:TL + 30]), reads=["ubuf"], writes=["ubuf"])


                def att_proj():
                    for c in range(2):
                        bq, bqk = proj_fm(2560 + c * 128)
                        P.emit("act", I(ACT, out=qTa[:, c, :], in_=bq[:], func=AF.Copy, scale=0.125), reads=[bqk], writes=["qTa"])
                        bk_, bkk = proj_fm(2816 + c * 128)
                        P.emit("dve", I(V.tensor_copy, out=kTa[:, c, TL:2 * TL], in_=bk_[:]), reads=[bkk, "kTa"], writes=["kTa"])
                    for s2 in range(2):
                        bank, bkey = gbank()
                        ops = []
                        for ss in range(2):
                            s = s2 * 2 + ss
                            for k in range(8):
                                ops.append(I(MM, bank[:, ss * 256:(ss + 1) * 256], lhsT=hbf[:, k, s * 128:(s + 1) * 128], rhs=winb[:, k, 2048:2304],
                                             start=(k == 0), stop=(k == 7)))
                        P.emit("pe", ops, reads=wsrc(3072, 3328)[2] + ["hbf"], writes=[bkey])
                        P.emit("act", I(A.copy, out=vat[:, 4 + 2 * s2:6 + 2 * s2, :], in_=bank[:].rearrange("p (s d) -> p s d", d=256)),
                               reads=[bkey, "vat"], writes=["vat"])

                def att_main(t):
                    obanks = {}

                    def stage1(j, c, hh2):
                        J = 4 * t + j
                        nh = max(0, 4 - J)
                        hh = 2 * c + hh2
                        base = hh2 * 64
                        bA, bB = (5, 6) if hh2 == 0 else (3, 4)
                        tS, PT, tSk, PTk = tS2[:, hh2], PT2[:, hh2], f"tS{hh2}", f"PT{hh2}"
                        ops = []
                        for r in range(5):
                            bankS = pb[bA] if r < 4 else pb[bB]
                            ops.append(I(MM, bankS[:, (r % 4) * 128:(r % 4 + 1) * 128], lhsT=kTa[base:base + 64, c, (j + r) * 128:(j + r + 1) * 128],
                                         rhs=qTa[base:base + 64, c, j * 128:(j + 1) * 128], start=True, stop=True))
                        P.emit("pe", ops, reads=["kTa", "qTa"], writes=[f"pb{bA}", f"pb{bB}"])
                        P.emit("dve", I(TT, out=tS[:, 0:4, :], in0=pb[bA][:].rearrange("p (r q) -> p r q", q=128),
                                        in1=biasb[:, hh, 0:4, :], op=ALU.add), reads=[f"pb{bA}", "biasb", tSk], writes=[tSk])
                        P.emit("dve", I(TT, out=tS[:, 4, :], in0=pb[bB][:, 0:128], in1=biasb[:, hh, 4, :], op=ALU.add),
                               reads=[f"pb{bB}", "biasb", tSk], writes=[tSk])
                        if nh > 0:
                            P.emit("act", I(ACT, out=PT[:, 0:nh, :], in_=tS[:, 0:nh, :], func=AF.Exp, bias=hbias), reads=[tSk, "flg", PTk], writes=[PTk])
                        P.emit("act", I(ACT, out=PT[:, nh:5, :], in_=tS[:, nh:5, :], func=AF.Exp), reads=[tSk, PTk], writes=[PTk])

                    def stage2(j, c, hh2):
                        hh = 2 * c + hh2
                        base = hh2 * 64
                        PT, PTk = PT2[:, hh2], f"PT{hh2}"
                        if (j, c) not in obanks:
                            obanks[(j, c)] = gbank()
                        obank, obk = obanks[(j, c)]
                        ops = []
                        for r in range(5):
                            ops.append(I(MM, obank[base:base + 64, 0:128], lhsT=vat[:, j + r, hh * 64:(hh + 1) * 64], rhs=PT[:, r, :],
                                         start=(r == 0), stop=(r == 4)))
                        for r in range(5):
                            ops.append(I(MM, obank[base:base + 64, 128:256], lhsT=ones_b[:], rhs=PT[:, r, :], start=(r == 0), stop=(r == 4)))
                        P.emit("pe", ops, reads=["vat", PTk, "ones_b"], writes=[obk])
                        if hh2 == 1:
                            P.emit("dve", I(V.reciprocal, out=rrec[:], in_=obank[:, 128:256]), reads=[obk], writes=["rrec"])
                            P.emit("dve", I(TT, out=ymix[:, 6 + c, j * 128:(j + 1) * 128], in0=obank[:, 0:128], in1=rrec[:], op=ALU.mult),
                                   reads=[obk, "rrec", f"ymix{6 + c}"], writes=[f"ymix{6 + c}"])

                    prev = None
                    for it in [(j, c, hh2) for j in range(4) for c in range(2) for hh2 in range(2)]:
                        stage1(*it)
                        yield
                        if prev is not None:
                            stage2(*prev)
                            yield
                        prev = it
                    stage2(*prev)
                    yield

                def att_shift():
                    P.emit("pool", I(G.tensor_copy, out=kTa[:, :, 0:TL], in_=kTa[:, :, TL:2 * TL]), reads=["kTa"], writes=["kTa"])
                    P.emit("pool", I(G.tensor_copy, out=vat[:, 0:4, :], in_=vat[:, 4:8, :]), reads=["vat"], writes=["vat"])


                def finish_tile(t):
                    tsl = slice(t * TL, (t + 1) * TL)
                    ymk = [f"ymix{i}" for i in range(8)]
                    if t == 0 and l == 0:
                        tap("ymix", ymix[:], ymk)
                    for m in range(8):
                        bank, bkey = gbank()
                        P.emit("pe", [I(MM, bank[:], lhsT=woutb[:, k, m * 128:(m + 1) * 128], rhs=ymix[:, k, :], start=(k == 0), stop=(k == 7)) for k in range(8)],
                               reads=WOUTK + ymk, writes=[bkey])
                        P.emit("dve", I(V.scalar_tensor_tensor, out=xt[:, m, :], in0=bank[:], scalar=G1[:, m:m + 1], in1=xt[:, m, :], op0=ALU.mult, op1=ALU.add),
                               reads=[bkey, f"vG1_{pm}", "xt"], writes=["xt"])
                    return ln_stats_part(xt, "xt")

                def ln1_tail(t, st):
                    tsl = slice(t * TL, (t + 1) * TL)
                    yield from ln_apply_gen(xt, "xt", ln1g, ln1b, st, bufs=[(uc[:, 0, :], "uc0"), (uc[:, 1, :], "uc1")])
                    if t == 0 and l == 0:
                        tap("x1", xt[:], ["xt"])
                    P.dma("sp", "xst", I(nc.sync.dma_start, out=fm(scrA[:, tsl]), in_=xt[:]), reads=["xt"], writes=["scrA"])
                    yield


                pay = xt[:].rearrange("p k n -> p (k n)")[:, 0:PAYW]
                interleave([prefetch_h(0)])
                for t in range(NTILE):
                    rec_v(t)
                    if t == NTILE - 1:
                        conv_glu()
                        att_proj()
                        interleave([rec_pre(h, h, t) for h in range(4)])
                    else:
                        interleave([rec_pre(h, h, t) for h in range(4)] + [prefetch_h(t + 1)])
                for hd in range(4):
                    P.emit("dve", I(V.tensor_scalar, out=pay[:, hd * 128:(hd + 1) * 128], in0=Sst[:, hd, scur[hd], :], scalar1=isA, scalar2=None, op0=ALU.mult),
                           reads=[f"S{hd}_{scur[hd]}", "flg"], writes=["xt"])
                P.emit("dve", I(V.tensor_scalar, out=pay[:, 512:1536].rearrange("p (c n) -> p c n", c=2), in0=kTa[:, :, TL:2 * TL], scalar1=isA, scalar2=None, op0=ALU.mult),
                       reads=["kTa", "flg", "xt"], writes=["xt"])
                P.emit("dve", I(V.tensor_scalar, out=pay[:, 1536:2560].rearrange("p (s n) -> p s n", s=4), in0=vat[:, 4:8, :], scalar1=isA, scalar2=None, op0=ALU.mult),
                       reads=["vat", "flg", "xt"], writes=["xt"])
                P.emit("dve", I(V.tensor_scalar, out=pay[:, 2560:2620].rearrange("p (c n) -> p c n", c=2), in0=ubuf[:, :, TL:TL + 30], scalar1=isA, scalar2=None, op0=ALU.mult),
                       reads=["ubuf", "flg", "xt"], writes=["xt"])
                P.dma("pool", "bnc", I(G.dma_start, out=bounce[l].ap(), in_=pay), reads=["xt"], writes=["bounce"])
                P.dma("pool", "cc", I(G.collective_compute, "AllReduce", ALU.add, replica_groups=[[2 * i, 2 * i + 1] for i in range(n_cores // 2)],
                                      ins=[bounce[l].ap().opt()], outs=[gath[l].ap().opt()]), reads=["bounce"], writes=["gath"], inc=1)
                P.dma("pool", "gth", I(G.dma_start, out=pay, in_=gath[l].ap()), reads=["gath", "xt"], writes=["xt"])
                for hd in range(4):
                    P.emit("dve", I(V.tensor_scalar, out=Sst[:, hd, 0, :], in0=pay[:, hd * 128:(hd + 1) * 128], scalar1=isB, scalar2=None, op0=ALU.mult),
                           reads=["xt", "flg", f"S{hd}_0", f"S{hd}_1"], writes=[f"S{hd}_0"])
                    scur[hd] = 0
                P.emit("dve", I(V.tensor_scalar, out=kTa[:, :, 0:TL], in0=pay[:, 512:1536].rearrange("p (c n) -> p c n", c=2), scalar1=isB, scalar2=None, op0=ALU.mult),
                       reads=["xt", "flg", "kTa"], writes=["kTa"])
                P.emit("dve", I(V.tensor_scalar, out=vat[:, 0:4, :], in0=pay[:, 1536:2560].rearrange("p (s n) -> p s n", s=4), scalar1=isB, scalar2=None, op0=ALU.mult),
                       reads=["xt", "flg", "vat"], writes=["vat"])
                P.emit("dve", I(V.tensor_scalar, out=ubuf[:, :, 0:30], in0=pay[:, 2560:2620].rearrange("p (c n) -> p c n", c=2), scalar1=isB, scalar2=None, op0=ALU.mult),
                       reads=["xt", "flg", "ubuf"], writes=["ubuf"])
                interleave([prefetch_h(0)])
                st_prev = None
                for t in range(NTILE):
                    if t == 0:
                        load_rec(0)
                    if t > 0:
                        interleave([rec_main(0, 0, t, True), rec_main(1, 1, t, True), ln1_tail(t - 1, st_prev)])
                    else:
                        interleave([rec_main(0, 0, t), rec_main(1, 1, t)])
                    interleave([rec_main(2, 0, t), rec_main(3, 1, t)])
                    if t + 1 < NTILE:
                        load_ep(0, 0, t + 1)
                        load_ep(1, 1, t + 1)
                    load_tile(t)
                    if t + 1 < NTILE:
                        load_rec(t + 1)
                    conv_glu()
                    conv_rest()
                    att_proj()
                    if t + 1 < NTILE:
                        interleave([att_main(t), prefetch_h(t + 1)])
                    else:
                        interleave([att_main(t)])
                    att_shift()
                    st_prev = finish_tile(t)
                interleave([ln1_tail(NTILE - 1, st_prev)])
            P.barrier()
            ck("mixer")
            with contextlib.ExitStack() as fs:
                fsb = lambda n, s, d=F32: fs.enter_context(nc.sbuf_tensor(f"{n}_{l}", s, d))
                xb = fsb("xb", [128, 8, FB])
                h2 = fsb("h2", [128, 8, FB], BF16)
                hid = fsb("hid", [128, 5, FB], BF16)
                wg = fsb("wg", [128, 8, 640], BF16)
                wu = fsb("wu", [128, 8, 640], BF16)
                w2 = fsb("w2", [128, 5, D], BF16)
                groups = [(0, 5), (5, 5), (10, 4), (14, 4), (18, 4)]
                if l + 1 < DEPTH:
                    load_wpre(l + 1)
                XBK = [f"xb{tt}" for tt in range(FT)]
                lnb2 = [fsb(f"lnb2_{i}", [128, 512]) for i in range(4)]
                lnrow = [lnb2[i][0:1, 0:256] for i in range(2)]
                modg = None
                if l + 1 < DEPTH:
                    stgF = ([fsb(f"stgF{i}", [128, 8, 256], BF16) for i in range(2)], lnrow)
                    modg = mod_gen(l + 1, stgF)

                def adv():
                    if modg is not None:
                        next(modg, None)
                WGK = [f"wg{k}" for k in range(8)]
                WUK = [f"wu{k}" for k in range(8)]
                for fb in range(NFB):
                    bsl = slice(fb * FB, (fb + 1) * FB)
                    for tt in range(FT):
                        csl = slice(tt * TL, (tt + 1) * TL)
                        P.dma("sp", f"xbl{tt}", I(nc.sync.dma_start, out=xb[:, :, csl], in_=fm(scrA[:, fb * FB + tt * TL:fb * FB + (tt + 1) * TL])),
                              reads=["scrA"], writes=[f"xb{tt}"])
                    for tt in range(FT):
                        csl = slice(tt * TL, (tt + 1) * TL)
                        for k in range(8):
                            P.emit("act", I(ACT, out=h2[:, k, csl], in_=xb[:, k, csl], func=AF.Identity, scale=A2[:, k:k + 1], bias=sh2[:, k:k + 1]),
                                   reads=[f"xb{tt}", f"vA2_{pm}", MK], writes=[f"h2_{tt}"])
                    for (f0, nf) in groups:
                        for k in range(8):
                            P.dma("pool", "wg", I(G.dma_start, out=wg[:, k, 0:nf * 128], in_=wf1_d[l][k * 128:(k + 1) * 128, f0 * 128:(f0 + nf) * 128]), writes=[f"wg{k}"])
                            P.dma("pool", "wu", I(G.dma_start, out=wu[:, k, 0:nf * 128], in_=wf1_d[l][k * 128:(k + 1) * 128, DFF + f0 * 128:DFF + (f0 + nf) * 128]), writes=[f"wu{k}"])
                        for fi in range(nf):
                            P.dma("pool", "w2", I(G.dma_start, out=w2[:, fi, :], in_=wf2_d[l][(f0 + fi) * 128:(f0 + fi + 1) * 128, :]), writes=[f"w2{fi}"])
                        W2K = [f"w2{fi}" for fi in range(nf)]
                        for fi in range(nf):
                            for tt in range(FT):
                                csl = slice(tt * TL, (tt + 1) * TL)
                                bg, bgk = gbank(5)
                                P.emit("pe", [I(MM, bg[:], lhsT=wg[:, k, fi * 128:(fi + 1) * 128], rhs=h2[:, k, csl], start=(k == 0), stop=(k == 7)) for k in range(8)],
                                       reads=WGK + [f"h2_{tt}"], writes=[bgk])
                                bu, buk = gbank(5)
                                P.emit("pe", [I(MM, bu[:], lhsT=wu[:, k, fi * 128:(fi + 1) * 128], rhs=h2[:, k, csl], start=(k == 0), stop=(k == 7)) for k in range(8)],
                                       reads=WUK + [f"h2_{tt}"], writes=[buk])
                                sg, sgk = wtmp()
                                P.emit("act", I(ACT, out=sg[:], in_=bg[:], func=AF.Silu), reads=[bgk], writes=[sgk])
                                P.emit("dve", I(TT, out=hid[:, fi, csl], in0=bu[:], in1=sg[:], op=ALU.mult), reads=[buk, sgk, "hid"], writes=["hid"])
                                if fb == 0:
                                    adv()
                        if l == 0 and fb == 0:
                            tap(f"hid{f0}", hid[:, :, 0:512], ["hid"])
                        for m in range(8):
                            for tt in range(FT):
                                csl = slice(tt * TL, (tt + 1) * TL)
                                bo, bok = gbank(5)
                                P.emit("pe", [I(MM, bo[:], lhsT=w2[:, fi, m * 128:(m + 1) * 128], rhs=hid[:, fi, csl], start=(fi == 0), stop=(fi == nf - 1)) for fi in range(nf)],
                                       reads=W2K + ["hid"], writes=[bok])
                                P.emit("dve", I(V.scalar_tensor_tensor, out=xb[:, m, csl], in0=bo[:], scalar=G2[:, m:m + 1], in1=xb[:, m, csl], op0=ALU.mult, op1=ALU.add),
                                       reads=[bok, f"vG2_{pm}", f"xb{tt}"], writes=[f"xb{tt}"])
                    if l == 0 and fb == 0:
                        tap("z2", xb[:], XBK)
                    if modg is not None:
                        for _ in modg:
                            pass
                    for t0 in range(0, FT, 2):
                        gens = []
                        for i, tt in enumerate(range(t0, min(FT, t0 + 2))):
                            csl = slice(tt * TL, (tt + 1) * TL)
                            st = ln_stats_part(xb, f"xb{tt}", csl, bk=((3, 4), (5, 6))[i],
                                               outs=((lnb2[2 * i], f"lnb2_{2 * i}"), (lnb2[2 * i + 1], f"lnb2_{2 * i + 1}")))
                            gens.append(ln_apply_gen(xb, f"xb{tt}", ln2g, ln2b, st, csl,
                                                     bufs=[(W[4 + 2 * i], f"W{4 + 2 * i}"), (W[5 + 2 * i], f"W{5 + 2 * i}")]))
                        interleave(gens)
                    if l == 0 and fb == 0:
                        tap("x_l0", xb[:], XBK)
                    P.dma("sp", "xbs", I(nc.sync.dma_start, out=fm(xdst[:, bsl]), in_=xb[:]), reads=XBK, writes=[xdk])
                if modg is not None:
                    for _ in modg:
                        pass
            P.barrier()

        try:
            for l in range(DEPTH):
                layer(l)
        except _Stop:
            pass
        P.final_wait("sp", ["outT"] + ["dbgo_" + n for n in dbg_out])
        P.run()
    return nc


def _consts():
    c = np.zeros((128, 1152), np.float32)
    c[:, 0:128] = np.eye(128, dtype=np.float32)
    s = np.arange(128)[:, None]
    t = np.arange(128)[None, :]
    m = ((s // 64 == t // 64) & (s <= t)).astype(np.float32)
    c[:, 128:640] = np.tile(m, (1, 4))
    r = np.ones((128, 512), np.float32)
    r[:, 0::64] = 0.0
    c[:, 640:1152] = r
    return c


def _pack_params(inp, depth):
    par = np.zeros((128, NPAR), np.float32)
    ch = lambda v: np.ascontiguousarray(np.asarray(v, np.float32).reshape(-1, 128).T)
    for l in range(depth):
        po = l * PL
        par[:, po:po + 48] = ch(inp["b_ada"][l])
        par[:, po + 48:po + 56] = ch(inp["ln1_g"][l])
        par[:, po + 56:po + 64] = ch(inp["ln1_b"][l])
        par[:, po + 64:po + 72] = ch(inp["ln2_g"][l])
        par[:, po + 72:po + 80] = ch(inp["ln2_b"][l])
        cw = np.asarray(inp["conv_w"][l], np.float32)
        par[:, po + 80:po + 142] = cw.reshape(31, 2, 128).transpose(2, 1, 0).reshape(128, 62)
        par[:, po + 142:po + 144] = ch(inp["conv_b"][l])
        par[:, po + 144:po + 146] = ch(inp["conv_ln_g"][l])
        par[:, po + 146:po + 148] = ch(inp["conv_ln_b"][l])
        par[:, po + 148:po + 149] = np.asarray(inp["rec_norm_g"][l], np.float32).reshape(128, 1)
    rl = np.asarray(inp["rec_lower_bound"], np.float32)
    par[:, 2 * PL:2 * PL + 8] = rl.reshape(2, 4, 128).transpose(2, 0, 1).reshape(128, 8)
    return par


def _bias_tiles(rel_bias_l):
    r = np.arange(5)[:, None, None]
    i = np.arange(128)[None, :, None]
    j = np.arange(128)[None, None, :]
    idx = np.clip((r - 4) * 128 + i - j, -128, 128) + 128
    valid = ~(((r == 0) & (i < 64) & (j >= 64)) | ((r == 4) & (i >= 64) & (j < 64)))
    tb = np.asarray(rel_bias_l, np.float32)
    g = tb[:, idx]
    g = np.where(valid[None], g, np.float32(NEG_BIG)).astype(np.float32)
    return np.ascontiguousarray(g.transpose(2, 0, 1, 3).reshape(128, 4 * 5 * 128))


def make_in_maps(inp, NT, depth, n_cores=8):
    inp = {k: np.asarray(v) for k, v in inp.items()}
    cst = _consts()
    par = _pack_params(inp, depth)
    shared = {"par": par, "cst": cst}
    for l in range(depth):
        shared[f"wada{l}"] = np.ascontiguousarray(inp["w_ada"][l], np.float32)
        shared[f"win{l}"] = np.ascontiguousarray(inp["w_in"][l], np.float32)
        shared[f"wout{l}"] = np.ascontiguousarray(inp["w_out"][l], np.float32)
        shared[f"wf1{l}"] = np.ascontiguousarray(inp["w_ffn_in"][l], np.float32)
        shared[f"wf2{l}"] = np.ascontiguousarray(inp["w_ffn_out"][l], np.float32)
        shared[f"bias{l}"] = _bias_tiles(inp["rel_bias"][l])
    maps = []
    for c in range(n_cores):
        b, half = c // 2, c % 2
        m = dict(shared)
        m["xT"] = np.ascontiguousarray(inp["x"][b, half * NT:(half + 1) * NT].T.astype(np.float32))
        m["cT"] = np.ascontiguousarray(inp["c"][b].astype(np.float32).reshape(8, 128).T)
        flg = np.zeros((128, 4), np.float32)
        flg[:, 0] = 1.0 if half == 0 else 0.0
        flg[:, 1] = 1.0 if half == 1 else 0.0
        flg[:, 2] = 0.0 if half == 1 else NEG_BIG
        m["flg"] = flg
        maps.append(m)
    return maps


_NC_CACHE = {}


def kernel(**inputs):
    T = inputs["x"].shape[1]
    B = inputs["x"].shape[0]
    NT = T // 2
    key = (NT, 2)
    if key not in _NC_CACHE:
        _NC_CACHE[key] = build(NT, 2)
    nc = _NC_CACHE[key]
    maps = make_in_maps(inputs, NT, 2)
    res = run_bass_kernel_spmd(nc, maps, core_ids=list(range(8)))
    out = np.empty((B, T, D), np.float32)
    for c in range(2 * B):
        b, half = c // 2, c % 2
        out[b, half * NT:(half + 1) * NT] = res.results[c]["outT"].T
    return out
```

```python
import contextlib
import numpy as np
import concourse.bass as bass
import concourse.mybir as mybir
from concourse.bass_utils import run_bass_kernel_spmd

F32 = mybir.dt.float32
BF16 = mybir.dt.bfloat16
AF = mybir.ActivationFunctionType
ALU = mybir.AluOpType

D = 1024
DIN = 3328
DFF = 2816
DEPTH_FULL = 2
ALPHA = (2 * DEPTH_FULL) ** 0.25
LN_EPS = 1e-5
NEG_BIG = -1e30
TL = 512
PL = 149
NPAR = 2 * PL + 8
SEM_CAP = 30000


class Prog:
    ENGS = ("pe", "act", "dve", "pool", "sp")

    def __init__(self, nc, same_engine_sync=True):
        self.nc = nc
        self.eng = {"pe": nc.tensor, "act": nc.scalar, "dve": nc.vector,
                    "pool": nc.gpsimd, "sp": nc.sync}
        self.plan = {e: [] for e in self.ENGS}
        self.seq = {e: 0 for e in self.ENGS}
        self.sems = {}
        self.waited = {e: {} for e in self.ENGS}
        self.res = {}
        self.same = same_engine_sync
        self.ctx = []
        self.dma_sems = {}
        self.ninst = 0
        self.stopped = False

    def _sem(self, name):
        cm = self.nc.semaphore(name)
        s = cm.__enter__()
        self.ctx.append(cm)
        return s

    def eng_sem(self, e, epoch):
        k = (e, epoch)
        if k not in self.sems:
            self.sems[k] = self._sem(f"s_{e}_{epoch}")
        return self.sems[k]

    def _need_wait(self, E, tok):
        if tok is None:
            return None
        if tok[0] == "eng":
            _, F, s = tok
            if F == E and (E in ("pe", "sp") or not self.same):
                return None
            key = ("eng", F)
            if self.waited[E].get(key, 0) >= s:
                return None
            self.waited[E][key] = s
            epoch, val = (s - 1) // SEM_CAP, (s - 1) % SEM_CAP + 1
            return (self.eng_sem(F, epoch), val)
        _, name, val = tok
        key = ("dma", name)
        if self.waited[E].get(key, 0) >= val:
            return None
        self.waited[E][key] = val
        return (self.dma_sems[name][0], val)

    def _deps(self, E, reads, writes, own_dma=None):
        toks = []
        for k in reads:
            r = self.res.get(k)
            if r and r["w"] is not None:
                toks.append(r["w"])
        for k in writes:
            r = self.res.get(k)
            if r:
                if r["w"] is not None:
                    toks.append(r["w"])
                toks.extend(r["r"])
        best = {}
        for t in toks:
            if t[0] == "dma" and own_dma is not None and t[1] == own_dma:
                continue
            key = (t[0], t[1])
            if key not in best or t[2] > best[key][2]:
                best[key] = t
        waits = []
        for t in best.values():
            w = self._need_wait(E, t)
            if w:
                waits.append(w)
        return waits

    def _update(self, tok, reads, writes):
        for k in reads:
            r = self.res.setdefault(k, {"w": None, "r": []})
            r["r"] = [t for t in r["r"] if not (t[0] == tok[0] and t[1] == tok[1])] + [tok]
        for k in writes:
            self.res[k] = {"w": tok, "r": []}

    def emit(self, E, fn, reads=(), writes=()):
        if self.stopped:
            return
        waits = self._deps(E, reads, writes)
        self.seq[E] += 1
        s = self.seq[E]
        sem = self.eng_sem(E, (s - 1) // SEM_CAP)
        eng = self.eng[E]

        ops = fn if isinstance(fn, list) else [fn]

        def run(waits=waits, ops=ops, sem=sem, eng=eng):
            for (ws, wv) in waits:
                eng.wait_ge(ws, wv)
            for (m_, a_, k_) in ops:
                inst = m_(*a_, **k_)
            inst.then_inc(sem, 1)
        self.plan[E].append(run)
        self._update(("eng", E, s), reads, writes)
        self.ninst += 1

    def dma(self, Q, name, fn, reads=(), writes=(), inc=16):
        if self.stopped:
            return
        if name not in self.dma_sems:
            self.dma_sems[name] = [self._sem(f"d_{name}"), 0]
        waits = self._deps(Q, reads, writes, own_dma=name)
        self.dma_sems[name][1] += inc
        val = self.dma_sems[name][1]
        sem = self.dma_sems[name][0]
        eng = self.eng[Q]

        def run(waits=waits, fn=fn, sem=sem, eng=eng, inc=inc):
            for (ws, wv) in waits:
                eng.wait_ge(ws, wv)
            m_, a_, k_ = fn
            if inc == 16:
                m_(*a_, **k_).then_inc(sem, 16)
            else:
                m_(*a_, **k_).then_inc(sem)
        self.plan[Q].append(run)
        self._update(("dma", name, val), reads, writes)
        self.ninst += 1

    def barrier(self):
        if self.stopped:
            return
        for E in self.ENGS:
            waits = []
            for F in self.ENGS:
                if F == E or F == "sp" or self.seq[F] == 0:
                    continue
                w = self._need_wait(E, ("eng", F, self.seq[F]))
                if w:
                    waits.append(w)
            for name, (sem, val) in self.dma_sems.items():
                if val > 0:
                    w = self._need_wait(E, ("dma", name, val))
                    if w:
                        waits.append(w)
            eng = self.eng[E]

            def run(waits=waits, eng=eng):
                for (ws, wv) in waits:
                    eng.wait_ge(ws, wv)
            self.plan[E].append(run)

    def final_wait(self, Q, keys):
        waits = self._deps(Q, keys, keys)
        eng = self.eng[Q]

        def run(waits=waits, eng=eng):
            for (ws, wv) in waits:
                eng.wait_ge(ws, wv)
        self.plan[Q].append(run)

    def run(self):
        nc = self.nc
        with nc.Block() as block:
            @block.tensor
            def _(e):
                for f in self.plan["pe"]:
                    f()

            @block.scalar
            def _(e):
                for f in self.plan["act"]:
                    f()

            @block.vector
            def _(e):
                for f in self.plan["dve"]:
                    f()

            @block.gpsimd
            def _(e):
                for f in self.plan["pool"]:
                    f()

            @block.sync
            def _(e):
                for f in self.plan["sp"]:
                    f()
        for cm in reversed(self.ctx):
            cm.__exit__(None, None, None)


def I(m, *a, **k):
    return (m, a, k)


class _Stop(Exception):
    pass


def build(NT, DEPTH=2, dbg=None, same_engine_sync=True, stop=None, n_cores=8):
    assert NT % TL == 0
    NTILE = NT // TL
    FB = min(NT, 2048)
    NFB = NT // FB
    FT = FB // TL
    nc = bass.Bass("TRN2", target_bir_lowering=False)
    dr = lambda n, s, kind="ExternalInput", d=F32: nc.dram_tensor(n, s, d, kind=kind).ap()
    xT = dr("xT", [D, NT])
    cT = dr("cT", [128, 8])
    par = dr("par", [128, NPAR])
    cst = dr("cst", [128, 128 + 512 + 512])
    wada = [dr(f"wada{l}", [D, 6 * D]) for l in range(DEPTH)]
    win_d = [dr(f"win{l}", [D, DIN]) for l in range(DEPTH)]
    wout_d = [dr(f"wout{l}", [D, D]) for l in range(DEPTH)]
    wf1_d = [dr(f"wf1{l}", [D, 2 * DFF]) for l in range(DEPTH)]
    wf2_d = [dr(f"wf2{l}", [DFF, D]) for l in range(DEPTH)]
    bias_d = [dr(f"bias{l}", [128, 4 * 5 * 128]) for l in range(DEPTH)]
    flg_d = dr("flg", [128, 4])
    PAYW = 512 + 1024 + 1024 + 60
    bounce = [nc.dram_tensor(f"bounce{l}", [128, PAYW], F32, kind="Internal") for l in range(DEPTH)]
    sEp = [dr(f"sEp{t}", [128, 2048], kind="Internal") for t in range(NT // TL)]
    sKh = [dr(f"sKh{t}", [128, 2048], kind="Internal", d=BF16) for t in range(NT // TL)]
    sKt = [dr(f"sKt{t}", [128, 2048], kind="Internal", d=BF16) for t in range(NT // TL)]
    sAd = [dr(f"sAd{t}", [128, 32], kind="Internal") for t in range(NT // TL)]
    sVt = [dr(f"sVt{t}", [128, 2048], kind="Internal", d=BF16) for t in range(NT // TL)]
    gath = [nc.dram_tensor(f"gath{l}", [128, PAYW], F32, kind="Internal") for l in range(DEPTH)]
    outT = dr("outT", [D, NT], kind="ExternalOutput")
    scrA = dr("scrA", [D, NT], kind="Internal")
    scrB = dr("scrB", [D, NT], kind="Internal")
    dbg_out = {}
    if dbg:
        for n, s in dbg.items():
            dbg_out[n] = dr("dbg_" + n, list(s), kind="ExternalOutput")

    P = Prog(nc, same_engine_sync)
    es = contextlib.ExitStack()
    sb = lambda n, s, d=F32: es.enter_context(nc.sbuf_tensor(n, s, d))
    ps = lambda n, s, d=F32: es.enter_context(nc.psum_tensor(n, s, d))
    fm = lambda ap: ap.rearrange("(kc p) n -> p kc n", p=128)
    V, A, G, T = nc.vector, nc.scalar, nc.gpsimd, nc.tensor
    MM = T.matmul
    ACT = A.activation

    with es:
        pb = [ps(f"pb{i}", [128, 512]) for i in range(7)]
        pT = ps("pT", [128, 1024], BF16)
        part = sb("part", [128, NPAR])
        cstt = sb("cstt", [128, 1152])
        ident = sb("ident", [128, 128], BF16)
        ones_d = sb("ones_d", [128, 128])
        ones_c = sb("ones_c", [128, 128])
        ones_r = sb("ones_r", [128, 128])
        ones_b = sb("ones_b", [128, 64], BF16)
        epsT = sb("epsT", [128, 4])
        cact = sb("cact", [128, 8], BF16)
        cf = sb("cf", [128, 8])
        modv2 = sb("modv", [128, 2, 48])
        cactf = sb("cactf", [128, 8])
        vec2 = sb("vec", [128, 2, 8, 8])
        lbt = sb("lbt", [128, 8, 4])
        omlb = sb("omlb", [128, 2, 4])
        W = [sb(f"W{i}", [128, 512]) for i in range(8)]
        wpre = sb("wpre", [128, 8, 1024], BF16)

        def load_wpre(lw):
            for k in range(8):
                P.dma("pool", "wpre", I(G.dma_start, out=wpre[:, k, :], in_=win_d[lw][k * 128:(k + 1) * 128, 1024:2048]), writes=[f"wpre{k}"])
        WPK = [f"wpre{k}" for k in range(8)]
        rstdT = sb("rstdT", [128, 512])
        nmrT = sb("nmrT", [128, 512])
        wctr = [0]

        def wtmp():
            i = wctr[0] % len(W)
            wctr[0] += 1
            return W[i], f"W{i}"

        gctr = [0]

        def gbank(n=2):
            i = gctr[0] % n
            gctr[0] += 1
            return pb[i], f"pb{i}"

        recmask = cstt[:, 128:640]
        resetm = cstt[:, 640:1152]
        b3 = lambda a: a[:].rearrange("p (c t) -> p c t", t=64)

        P.dma("sp", "ld0", I(nc.sync.dma_start, out=part[:], in_=par), writes=["part"])
        P.dma("sp", "ld1", I(nc.sync.dma_start, out=cstt[:], in_=cst), writes=["cstt"])
        P.dma("sp", "ld2", I(nc.sync.dma_start, out=cf[:], in_=cT), writes=["cf"])
        flg = sb("flgs", [128, 4])
        P.dma("sp", "ld3", I(nc.sync.dma_start, out=flg[:], in_=flg_d), writes=["flg"])
        isA, isB, hbias = flg[:, 0:1], flg[:, 1:2], flg[:, 2:3]
        P.emit("pool", I(G.memset, ones_d[:], 1.0 / D), writes=["ones_d"])
        P.emit("pool", I(G.memset, ones_c[:], 1.0 / 256), writes=["ones_c"])
        P.emit("pool", I(G.memset, ones_r[:], 1.0 / 128), writes=["ones_r"])
        P.emit("pool", I(G.memset, ones_b[:], 1.0), writes=["ones_b"])
        P.emit("pool", I(G.memset, epsT[:, 0:1], LN_EPS), writes=["epsT"])
        P.emit("pool", I(G.memset, epsT[:, 1:2], LN_EPS / (ALPHA * ALPHA)), writes=["epsT"])
        P.emit("pool", I(G.memset, epsT[:, 2:3], 1.0), writes=["epsT"])
        P.emit("dve", I(V.tensor_copy, out=ident[:], in_=cstt[:, 0:128]), reads=["cstt"], writes=["ident"])
        P.emit("act", I(ACT, out=cact[:], in_=cf[:], func=AF.Silu), reads=["cf"], writes=["cact"])
        P.emit("act", I(ACT, out=cactf[:], in_=cf[:], func=AF.Silu), reads=["cf"], writes=["cactf"])
        rl = part[:, 2 * PL:2 * PL + 8].rearrange("p (l c) -> p l c", c=4)
        r0, r1 = rl[:, 0, :], rl[:, 1, :]
        mx, e0, e1, ssum, rs, s0, s1, c1 = [lbt[:, i, :] for i in range(8)]
        TT = V.tensor_tensor
        P.emit("dve", I(V.tensor_max, out=mx, in0=r0, in1=r1), reads=["part"], writes=["lb_mx"])
        P.emit("dve", I(TT, out=e0, in0=r0, in1=mx, op=ALU.subtract), reads=["part", "lb_mx"], writes=["lb_e0"])
        P.emit("dve", I(TT, out=e1, in0=r1, in1=mx, op=ALU.subtract), reads=["part", "lb_mx"], writes=["lb_e1"])
        P.emit("act", I(ACT, out=e0, in_=e0, func=AF.Exp), reads=["lb_e0"], writes=["lb_e0"])
        P.emit("act", I(ACT, out=e1, in_=e1, func=AF.Exp), reads=["lb_e1"], writes=["lb_e1"])
        P.emit("dve", I(TT, out=ssum, in0=e0, in1=e1, op=ALU.add), reads=["lb_e0", "lb_e1"], writes=["lb_s"])
        P.emit("dve", I(V.reciprocal, out=rs, in_=ssum), reads=["lb_s"], writes=["lb_rs"])
        P.emit("dve", I(TT, out=s0, in0=e0, in1=rs, op=ALU.mult), reads=["lb_e0", "lb_rs"], writes=["lb_s0"])
        P.emit("dve", I(TT, out=s1, in0=e1, in1=rs, op=ALU.mult), reads=["lb_e1", "lb_rs"], writes=["lb_s1"])
        P.emit("dve", I(TT, out=c1, in0=s0, in1=s1, op=ALU.add), reads=["lb_s0", "lb_s1"], writes=["lb_c1"])
        P.emit("dve", I(TT, out=mx, in0=s0, in1=s0, op=ALU.subtract), reads=["lb_s0"], writes=["lb_mx"])
        P.emit("dve", I(TT, out=c1, in0=c1, in1=s0, op=ALU.subtract), reads=["lb_c1", "lb_s0"], writes=["lb_c1"])
        P.emit("dve", I(V.tensor_scalar, out=omlb[:, 0, :], in0=mx, scalar1=-1.0, scalar2=1.0, op0=ALU.mult, op1=ALU.add),
               reads=["lb_mx"], writes=["omlb"])
        P.emit("dve", I(V.tensor_scalar, out=omlb[:, 1, :], in0=c1, scalar1=-1.0, scalar2=1.0, op0=ALU.mult, op1=ALU.add),
               reads=["lb_c1", "omlb"], writes=["omlb"])
        nomlb = sb("nomlb", [128, 2, 4])
        P.emit("dve", I(V.tensor_scalar, out=nomlb[:], in0=omlb[:], scalar1=-1.0, scalar2=None, op0=ALU.mult), reads=["omlb"], writes=["nomlb"])

        def tap(name, src_ap, keys):
            if name in dbg_out:
                P.dma("pool", "dbg_" + name, I(G.dma_start, out=dbg_out[name], in_=src_ap), reads=keys, writes=["dbgo_" + name])

        def ln_stats(epscol, bk=(3, 4), outs=None):
            pm_, pq_ = pb[bk[0]], pb[bk[1]]
            km_, kq_ = f"pb{bk[0]}", f"pb{bk[1]}"
            m2, m2k = wtmp()
            P.emit("act", I(ACT, out=m2[:], in_=pm_[:], func=AF.Square), reads=[km_], writes=[m2k])
            var, vark = wtmp()
            P.emit("dve", I(TT, out=var[:], in0=pq_[:], in1=m2[:], op=ALU.subtract), reads=[kq_, m2k], writes=[vark])
            sd, sdk = wtmp()
            P.emit("act", I(ACT, out=sd[:], in_=var[:], func=AF.Ln, bias=epsT[:, epscol:epscol + 1]), reads=[vark, "epsT"], writes=[sdk])
            (rstd, rsk), (nmr, nmk) = outs if outs else ((rstdT, "rstdT"), (nmrT, "nmrT"))
            P.emit("act", I(ACT, out=rstd[:], in_=sd[:], func=AF.Exp, scale=-0.5), reads=[sdk], writes=[rsk])
            P.emit("dve", I(V.scalar_tensor_tensor, out=nmr[:], in0=pm_[:], scalar=-1.0, in1=rstd[:], op0=ALU.mult, op1=ALU.mult),
                   reads=[km_, rsk], writes=[nmk])
            return rstd, rsk, nmr, nmk

        def ln_stats_part(xt, xkey, csl=slice(None), bk=(3, 4), outs=None):
            for m in range(8):
                q_, qk_ = wtmp()
                P.emit("act", I(ACT, out=q_[:], in_=xt[:, m, csl], func=AF.Square), reads=[xkey], writes=[qk_])
                P.emit("pe", [I(MM, pb[bk[0]][:], lhsT=ones_d[:], rhs=xt[:, m, csl], start=(m == 0), stop=(m == 7)),
                              I(MM, pb[bk[1]][:], lhsT=ones_d[:], rhs=q_[:], start=(m == 0), stop=(m == 7))],
                       reads=[xkey, qk_, "ones_d"], writes=[f"pb{bk[0]}", f"pb{bk[1]}"])
            return ln_stats(1, bk, outs)

        def ln_apply_gen(xt, xkey, lng, lnb, st, csl=slice(None), bufs=None):
            rstd, rsk, nmr, nmk = st
            for m in range(8):
                ta, tak = bufs[0] if bufs else wtmp()
                P.emit("dve", I(TT, out=ta[:], in0=xt[:, m, csl], in1=rstd[:], op=ALU.mult), reads=[xkey, rsk], writes=[tak])
                yield
                tb, tbk = bufs[1] if bufs else wtmp()
                P.emit("pool", I(G.tensor_tensor, out=tb[:], in0=ta[:], in1=nmr[:], op=ALU.add), reads=[tak, nmk], writes=[tbk])
                yield
                P.emit("act", I(ACT, out=xt[:, m, csl], in_=tb[:], func=AF.Identity, scale=lng[:, m:m + 1], bias=lnb[:, m:m + 1]),
                       reads=[tbk, "part", xkey], writes=[xkey])
                yield

        def ln_apply(xt, xkey, lng, lnb, csl=slice(None)):
            st = ln_stats_part(xt, xkey, csl)
            for _ in ln_apply_gen(xt, xkey, lng, lnb, st, csl):
                pass

        def ck(name):
            if stop == name:
                P.stopped = True

        def interleave(gens):
            gens = list(gens)
            while gens:
                for g in list(gens):
                    try:
                        next(g)
                    except StopIteration:
                        gens.remove(g)

        def mod_gen(lm, stg):
            pm = lm % 2
            po_ = lm * PL
            NG = 24
            stgw, rowb = stg
            def ld(g):
                P.dma("pool", f"wa{g % 2}", I(G.dma_start, out=stgw[g % 2][:], in_=fm(wada[lm][:, g * 256:(g + 1) * 256])), writes=[f"stg{g % 2}"])
            ld(0)
            for g in range(NG):
                if g + 1 < NG:
                    ld(g + 1)
                yield
                P.emit("pe", [I(MM, pb[6][0:1, 0:256], lhsT=cact[:, k:k + 1], rhs=stgw[g % 2][:, k, :], start=(k == 0), stop=(k == 7)) for k in range(8)],
                       reads=[f"stg{g % 2}", "cact"], writes=["pb6"])
                P.emit("act", I(A.copy, out=rowb[g % 2], in_=pb[6][0:1, 0:256]), reads=["pb6"], writes=[f"rowb{g % 2}", f"lnb2_{g % 2}"])
                yield
                P.emit("pe", [I(MM, pb[5][:, g * 2 + j:g * 2 + j + 1], lhsT=rowb[g % 2][:, j * 128:(j + 1) * 128], rhs=epsT[0:1, 2:3], start=True, stop=True)
                              for j in range(2)], reads=[f"rowb{g % 2}", f"lnb2_{g % 2}", "epsT"], writes=["pb5"])
                yield
            mv = modv2[:, pm, :]
            mk = f"modv{pm}"
            P.emit("dve", I(TT, out=mv, in0=pb[5][:, 0:48], in1=part[:, po_:po_ + 48], op=ALU.add), reads=["pb5", "part"], writes=[mk])
            sh1, sc1, g1, sh2, sc2, g2 = [modv2[:, pm, i * 8:(i + 1) * 8] for i in range(6)]
            A1_, B1_, G1_, G2_, A2_ = [vec2[:, pm, i, :] for i in range(5)]
            sfx = f"_{pm}"
            P.emit("dve", I(V.tensor_scalar, out=A1_, in0=sc1, scalar1=1.0, scalar2=None, op0=ALU.add), reads=[mk], writes=["vA1" + sfx])
            P.emit("dve", I(V.tensor_copy, out=B1_, in_=sh1), reads=[mk], writes=["vB1" + sfx])
            P.emit("dve", I(V.tensor_scalar, out=G1_, in0=g1, scalar1=1.0, scalar2=1.0 / ALPHA, op0=ALU.add, op1=ALU.mult), reads=[mk], writes=["vG1" + sfx])
            P.emit("dve", I(V.tensor_scalar, out=G2_, in0=g2, scalar1=1.0, scalar2=1.0 / ALPHA, op0=ALU.add, op1=ALU.mult), reads=[mk], writes=["vG2" + sfx])
            P.emit("dve", I(V.tensor_scalar, out=A2_, in0=sc2, scalar1=1.0, scalar2=None, op0=ALU.add), reads=[mk], writes=["vA2" + sfx])
            yield

        def layer(l):
            po = l * PL
            bada = part[:, po:po + 48]
            ln1g, ln1b = part[:, po + 48:po + 56], part[:, po + 56:po + 64]
            ln2g, ln2b = part[:, po + 64:po + 72], part[:, po + 72:po + 80]
            convw = part[:, po + 80:po + 142]
            convb, convg, convlb = part[:, po + 142:po + 144], part[:, po + 144:po + 146], part[:, po + 146:po + 148]
            normg = part[:, po + 148:po + 149]
            xsrc, xsk = (xT, "xT") if l == 0 else (scrB, "scrB")
            xdst, xdk = (outT, "outT") if l == DEPTH - 1 else (scrB, "scrB")

            pm = l % 2
            if l == 0:
                load_wpre(0)
            if l == 0:
                with contextlib.ExitStack() as ls:
                    stg = ([ls.enter_context(nc.sbuf_tensor(f"stgA{i}", [128, 8, 256], BF16)) for i in range(2)],
                           [ls.enter_context(nc.sbuf_tensor(f"rowA{i}", [1, 256], F32))[0:1, :] for i in range(2)])
                    for _ in mod_gen(0, stg):
                        pass
                P.barrier()
            modv = modv2[:, pm, :]
            MK = f"modv{pm}"
            sh1, sc1, g1, sh2, sc2, g2 = [modv2[:, pm, i * 8:(i + 1) * 8] for i in range(6)]
            A1, B1, G1, G2, A2 = [vec2[:, pm, i, :] for i in range(5)]
            if l == 0:
                tap("modv", modv, [MK])
            ck("mod")
            with contextlib.ExitStack() as ms:
                msb = lambda n, s, d=F32: ms.enter_context(nc.sbuf_tensor(f"{n}_{l}", s, d))
                winb = msb("winb", [128, 8, DIN - 1024], BF16)
                woutb = msb("woutb", [128, 8, D], BF16)
                biasb = msb("biasb", [128, 4, 5, 128])
                diag = msb("diag", [128, 62, 128], BF16)
                xt = msb("xt", [128, 8, TL])
                hbf = msb("hbf", [128, 8, TL], BF16)
                qt_bf = msb("qt_bf", [128, 4, TL], BF16)
                kh_bf = msb("kh_bf", [128, 4, TL], BF16)
                kh_tm = msb("kh_tm", [128, 4, 512], BF16)
                v_tm = msb("v_tm", [128, 4, 512], BF16)
                attnT = msb("attnT", [128, 2, 512], BF16)
                adec = msb("adec", [128, 4, 8])
                Sst = msb("Sst", [128, 4, 2, 128])
                Sbf = msb("Sbf", [128, 2, 8, 128], BF16)
                ymix = msb("ymix", [128, 8, TL], BF16)
                ubuf = msb("ubuf", [128, 2, 30 + TL], BF16)
                uc = msb("uc", [128, 2, TL])
                qTa = msb("qTa", [128, 2, TL], BF16)
                kTa = msb("kTa", [128, 2, 2 * TL], BF16)
                vat = msb("vat", [128, 8, 256], BF16)
                tS2 = msb("tS", [128, 2, 5, 128])
                PT2 = msb("PT", [128, 2, 5, 128], BF16)
                rrec = msb("rrec", [128, 128])
                WOUTK = [f"woutb{k}" for k in range(8)]

                WBLK = [(512, 1024), (2048, 2560), (0, 512), (2560, 3328)]

                def wsrc(c0, c1_):
                    if 1024 <= c0 and c1_ <= 2048:
                        return wpre, c0 - 1024, WPK
                    loc = c0 if c0 < 1024 else c0 - 1024
                    keys = [f"winb{k}_{b}" for k in range(8) for b, (a0, a1) in enumerate(WBLK) if a0 < c1_ and c0 < a1]
                    return winb, loc, keys
                for b, (c0, c1_) in enumerate(WBLK):
                    loc = c0 if c0 < 1024 else c0 - 1024
                    for k in range(8):
                        P.dma("pool", f"winb{b}", I(G.dma_start, out=winb[:, k, loc:loc + (c1_ - c0)], in_=win_d[l][k * 128:(k + 1) * 128, c0:c1_]), writes=[f"winb{k}_{b}"])
                for k in range(8):
                    P.dma("pool", "woutb", I(G.dma_start, out=woutb[:, k, :], in_=wout_d[l][k * 128:(k + 1) * 128, :]), writes=[f"woutb{k}"])
                P.dma("sp", "biasb", I(nc.sync.dma_start, out=biasb[:].rearrange("p h r j -> p (h r j)"), in_=bias_d[l]), writes=["biasb"])
                for cj in range(62):
                    P.emit("pool", I(G.tensor_scalar, out=diag[:, cj, :], in0=cstt[:, 0:128], scalar1=convw[:, cj:cj + 1], scalar2=None, op0=ALU.mult),
                           reads=["cstt", "part"], writes=["diag"])
                P.emit("dve", I(V.memset, Sst[:], 0.0), writes=[f"S{h}_{c}" for h in range(4) for c in range(2)])
                P.emit("dve", I(V.memset, ubuf[:], 0.0), writes=["ubuf"])
                P.emit("dve", I(V.memset, kTa[:], 0.0), writes=["kTa"])
                P.emit("dve", I(V.memset, vat[:], 0.0), writes=["vat"])
                scur = [0, 0, 0, 0]
                ck("wload")

                def proj_fm(col):
                    bank, bkey = gbank()
                    wt_, lc_, wk_ = wsrc(col, col + 128)
                    P.emit("pe", [I(MM, bank[:], lhsT=wt_[:, k, lc_:lc_ + 128], rhs=hbf[:, k, :], start=(k == 0), stop=(k == 7)) for k in range(8)],
                           reads=wk_ + ["hbf"], writes=[bkey])
                    return bank, bkey

                TB = [[(W[i], f"W{i}") for i in range(0, 4)], [(W[i], f"W{i}") for i in range(4, 8)]]
                tsf = tS2[:].rearrange("p a r q -> p (a r q)")
                phys = [(W[i], f"W{i}") for i in range(8)] + [(rstdT, "rstdT"), (nmrT, "nmrT"), (tsf[:, 0:512], "tS0"), (tsf[:, 640:1152], "tS1")]
                TBA = [[phys[3 * i], phys[3 * i + 1], phys[3 * i + 2], phys[3 * i + 2]] for i in range(4)]

                def load_tile(t):
                    tsl = slice(t * TL, (t + 1) * TL)
                    P.dma("sp", "xld", I(nc.sync.dma_start, out=xt[:], in_=fm(xsrc[:, tsl])), reads=[xsk], writes=["xt"])

                def prefetch_h(t):
                    for k in range(8):
                        P.dma("sp", f"xpf{k % 2}", I(nc.sync.dma_start, out=uc[:, k % 2, :], in_=xsrc[k * 128:(k + 1) * 128, t * TL:(t + 1) * TL]),
                              reads=[xsk], writes=[f"uc{k % 2}"])
                        P.emit("act", I(ACT, out=hbf[:, k, :], in_=uc[:, k % 2, :], func=AF.Identity, scale=A1[:, k:k + 1], bias=B1[:, k:k + 1]),
                               reads=[f"uc{k % 2}", f"vA1_{pm}", f"vB1_{pm}"], writes=["hbf"])
                        yield

                def rec_v(t):
                    for s in range(4):
                        bank, bkey = gbank()
                        P.emit("pe", [I(MM, bank[:], lhsT=hbf[:, k, s * 128:(s + 1) * 128], rhs=wpre[:, k, 512:1024], start=(k == 0), stop=(k == 7)) for k in range(8)],
                               reads=WPK + ["hbf"], writes=[bkey])
                        P.emit("act", I(A.copy, out=v_tm[:, s, :], in_=bank[:]), reads=[bkey], writes=["v_tm"])
                    P.dma("sp", "sVt", I(nc.sync.dma_start, out=sVt[t], in_=v_tm[:].rearrange("p s n -> p (s n)")), reads=["v_tm"], writes=[f"sVt{t}"])

                def rec_pre(hd, th, t):
                    (B0, B0k), (B1, B1k), _, (B3, B3k) = TBA[th]
                    pnA, pnB = (5, 6) if th % 2 == 0 else (3, 4)
                    hs = slice(hd * 128, (hd + 1) * 128)
                    h5 = slice(hd * 512, (hd + 1) * 512)
                    bank, bkey = proj_fm(1024 + hd * 128)
                    sg, sgk = B0, B0k
                    P.emit("act", I(ACT, out=sg[:], in_=bank[:], func=AF.Sigmoid, scale=-1.0), reads=[bkey], writes=[sgk])
                    yield
                    kk, kkk = B1, B1k
                    P.emit("dve", I(V.tensor_scalar, out=kk[:], in0=sg[:], scalar1=omlb[:, l, hd:hd + 1], scalar2=None, op0=ALU.mult),
                           reads=[sgk, "omlb"], writes=[kkk])
                    yield
                    lf, lfk = B3, B3k
                    P.emit("act", I(ACT, out=lf[:], in_=sg[:], func=AF.Ln, scale=nomlb[:, l, hd:hd + 1], bias=epsT[:, 2:3]), reads=[sgk, "nomlb", "epsT"], writes=[lfk])
                    yield
                    bcum, bck = B0, B0k
                    P.emit("dve", I(V.tensor_tensor_scan, out=bcum[:], data0=resetm, data1=lf[:], initial=0.0, op0=ALU.mult, op1=ALU.add),
                           reads=[lfk, "cstt"], writes=[bck])
                    yield
                    bc, bcK = B3, B3k
                    P.emit("dve", I(TT, out=b3(bc), in0=b3(bcum), in1=b3(bcum)[:, :, 63:64].broadcast_to([128, 8, 64]), op=ALU.subtract),
                           reads=[bck], writes=[bcK])
                    yield
                    P.emit("act", I(ACT, out=adec[:, hd, :], in_=b3(bcum)[:, :, 63], func=AF.Exp), reads=[bck], writes=[f"adec{hd}"])
                    P.dma("sp", f"sAd{hd}", I(nc.sync.dma_start, out=sAd[t][:, hd * 8:(hd + 1) * 8], in_=adec[:, hd, :]), reads=[f"adec{hd}"], writes=[f"sAd{t}_{hd}"])
                    yield
                    Ep, Epk = B0, B0k
                    P.emit("act", I(ACT, out=Ep[:], in_=bc[:], func=AF.Exp), reads=[bcK, bck], writes=[Epk])
                    P.dma("sp", f"sEp{th}", I(nc.sync.dma_start, out=sEp[t][:, h5], in_=Ep[:]), reads=[Epk], writes=[f"sEp{t}_{hd}"])
                    yield
                    Em, Emk = B3, B3k
                    P.emit("act", I(ACT, out=Em[:], in_=bc[:], func=AF.Exp, scale=-1.0), reads=[bcK], writes=[Emk])
                    yield
                    P.emit("dve", I(TT, out=kh_bf[:, hd, :], in0=kk[:], in1=Em[:], op=ALU.mult), reads=[kkk, Emk], writes=[f"kh{hd}"])
                    P.dma("sp", f"sKh{hd}", I(nc.sync.dma_start, out=sKh[t][:, h5], in_=kh_bf[:, hd, :]), reads=[f"kh{hd}"], writes=[f"sKh{t}_{hd}"])
                    yield
                    P.emit("pe", [I(T.transpose, pT[:, s * 128:(s + 1) * 128], kh_bf[:, hd, s * 128:(s + 1) * 128], ident[:]) for s in range(4)],
                           reads=[f"kh{hd}", "ident"], writes=["pT"])
                    P.emit("act", I(A.copy, out=kh_tm[:, :, hs], in_=pT[:, 0:512].rearrange("p (s d) -> p s d", d=128)),
                           reads=["pT"], writes=[f"khtm{hd}"])
                    P.dma("sp", f"sKt{hd}", I(nc.sync.dma_start, out=sKt[t].rearrange("p (s d) -> p s d", d=512)[:, :, hs], in_=kh_tm[:, :, hs]),
                          reads=[f"khtm{hd}"], writes=[f"sKt{t}_{hd}"])
                    yield
                    ops = []
                    for n in range(8):
                        pr, base = n // 2, (n % 2) * 64
                        bankp = pb[pnA if n % 2 == 0 else pnB]
                        ops.append(I(MM, bankp[:, pr * 128:(pr + 1) * 128], lhsT=kh_tm[base:base + 64, pr, hs],
                                     rhs=v_tm[base:base + 64, pr, hs], start=True, stop=True))
                    P.emit("pe", ops, reads=[f"khtm{hd}", "v_tm"], writes=[f"pb{pnA}", f"pb{pnB}"])
                    for n in range(8):
                        cur = scur[hd]
                        bankp = pb[pnA if n % 2 == 0 else pnB]
                        P.emit("dve", I(V.scalar_tensor_tensor, out=Sst[:, hd, 1 - cur, :], in0=Sst[:, hd, cur, :], scalar=adec[:, hd, n:n + 1],
                                        in1=bankp[:, (n // 2) * 128:(n // 2 + 1) * 128], op0=ALU.mult, op1=ALU.add),
                               reads=[f"S{hd}_{cur}", f"adec{hd}", f"pb{pnA}", f"pb{pnB}"], writes=[f"S{hd}_{1 - cur}"])
                        scur[hd] = 1 - cur
                    yield

                def load_rec(t):
                    P.dma("sp", "lKh", I(nc.sync.dma_start, out=kh_bf[:].rearrange("p h n -> p (h n)"), in_=sKh[t]),
                          reads=[f"sKh{t}_{h}" for h in range(4)], writes=[f"kh{h}" for h in range(4)])
                    P.dma("sp", "lKt", I(nc.sync.dma_start, out=kh_tm[:].rearrange("p s n -> p (s n)"), in_=sKt[t]),
                          reads=[f"sKt{t}_{h}" for h in range(4)], writes=[f"khtm{h}" for h in range(4)])
                    P.dma("sp", "lVt", I(nc.sync.dma_start, out=v_tm[:].rearrange("p s n -> p (s n)"), in_=sVt[t]), reads=[f"sVt{t}"], writes=["v_tm"])
                    P.dma("sp", "lAd", I(nc.sync.dma_start, out=adec[:].rearrange("p h n -> p (h n)"), in_=sAd[t]),
                          reads=[f"sAd{t}_{h}" for h in range(4)], writes=[f"adec{h}" for h in range(4)])

                def rec_main(hd, th, t):
                    (B0, B0k), (B1, B1k), (B2, B2k), (B3, B3k) = TB[th]
                    pO, pOk = (pb[4], "pb4") if th == 0 else (pb[2], "pb2")
                    aT, aTk = attnT[:, th, :], f"attnT{th}"
                    hs = slice(hd * 128, (hd + 1) * 128)
                    Ep, Epk = B2, B2k
                    P.dma("sp", f"lEp{th}", I(nc.sync.dma_start, out=Ep[:], in_=sEp[t][:, hd * 512:(hd + 1) * 512]), reads=[f"sEp{t}_{hd}"], writes=[Epk])
                    yield
                    bankq, bqk = proj_fm(512 + hd * 128)
                    P.emit("dve", I(TT, out=qt_bf[:, hd, :], in0=bankq[:], in1=Ep[:], op=ALU.mult), reads=[bqk, Epk], writes=[f"qt{hd}"])
                    yield
                    P.emit("pe", [I(MM, pb[3][:, pr * 128:(pr + 1) * 128], lhsT=kh_bf[:, hd, pr * 128:(pr + 1) * 128],
                                    rhs=qt_bf[:, hd, pr * 128:(pr + 1) * 128], start=True, stop=True) for pr in range(4)],
                           reads=[f"kh{hd}", f"qt{hd}"], writes=["pb3"])
                    P.emit("dve", I(TT, out=aT, in0=pb[3][:], in1=recmask, op=ALU.mult), reads=["pb3", "cstt"], writes=[aTk])
                    yield
                    ops = []
                    for n in range(8):
                        pr, base = n // 2, (n % 2) * 64
                        bankp = pb[5 + n % 2]
                        ops.append(I(MM, bankp[:, pr * 128:(pr + 1) * 128], lhsT=kh_tm[base:base + 64, pr, hs],
                                     rhs=v_tm[base:base + 64, pr, hs], start=True, stop=True))
                    P.emit("pe", ops, reads=[f"khtm{hd}", "v_tm"], writes=["pb5", "pb6"])
                    for n in range(8):
                        cur = scur[hd]
                        bankp = pb[5 + n % 2]
                        P.emit("act", I(ACT, out=Sbf[:, th, n, :], in_=Sst[:, hd, cur, :], func=AF.Identity, scale=adec[:, hd, n:n + 1]),
                               reads=[f"S{hd}_{cur}", f"adec{hd}"], writes=[f"Sbf{th}_{n}"])
                        P.emit("dve", I(V.scalar_tensor_tensor, out=Sst[:, hd, 1 - cur, :], in0=Sst[:, hd, cur, :], scalar=adec[:, hd, n:n + 1],
                                        in1=bankp[:, (n // 2) * 128:(n // 2 + 1) * 128], op0=ALU.mult, op1=ALU.add),
                               reads=[f"S{hd}_{cur}", f"adec{hd}", "pb5", "pb6"], writes=[f"S{hd}_{1 - cur}"])
                        scur[hd] = 1 - cur
                    yield
                    ops = []
                    for pr in range(4):
                        ops.append(I(MM, pO[:, pr * 128:(pr + 1) * 128], lhsT=v_tm[:, pr, hs], rhs=aT[:, pr * 128:(pr + 1) * 128], start=True, stop=False))
                        for n in (2 * pr, 2 * pr + 1):
                            ops.append(I(MM, pO[:, n * 64:(n + 1) * 64], lhsT=Sbf[:, th, n, :], rhs=qt_bf[:, hd, n * 64:(n + 1) * 64],
                                         start=False, stop=(n % 2 == 1)))
                    P.emit("pe", ops, reads=["v_tm", aTk, f"qt{hd}"] + [f"Sbf{th}_{n}" for n in range(8)], writes=[pOk])
                    yield
                    osq, osqk = B2, B2k
                    P.emit("act", I(ACT, out=osq[:], in_=pO[:], func=AF.Square), reads=[pOk], writes=[osqk])
                    yield
                    P.emit("pe", I(MM, pb[3][:], lhsT=ones_r[:], rhs=osq[:], start=True, stop=True), reads=[osqk, "ones_r"], writes=["pb3"])
                    sd, sdk = B0, B0k
                    P.emit("act", I(ACT, out=sd[:], in_=pb[3][:], func=AF.Ln, bias=epsT[:, 0:1]), reads=["pb3", "epsT"], writes=[sdk])
                    yield
                    rstd, rsk = B0, B0k
                    P.emit("act", I(ACT, out=rstd[:], in_=sd[:], func=AF.Exp, scale=-0.5), reads=[sdk], writes=[rsk])
                    yield
                    t1, t1k = B3, B3k
                    P.emit("dve", I(TT, out=t1[:], in0=pO[:], in1=rstd[:], op=ALU.mult), reads=[pOk, rsk], writes=[t1k])
                    yield
                    bankg, bgk = proj_fm(2048 + hd * 128)
                    sgt, sgtk = B1, B1k
                    P.emit("act", I(ACT, out=sgt[:], in_=bankg[:], func=AF.Silu), reads=[bgk], writes=[sgtk])
                    yield
                    P.emit("dve", I(V.scalar_tensor_tensor, out=ymix[:, 2 + hd, :], in0=t1[:], scalar=normg, in1=sgt[:], op0=ALU.mult, op1=ALU.mult),
                           reads=[t1k, sgtk, "part"], writes=[f"ymix{2 + hd}"])
                    yield

                def conv_glu():
                    for c in range(2):
                        bankg, bgk = proj_fm(256 + c * 128)
                        sg, sgk = wtmp()
                        P.emit("act", I(ACT, out=sg[:], in_=bankg[:], func=AF.Sigmoid), reads=[bgk], writes=[sgk])
                        bankv, bvk = proj_fm(c * 128)
                        P.emit("dve", I(TT, out=ubuf[:, c, 30:30 + TL], in0=bankv[:], in1=sg[:], op=ALU.mult), reads=[bvk, sgk, "ubuf"], writes=["ubuf"])

                def conv_rest():
                    usq = []
                    for c in range(2):
                        P.emit("pe", [I(MM, pb[5 + c][:], lhsT=diag[:, c * 31 + j, :], rhs=ubuf[:, c, j:j + TL], start=(j == 0), stop=(j == 30)) for j in range(31)],
                               reads=["diag", "ubuf"], writes=[f"pb{5 + c}"])
                        P.emit("act", I(ACT, out=uc[:, c, :], in_=pb[5 + c][:], func=AF.Identity, bias=convb[:, c:c + 1]), reads=[f"pb{5 + c}", "part"], writes=[f"uc{c}"])
                        q_, qk_ = wtmp()
                        P.emit("act", I(ACT, out=q_[:], in_=uc[:, c, :], func=AF.Square), reads=[f"uc{c}"], writes=[qk_])
                        usq.append((q_, qk_))
                    P.emit("pe", [I(MM, pb[3][:], lhsT=ones_c[:], rhs=uc[:, c, :], start=(c == 0), stop=(c == 1)) for c in range(2)]
                           + [I(MM, pb[4][:], lhsT=ones_c[:], rhs=usq[c][0][:], start=(c == 0), stop=(c == 1)) for c in range(2)],
                           reads=["uc0", "uc1", usq[0][1], usq[1][1], "ones_c"], writes=["pb3", "pb4"])
                    rstd, rsk, nmr, nmk = ln_stats(0)
                    for c in range(2):
                        ta, tak = wtmp()
                        P.emit("dve", I(TT, out=ta[:], in0=uc[:, c, :], in1=rstd[:], op=ALU.mult), reads=[f"uc{c}", rsk], writes=[tak])
                        tb, tbk = wtmp()
                        P.emit("pool", I(G.tensor_tensor, out=tb[:], in0=ta[:], in1=nmr[:], op=ALU.add), reads=[tak, nmk], writes=[tbk])
                        P.emit("act", I(ACT, out=ymix[:, c, :], in_=tb[:], func=AF.Silu, scale=convg[:, c:c + 1], bias=convlb[:, c:c + 1]),
                               reads=[tbk, "part"], writes=[f"ymix{c}"])
                    P.emit("pool", I(G.tensor_copy, out=ubuf[:, :, 0:30], in_=ubuf[:, :, TL:TL + 30]), reads=["ubuf"], writes=["ubuf"])


                def att_proj():
                    for c in range(2):
                        bq, bqk = proj_fm(2560 + c * 128)
                        P.emit("act", I(ACT, out=qTa[:, c, :], in_=bq[:], func=AF.Copy, scale=0.125), reads=[bqk], writes=["qTa"])
                        bk_, bkk = proj_fm(2816 + c * 128)
                        P.emit("dve", I(V.tensor_copy, out=kTa[:, c, TL:2 * TL], in_=bk_[:]), reads=[bkk, "kTa"], writes=["kTa"])
                    for s2 in range(2):
                        bank, bkey = gbank()
                        ops = []
                        for ss in range(2):
                            s = s2 * 2 + ss
                            for k in range(8):
                                ops.append(I(MM, bank[:, ss * 256:(ss + 1) * 256], lhsT=hbf[:, k, s * 128:(s + 1) * 128], rhs=winb[:, k, 2048:2304],
                                             start=(k == 0), stop=(k == 7)))
                        P.emit("pe", ops, reads=wsrc(3072, 3328)[2] + ["hbf"], writes=[bkey])
                        P.emit("act", I(A.copy, out=vat[:, 4 + 2 * s2:6 + 2 * s2, :], in_=bank[:].rearrange("p (s d) -> p s d", d=256)),
                               reads=[bkey, "vat"], writes=["vat"])

                def att_main(t):
                    obanks = {}

                    def stage1(j, c, hh2):
                        J = 4 * t + j
                        nh = max(0, 4 - J)
                        hh = 2 * c + hh2
                        base = hh2 * 64
                        bA, bB = (5, 6) if hh2 == 0 else (3, 4)
                        tS, PT, tSk, PTk = tS2[:, hh2], PT2[:, hh2], f"tS{hh2}", f"PT{hh2}"
                        ops = []
                        for r in range(5):
                            bankS = pb[bA] if r < 4 else pb[bB]
                            ops.append(I(MM, bankS[:, (r % 4) * 128:(r % 4 + 1) * 128], lhsT=kTa[base:base + 64, c, (j + r) * 128:(j + r + 1) * 128],
                                         rhs=qTa[base:base + 64, c, j * 128:(j + 1) * 128], start=True, stop=True))
                        P.emit("pe", ops, reads=["kTa", "qTa"], writes=[f"pb{bA}", f"pb{bB}"])
                        P.emit("dve", I(TT, out=tS[:, 0:4, :], in0=pb[bA][:].rearrange("p (r q) -> p r q", q=128),
                                        in1=biasb[:, hh, 0:4, :], op=ALU.add), reads=[f"pb{bA}", "biasb", tSk], writes=[tSk])
                        P.emit("dve", I(TT, out=tS[:, 4, :], in0=pb[bB][:, 0:128], in1=biasb[:, hh, 4, :], op=ALU.add),
                               reads=[f"pb{bB}", "biasb", tSk], writes=[tSk])
                        if nh > 0:
                            P.emit("act", I(ACT, out=PT[:, 0:nh, :], in_=tS[:, 0:nh, :], func=AF.Exp, bias=hbias), reads=[tSk, "flg", PTk], writes=[PTk])
                        P.emit("act", I(ACT, out=PT[:, nh:5, :], in_=tS[:, nh:5, :], func=AF.Exp), reads=[tSk, PTk], writes=[PTk])

                    def stage2(j, c, hh2):
                        hh = 2 * c + hh2
                        base = hh2 * 64
                        PT, PTk = PT2[:, hh2], f"PT{hh2}"
                        if (j, c) not in obanks:
                            obanks[(j, c)] = gbank()
                        obank, obk = obanks[(j, c)]
                        ops = []
                        for r in range(5):
                            ops.append(I(MM, obank[base:base + 64, 0:128], lhsT=vat[:, j + r, hh * 64:(hh + 1) * 64], rhs=PT[:, r, :],
                                         start=(r == 0), stop=(r == 4)))
                        for r in range(5):
                            ops.append(I(MM, obank[base:base + 64, 128:256], lhsT=ones_b[:], rhs=PT[:, r, :], start=(r == 0), stop=(r == 4)))
                        P.emit("pe", ops, reads=["vat", PTk, "ones_b"], writes=[obk])
                        if hh2 == 1:
                            P.emit("dve", I(V.reciprocal, out=rrec[:], in_=obank[:, 128:256]), reads=[obk], writes=["rrec"])
                            P.emit("dve", I(TT, out=ymix[:, 6 + c, j * 128:(j + 1) * 128], in0=obank[:, 0:128], in1=rrec[:], op=ALU.mult),
                                   reads=[obk, "rrec", f"ymix{6 + c}"], writes=[f"ymix{6 + c}"])

                    prev = None
                    for it in [(j, c, hh2) for j in range(4) for c in range(2) for hh2 in range(2)]:
                        stage1(*it)
                        yield
                        if prev is not None:
                            stage2(*prev)
                            yield
                        prev = it
                    stage2(*prev)
                    yield

                def att_shift():
                    P.emit("pool", I(G.tensor_copy, out=kTa[:, :, 0:TL], in_=kTa[:, :, TL:2 * TL]), reads=["kTa"], writes=["kTa"])
                    P.emit("pool", I(G.tensor_copy, out=vat[:, 0:4, :], in_=vat[:, 4:8, :]), reads=["vat"], writes=["vat"])


                def finish_tile(t):
                    tsl = slice(t * TL, (t + 1) * TL)
                    ymk = [f"ymix{i}" for i in range(8)]
                    if t == 0 and l == 0:
                        tap("ymix", ymix[:], ymk)
                    for m in range(8):
                        bank, bkey = gbank()
                        P.emit("pe", [I(MM, bank[:], lhsT=woutb[:, k, m * 128:(m + 1) * 128], rhs=ymix[:, k, :], start=(k == 0), stop=(k == 7)) for k in range(8)],
                               reads=WOUTK + ymk, writes=[bkey])
                        P.emit("dve", I(V.scalar_tensor_tensor, out=xt[:, m, :], in0=bank[:], scalar=G1[:, m:m + 1], in1=xt[:, m, :], op0=ALU.mult, op1=ALU.add),
                               reads=[bkey, f"vG1_{pm}", "xt"], writes=["xt"])
                    return ln_stats_part(xt, "xt")

                def ln1_tail(t, st):
                    tsl = slice(t * TL, (t + 1) * TL)
                    yield from ln_apply_gen(xt, "xt", ln1g, ln1b, st, bufs=[(uc[:, 0, :], "uc0"), (uc[:, 1, :], "uc1")])
                    if t == 0 and l == 0:
                        tap("x1", xt[:], ["xt"])
                    P.dma("sp", "xst", I(nc.sync.dma_start, out=fm(scrA[:, tsl]), in_=xt[:]), reads=["xt"], writes=["scrA"])
                    yield


                pay = xt[:].rearrange("p k n -> p (k n)")[:, 0:PAYW]
                interleave([prefetch_h(0)])
                for t in range(NTILE):
                    rec_v(t)
                    if t == NTILE - 1:
                        conv_glu()
                        att_proj()
                        interleave([rec_pre(h, h, t) for h in range(4)])
                    else:
                        interleave([rec_pre(h, h, t) for h in range(4)] + [prefetch_h(t + 1)])
                for hd in range(4):
                    P.emit("dve", I(V.tensor_scalar, out=pay[:, hd * 128:(hd + 1) * 128], in0=Sst[:, hd, scur[hd], :], scalar1=isA, scalar2=None, op0=ALU.mult),
                           reads=[f"S{hd}_{scur[hd]}", "flg"], writes=["xt"])
                P.emit("dve", I(V.tensor_scalar, out=pay[:, 512:1536].rearrange("p (c n) -> p c n", c=2), in0=kTa[:, :, TL:2 * TL], scalar1=isA, scalar2=None, op0=ALU.mult),
                       reads=["kTa", "flg", "xt"], writes=["xt"])
                P.emit("dve", I(V.tensor_scalar, out=pay[:, 1536:2560].rearrange("p (s n) -> p s n", s=4), in0=vat[:, 4:8, :], scalar1=isA, scalar2=None, op0=ALU.mult),
                       reads=["vat", "flg", "xt"], writes=["xt"])
                P.emit("dve", I(V.tensor_scalar, out=pay[:, 2560:2620].rearrange("p (c n) -> p c n", c=2), in0=ubuf[:, :, TL:TL + 30], scalar1=isA, scalar2=None, op0=ALU.mult),
                       reads=["ubuf", "flg", "xt"], writes=["xt"])
                P.dma("pool", "bnc", I(G.dma_start, out=bounce[l].ap(), in_=pay), reads=["xt"], writes=["bounce"])
                P.dma("pool", "cc", I(G.collective_compute, "AllReduce", ALU.add, replica_groups=[[2 * i, 2 * i + 1] for i in range(n_cores // 2)],
                                      ins=[bounce[l].ap().opt()], outs=[gath[l].ap().opt()]), reads=["bounce"], writes=["gath"], inc=1)
                P.dma("pool", "gth", I(G.dma_start, out=pay, in_=gath[l].ap()), reads=["gath", "xt"], writes=["xt"])
                for hd in range(4):
                    P.emit("dve", I(V.tensor_scalar, out=Sst[:, hd, 0, :], in0=pay[:, hd * 128:(hd + 1) * 128], scalar1=isB, scalar2=None, op0=ALU.mult),
                           reads=["xt", "flg", f"S{hd}_0", f"S{hd}_1"], writes=[f"S{hd}_0"])
                    scur[hd] = 0
                P.emit("dve", I(V.tensor_scalar, out=kTa[:, :, 0:TL], in0=pay[:, 512:1536].rearrange("p (c n) -> p c n", c=2), scalar1=isB, scalar2=None, op0=ALU.mult),
                       reads=["xt", "flg", "kTa"], writes=["kTa"])
                P.emit("dve", I(V.tensor_scalar, out=vat[:, 0:4, :], in0=pay[:, 1536:2560].rearrange("p (s n) -> p s n", s=4), scalar1=isB, scalar2=None, op0=ALU.mult),
                       reads=["xt", "flg", "vat"], writes=["vat"])
                P.emit("dve", I(V.tensor_scalar, out=ubuf[:, :, 0:30], in0=pay[:, 2560:2620].rearrange("p (c n) -> p c n", c=2), scalar1=isB, scalar2=None, op0=ALU.mult),
                       reads=["xt", "flg", "ubuf"], writes=["ubuf"])
                interleave([prefetch_h(0)])
                st_prev = None
                for t in range(NTILE):
                    if t == 0:
                        load_rec(0)
                    if t > 0:
                        interleave([rec_main(0, 0, t), rec_main(1, 1, t), ln1_tail(t - 1, st_prev)])
                    else:
                        interleave([rec_main(0, 0, t), rec_main(1, 1, t)])
                    interleave([rec_main(2, 0, t), rec_main(3, 1, t)])
                    load_tile(t)
                    if t + 1 < NTILE:
                        load_rec(t + 1)
                    conv_glu()
                    conv_rest()
                    att_proj()
                    if t + 1 < NTILE:
                        interleave([att_main(t), prefetch_h(t + 1)])
                    else:
                        interleave([att_main(t)])
                    att_shift()
                    st_prev = finish_tile(t)
                interleave([ln1_tail(NTILE - 1, st_prev)])
            P.barrier()
            ck("mixer")
            with contextlib.ExitStack() as fs:
                fsb = lambda n, s, d=F32: fs.enter_context(nc.sbuf_tensor(f"{n}_{l}", s, d))
                xb = fsb("xb", [128, 8, FB])
                h2 = fsb("h2", [128, 8, FB], BF16)
                hid = fsb("hid", [128, 5, FB], BF16)
                wg = fsb("wg", [128, 8, 640], BF16)
                wu = fsb("wu", [128, 8, 640], BF16)
                w2 = fsb("w2", [128, 5, D], BF16)
                groups = [(0, 5), (5, 5), (10, 4), (14, 4), (18, 4)]
                if l + 1 < DEPTH:
                    load_wpre(l + 1)
                XBK = [f"xb{tt}" for tt in range(FT)]
                lnb2 = [fsb(f"lnb2_{i}", [128, 512]) for i in range(4)]
                lnrow = [lnb2[i][0:1, 0:256] for i in range(2)]
                modg = None
                if l + 1 < DEPTH:
                    stgF = ([fsb(f"stgF{i}", [128, 8, 256], BF16) for i in range(2)], lnrow)
                    modg = mod_gen(l + 1, stgF)

                def adv():
                    if modg is not None:
                        next(modg, None)
                WGK = [f"wg{k}" for k in range(8)]
                WUK = [f"wu{k}" for k in range(8)]
                for fb in range(NFB):
                    bsl = slice(fb * FB, (fb + 1) * FB)
                    for tt in range(FT):
                        csl = slice(tt * TL, (tt + 1) * TL)
                        P.dma("sp", f"xbl{tt}", I(nc.sync.dma_start, out=xb[:, :, csl], in_=fm(scrA[:, fb * FB + tt * TL:fb * FB + (tt + 1) * TL])),
                              reads=["scrA"], writes=[f"xb{tt}"])
                    for tt in range(FT):
                        csl = slice(tt * TL, (tt + 1) * TL)
                        for k in range(8):
                            P.emit("act", I(ACT, out=h2[:, k, csl], in_=xb[:, k, csl], func=AF.Identity, scale=A2[:, k:k + 1], bias=sh2[:, k:k + 1]),
                                   reads=[f"xb{tt}", f"vA2_{pm}", MK], writes=[f"h2_{tt}"])
                    for (f0, nf) in groups:
                        for k in range(8):
                            P.dma("pool", "wg", I(G.dma_start, out=wg[:, k, 0:nf * 128], in_=wf1_d[l][k * 128:(k + 1) * 128, f0 * 128:(f0 + nf) * 128]), writes=[f"wg{k}"])
                        for k in range(8):
                            P.dma("pool", "wu", I(G.dma_start, out=wu[:, k, 0:nf * 128], in_=wf1_d[l][k * 128:(k + 1) * 128, DFF + f0 * 128:DFF + (f0 + nf) * 128]), writes=[f"wu{k}"])
                        for fi in range(nf):
                            P.dma("pool", "w2", I(G.dma_start, out=w2[:, fi, :], in_=wf2_d[l][(f0 + fi) * 128:(f0 + fi + 1) * 128, :]), writes=[f"w2{fi}"])
                        W2K = [f"w2{fi}" for fi in range(nf)]
                        for fi in range(nf):
                            for tt in range(FT):
                                csl = slice(tt * TL, (tt + 1) * TL)
                                bg, bgk = gbank(5)
                                P.emit("pe", [I(MM, bg[:], lhsT=wg[:, k, fi * 128:(fi + 1) * 128], rhs=h2[:, k, csl], start=(k == 0), stop=(k == 7)) for k in range(8)],
                                       reads=WGK + [f"h2_{tt}"], writes=[bgk])
                                bu, buk = gbank(5)
                                P.emit("pe", [I(MM, bu[:], lhsT=wu[:, k, fi * 128:(fi + 1) * 128], rhs=h2[:, k, csl], start=(k == 0), stop=(k == 7)) for k in range(8)],
                                       reads=WUK + [f"h2_{tt}"], writes=[buk])
                                sg, sgk = wtmp()
                                P.emit("act", I(ACT, out=sg[:], in_=bg[:], func=AF.Silu), reads=[bgk], writes=[sgk])
                                P.emit("dve", I(TT, out=hid[:, fi, csl], in0=bu[:], in1=sg[:], op=ALU.mult), reads=[buk, sgk, "hid"], writes=["hid"])
                                if fb == 0:
                                    adv()
                        if l == 0 and fb == 0:
                            tap(f"hid{f0}", hid[:, :, 0:512], ["hid"])
                        for m in range(8):
                            for tt in range(FT):
                                csl = slice(tt * TL, (tt + 1) * TL)
                                bo, bok = gbank(5)
                                P.emit("pe", [I(MM, bo[:], lhsT=w2[:, fi, m * 128:(m + 1) * 128], rhs=hid[:, fi, csl], start=(fi == 0), stop=(fi == nf - 1)) for fi in range(nf)],
                                       reads=W2K + ["hid"], writes=[bok])
                                P.emit("dve", I(V.scalar_tensor_tensor, out=xb[:, m, csl], in0=bo[:], scalar=G2[:, m:m + 1], in1=xb[:, m, csl], op0=ALU.mult, op1=ALU.add),
                                       reads=[bok, f"vG2_{pm}", f"xb{tt}"], writes=[f"xb{tt}"])
                    if l == 0 and fb == 0:
                        tap("z2", xb[:], XBK)
                    if modg is not None:
                        for _ in modg:
                            pass
                    for t0 in range(0, FT, 2):
                        gens = []
                        for i, tt in enumerate(range(t0, min(FT, t0 + 2))):
                            csl = slice(tt * TL, (tt + 1) * TL)
                            st = ln_stats_part(xb, f"xb{tt}", csl, bk=((3, 4), (5, 6))[i],
                                               outs=((lnb2[2 * i], f"lnb2_{2 * i}"), (lnb2[2 * i + 1], f"lnb2_{2 * i + 1}")))
                            gens.append(ln_apply_gen(xb, f"xb{tt}", ln2g, ln2b, st, csl,
                                                     bufs=[(W[4 + 2 * i], f"W{4 + 2 * i}"), (W[5 + 2 * i], f"W{5 + 2 * i}")]))
                        interleave(gens)
                    if l == 0 and fb == 0:
                        tap("x_l0", xb[:], XBK)
                    P.dma("sp", "xbs", I(nc.sync.dma_start, out=fm(xdst[:, bsl]), in_=xb[:]), reads=XBK, writes=[xdk])
                if modg is not None:
                    for _ in modg:
                        pass
            P.barrier()

        try:
            for l in range(DEPTH):
                layer(l)
        except _Stop:
            pass
        P.final_wait("sp", ["outT"] + ["dbgo_" + n for n in dbg_out])
        P.run()
    return nc


def _consts():
    c = np.zeros((128, 1152), np.float32)
    c[:, 0:128] = np.eye(128, dtype=np.float32)
    s = np.arange(128)[:, None]
    t = np.arange(128)[None, :]
    m = ((s // 64 == t // 64) & (s <= t)).astype(np.float32)
    c[:, 128:640] = np.tile(m, (1, 4))
    r = np.ones((128, 512), np.float32)
    r[:, 0::64] = 0.0
    c[:, 640:1152] = r
    return c


def _pack_params(inp, depth):
    par = np.zeros((128, NPAR), np.float32)
    ch = lambda v: np.ascontiguousarray(np.asarray(v, np.float32).reshape(-1, 128).T)
    for l in range(depth):
        po = l * PL
        par[:, po:po + 48] = ch(inp["b_ada"][l])
        par[:, po + 48:po + 56] = ch(inp["ln1_g"][l])
        par[:, po + 56:po + 64] = ch(inp["ln1_b"][l])
        par[:, po + 64:po + 72] = ch(inp["ln2_g"][l])
        par[:, po + 72:po + 80] = ch(inp["ln2_b"][l])
        cw = np.asarray(inp["conv_w"][l], np.float32)
        par[:, po + 80:po + 142] = cw.reshape(31, 2, 128).transpose(2, 1, 0).reshape(128, 62)
        par[:, po + 142:po + 144] = ch(inp["conv_b"][l])
        par[:, po + 144:po + 146] = ch(inp["conv_ln_g"][l])
        par[:, po + 146:po + 148] = ch(inp["conv_ln_b"][l])
        par[:, po + 148:po + 149] = np.asarray(inp["rec_norm_g"][l], np.float32).reshape(128, 1)
    rl = np.asarray(inp["rec_lower_bound"], np.float32)
    par[:, 2 * PL:2 * PL + 8] = rl.reshape(2, 4, 128).transpose(2, 0, 1).reshape(128, 8)
    return par


def _bias_tiles(rel_bias_l):
    r = np.arange(5)[:, None, None]
    i = np.arange(128)[None, :, None]
    j = np.arange(128)[None, None, :]
    idx = np.clip((r - 4) * 128 + i - j, -128, 128) + 128
    valid = ~(((r == 0) & (i < 64) & (j >= 64)) | ((r == 4) & (i >= 64) & (j < 64)))
    tb = np.asarray(rel_bias_l, np.float32)
    g = tb[:, idx]
    g = np.where(valid[None], g, np.float32(NEG_BIG)).astype(np.float32)
    return np.ascontiguousarray(g.transpose(2, 0, 1, 3).reshape(128, 4 * 5 * 128))


def make_in_maps(inp, NT, depth, n_cores=8):
    inp = {k: np.asarray(v) for k, v in inp.items()}
    cst = _consts()
    par = _pack_params(inp, depth)
    shared = {"par": par, "cst": cst}
    for l in range(depth):
        shared[f"wada{l}"] = np.ascontiguousarray(inp["w_ada"][l], np.float32)
        shared[f"win{l}"] = np.ascontiguousarray(inp["w_in"][l], np.float32)
        shared[f"wout{l}"] = np.ascontiguousarray(inp["w_out"][l], np.float32)
        shared[f"wf1{l}"] = np.ascontiguousarray(inp["w_ffn_in"][l], np.float32)
        shared[f"wf2{l}"] = np.ascontiguousarray(inp["w_ffn_out"][l], np.float32)
        shared[f"bias{l}"] = _bias_tiles(inp["rel_bias"][l])
    maps = []
    for c in range(n_cores):
        b, half = c // 2, c % 2
        m = dict(shared)
        m["xT"] = np.ascontiguousarray(inp["x"][b, half * NT:(half + 1) * NT].T.astype(np.float32))
        m["cT"] = np.ascontiguousarray(inp["c"][b].astype(np.float32).reshape(8, 128).T)
        flg = np.zeros((128, 4), np.float32)
        flg[:, 0] = 1.0 if half == 0 else 0.0
        flg[:, 1] = 1.0 if half == 1 else 0.0
        flg[:, 2] = 0.0 if half == 1 else NEG_BIG
        m["flg"] = flg
        maps.append(m)
    return maps


_NC_CACHE = {}


def kernel(**inputs):
    T = inputs["x"].shape[1]
    B = inputs["x"].shape[0]
    NT = T // 2
    key = (NT, 2)
    if key not in _NC_CACHE:
        _NC_CACHE[key] = build(NT, 2)
    nc = _NC_CACHE[key]
    maps = make_in_maps(inputs, NT, 2)
    res = run_bass_kernel_spmd(nc, maps, core_ids=list(range(8)))
    out = np.empty((B, T, D), np.float32)
    for c in range(2 * B):
        b, half = c // 2, c % 2
        out[b, half * NT:(half + 1) * NT] = res.results[c]["outT"].T
    return out
```

```python
import contextlib
import numpy as np
import concourse.bass as bass
import concourse.mybir as mybir
from concourse.bass_utils import run_bass_kernel_spmd

F32 = mybir.dt.float32
BF16 = mybir.dt.bfloat16
AF = mybir.ActivationFunctionType
ALU = mybir.AluOpType

D = 1024
DIN = 3328
DFF = 2816
DEPTH_FULL = 2
ALPHA = (2 * DEPTH_FULL) ** 0.25
LN_EPS = 1e-5
NEG_BIG = -1e30
TL = 512
PL = 149
NPAR = 2 * PL + 8
SEM_CAP = 30000


class Prog:
    ENGS = ("pe", "act", "dve", "pool", "sp")

    def __init__(self, nc, same_engine_sync=True):
        self.nc = nc
        self.eng = {"pe": nc.tensor, "act": nc.scalar, "dve": nc.vector,
                    "pool": nc.gpsimd, "sp": nc.sync}
        self.plan = {e: [] for e in self.ENGS}
        self.seq = {e: 0 for e in self.ENGS}
        self.sems = {}
        self.waited = {e: {} for e in self.ENGS}
        self.res = {}
        self.same = same_engine_sync
        self.ctx = []
        self.dma_sems = {}
        self.ninst = 0
        self.stopped = False

    def _sem(self, name):
        cm = self.nc.semaphore(name)
        s = cm.__enter__()
        self.ctx.append(cm)
        return s

    def eng_sem(self, e, epoch):
        k = (e, epoch)
        if k not in self.sems:
            self.sems[k] = self._sem(f"s_{e}_{epoch}")
        return self.sems[k]

    def _need_wait(self, E, tok):
        if tok is None:
            return None
        if tok[0] == "eng":
            _, F, s = tok
            if F == E and (E in ("pe", "sp") or not self.same):
                return None
            key = ("eng", F)
            if self.waited[E].get(key, 0) >= s:
                return None
            self.waited[E][key] = s
            epoch, val = (s - 1) // SEM_CAP, (s - 1) % SEM_CAP + 1
            return (self.eng_sem(F, epoch), val)
        _, name, val = tok
        key = ("dma", name)
        if self.waited[E].get(key, 0) >= val:
            return None
        self.waited[E][key] = val
        return (self.dma_sems[name][0], val)

    def _deps(self, E, reads, writes, own_dma=None):
        toks = []
        for k in reads:
            r = self.res.get(k)
            if r and r["w"] is not None:
                toks.append(r["w"])
        for k in writes:
            r = self.res.get(k)
            if r:
                if r["w"] is not None:
                    toks.append(r["w"])
                toks.extend(r["r"])
        best = {}
        for t in toks:
            if t[0] == "dma" and own_dma is not None and t[1] == own_dma:
                continue
            key = (t[0], t[1])
            if key not in best or t[2] > best[key][2]:
                best[key] = t
        waits = []
        for t in best.values():
            w = self._need_wait(E, t)
            if w:
                waits.append(w)
        return waits

    def _update(self, tok, reads, writes):
        for k in reads:
            r = self.res.setdefault(k, {"w": None, "r": []})
            r["r"] = [t for t in r["r"] if not (t[0] == tok[0] and t[1] == tok[1])] + [tok]
        for k in writes:
            self.res[k] = {"w": tok, "r": []}

    def emit(self, E, fn, reads=(), writes=()):
        if self.stopped:
            return
        waits = self._deps(E, reads, writes)
        self.seq[E] += 1
        s = self.seq[E]
        sem = self.eng_sem(E, (s - 1) // SEM_CAP)
        eng = self.eng[E]

        ops = fn if isinstance(fn, list) else [fn]

        def run(waits=waits, ops=ops, sem=sem, eng=eng):
            for (ws, wv) in waits:
                eng.wait_ge(ws, wv)
            for (m_, a_, k_) in ops:
                inst = m_(*a_, **k_)
            inst.then_inc(sem, 1)
        self.plan[E].append(run)
        self._update(("eng", E, s), reads, writes)
        self.ninst += 1

    def dma(self, Q, name, fn, reads=(), writes=(), inc=16):
        if self.stopped:
            return
        if name not in self.dma_sems:
            self.dma_sems[name] = [self._sem(f"d_{name}"), 0]
        waits = self._deps(Q, reads, writes, own_dma=name)
        self.dma_sems[name][1] += inc
        val = self.dma_sems[name][1]
        sem = self.dma_sems[name][0]
        eng = self.eng[Q]

        def run(waits=waits, fn=fn, sem=sem, eng=eng, inc=inc):
            for (ws, wv) in waits:
                eng.wait_ge(ws, wv)
            m_, a_, k_ = fn
            if inc == 16:
                m_(*a_, **k_).then_inc(sem, 16)
            else:
                m_(*a_, **k_).then_inc(sem)
        self.plan[Q].append(run)
        self._update(("dma", name, val), reads, writes)
        self.ninst += 1

    def barrier(self):
        if self.stopped:
            return
        for E in self.ENGS:
            waits = []
            for F in self.ENGS:
                if F == E or F == "sp" or self.seq[F] == 0:
                    continue
                w = self._need_wait(E, ("eng", F, self.seq[F]))
                if w:
                    waits.append(w)
            for name, (sem, val) in self.dma_sems.items():
                if val > 0:
                    w = self._need_wait(E, ("dma", name, val))
                    if w:
                        waits.append(w)
            eng = self.eng[E]

            def run(waits=waits, eng=eng):
                for (ws, wv) in waits:
                    eng.wait_ge(ws, wv)
            self.plan[E].append(run)

    def final_wait(self, Q, keys):
        waits = self._deps(Q, keys, keys)
        eng = self.eng[Q]

        def run(waits=waits, eng=eng):
            for (ws, wv) in waits:
                eng.wait_ge(ws, wv)
        self.plan[Q].append(run)

    def run(self):
        nc = self.nc
        with nc.Block() as block:
            @block.tensor
            def _(e):
                for f in self.plan["pe"]:
                    f()

            @block.scalar
            def _(e):
                for f in self.plan["act"]:
                    f()

            @block.vector
            def _(e):
                for f in self.plan["dve"]:
                    f()

            @block.gpsimd
            def _(e):
                for f in self.plan["pool"]:
                    f()

            @block.sync
            def _(e):
                for f in self.plan["sp"]:
                    f()
        for cm in reversed(self.ctx):
            cm.__exit__(None, None, None)


def I(m, *a, **k):
    return (m, a, k)


class _Stop(Exception):
    pass


def build(NT, DEPTH=2, dbg=None, same_engine_sync=True, stop=None, n_cores=8):
    assert NT % TL == 0
    NTILE = NT // TL
    FB = min(NT, 2048)
    NFB = NT // FB
    FT = FB // TL
    nc = bass.Bass("TRN2", target_bir_lowering=False)
    dr = lambda n, s, kind="ExternalInput", d=F32: nc.dram_tensor(n, s, d, kind=kind).ap()
    xT = dr("xT", [D, NT])
    cT = dr("cT", [128, 8])
    par = dr("par", [128, NPAR])
    cst = dr("cst", [128, 128 + 512 + 512])
    wada = [dr(f"wada{l}", [D, 6 * D]) for l in range(DEPTH)]
    win_d = [dr(f"win{l}", [D, DIN]) for l in range(DEPTH)]
    wout_d = [dr(f"wout{l}", [D, D]) for l in range(DEPTH)]
    wf1_d = [dr(f"wf1{l}", [D, 2 * DFF]) for l in range(DEPTH)]
    wf2_d = [dr(f"wf2{l}", [DFF, D]) for l in range(DEPTH)]
    bias_d = [dr(f"bias{l}", [128, 4 * 5 * 128]) for l in range(DEPTH)]
    flg_d = dr("flg", [128, 4])
    PAYW = 512 + 1024 + 1024 + 60
    bounce = [nc.dram_tensor(f"bounce{l}", [128, PAYW], F32, kind="Internal") for l in range(DEPTH)]
    sEp = [dr(f"sEp{t}", [128, 2048], kind="Internal") for t in range(NT // TL)]
    sKh = [dr(f"sKh{t}", [128, 2048], kind="Internal", d=BF16) for t in range(NT // TL)]
    sKt = [dr(f"sKt{t}", [128, 2048], kind="Internal", d=BF16) for t in range(NT // TL)]
    sAd = [dr(f"sAd{t}", [128, 32], kind="Internal") for t in range(NT // TL)]
    sVt = [dr(f"sVt{t}", [128, 2048], kind="Internal", d=BF16) for t in range(NT // TL)]
    gath = [nc.dram_tensor(f"gath{l}", [128, PAYW], F32, kind="Internal") for l in range(DEPTH)]
    outT = dr("outT", [D, NT], kind="ExternalOutput")
    scrA = dr("scrA", [D, NT], kind="Internal")
    scrB = dr("scrB", [D, NT], kind="Internal")
    dbg_out = {}
    if dbg:
        for n, s in dbg.items():
            dbg_out[n] = dr("dbg_" + n, list(s), kind="ExternalOutput")

    P = Prog(nc, same_engine_sync)
    es = contextlib.ExitStack()
    sb = lambda n, s, d=F32: es.enter_context(nc.sbuf_tensor(n, s, d))
    ps = lambda n, s, d=F32: es.enter_context(nc.psum_tensor(n, s, d))
    fm = lambda ap: ap.rearrange("(kc p) n -> p kc n", p=128)
    V, A, G, T = nc.vector, nc.scalar, nc.gpsimd, nc.tensor
    MM = T.matmul
    ACT = A.activation

    with es:
        pb = [ps(f"pb{i}", [128, 512]) for i in range(7)]
        pT = ps("pT", [128, 1024], BF16)
        part = sb("part", [128, NPAR])
        cstt = sb("cstt", [128, 1152])
        ident = sb("ident", [128, 128], BF16)
        ones_d = sb("ones_d", [128, 128])
        ones_c = sb("ones_c", [128, 128])
        ones_r = sb("ones_r", [128, 128])
        ones_b = sb("ones_b", [128, 64], BF16)
        epsT = sb("epsT", [128, 4])
        cact = sb("cact", [128, 8], BF16)
        cf = sb("cf", [128, 8])
        modv2 = sb("modv", [128, 2, 48])
        cactf = sb("cactf", [128, 8])
        vec2 = sb("vec", [128, 2, 8, 8])
        lbt = sb("lbt", [128, 8, 4])
        omlb = sb("omlb", [128, 2, 4])
        W = [sb(f"W{i}", [128, 512]) for i in range(8)]
        wpre = sb("wpre", [128, 8, 1024], BF16)

        def load_wpre(lw):
            for k in range(8):
                P.dma("pool", "wpre", I(G.dma_start, out=wpre[:, k, :], in_=win_d[lw][k * 128:(k + 1) * 128, 1024:2048]), writes=[f"wpre{k}"])
        WPK = [f"wpre{k}" for k in range(8)]
        rstdT = sb("rstdT", [128, 512])
        nmrT = sb("nmrT", [128, 512])
        wctr = [0]

        def wtmp():
            i = wctr[0] % len(W)
            wctr[0] += 1
            return W[i], f"W{i}"

        gctr = [0]

        def gbank(n=2):
            i = gctr[0] % n
            gctr[0] += 1
            return pb[i], f"pb{i}"

        recmask = cstt[:, 128:640]
        resetm = cstt[:, 640:1152]
        b3 = lambda a: a[:].rearrange("p (c t) -> p c t", t=64)

        P.dma("sp", "ld0", I(nc.sync.dma_start, out=part[:], in_=par), writes=["part"])
        P.dma("sp", "ld1", I(nc.sync.dma_start, out=cstt[:], in_=cst), writes=["cstt"])
        P.dma("sp", "ld2", I(nc.sync.dma_start, out=cf[:], in_=cT), writes=["cf"])
        flg = sb("flgs", [128, 4])
        P.dma("sp", "ld3", I(nc.sync.dma_start, out=flg[:], in_=flg_d), writes=["flg"])
        isA, isB, hbias = flg[:, 0:1], flg[:, 1:2], flg[:, 2:3]
        P.emit("pool", I(G.memset, ones_d[:], 1.0 / D), writes=["ones_d"])
        P.emit("pool", I(G.memset, ones_c[:], 1.0 / 256), writes=["ones_c"])
        P.emit("pool", I(G.memset, ones_r[:], 1.0 / 128), writes=["ones_r"])
        P.emit("pool", I(G.memset, ones_b[:], 1.0), writes=["ones_b"])
        P.emit("pool", I(G.memset, epsT[:, 0:1], LN_EPS), writes=["epsT"])
        P.emit("pool", I(G.memset, epsT[:, 1:2], LN_EPS / (ALPHA * ALPHA)), writes=["epsT"])
        P.emit("pool", I(G.memset, epsT[:, 2:3], 1.0), writes=["epsT"])
        P.emit("dve", I(V.tensor_copy, out=ident[:], in_=cstt[:, 0:128]), reads=["cstt"], writes=["ident"])
        P.emit("act", I(ACT, out=cact[:], in_=cf[:], func=AF.Silu), reads=["cf"], writes=["cact"])
        P.emit("act", I(ACT, out=cactf[:], in_=cf[:], func=AF.Silu), reads=["cf"], writes=["cactf"])
        rl = part[:, 2 * PL:2 * PL + 8].rearrange("p (l c) -> p l c", c=4)
        r0, r1 = rl[:, 0, :], rl[:, 1, :]
        mx, e0, e1, ssum, rs, s0, s1, c1 = [lbt[:, i, :] for i in range(8)]
        TT = V.tensor_tensor
        P.emit("dve", I(V.tensor_max, out=mx, in0=r0, in1=r1), reads=["part"], writes=["lb_mx"])
        P.emit("dve", I(TT, out=e0, in0=r0, in1=mx, op=ALU.subtract), reads=["part", "lb_mx"], writes=["lb_e0"])
        P.emit("dve", I(TT, out=e1, in0=r1, in1=mx, op=ALU.subtract), reads=["part", "lb_mx"], writes=["lb_e1"])
        P.emit("act", I(ACT, out=e0, in_=e0, func=AF.Exp), reads=["lb_e0"], writes=["lb_e0"])
        P.emit("act", I(ACT, out=e1, in_=e1, func=AF.Exp), reads=["lb_e1"], writes=["lb_e1"])
        P.emit("dve", I(TT, out=ssum, in0=e0, in1=e1, op=ALU.add), reads=["lb_e0", "lb_e1"], writes=["lb_s"])
        P.emit("dve", I(V.reciprocal, out=rs, in_=ssum), reads=["lb_s"], writes=["lb_rs"])
        P.emit("dve", I(TT, out=s0, in0=e0, in1=rs, op=ALU.mult), reads=["lb_e0", "lb_rs"], writes=["lb_s0"])
        P.emit("dve", I(TT, out=s1, in0=e1, in1=rs, op=ALU.mult), reads=["lb_e1", "lb_rs"], writes=["lb_s1"])
        P.emit("dve", I(TT, out=c1, in0=s0, in1=s1, op=ALU.add), reads=["lb_s0", "lb_s1"], writes=["lb_c1"])
        P.emit("dve", I(TT, out=mx, in0=s0, in1=s0, op=ALU.subtract), reads=["lb_s0"], writes=["lb_mx"])
        P.emit("dve", I(TT, out=c1, in0=c1, in1=s0, op=ALU.subtract), reads=["lb_c1", "lb_s0"], writes=["lb_c1"])
        P.emit("dve", I(V.tensor_scalar, out=omlb[:, 0, :], in0=mx, scalar1=-1.0, scalar2=1.0, op0=ALU.mult, op1=ALU.add),
               reads=["lb_mx"], writes=["omlb"])
        P.emit("dve", I(V.tensor_scalar, out=omlb[:, 1, :], in0=c1, scalar1=-1.0, scalar2=1.0, op0=ALU.mult, op1=ALU.add),
               reads=["lb_c1", "omlb"], writes=["omlb"])
        nomlb = sb("nomlb", [128, 2, 4])
        P.emit("dve", I(V.tensor_scalar, out=nomlb[:], in0=omlb[:], scalar1=-1.0, scalar2=None, op0=ALU.mult), reads=["omlb"], writes=["nomlb"])

        def tap(name, src_ap, keys):
            if name in dbg_out:
                P.dma("pool", "dbg_" + name, I(G.dma_start, out=dbg_out[name], in_=src_ap), reads=keys, writes=["dbgo_" + name])

        def ln_stats(epscol, bk=(3, 4), outs=None):
            pm_, pq_ = pb[bk[0]], pb[bk[1]]
            km_, kq_ = f"pb{bk[0]}", f"pb{bk[1]}"
            m2, m2k = wtmp()
            P.emit("act", I(ACT, out=m2[:], in_=pm_[:], func=AF.Square), reads=[km_], writes=[m2k])
            var, vark = wtmp()
            P.emit("dve", I(TT, out=var[:], in0=pq_[:], in1=m2[:], op=ALU.subtract), reads=[kq_, m2k], writes=[vark])
            sd, sdk = wtmp()
            P.emit("act", I(ACT, out=sd[:], in_=var[:], func=AF.Ln, bias=epsT[:, epscol:epscol + 1]), reads=[vark, "epsT"], writes=[sdk])
            (rstd, rsk), (nmr, nmk) = outs if outs else ((rstdT, "rstdT"), (nmrT, "nmrT"))
            P.emit("act", I(ACT, out=rstd[:], in_=sd[:], func=AF.Exp, scale=-0.5), reads=[sdk], writes=[rsk])
            P.emit("dve", I(V.scalar_tensor_tensor, out=nmr[:], in0=pm_[:], scalar=-1.0, in1=rstd[:], op0=ALU.mult, op1=ALU.mult),
                   reads=[km_, rsk], writes=[nmk])
            return rstd, rsk, nmr, nmk

        def ln_stats_part(xt, xkey, csl=slice(None), bk=(3, 4), outs=None):
            for m in range(8):
                q_, qk_ = wtmp()
                P.emit("act", I(ACT, out=q_[:], in_=xt[:, m, csl], func=AF.Square), reads=[xkey], writes=[qk_])
                P.emit("pe", [I(MM, pb[bk[0]][:], lhsT=ones_d[:], rhs=xt[:, m, csl], start=(m == 0), stop=(m == 7)),
                              I(MM, pb[bk[1]][:], lhsT=ones_d[:], rhs=q_[:], start=(m == 0), stop=(m == 7))],
                       reads=[xkey, qk_, "ones_d"], writes=[f"pb{bk[0]}", f"pb{bk[1]}"])
            return ln_stats(1, bk, outs)

        def ln_apply_gen(xt, xkey, lng, lnb, st, csl=slice(None), bufs=None):
            rstd, rsk, nmr, nmk = st
            for m in range(8):
                ta, tak = bufs[0] if bufs else wtmp()
                P.emit("dve", I(TT, out=ta[:], in0=xt[:, m, csl], in1=rstd[:], op=ALU.mult), reads=[xkey, rsk], writes=[tak])
                yield
                tb, tbk = bufs[1] if bufs else wtmp()
                P.emit("pool", I(G.tensor_tensor, out=tb[:], in0=ta[:], in1=nmr[:], op=ALU.add), reads=[tak, nmk], writes=[tbk])
                yield
                P.emit("act", I(ACT, out=xt[:, m, csl], in_=tb[:], func=AF.Identity, scale=lng[:, m:m + 1], bias=lnb[:, m:m + 1]),
                       reads=[tbk, "part", xkey], writes=[xkey])
                yield

        def ln_apply(xt, xkey, lng, lnb, csl=slice(None)):
            st = ln_stats_part(xt, xkey, csl)
            for _ in ln_apply_gen(xt, xkey, lng, lnb, st, csl):
                pass

        def ck(name):
            if stop == name:
                P.stopped = True

        def interleave(gens):
            gens = list(gens)
            while gens:
                for g in list(gens):
                    try:
                        next(g)
                    except StopIteration:
                        gens.remove(g)

        def mod_gen(lm, stg):
            pm = lm % 2
            po_ = lm * PL
            NG = 24
            stgw, rowb = stg
            def ld(g):
                P.dma("pool", f"wa{g % 2}", I(G.dma_start, out=stgw[g % 2][:], in_=fm(wada[lm][:, g * 256:(g + 1) * 256])), writes=[f"stg{g % 2}"])
            ld(0)
            for g in range(NG):
                if g + 1 < NG:
                    ld(g + 1)
                yield
                P.emit("pe", [I(MM, pb[6][0:1, 0:256], lhsT=cact[:, k:k + 1], rhs=stgw[g % 2][:, k, :], start=(k == 0), stop=(k == 7)) for k in range(8)],
                       reads=[f"stg{g % 2}", "cact"], writes=["pb6"])
                P.emit("act", I(A.copy, out=rowb[g % 2], in_=pb[6][0:1, 0:256]), reads=["pb6"], writes=[f"rowb{g % 2}", f"lnb2_{g % 2}"])
                yield
                P.emit("pe", [I(MM, pb[5][:, g * 2 + j:g * 2 + j + 1], lhsT=rowb[g % 2][:, j * 128:(j + 1) * 128], rhs=epsT[0:1, 2:3], start=True, stop=True)
                              for j in range(2)], reads=[f"rowb{g % 2}", f"lnb2_{g % 2}", "epsT"], writes=["pb5"])
                yield
            mv = modv2[:, pm, :]
            mk = f"modv{pm}"
            P.emit("dve", I(TT, out=mv, in0=pb[5][:, 0:48], in1=part[:, po_:po_ + 48], op=ALU.add), reads=["pb5", "part"], writes=[mk])
            sh1, sc1, g1, sh2, sc2, g2 = [modv2[:, pm, i * 8:(i + 1) * 8] for i in range(6)]
            A1_, B1_, G1_, G2_, A2_ = [vec2[:, pm, i, :] for i in range(5)]
            sfx = f"_{pm}"
            P.emit("dve", I(V.tensor_scalar, out=A1_, in0=sc1, scalar1=1.0, scalar2=None, op0=ALU.add), reads=[mk], writes=["vA1" + sfx])
            P.emit("dve", I(V.tensor_copy, out=B1_, in_=sh1), reads=[mk], writes=["vB1" + sfx])
            P.emit("dve", I(V.tensor_scalar, out=G1_, in0=g1, scalar1=1.0, scalar2=1.0 / ALPHA, op0=ALU.add, op1=ALU.mult), reads=[mk], writes=["vG1" + sfx])
            P.emit("dve", I(V.tensor_scalar, out=G2_, in0=g2, scalar1=1.0, scalar2=1.0 / ALPHA, op0=ALU.add, op1=ALU.mult), reads=[mk], writes=["vG2" + sfx])
            P.emit("dve", I(V.tensor_scalar, out=A2_, in0=sc2, scalar1=1.0, scalar2=None, op0=ALU.add), reads=[mk], writes=["vA2" + sfx])
            yield

        def layer(l):
            po = l * PL
            bada = part[:, po:po + 48]
            ln1g, ln1b = part[:, po + 48:po + 56], part[:, po + 56:po + 64]
            ln2g, ln2b = part[:, po + 64:po + 72], part[:, po + 72:po + 80]
            convw = part[:, po + 80:po + 142]
            convb, convg, convlb = part[:, po + 142:po + 144], part[:, po + 144:po + 146], part[:, po + 146:po + 148]
            normg = part[:, po + 148:po + 149]
            xsrc, xsk = (xT, "xT") if l == 0 else (scrB, "scrB")
            xdst, xdk = (outT, "outT") if l == DEPTH - 1 else (scrB, "scrB")
            XSK = [xsk] + [k_ for k_ in list(P.res.keys()) if str(k_).startswith(xsk + "_")]

            pm = l % 2
            if l == 0:
                load_wpre(0)
            if l == 0:
                with contextlib.ExitStack() as ls:
                    stg = ([ls.enter_context(nc.sbuf_tensor(f"stgA{i}", [128, 8, 256], BF16)) for i in range(2)],
                           [ls.enter_context(nc.sbuf_tensor(f"rowA{i}", [1, 256], F32))[0:1, :] for i in range(2)])
                    for _ in mod_gen(0, stg):
                        pass
                P.barrier()
            modv = modv2[:, pm, :]
            MK = f"modv{pm}"
            sh1, sc1, g1, sh2, sc2, g2 = [modv2[:, pm, i * 8:(i + 1) * 8] for i in range(6)]
            A1, B1, G1, G2, A2 = [vec2[:, pm, i, :] for i in range(5)]
            if l == 0:
                tap("modv", modv, [MK])
            ck("mod")
            with contextlib.ExitStack() as ms:
                msb = lambda n, s, d=F32: ms.enter_context(nc.sbuf_tensor(f"{n}_{l}", s, d))
                winb = msb("winb", [128, 8, DIN - 1024], BF16)
                woutb = msb("woutb", [128, 8, D], BF16)
                biasb = msb("biasb", [128, 4, 5, 128])
                diag = msb("diag", [128, 62, 128], BF16)
                xt = msb("xt", [128, 8, TL])
                hbf = msb("hbf", [128, 8, TL], BF16)
                qt_bf = msb("qt_bf", [128, 4, TL], BF16)
                kh_bf = msb("kh_bf", [128, 4, TL], BF16)
                kh_tm = msb("kh_tm", [128, 4, 512], BF16)
                v_tm = msb("v_tm", [128, 4, 512], BF16)
                attnT = msb("attnT", [128, 2, 512], BF16)
                adec = msb("adec", [128, 4, 8])
                Sst = msb("Sst", [128, 4, 2, 128])
                Sbf = msb("Sbf", [128, 2, 8, 128], BF16)
                ymix = msb("ymix", [128, 8, TL], BF16)
                ubuf = msb("ubuf", [128, 2, 30 + TL], BF16)
                uc = msb("uc", [128, 2, TL])
                qTa = msb("qTa", [128, 2, TL], BF16)
                kTa = msb("kTa", [128, 2, 2 * TL], BF16)
                vat = msb("vat", [128, 8, 256], BF16)
                tS2 = msb("tS", [128, 2, 5, 128])
                PT2 = msb("PT", [128, 2, 5, 128], BF16)
                rrec = msb("rrec", [128, 128])
                WOUTK = [f"woutb{k}" for k in range(8)]

                WBLK = [(512, 1024), (2048, 2560), (0, 512), (2560, 3328)]

                def wsrc(c0, c1_):
                    if 1024 <= c0 and c1_ <= 2048:
                        return wpre, c0 - 1024, WPK
                    loc = c0 if c0 < 1024 else c0 - 1024
                    keys = [f"winb{k}_{b}" for k in range(8) for b, (a0, a1) in enumerate(WBLK) if a0 < c1_ and c0 < a1]
                    return winb, loc, keys
                for b, (c0, c1_) in enumerate(WBLK):
                    loc = c0 if c0 < 1024 else c0 - 1024
                    for k in range(8):
                        P.dma("pool", f"winb{b}", I(G.dma_start, out=winb[:, k, loc:loc + (c1_ - c0)], in_=win_d[l][k * 128:(k + 1) * 128, c0:c1_]), writes=[f"winb{k}_{b}"])
                for k in range(8):
                    P.dma("pool", "woutb", I(G.dma_start, out=woutb[:, k, :], in_=wout_d[l][k * 128:(k + 1) * 128, :]), writes=[f"woutb{k}"])
                P.dma("sp", "biasb", I(nc.sync.dma_start, out=biasb[:].rearrange("p h r j -> p (h r j)"), in_=bias_d[l]), writes=["biasb"])
                for cj in range(62):
                    P.emit("pool", I(G.tensor_scalar, out=diag[:, cj, :], in0=cstt[:, 0:128], scalar1=convw[:, cj:cj + 1], scalar2=None, op0=ALU.mult),
                           reads=["cstt", "part"], writes=["diag"])
                P.emit("dve", I(V.memset, Sst[:], 0.0), writes=[f"S{h}_{c}" for h in range(4) for c in range(2)])
                P.emit("dve", I(V.memset, ubuf[:], 0.0), writes=["ubuf"])
                P.emit("dve", I(V.memset, kTa[:], 0.0), writes=["kTa"])
                P.emit("dve", I(V.memset, vat[:], 0.0), writes=["vat"])
                scur = [0, 0, 0, 0]
                ck("wload")

                def proj_fm(col):
                    bank, bkey = gbank()
                    wt_, lc_, wk_ = wsrc(col, col + 128)
                    P.emit("pe", [I(MM, bank[:], lhsT=wt_[:, k, lc_:lc_ + 128], rhs=hbf[:, k, :], start=(k == 0), stop=(k == 7)) for k in range(8)],
                           reads=wk_ + ["hbf"], writes=[bkey])
                    return bank, bkey

                TB = [[(W[i], f"W{i}") for i in range(0, 4)], [(W[i], f"W{i}") for i in range(4, 8)]]
                tsf = tS2[:].rearrange("p a r q -> p (a r q)")
                phys = [(W[i], f"W{i}") for i in range(8)] + [(rstdT, "rstdT"), (nmrT, "nmrT"), (tsf[:, 0:512], "tS0"), (tsf[:, 640:1152], "tS1")]
                TBA = [[phys[3 * i], phys[3 * i + 1], phys[3 * i + 2], phys[3 * i + 2]] for i in range(4)]

                def load_tile(t):
                    tsl = slice(t * TL, (t + 1) * TL)
                    P.dma("sp", "xld", I(nc.sync.dma_start, out=xt[:], in_=fm(xsrc[:, tsl])), reads=XSK, writes=["xt"])

                def prefetch_h(t):
                    for k in range(8):
                        P.dma("sp", f"xpf{k % 2}", I(nc.sync.dma_start, out=uc[:, k % 2, :], in_=xsrc[k * 128:(k + 1) * 128, t * TL:(t + 1) * TL]),
                              reads=XSK, writes=[f"uc{k % 2}"])
                        P.emit("act", I(ACT, out=hbf[:, k, :], in_=uc[:, k % 2, :], func=AF.Identity, scale=A1[:, k:k + 1], bias=B1[:, k:k + 1]),
                               reads=[f"uc{k % 2}", f"vA1_{pm}", f"vB1_{pm}"], writes=["hbf"])
                        yield

                def rec_v(t):
                    for s in range(4):
                        bank, bkey = gbank()
                        P.emit("pe", [I(MM, bank[:], lhsT=hbf[:, k, s * 128:(s + 1) * 128], rhs=wpre[:, k, 512:1024], start=(k == 0), stop=(k == 7)) for k in range(8)],
                               reads=WPK + ["hbf"], writes=[bkey])
                        P.emit("act", I(A.copy, out=v_tm[:, s, :], in_=bank[:]), reads=[bkey], writes=["v_tm"])
                    P.dma("sp", "sVt", I(nc.sync.dma_start, out=sVt[t], in_=v_tm[:].rearrange("p s n -> p (s n)")), reads=["v_tm"], writes=[f"sVt{t}"])

                def rec_pre(hd, th, t):
                    (B0, B0k), (B1, B1k), _, (B3, B3k) = TBA[th]
                    pnA, pnB = (5, 6) if th % 2 == 0 else (3, 4)
                    hs = slice(hd * 128, (hd + 1) * 128)
                    h5 = slice(hd * 512, (hd + 1) * 512)
                    bank, bkey = proj_fm(1024 + hd * 128)
                    sg, sgk = B0, B0k
                    P.emit("act", I(ACT, out=sg[:], in_=bank[:], func=AF.Sigmoid, scale=-1.0), reads=[bkey], writes=[sgk])
                    yield
                    kk, kkk = B1, B1k
                    P.emit("dve", I(V.tensor_scalar, out=kk[:], in0=sg[:], scalar1=omlb[:, l, hd:hd + 1], scalar2=None, op0=ALU.mult),
                           reads=[sgk, "omlb"], writes=[kkk])
                    yield
                    lf, lfk = B3, B3k
                    P.emit("act", I(ACT, out=lf[:], in_=sg[:], func=AF.Ln, scale=nomlb[:, l, hd:hd + 1], bias=epsT[:, 2:3]), reads=[sgk, "nomlb", "epsT"], writes=[lfk])
                    yield
                    bcum, bck = B0, B0k
                    P.emit("dve", I(V.tensor_tensor_scan, out=bcum[:], data0=resetm, data1=lf[:], initial=0.0, op0=ALU.mult, op1=ALU.add),
                           reads=[lfk, "cstt"], writes=[bck])
                    yield
                    bc, bcK = B3, B3k
                    P.emit("dve", I(TT, out=b3(bc), in0=b3(bcum), in1=b3(bcum)[:, :, 63:64].broadcast_to([128, 8, 64]), op=ALU.subtract),
                           reads=[bck], writes=[bcK])
                    yield
                    P.emit("act", I(ACT, out=adec[:, hd, :], in_=b3(bcum)[:, :, 63], func=AF.Exp), reads=[bck], writes=[f"adec{hd}"])
                    P.dma("sp", f"sAd{hd}", I(nc.sync.dma_start, out=sAd[t][:, hd * 8:(hd + 1) * 8], in_=adec[:, hd, :]), reads=[f"adec{hd}"], writes=[f"sAd{t}_{hd}"])
                    yield
                    Ep, Epk = B0, B0k
                    P.emit("act", I(ACT, out=Ep[:], in_=bc[:], func=AF.Exp), reads=[bcK, bck], writes=[Epk])
                    P.dma("sp", f"sEp{th}", I(nc.sync.dma_start, out=sEp[t][:, h5], in_=Ep[:]), reads=[Epk], writes=[f"sEp{t}_{hd}"])
                    yield
                    Em, Emk = B3, B3k
                    P.emit("act", I(ACT, out=Em[:], in_=bc[:], func=AF.Exp, scale=-1.0), reads=[bcK], writes=[Emk])
                    yield
                    P.emit("dve", I(TT, out=kh_bf[:, hd, :], in0=kk[:], in1=Em[:], op=ALU.mult), reads=[kkk, Emk], writes=[f"kh{hd}"])
                    P.dma("sp", f"sKh{hd}", I(nc.sync.dma_start, out=sKh[t][:, h5], in_=kh_bf[:, hd, :]), reads=[f"kh{hd}"], writes=[f"sKh{t}_{hd}"])
                    yield
                    P.emit("pe", [I(T.transpose, pT[:, s * 128:(s + 1) * 128], kh_bf[:, hd, s * 128:(s + 1) * 128], ident[:]) for s in range(4)],
                           reads=[f"kh{hd}", "ident"], writes=["pT"])
                    P.emit("act", I(A.copy, out=kh_tm[:, :, hs], in_=pT[:, 0:512].rearrange("p (s d) -> p s d", d=128)),
                           reads=["pT"], writes=[f"khtm{hd}"])
                    P.dma("sp", f"sKt{hd}", I(nc.sync.dma_start, out=sKt[t].rearrange("p (s d) -> p s d", d=512)[:, :, hs], in_=kh_tm[:, :, hs]),
                          reads=[f"khtm{hd}"], writes=[f"sKt{t}_{hd}"])
                    yield
                    ops = []
                    for n in range(8):
                        pr, base = n // 2, (n % 2) * 64
                        bankp = pb[pnA if n % 2 == 0 else pnB]
                        ops.append(I(MM, bankp[:, pr * 128:(pr + 1) * 128], lhsT=kh_tm[base:base + 64, pr, hs],
                                     rhs=v_tm[base:base + 64, pr, hs], start=True, stop=True))
                    P.emit("pe", ops, reads=[f"khtm{hd}", "v_tm"], writes=[f"pb{pnA}", f"pb{pnB}"])
                    for n in range(8):
                        cur = scur[hd]
                        bankp = pb[pnA if n % 2 == 0 else pnB]
                        P.emit("dve", I(V.scalar_tensor_tensor, out=Sst[:, hd, 1 - cur, :], in0=Sst[:, hd, cur, :], scalar=adec[:, hd, n:n + 1],
                                        in1=bankp[:, (n // 2) * 128:(n // 2 + 1) * 128], op0=ALU.mult, op1=ALU.add),
                               reads=[f"S{hd}_{cur}", f"adec{hd}", f"pb{pnA}", f"pb{pnB}"], writes=[f"S{hd}_{1 - cur}"])
                        scur[hd] = 1 - cur
                    yield

                def load_rec(t):
                    P.dma("sp", "lKh", I(nc.sync.dma_start, out=kh_bf[:].rearrange("p h n -> p (h n)"), in_=sKh[t]),
                          reads=[f"sKh{t}_{h}" for h in range(4)], writes=[f"kh{h}" for h in range(4)])
                    P.dma("sp", "lKt", I(nc.sync.dma_start, out=kh_tm[:].rearrange("p s n -> p (s n)"), in_=sKt[t]),
                          reads=[f"sKt{t}_{h}" for h in range(4)], writes=[f"khtm{h}" for h in range(4)])
                    P.dma("sp", "lVt", I(nc.sync.dma_start, out=v_tm[:].rearrange("p s n -> p (s n)"), in_=sVt[t]), reads=[f"sVt{t}"], writes=["v_tm"])
                    P.dma("sp", "lAd", I(nc.sync.dma_start, out=adec[:].rearrange("p h n -> p (h n)"), in_=sAd[t]),
                          reads=[f"sAd{t}_{h}" for h in range(4)], writes=[f"adec{h}" for h in range(4)])

                def rec_main(hd, th, t):
                    (B0, B0k), (B1, B1k), (B2, B2k), (B3, B3k) = TB[th]
                    pO, pOk = (pb[4], "pb4") if th == 0 else (pb[2], "pb2")
                    aT, aTk = attnT[:, th, :], f"attnT{th}"
                    hs = slice(hd * 128, (hd + 1) * 128)
                    Ep, Epk = B2, B2k
                    P.dma("sp", f"lEp{th}", I(nc.sync.dma_start, out=Ep[:], in_=sEp[t][:, hd * 512:(hd + 1) * 512]), reads=[f"sEp{t}_{hd}"], writes=[Epk])
                    yield
                    bankq, bqk = proj_fm(512 + hd * 128)
                    P.emit("dve", I(TT, out=qt_bf[:, hd, :], in0=bankq[:], in1=Ep[:], op=ALU.mult), reads=[bqk, Epk], writes=[f"qt{hd}"])
                    yield
                    P.emit("pe", [I(MM, pb[3][:, pr * 128:(pr + 1) * 128], lhsT=kh_bf[:, hd, pr * 128:(pr + 1) * 128],
                                    rhs=qt_bf[:, hd, pr * 128:(pr + 1) * 128], start=True, stop=True) for pr in range(4)],
                           reads=[f"kh{hd}", f"qt{hd}"], writes=["pb3"])
                    P.emit("dve", I(TT, out=aT, in0=pb[3][:], in1=recmask, op=ALU.mult), reads=["pb3", "cstt"], writes=[aTk])
                    yield
                    ops = []
                    for n in range(8):
                        pr, base = n // 2, (n % 2) * 64
                        bankp = pb[5 + n % 2]
                        ops.append(I(MM, bankp[:, pr * 128:(pr + 1) * 128], lhsT=kh_tm[base:base + 64, pr, hs],
                                     rhs=v_tm[base:base + 64, pr, hs], start=True, stop=True))
                    P.emit("pe", ops, reads=[f"khtm{hd}", "v_tm"], writes=["pb5", "pb6"])
                    for n in range(8):
                        cur = scur[hd]
                        bankp = pb[5 + n % 2]
                        P.emit("act", I(ACT, out=Sbf[:, th, n, :], in_=Sst[:, hd, cur, :], func=AF.Identity, scale=adec[:, hd, n:n + 1]),
                               reads=[f"S{hd}_{cur}", f"adec{hd}"], writes=[f"Sbf{th}_{n}"])
                        P.emit("dve", I(V.scalar_tensor_tensor, out=Sst[:, hd, 1 - cur, :], in0=Sst[:, hd, cur, :], scalar=adec[:, hd, n:n + 1],
                                        in1=bankp[:, (n // 2) * 128:(n // 2 + 1) * 128], op0=ALU.mult, op1=ALU.add),
                               reads=[f"S{hd}_{cur}", f"adec{hd}", "pb5", "pb6"], writes=[f"S{hd}_{1 - cur}"])
                        scur[hd] = 1 - cur
                    yield
                    ops = []
                    for pr in range(4):
                        ops.append(I(MM, pO[:, pr * 128:(pr + 1) * 128], lhsT=v_tm[:, pr, hs], rhs=aT[:, pr * 128:(pr + 1) * 128], start=True, stop=False))
                        for n in (2 * pr, 2 * pr + 1):
                            ops.append(I(MM, pO[:, n * 64:(n + 1) * 64], lhsT=Sbf[:, th, n, :], rhs=qt_bf[:, hd, n * 64:(n + 1) * 64],
                                         start=False, stop=(n % 2 == 1)))
                    P.emit("pe", ops, reads=["v_tm", aTk, f"qt{hd}"] + [f"Sbf{th}_{n}" for n in range(8)], writes=[pOk])
                    yield
                    osq, osqk = B2, B2k
                    P.emit("act", I(ACT, out=osq[:], in_=pO[:], func=AF.Square), reads=[pOk], writes=[osqk])
                    yield
                    P.emit("pe", I(MM, pb[3][:], lhsT=ones_r[:], rhs=osq[:], start=True, stop=True), reads=[osqk, "ones_r"], writes=["pb3"])
                    sd, sdk = B0, B0k
                    P.emit("act", I(ACT, out=sd[:], in_=pb[3][:], func=AF.Ln, bias=epsT[:, 0:1]), reads=["pb3", "epsT"], writes=[sdk])
                    yield
                    rstd, rsk = B0, B0k
                    P.emit("act", I(ACT, out=rstd[:], in_=sd[:], func=AF.Exp, scale=-0.5), reads=[sdk], writes=[rsk])
                    yield
                    t1, t1k = B3, B3k
                    P.emit("dve", I(TT, out=t1[:], in0=pO[:], in1=rstd[:], op=ALU.mult), reads=[pOk, rsk], writes=[t1k])
                    yield
                    bankg, bgk = proj_fm(2048 + hd * 128)
                    sgt, sgtk = B1, B1k
                    P.emit("act", I(ACT, out=sgt[:], in_=bankg[:], func=AF.Silu), reads=[bgk], writes=[sgtk])
                    yield
                    P.emit("dve", I(V.scalar_tensor_tensor, out=ymix[:, 2 + hd, :], in0=t1[:], scalar=normg, in1=sgt[:], op0=ALU.mult, op1=ALU.mult),
                           reads=[t1k, sgtk, "part"], writes=[f"ymix{2 + hd}"])
                    yield

                def conv_glu():
                    for c in range(2):
                        bankg, bgk = proj_fm(256 + c * 128)
                        sg, sgk = wtmp()
                        P.emit("act", I(ACT, out=sg[:], in_=bankg[:], func=AF.Sigmoid), reads=[bgk], writes=[sgk])
                        bankv, bvk = proj_fm(c * 128)
                        P.emit("dve", I(TT, out=ubuf[:, c, 30:30 + TL], in0=bankv[:], in1=sg[:], op=ALU.mult), reads=[bvk, sgk, "ubuf"], writes=["ubuf"])

                def conv_rest():
                    usq = []
                    for c in range(2):
                        P.emit("pe", [I(MM, pb[5 + c][:], lhsT=diag[:, c * 31 + j, :], rhs=ubuf[:, c, j:j + TL], start=(j == 0), stop=(j == 30)) for j in range(31)],
                               reads=["diag", "ubuf"], writes=[f"pb{5 + c}"])
                        P.emit("act", I(ACT, out=uc[:, c, :], in_=pb[5 + c][:], func=AF.Identity, bias=convb[:, c:c + 1]), reads=[f"pb{5 + c}", "part"], writes=[f"uc{c}"])
                        q_, qk_ = wtmp()
                        P.emit("act", I(ACT, out=q_[:], in_=uc[:, c, :], func=AF.Square), reads=[f"uc{c}"], writes=[qk_])
                        usq.append((q_, qk_))
                    P.emit("pe", [I(MM, pb[3][:], lhsT=ones_c[:], rhs=uc[:, c, :], start=(c == 0), stop=(c == 1)) for c in range(2)]
                           + [I(MM, pb[4][:], lhsT=ones_c[:], rhs=usq[c][0][:], start=(c == 0), stop=(c == 1)) for c in range(2)],
                           reads=["uc0", "uc1", usq[0][1], usq[1][1], "ones_c"], writes=["pb3", "pb4"])
                    rstd, rsk, nmr, nmk = ln_stats(0)
                    for c in range(2):
                        ta, tak = wtmp()
                        P.emit("dve", I(TT, out=ta[:], in0=uc[:, c, :], in1=rstd[:], op=ALU.mult), reads=[f"uc{c}", rsk], writes=[tak])
                        tb, tbk = wtmp()
                        P.emit("pool", I(G.tensor_tensor, out=tb[:], in0=ta[:], in1=nmr[:], op=ALU.add), reads=[tak, nmk], writes=[tbk])
                        P.emit("act", I(ACT, out=ymix[:, c, :], in_=tb[:], func=AF.Silu, scale=convg[:, c:c + 1], bias=convlb[:, c:c + 1]),
                               reads=[tbk, "part"], writes=[f"ymix{c}"])
                    P.emit("pool", I(G.tensor_copy, out=ubuf[:, :, 0:30], in_=ubuf[:, :, TL:TL + 30]), reads=["ubuf"], writes=["ubuf"])


                def att_proj():
                    for c in range(2):
                        bq, bqk = proj_fm(2560 + c * 128)
                        P.emit("act", I(ACT, out=qTa[:, c, :], in_=bq[:], func=AF.Copy, scale=0.125), reads=[bqk], writes=["qTa"])
                        bk_, bkk = proj_fm(2816 + c * 128)
                        P.emit("dve", I(V.tensor_copy, out=kTa[:, c, TL:2 * TL], in_=bk_[:]), reads=[bkk, "kTa"], writes=["kTa"])
                    for s2 in range(2):
                        bank, bkey = gbank()
                        ops = []
                        for ss in range(2):
                            s = s2 * 2 + ss
                            for k in range(8):
                                ops.append(I(MM, bank[:, ss * 256:(ss + 1) * 256], lhsT=hbf[:, k, s * 128:(s + 1) * 128], rhs=winb[:, k, 2048:2304],
                                             start=(k == 0), stop=(k == 7)))
                        P.emit("pe", ops, reads=wsrc(3072, 3328)[2] + ["hbf"], writes=[bkey])
                        P.emit("act", I(A.copy, out=vat[:, 4 + 2 * s2:6 + 2 * s2, :], in_=bank[:].rearrange("p (s d) -> p s d", d=256)),
                               reads=[bkey, "vat"], writes=["vat"])

                def att_main(t):
                    obanks = {}

                    def stage1(j, c, hh2):
                        J = 4 * t + j
                        nh = max(0, 4 - J)
                        hh = 2 * c + hh2
                        base = hh2 * 64
                        bA, bB = (5, 6) if hh2 == 0 else (3, 4)
                        tS, PT, tSk, PTk = tS2[:, hh2], PT2[:, hh2], f"tS{hh2}", f"PT{hh2}"
                        ops = []
                        for r in range(5):
                            bankS = pb[bA] if r < 4 else pb[bB]
                            ops.append(I(MM, bankS[:, (r % 4) * 128:(r % 4 + 1) * 128], lhsT=kTa[base:base + 64, c, (j + r) * 128:(j + r + 1) * 128],
                                         rhs=qTa[base:base + 64, c, j * 128:(j + 1) * 128], start=True, stop=True))
                        P.emit("pe", ops, reads=["kTa", "qTa"], writes=[f"pb{bA}", f"pb{bB}"])
                        P.emit("dve", I(TT, out=tS[:, 0:4, :], in0=pb[bA][:].rearrange("p (r q) -> p r q", q=128),
                                        in1=biasb[:, hh, 0:4, :], op=ALU.add), reads=[f"pb{bA}", "biasb", tSk], writes=[tSk])
                        P.emit("dve", I(TT, out=tS[:, 4, :], in0=pb[bB][:, 0:128], in1=biasb[:, hh, 4, :], op=ALU.add),
                               reads=[f"pb{bB}", "biasb", tSk], writes=[tSk])
                        if nh > 0:
                            P.emit("act", I(ACT, out=PT[:, 0:nh, :], in_=tS[:, 0:nh, :], func=AF.Exp, bias=hbias), reads=[tSk, "flg", PTk], writes=[PTk])
                        P.emit("act", I(ACT, out=PT[:, nh:5, :], in_=tS[:, nh:5, :], func=AF.Exp), reads=[tSk, PTk], writes=[PTk])

                    def stage2(j, c, hh2):
                        hh = 2 * c + hh2
                        base = hh2 * 64
                        PT, PTk = PT2[:, hh2], f"PT{hh2}"
                        if (j, c) not in obanks:
                            obanks[(j, c)] = gbank()
                        obank, obk = obanks[(j, c)]
                        ops = []
                        for r in range(5):
                            ops.append(I(MM, obank[base:base + 64, 0:128], lhsT=vat[:, j + r, hh * 64:(hh + 1) * 64], rhs=PT[:, r, :],
                                         start=(r == 0), stop=(r == 4)))
                        for r in range(5):
                            ops.append(I(MM, obank[base:base + 64, 128:256], lhsT=ones_b[:], rhs=PT[:, r, :], start=(r == 0), stop=(r == 4)))
                        P.emit("pe", ops, reads=["vat", PTk, "ones_b"], writes=[obk])
                        if hh2 == 1:
                            P.emit("dve", I(V.reciprocal, out=rrec[:], in_=obank[:, 128:256]), reads=[obk], writes=["rrec"])
                            P.emit("dve", I(TT, out=ymix[:, 6 + c, j * 128:(j + 1) * 128], in0=obank[:, 0:128], in1=rrec[:], op=ALU.mult),
                                   reads=[obk, "rrec", f"ymix{6 + c}"], writes=[f"ymix{6 + c}"])

                    prev = None
                    for it in [(j, c, hh2) for j in range(4) for c in range(2) for hh2 in range(2)]:
                        stage1(*it)
                        yield
                        if prev is not None:
                            stage2(*prev)
                            yield
                        prev = it
                    stage2(*prev)
                    yield

                def att_shift():
                    P.emit("pool", I(G.tensor_copy, out=kTa[:, :, 0:TL], in_=kTa[:, :, TL:2 * TL]), reads=["kTa"], writes=["kTa"])
                    P.emit("pool", I(G.tensor_copy, out=vat[:, 0:4, :], in_=vat[:, 4:8, :]), reads=["vat"], writes=["vat"])


                def finish_tile(t):
                    tsl = slice(t * TL, (t + 1) * TL)
                    ymk = [f"ymix{i}" for i in range(8)]
                    if t == 0 and l == 0:
                        tap("ymix", ymix[:], ymk)
                    for m in range(8):
                        bank, bkey = gbank()
                        P.emit("pe", [I(MM, bank[:], lhsT=woutb[:, k, m * 128:(m + 1) * 128], rhs=ymix[:, k, :], start=(k == 0), stop=(k == 7)) for k in range(8)],
                               reads=WOUTK + ymk, writes=[bkey])
                        P.emit("dve", I(V.scalar_tensor_tensor, out=xt[:, m, :], in0=bank[:], scalar=G1[:, m:m + 1], in1=xt[:, m, :], op0=ALU.mult, op1=ALU.add),
                               reads=[bkey, f"vG1_{pm}", "xt"], writes=["xt"])
                    return ln_stats_part(xt, "xt")

                def ln1_tail(t, st):
                    tsl = slice(t * TL, (t + 1) * TL)
                    yield from ln_apply_gen(xt, "xt", ln1g, ln1b, st, bufs=[(uc[:, 0, :], "uc0"), (uc[:, 1, :], "uc1")])
                    if t == 0 and l == 0:
                        tap("x1", xt[:], ["xt"])
                    P.dma("sp", "xst", I(nc.sync.dma_start, out=fm(scrA[:, tsl]), in_=xt[:]), reads=["xt"], writes=["scrA"])
                    yield


                pay = xt[:].rearrange("p k n -> p (k n)")[:, 0:PAYW]
                interleave([prefetch_h(0)])
                for t in range(NTILE):
                    rec_v(t)
                    if t == NTILE - 1:
                        conv_glu()
                        att_proj()
                        interleave([rec_pre(h, h, t) for h in range(4)])
                    else:
                        interleave([rec_pre(h, h, t) for h in range(4)] + [prefetch_h(t + 1)])
                for hd in range(4):
                    P.emit("dve", I(V.tensor_scalar, out=pay[:, hd * 128:(hd + 1) * 128], in0=Sst[:, hd, scur[hd], :], scalar1=isA, scalar2=None, op0=ALU.mult),
                           reads=[f"S{hd}_{scur[hd]}", "flg"], writes=["xt"])
                P.emit("dve", I(V.tensor_scalar, out=pay[:, 512:1536].rearrange("p (c n) -> p c n", c=2), in0=kTa[:, :, TL:2 * TL], scalar1=isA, scalar2=None, op0=ALU.mult),
                       reads=["kTa", "flg", "xt"], writes=["xt"])
                P.emit("dve", I(V.tensor_scalar, out=pay[:, 1536:2560].rearrange("p (s n) -> p s n", s=4), in0=vat[:, 4:8, :], scalar1=isA, scalar2=None, op0=ALU.mult),
                       reads=["vat", "flg", "xt"], writes=["xt"])
                P.emit("dve", I(V.tensor_scalar, out=pay[:, 2560:2620].rearrange("p (c n) -> p c n", c=2), in0=ubuf[:, :, TL:TL + 30], scalar1=isA, scalar2=None, op0=ALU.mult),
                       reads=["ubuf", "flg", "xt"], writes=["xt"])
                P.dma("pool", "bnc", I(G.dma_start, out=bounce[l].ap(), in_=pay), reads=["xt"], writes=["bounce"])
                P.dma("pool", "cc", I(G.collective_compute, "AllReduce", ALU.add, replica_groups=[[2 * i, 2 * i + 1] for i in range(n_cores // 2)],
                                      ins=[bounce[l].ap().opt()], outs=[gath[l].ap().opt()]), reads=["bounce"], writes=["gath"], inc=1)
                P.dma("pool", "gth", I(G.dma_start, out=pay, in_=gath[l].ap()), reads=["gath", "xt"], writes=["xt"])
                for hd in range(4):
                    P.emit("dve", I(V.tensor_scalar, out=Sst[:, hd, 0, :], in0=pay[:, hd * 128:(hd + 1) * 128], scalar1=isB, scalar2=None, op0=ALU.mult),
                           reads=["xt", "flg", f"S{hd}_0", f"S{hd}_1"], writes=[f"S{hd}_0"])
                    scur[hd] = 0
                P.emit("dve", I(V.tensor_scalar, out=kTa[:, :, 0:TL], in0=pay[:, 512:1536].rearrange("p (c n) -> p c n", c=2), scalar1=isB, scalar2=None, op0=ALU.mult),
                       reads=["xt", "flg", "kTa"], writes=["kTa"])
                P.emit("dve", I(V.tensor_scalar, out=vat[:, 0:4, :], in0=pay[:, 1536:2560].rearrange("p (s n) -> p s n", s=4), scalar1=isB, scalar2=None, op0=ALU.mult),
                       reads=["xt", "flg", "vat"], writes=["vat"])
                P.emit("dve", I(V.tensor_scalar, out=ubuf[:, :, 0:30], in0=pay[:, 2560:2620].rearrange("p (c n) -> p c n", c=2), scalar1=isB, scalar2=None, op0=ALU.mult),
                       reads=["xt", "flg", "ubuf"], writes=["ubuf"])
                interleave([prefetch_h(0)])
                st_prev = None
                for t in range(NTILE):
                    if t == 0:
                        load_rec(0)
                    if t > 0:
                        interleave([rec_main(0, 0, t), rec_main(1, 1, t), ln1_tail(t - 1, st_prev)])
                    else:
                        interleave([rec_main(0, 0, t), rec_main(1, 1, t)])
                    interleave([rec_main(2, 0, t), rec_main(3, 1, t)])
                    load_tile(t)
                    if t + 1 < NTILE:
                        load_rec(t + 1)
                    conv_glu()
                    conv_rest()
                    att_proj()
                    if t + 1 < NTILE:
                        interleave([att_main(t), prefetch_h(t + 1)])
                    else:
                        interleave([att_main(t)])
                    att_shift()
                    st_prev = finish_tile(t)
                interleave([ln1_tail(NTILE - 1, st_prev)])
            P.barrier()
            ck("mixer")
            with contextlib.ExitStack() as fs:
                fsb = lambda n, s, d=F32: fs.enter_context(nc.sbuf_tensor(f"{n}_{l}", s, d))
                xb = fsb("xb", [128, 8, FB])
                h2 = fsb("h2", [128, 8, FB], BF16)
                hid = fsb("hid", [128, 5, FB], BF16)
                wg = fsb("wg", [128, 8, 640], BF16)
                wu = fsb("wu", [128, 8, 640], BF16)
                w2 = fsb("w2", [128, 5, D], BF16)
                groups = [(0, 5), (5, 5), (10, 4), (14, 4), (18, 4)]
                if l + 1 < DEPTH:
                    load_wpre(l + 1)
                XBK = [f"xb{tt}" for tt in range(FT)]
                lnb2 = [fsb(f"lnb2_{i}", [128, 512]) for i in range(4)]
                lnrow = [lnb2[i][0:1, 0:256] for i in range(2)]
                modg = None
                if l + 1 < DEPTH:
                    stgF = ([fsb(f"stgF{i}", [128, 8, 256], BF16) for i in range(2)], lnrow)
                    modg = mod_gen(l + 1, stgF)

                def adv():
                    if modg is not None:
                        next(modg, None)
                WGK = [f"wg{k}" for k in range(8)]
                WUK = [f"wu{k}" for k in range(8)]
                for fb in range(NFB):
                    bsl = slice(fb * FB, (fb + 1) * FB)
                    for tt in range(FT):
                        csl = slice(tt * TL, (tt + 1) * TL)
                        P.dma("sp", f"xbl{tt}", I(nc.sync.dma_start, out=xb[:, :, csl], in_=fm(scrA[:, fb * FB + tt * TL:fb * FB + (tt + 1) * TL])),
                              reads=["scrA"], writes=[f"xb{tt}"])
                    for tt in range(FT):
                        csl = slice(tt * TL, (tt + 1) * TL)
                        for k in range(8):
                            P.emit("act", I(ACT, out=h2[:, k, csl], in_=xb[:, k, csl], func=AF.Identity, scale=A2[:, k:k + 1], bias=sh2[:, k:k + 1]),
                                   reads=[f"xb{tt}", f"vA2_{pm}", MK], writes=[f"h2_{tt}"])
                    for (f0, nf) in groups:
                        for k in range(8):
                            P.dma("pool", "wg", I(G.dma_start, out=wg[:, k, 0:nf * 128], in_=wf1_d[l][k * 128:(k + 1) * 128, f0 * 128:(f0 + nf) * 128]), writes=[f"wg{k}"])
                            P.dma("pool", "wu", I(G.dma_start, out=wu[:, k, 0:nf * 128], in_=wf1_d[l][k * 128:(k + 1) * 128, DFF + f0 * 128:DFF + (f0 + nf) * 128]), writes=[f"wu{k}"])
                        for fi in range(nf):
                            P.dma("pool", "w2", I(G.dma_start, out=w2[:, fi, :], in_=wf2_d[l][(f0 + fi) * 128:(f0 + fi + 1) * 128, :]), writes=[f"w2{fi}"])
                        W2K = [f"w2{fi}" for fi in range(nf)]
                        for fi in range(nf):
                            for tt in range(FT):
                                csl = slice(tt * TL, (tt + 1) * TL)
                                bg, bgk = gbank(5)
                                P.emit("pe", [I(MM, bg[:], lhsT=wg[:, k, fi * 128:(fi + 1) * 128], rhs=h2[:, k, csl], start=(k == 0), stop=(k == 7)) for k in range(8)],
                                       reads=WGK + [f"h2_{tt}"], writes=[bgk])
                                bu, buk = gbank(5)
                                P.emit("pe", [I(MM, bu[:], lhsT=wu[:, k, fi * 128:(fi + 1) * 128], rhs=h2[:, k, csl], start=(k == 0), stop=(k == 7)) for k in range(8)],
                                       reads=WUK + [f"h2_{tt}"], writes=[buk])
                                sg, sgk = wtmp()
                                P.emit("act", I(ACT, out=sg[:], in_=bg[:], func=AF.Silu), reads=[bgk], writes=[sgk])
                                P.emit("dve", I(TT, out=hid[:, fi, csl], in0=bu[:], in1=sg[:], op=ALU.mult), reads=[buk, sgk, "hid"], writes=["hid"])
                                if fb == 0:
                                    adv()
                        if l == 0 and fb == 0:
                            tap(f"hid{f0}", hid[:, :, 0:512], ["hid"])
                        for m in range(8):
                            for tt in range(FT):
                                csl = slice(tt * TL, (tt + 1) * TL)
                                bo, bok = gbank(5)
                                P.emit("pe", [I(MM, bo[:], lhsT=w2[:, fi, m * 128:(m + 1) * 128], rhs=hid[:, fi, csl], start=(fi == 0), stop=(fi == nf - 1)) for fi in range(nf)],
                                       reads=W2K + ["hid"], writes=[bok])
                                P.emit("dve", I(V.scalar_tensor_tensor, out=xb[:, m, csl], in0=bo[:], scalar=G2[:, m:m + 1], in1=xb[:, m, csl], op0=ALU.mult, op1=ALU.add),
                                       reads=[bok, f"vG2_{pm}", f"xb{tt}"], writes=[f"xb{tt}"])
                    if l == 0 and fb == 0:
                        tap("z2", xb[:], XBK)
                    if modg is not None:
                        for _ in modg:
                            pass
                    for t0 in range(0, FT, 2):
                        gens = []
                        for i, tt in enumerate(range(t0, min(FT, t0 + 2))):
                            csl = slice(tt * TL, (tt + 1) * TL)
                            st = ln_stats_part(xb, f"xb{tt}", csl, bk=((3, 4), (5, 6))[i],
                                               outs=((lnb2[2 * i], f"lnb2_{2 * i}"), (lnb2[2 * i + 1], f"lnb2_{2 * i + 1}")))
                            gens.append(ln_apply_gen(xb, f"xb{tt}", ln2g, ln2b, st, csl,
                                                     bufs=[(W[4 + 2 * i], f"W{4 + 2 * i}"), (W[5 + 2 * i], f"W{5 + 2 * i}")]))
                        interleave(gens)
                        t1_ = min(FT, t0 + 2)
                        P.dma("sp", "xbs", I(nc.sync.dma_start, out=fm(xdst[:, fb * FB + t0 * TL:fb * FB + t1_ * TL]), in_=xb[:, :, t0 * TL:t1_ * TL]),
                              reads=[f"xb{tt}" for tt in range(t0, t1_)], writes=[f"{xdk}_{fb}_{t0}"])
                    if l == 0 and fb == 0:
                        tap("x_l0", xb[:], XBK)
                if modg is not None:
                    for _ in modg:
                        pass
            P.barrier()

        try:
            for l in range(DEPTH):
                layer(l)
        except _Stop:
            pass
        P.final_wait("sp", [k for k in list(P.res.keys()) if str(k).startswith("outT")] + ["dbgo_" + n for n in dbg_out])
        P.run()
    return nc


def _consts():
    c = np.zeros((128, 1152), np.float32)
    c[:, 0:128] = np.eye(128, dtype=np.float32)
    s = np.arange(128)[:, None]
    t = np.arange(128)[None, :]
    m = ((s // 64 == t // 64) & (s <= t)).astype(np.float32)
    c[:, 128:640] = np.tile(m, (1, 4))
    r = np.ones((128, 512), np.float32)
    r[:, 0::64] = 0.0
    c[:, 640:1152] = r
    return c


def _pack_params(inp, depth):
    par = np.zeros((128, NPAR), np.float32)
    ch = lambda v: np.ascontiguousarray(np.asarray(v, np.float32).reshape(-1, 128).T)
    for l in range(depth):
        po = l * PL
        par[:, po:po + 48] = ch(inp["b_ada"][l])
        par[:, po + 48:po + 56] = ch(inp["ln1_g"][l])
        par[:, po + 56:po + 64] = ch(inp["ln1_b"][l])
        par[:, po + 64:po + 72] = ch(inp["ln2_g"][l])
        par[:, po + 72:po + 80] = ch(inp["ln2_b"][l])
        cw = np.asarray(inp["conv_w"][l], np.float32)
        par[:, po + 80:po + 142] = cw.reshape(31, 2, 128).transpose(2, 1, 0).reshape(128, 62)
        par[:, po + 142:po + 144] = ch(inp["conv_b"][l])
        par[:, po + 144:po + 146] = ch(inp["conv_ln_g"][l])
        par[:, po + 146:po + 148] = ch(inp["conv_ln_b"][l])
        par[:, po + 148:po + 149] = np.asarray(inp["rec_norm_g"][l], np.float32).reshape(128, 1)
    rl = np.asarray(inp["rec_lower_bound"], np.float32)
    par[:, 2 * PL:2 * PL + 8] = rl.reshape(2, 4, 128).transpose(2, 0, 1).reshape(128, 8)
    return par


def _bias_tiles(rel_bias_l):
    r = np.arange(5)[:, None, None]
    i = np.arange(128)[None, :, None]
    j = np.arange(128)[None, None, :]
    idx = np.clip((r - 4) * 128 + i - j, -128, 128) + 128
    valid = ~(((r == 0) & (i < 64) & (j >= 64)) | ((r == 4) & (i >= 64) & (j < 64)))
    tb = np.asarray(rel_bias_l, np.float32)
    g = tb[:, idx]
    g = np.where(valid[None], g, np.float32(NEG_BIG)).astype(np.float32)
    return np.ascontiguousarray(g.transpose(2, 0, 1, 3).reshape(128, 4 * 5 * 128))


def make_in_maps(inp, NT, depth, n_cores=8):
    inp = {k: np.asarray(v) for k, v in inp.items()}
    cst = _consts()
    par = _pack_params(inp, depth)
    shared = {"par": par, "cst": cst}
    for l in range(depth):
        shared[f"wada{l}"] = np.ascontiguousarray(inp["w_ada"][l], np.float32)
        shared[f"win{l}"] = np.ascontiguousarray(inp["w_in"][l], np.float32)
        shared[f"wout{l}"] = np.ascontiguousarray(inp["w_out"][l], np.float32)
        shared[f"wf1{l}"] = np.ascontiguousarray(inp["w_ffn_in"][l], np.float32)
        shared[f"wf2{l}"] = np.ascontiguousarray(inp["w_ffn_out"][l], np.float32)
        shared[f"bias{l}"] = _bias_tiles(inp["rel_bias"][l])
    maps = []
    for c in range(n_cores):
        b, half = c // 2, c % 2
        m = dict(shared)
        m["xT"] = np.ascontiguousarray(inp["x"][b, half * NT:(half + 1) * NT].T.astype(np.float32))
        m["cT"] = np.ascontiguousarray(inp["c"][b].astype(np.float32).reshape(8, 128).T)
        flg = np.zeros((128, 4), np.float32)
        flg[:, 0] = 1.0 if half == 0 else 0.0
        flg[:, 1] = 1.0 if half == 1 else 0.0
        flg[:, 2] = 0.0 if half == 1 else NEG_BIG
        m["flg"] = flg
        maps.append(m)
    return maps


_NC_CACHE = {}


def kernel(**inputs):
    T = inputs["x"].shape[1]
    B = inputs["x"].shape[0]
    NT = T // 2
    key = (NT, 2)
    if key not in _NC_CACHE:
        _NC_CACHE[key] = build(NT, 2)
    nc = _NC_CACHE[key]
    maps = make_in_maps(inputs, NT, 2)
    res = run_bass_kernel_spmd(nc, maps, core_ids=list(range(8)))
    out = np.empty((B, T, D), np.float32)
    for c in range(2 * B):
        b, half = c // 2, c % 2
        out[b, half * NT:(half + 1) * NT] = res.results[c]["outT"].T
    return out
```

```python
import contextlib
import numpy as np
import concourse.bass as bass
import concourse.mybir as mybir
from concourse.bass_utils import run_bass_kernel_spmd

F32 = mybir.dt.float32
BF16 = mybir.dt.bfloat16
AF = mybir.ActivationFunctionType
ALU = mybir.AluOpType

D = 1024
DIN = 3328
DFF = 2816
DEPTH_FULL = 2
ALPHA = (2 * DEPTH_FULL) ** 0.25
LN_EPS = 1e-5
NEG_BIG = -1e30
TL = 512
PL = 149
NPAR = 2 * PL + 8
SEM_CAP = 30000


class Prog:
    ENGS = ("pe", "act", "dve", "pool", "sp")

    def __init__(self, nc, same_engine_sync=True):
        self.nc = nc
        self.eng = {"pe": nc.tensor, "act": nc.scalar, "dve": nc.vector,
                    "pool": nc.gpsimd, "sp": nc.sync}
        self.plan = {e: [] for e in self.ENGS}
        self.seq = {e: 0 for e in self.ENGS}
        self.sems = {}
        self.waited = {e: {} for e in self.ENGS}
        self.res = {}
        self.same = same_engine_sync
        self.ctx = []
        self.dma_sems = {}
        self.ninst = 0
        self.stopped = False

    def _sem(self, name):
        cm = self.nc.semaphore(name)
        s = cm.__enter__()
        self.ctx.append(cm)
        return s

    def eng_sem(self, e, epoch):
        k = (e, epoch)
        if k not in self.sems:
            self.sems[k] = self._sem(f"s_{e}_{epoch}")
        return self.sems[k]

    def _need_wait(self, E, tok):
        if tok is None:
            return None
        if tok[0] == "eng":
            _, F, s = tok
            if F == E and (E in ("pe", "sp") or not self.same):
                return None
            key = ("eng", F)
            if self.waited[E].get(key, 0) >= s:
                return None
            self.waited[E][key] = s
            epoch, val = (s - 1) // SEM_CAP, (s - 1) % SEM_CAP + 1
            return (self.eng_sem(F, epoch), val)
        _, name, val = tok
        key = ("dma", name)
        if self.waited[E].get(key, 0) >= val:
            return None
        self.waited[E][key] = val
        return (self.dma_sems[name][0], val)

    def _deps(self, E, reads, writes, own_dma=None):
        toks = []
        for k in reads:
            r = self.res.get(k)
            if r and r["w"] is not None:
                toks.append(r["w"])
        for k in writes:
            r = self.res.get(k)
            if r:
                if r["w"] is not None:
                    toks.append(r["w"])
                toks.extend(r["r"])
        best = {}
        for t in toks:
            if t[0] == "dma" and own_dma is not None and t[1] == own_dma:
                continue
            key = (t[0], t[1])
            if key not in best or t[2] > best[key][2]:
                best[key] = t
        waits = []
        for t in best.values():
            w = self._need_wait(E, t)
            if w:
                waits.append(w)
        return waits

    def _update(self, tok, reads, writes):
        for k in reads:
            r = self.res.setdefault(k, {"w": None, "r": []})
            r["r"] = [t for t in r["r"] if not (t[0] == tok[0] and t[1] == tok[1])] + [tok]
        for k in writes:
            self.res[k] = {"w": tok, "r": []}

    def emit(self, E, fn, reads=(), writes=()):
        if self.stopped:
            return
        waits = self._deps(E, reads, writes)
        self.seq[E] += 1
        s = self.seq[E]
        sem = self.eng_sem(E, (s - 1) // SEM_CAP)
        eng = self.eng[E]

        ops = fn if isinstance(fn, list) else [fn]

        def run(waits=waits, ops=ops, sem=sem, eng=eng):
            for (ws, wv) in waits:
                eng.wait_ge(ws, wv)
            for (m_, a_, k_) in ops:
                inst = m_(*a_, **k_)
            inst.then_inc(sem, 1)
        self.plan[E].append(run)
        self._update(("eng", E, s), reads, writes)
        self.ninst += 1

    def dma(self, Q, name, fn, reads=(), writes=(), inc=16):
        if self.stopped:
            return
        if name not in self.dma_sems:
            self.dma_sems[name] = [self._sem(f"d_{name}"), 0]
        waits = self._deps(Q, reads, writes, own_dma=name)
        self.dma_sems[name][1] += inc
        val = self.dma_sems[name][1]
        sem = self.dma_sems[name][0]
        eng = self.eng[Q]

        def run(waits=waits, fn=fn, sem=sem, eng=eng, inc=inc):
            for (ws, wv) in waits:
                eng.wait_ge(ws, wv)
            m_, a_, k_ = fn
            if inc == 16:
                m_(*a_, **k_).then_inc(sem, 16)
            else:
                m_(*a_, **k_).then_inc(sem)
        self.plan[Q].append(run)
        self._update(("dma", name, val), reads, writes)
        self.ninst += 1

    def barrier(self):
        if self.stopped:
            return
        for E in self.ENGS:
            waits = []
            for F in self.ENGS:
                if F == E or F == "sp" or self.seq[F] == 0:
                    continue
                w = self._need_wait(E, ("eng", F, self.seq[F]))
                if w:
                    waits.append(w)
            for name, (sem, val) in self.dma_sems.items():
                if val > 0:
                    w = self._need_wait(E, ("dma", name, val))
                    if w:
                        waits.append(w)
            eng = self.eng[E]

            def run(waits=waits, eng=eng):
                for (ws, wv) in waits:
                    eng.wait_ge(ws, wv)
            self.plan[E].append(run)

    def final_wait(self, Q, keys):
        waits = self._deps(Q, keys, keys)
        eng = self.eng[Q]

        def run(waits=waits, eng=eng):
            for (ws, wv) in waits:
                eng.wait_ge(ws, wv)
        self.plan[Q].append(run)

    def run(self):
        nc = self.nc
        with nc.Block() as block:
            @block.tensor
            def _(e):
                for f in self.plan["pe"]:
                    f()

            @block.scalar
            def _(e):
                for f in self.plan["act"]:
                    f()

            @block.vector
            def _(e):
                for f in self.plan["dve"]:
                    f()

            @block.gpsimd
            def _(e):
                for f in self.plan["pool"]:
                    f()

            @block.sync
            def _(e):
                for f in self.plan["sp"]:
                    f()
        for cm in reversed(self.ctx):
            cm.__exit__(None, None, None)


def I(m, *a, **k):
    return (m, a, k)


class _Stop(Exception):
    pass


def build(NT, DEPTH=2, dbg=None, same_engine_sync=True, stop=None, n_cores=8):
    assert NT % TL == 0
    NTILE = NT // TL
    FB = min(NT, 2048)
    NFB = NT // FB
    FT = FB // TL
    nc = bass.Bass("TRN2", target_bir_lowering=False)
    dr = lambda n, s, kind="ExternalInput", d=F32: nc.dram_tensor(n, s, d, kind=kind).ap()
    xT = dr("xT", [D, NT])
    cT = dr("cT", [128, 8])
    par = dr("par", [128, NPAR])
    cst = dr("cst", [128, 128 + 512 + 512])
    wada = [dr(f"wada{l}", [D, 6 * D]) for l in range(DEPTH)]
    win_d = [dr(f"win{l}", [D, DIN]) for l in range(DEPTH)]
    wout_d = [dr(f"wout{l}", [D, D]) for l in range(DEPTH)]
    wf1_d = [dr(f"wf1{l}", [D, 2 * DFF]) for l in range(DEPTH)]
    wf2_d = [dr(f"wf2{l}", [DFF, D]) for l in range(DEPTH)]
    bias_d = [dr(f"bias{l}", [128, 4 * 5 * 128]) for l in range(DEPTH)]
    flg_d = dr("flg", [128, 4])
    PAYW = 512 + 1024 + 1024 + 60
    bounce = [nc.dram_tensor(f"bounce{l}", [128, PAYW], F32, kind="Internal") for l in range(DEPTH)]
    sEp = [dr(f"sEp{t}", [128, 2048], kind="Internal") for t in range(NT // TL)]
    sKh = [dr(f"sKh{t}", [128, 2048], kind="Internal", d=BF16) for t in range(NT // TL)]
    sKt = [dr(f"sKt{t}", [128, 2048], kind="Internal", d=BF16) for t in range(NT // TL)]
    sAd = [dr(f"sAd{t}", [128, 32], kind="Internal") for t in range(NT // TL)]
    sVt = [dr(f"sVt{t}", [128, 2048], kind="Internal", d=BF16) for t in range(NT // TL)]
    gath = [nc.dram_tensor(f"gath{l}", [128, PAYW], F32, kind="Internal") for l in range(DEPTH)]
    outT = dr("outT", [D, NT], kind="ExternalOutput")
    scrA = dr("scrA", [D, NT], kind="Internal")
    scrB = dr("scrB", [D, NT], kind="Internal")
    dbg_out = {}
    if dbg:
        for n, s in dbg.items():
            dbg_out[n] = dr("dbg_" + n, list(s), kind="ExternalOutput")

    P = Prog(nc, same_engine_sync)
    es = contextlib.ExitStack()
    sb = lambda n, s, d=F32: es.enter_context(nc.sbuf_tensor(n, s, d))
    ps = lambda n, s, d=F32: es.enter_context(nc.psum_tensor(n, s, d))
    fm = lambda ap: ap.rearrange("(kc p) n -> p kc n", p=128)
    V, A, G, T = nc.vector, nc.scalar, nc.gpsimd, nc.tensor
    MM = T.matmul
    ACT = A.activation

    with es:
        pb = [ps(f"pb{i}", [128, 512]) for i in range(7)]
        pT = ps("pT", [128, 1024], BF16)
        part = sb("part", [128, NPAR])
        cstt = sb("cstt", [128, 1152])
        ident = sb("ident", [128, 128], BF16)
        ones_d = sb("ones_d", [128, 128])
        ones_c = sb("ones_c", [128, 128])
        ones_r = sb("ones_r", [128, 128])
        ones_b = sb("ones_b", [128, 64], BF16)
        epsT = sb("epsT", [128, 4])
        cact = sb("cact", [128, 8], BF16)
        cf = sb("cf", [128, 8])
        modv2 = sb("modv", [128, 2, 48])
        cactf = sb("cactf", [128, 8])
        vec2 = sb("vec", [128, 2, 8, 8])
        lbt = sb("lbt", [128, 8, 4])
        omlb = sb("omlb", [128, 2, 4])
        W = [sb(f"W{i}", [128, 512]) for i in range(8)]
        wpre = sb("wpre", [128, 8, 1024], BF16)

        def load_wpre(lw):
            for k in range(8):
                P.dma("pool", "wpre", I(G.dma_start, out=wpre[:, k, :], in_=win_d[lw][k * 128:(k + 1) * 128, 1024:2048]), writes=[f"wpre{k}"])
        WPK = [f"wpre{k}" for k in range(8)]
        rstdT = sb("rstdT", [128, 512])
        nmrT = sb("nmrT", [128, 512])
        wctr = [0]

        def wtmp():
            i = wctr[0] % len(W)
            wctr[0] += 1
            return W[i], f"W{i}"

        gctr = [0]

        def gbank(n=2):
            i = gctr[0] % n
            gctr[0] += 1
            return pb[i], f"pb{i}"

        recmask = cstt[:, 128:640]
        resetm = cstt[:, 640:1152]
        b3 = lambda a: a[:].rearrange("p (c t) -> p c t", t=64)

        P.dma("sp", "ld0", I(nc.sync.dma_start, out=part[:], in_=par), writes=["part"])
        P.dma("sp", "ld1", I(nc.sync.dma_start, out=cstt[:], in_=cst), writes=["cstt"])
        P.dma("sp", "ld2", I(nc.sync.dma_start, out=cf[:], in_=cT), writes=["cf"])
        flg = sb("flgs", [128, 4])
        P.dma("sp", "ld3", I(nc.sync.dma_start, out=flg[:], in_=flg_d), writes=["flg"])
        isA, isB, hbias = flg[:, 0:1], flg[:, 1:2], flg[:, 2:3]
        P.emit("pool", I(G.memset, ones_d[:], 1.0 / D), writes=["ones_d"])
        P.emit("pool", I(G.memset, ones_c[:], 1.0 / 256), writes=["ones_c"])
        P.emit("pool", I(G.memset, ones_r[:], 1.0 / 128), writes=["ones_r"])
        P.emit("pool", I(G.memset, ones_b[:], 1.0), writes=["ones_b"])
        P.emit("pool", I(G.memset, epsT[:, 0:1], LN_EPS), writes=["epsT"])
        P.emit("pool", I(G.memset, epsT[:, 1:2], LN_EPS / (ALPHA * ALPHA)), writes=["epsT"])
        P.emit("pool", I(G.memset, epsT[:, 2:3], 1.0), writes=["epsT"])
        P.emit("dve", I(V.tensor_copy, out=ident[:], in_=cstt[:, 0:128]), reads=["cstt"], writes=["ident"])
        P.emit("act", I(ACT, out=cact[:], in_=cf[:], func=AF.Silu), reads=["cf"], writes=["cact"])
        P.emit("act", I(ACT, out=cactf[:], in_=cf[:], func=AF.Silu), reads=["cf"], writes=["cactf"])
        rl = part[:, 2 * PL:2 * PL + 8].rearrange("p (l c) -> p l c", c=4)
        r0, r1 = rl[:, 0, :], rl[:, 1, :]
        mx, e0, e1, ssum, rs, s0, s1, c1 = [lbt[:, i, :] for i in range(8)]
        TT = V.tensor_tensor
        P.emit("dve", I(V.tensor_max, out=mx, in0=r0, in1=r1), reads=["part"], writes=["lb_mx"])
        P.emit("dve", I(TT, out=e0, in0=r0, in1=mx, op=ALU.subtract), reads=["part", "lb_mx"], writes=["lb_e0"])
        P.emit("dve", I(TT, out=e1, in0=r1, in1=mx, op=ALU.subtract), reads=["part", "lb_mx"], writes=["lb_e1"])
        P.emit("act", I(ACT, out=e0, in_=e0, func=AF.Exp), reads=["lb_e0"], writes=["lb_e0"])
        P.emit("act", I(ACT, out=e1, in_=e1, func=AF.Exp), reads=["lb_e1"], writes=["lb_e1"])
        P.emit("dve", I(TT, out=ssum, in0=e0, in1=e1, op=ALU.add), reads=["lb_e0", "lb_e1"], writes=["lb_s"])
        P.emit("dve", I(V.reciprocal, out=rs, in_=ssum), reads=["lb_s"], writes=["lb_rs"])
        P.emit("dve", I(TT, out=s0, in0=e0, in1=rs, op=ALU.mult), reads=["lb_e0", "lb_rs"], writes=["lb_s0"])
        P.emit("dve", I(TT, out=s1, in0=e1, in1=rs, op=ALU.mult), reads=["lb_e1", "lb_rs"], writes=["lb_s1"])
        P.emit("dve", I(TT, out=c1, in0=s0, in1=s1, op=ALU.add), reads=["lb_s0", "lb_s1"], writes=["lb_c1"])
        P.emit("dve", I(TT, out=mx, in0=s0, in1=s0, op=ALU.subtract), reads=["lb_s0"], writes=["lb_mx"])
        P.emit("dve", I(TT, out=c1, in0=c1, in1=s0, op=ALU.subtract), reads=["lb_c1", "lb_s0"], writes=["lb_c1"])
        P.emit("dve", I(V.tensor_scalar, out=omlb[:, 0, :], in0=mx, scalar1=-1.0, scalar2=1.0, op0=ALU.mult, op1=ALU.add),
               reads=["lb_mx"], writes=["omlb"])
        P.emit("dve", I(V.tensor_scalar, out=omlb[:, 1, :], in0=c1, scalar1=-1.0, scalar2=1.0, op0=ALU.mult, op1=ALU.add),
               reads=["lb_c1", "omlb"], writes=["omlb"])
        nomlb = sb("nomlb", [128, 2, 4])
        P.emit("dve", I(V.tensor_scalar, out=nomlb[:], in0=omlb[:], scalar1=-1.0, scalar2=None, op0=ALU.mult), reads=["omlb"], writes=["nomlb"])

        def tap(name, src_ap, keys):
            if name in dbg_out:
                P.dma("pool", "dbg_" + name, I(G.dma_start, out=dbg_out[name], in_=src_ap), reads=keys, writes=["dbgo_" + name])

        def ln_stats(epscol, bk=(3, 4), outs=None):
            pm_, pq_ = pb[bk[0]], pb[bk[1]]
            km_, kq_ = f"pb{bk[0]}", f"pb{bk[1]}"
            m2, m2k = wtmp()
            P.emit("act", I(ACT, out=m2[:], in_=pm_[:], func=AF.Square), reads=[km_], writes=[m2k])
            var, vark = wtmp()
            P.emit("dve", I(TT, out=var[:], in0=pq_[:], in1=m2[:], op=ALU.subtract), reads=[kq_, m2k], writes=[vark])
            sd, sdk = wtmp()
            P.emit("act", I(ACT, out=sd[:], in_=var[:], func=AF.Ln, bias=epsT[:, epscol:epscol + 1]), reads=[vark, "epsT"], writes=[sdk])
            (rstd, rsk), (nmr, nmk) = outs if outs else ((rstdT, "rstdT"), (nmrT, "nmrT"))
            P.emit("act", I(ACT, out=rstd[:], in_=sd[:], func=AF.Exp, scale=-0.5), reads=[sdk], writes=[rsk])
            P.emit("dve", I(V.scalar_tensor_tensor, out=nmr[:], in0=pm_[:], scalar=-1.0, in1=rstd[:], op0=ALU.mult, op1=ALU.mult),
                   reads=[km_, rsk], writes=[nmk])
            return rstd, rsk, nmr, nmk

        def ln_stats_part(xt, xkey, csl=slice(None), bk=(3, 4), outs=None):
            for m in range(8):
                q_, qk_ = wtmp()
                P.emit("act", I(ACT, out=q_[:], in_=xt[:, m, csl], func=AF.Square), reads=[xkey], writes=[qk_])
                P.emit("pe", [I(MM, pb[bk[0]][:], lhsT=ones_d[:], rhs=xt[:, m, csl], start=(m == 0), stop=(m == 7)),
                              I(MM, pb[bk[1]][:], lhsT=ones_d[:], rhs=q_[:], start=(m == 0), stop=(m == 7))],
                       reads=[xkey, qk_, "ones_d"], writes=[f"pb{bk[0]}", f"pb{bk[1]}"])
            return ln_stats(1, bk, outs)

        def ln_apply_gen(xt, xkey, lng, lnb, st, csl=slice(None), bufs=None):
            rstd, rsk, nmr, nmk = st
            for m in range(8):
                ta, tak = bufs[0] if bufs else wtmp()
                P.emit("dve", I(TT, out=ta[:], in0=xt[:, m, csl], in1=rstd[:], op=ALU.mult), reads=[xkey, rsk], writes=[tak])
                yield
                tb, tbk = bufs[1] if bufs else wtmp()
                if m % 2 == 0:
                    P.emit("pool", I(G.tensor_tensor, out=tb[:], in0=ta[:], in1=nmr[:], op=ALU.add), reads=[tak, nmk], writes=[tbk])
                else:
                    P.emit("dve", I(TT, out=tb[:], in0=ta[:], in1=nmr[:], op=ALU.add), reads=[tak, nmk], writes=[tbk])
                yield
                P.emit("act", I(ACT, out=xt[:, m, csl], in_=tb[:], func=AF.Identity, scale=lng[:, m:m + 1], bias=lnb[:, m:m + 1]),
                       reads=[tbk, "part", xkey], writes=[xkey])
                yield

        def ln_apply(xt, xkey, lng, lnb, csl=slice(None)):
            st = ln_stats_part(xt, xkey, csl)
            for _ in ln_apply_gen(xt, xkey, lng, lnb, st, csl):
                pass

        def ck(name):
            if stop == name:
                P.stopped = True

        def interleave(gens):
            gens = list(gens)
            while gens:
                for g in list(gens):
                    try:
                        next(g)
                    except StopIteration:
                        gens.remove(g)

        def mod_gen(lm, stg):
            pm = lm % 2
            po_ = lm * PL
            NG = 24
            stgw, rowb = stg
            def ld(g):
                P.dma("pool", f"wa{g % 2}", I(G.dma_start, out=stgw[g % 2][:], in_=fm(wada[lm][:, g * 256:(g + 1) * 256])), writes=[f"stg{g % 2}"])
            ld(0)
            for g in range(NG):
                if g + 1 < NG:
                    ld(g + 1)
                yield
                P.emit("pe", [I(MM, pb[6][0:1, 0:256], lhsT=cact[:, k:k + 1], rhs=stgw[g % 2][:, k, :], start=(k == 0), stop=(k == 7)) for k in range(8)],
                       reads=[f"stg{g % 2}", "cact"], writes=["pb6"])
                P.emit("act", I(A.copy, out=rowb[g % 2], in_=pb[6][0:1, 0:256]), reads=["pb6"], writes=[f"rowb{g % 2}", f"lnb2_{g % 2}"])
                yield
                P.emit("pe", [I(MM, pb[5][:, g * 2 + j:g * 2 + j + 1], lhsT=rowb[g % 2][:, j * 128:(j + 1) * 128], rhs=epsT[0:1, 2:3], start=True, stop=True)
                              for j in range(2)], reads=[f"rowb{g % 2}", f"lnb2_{g % 2}", "epsT"], writes=["pb5"])
                yield
            mv = modv2[:, pm, :]
            mk = f"modv{pm}"
            P.emit("dve", I(TT, out=mv, in0=pb[5][:, 0:48], in1=part[:, po_:po_ + 48], op=ALU.add), reads=["pb5", "part"], writes=[mk])
            sh1, sc1, g1, sh2, sc2, g2 = [modv2[:, pm, i * 8:(i + 1) * 8] for i in range(6)]
            A1_, B1_, G1_, G2_, A2_ = [vec2[:, pm, i, :] for i in range(5)]
            sfx = f"_{pm}"
            P.emit("dve", I(V.tensor_scalar, out=A1_, in0=sc1, scalar1=1.0, scalar2=None, op0=ALU.add), reads=[mk], writes=["vA1" + sfx])
            P.emit("dve", I(V.tensor_copy, out=B1_, in_=sh1), reads=[mk], writes=["vB1" + sfx])
            P.emit("dve", I(V.tensor_scalar, out=G1_, in0=g1, scalar1=1.0, scalar2=1.0 / ALPHA, op0=ALU.add, op1=ALU.mult), reads=[mk], writes=["vG1" + sfx])
            P.emit("dve", I(V.tensor_scalar, out=G2_, in0=g2, scalar1=1.0, scalar2=1.0 / ALPHA, op0=ALU.add, op1=ALU.mult), reads=[mk], writes=["vG2" + sfx])
            P.emit("dve", I(V.tensor_scalar, out=A2_, in0=sc2, scalar1=1.0, scalar2=None, op0=ALU.add), reads=[mk], writes=["vA2" + sfx])
            yield

        def layer(l):
            po = l * PL
            bada = part[:, po:po + 48]
            ln1g, ln1b = part[:, po + 48:po + 56], part[:, po + 56:po + 64]
            ln2g, ln2b = part[:, po + 64:po + 72], part[:, po + 72:po + 80]
            convw = part[:, po + 80:po + 142]
            convb, convg, convlb = part[:, po + 142:po + 144], part[:, po + 144:po + 146], part[:, po + 146:po + 148]
            normg = part[:, po + 148:po + 149]
            xsrc, xsk = (xT, "xT") if l == 0 else (scrB, "scrB")
            xdst, xdk = (outT, "outT") if l == DEPTH - 1 else (scrB, "scrB")
            XSK = [xsk] + [k_ for k_ in list(P.res.keys()) if str(k_).startswith(xsk + "_")]

            pm = l % 2
            if l == 0:
                load_wpre(0)
            if l == 0:
                with contextlib.ExitStack() as ls:
                    stg = ([ls.enter_context(nc.sbuf_tensor(f"stgA{i}", [128, 8, 256], BF16)) for i in range(2)],
                           [ls.enter_context(nc.sbuf_tensor(f"rowA{i}", [1, 256], F32))[0:1, :] for i in range(2)])
                    for _ in mod_gen(0, stg):
                        pass
                P.barrier()
            modv = modv2[:, pm, :]
            MK = f"modv{pm}"
            sh1, sc1, g1, sh2, sc2, g2 = [modv2[:, pm, i * 8:(i + 1) * 8] for i in range(6)]
            A1, B1, G1, G2, A2 = [vec2[:, pm, i, :] for i in range(5)]
            if l == 0:
                tap("modv", modv, [MK])
            ck("mod")
            with contextlib.ExitStack() as ms:
                msb = lambda n, s, d=F32: ms.enter_context(nc.sbuf_tensor(f"{n}_{l}", s, d))
                winb = msb("winb", [128, 8, DIN - 1024], BF16)
                woutb = msb("woutb", [128, 8, D], BF16)
                biasb = msb("biasb", [128, 4, 5, 128])
                diag = msb("diag", [128, 62, 128], BF16)
                xt = msb("xt", [128, 8, TL])
                hbf = msb("hbf", [128, 8, TL], BF16)
                qt_bf = msb("qt_bf", [128, 4, TL], BF16)
                kh_bf = msb("kh_bf", [128, 4, TL], BF16)
                kh_tm = msb("kh_tm", [128, 4, 512], BF16)
                v_tm = msb("v_tm", [128, 4, 512], BF16)
                attnT = msb("attnT", [128, 2, 512], BF16)
                adec = msb("adec", [128, 4, 8])
                Sst = msb("Sst", [128, 4, 2, 128])
                Sbf = msb("Sbf", [128, 2, 8, 128], BF16)
                ymix = msb("ymix", [128, 8, TL], BF16)
                ubuf = msb("ubuf", [128, 2, 30 + TL], BF16)
                uc = msb("uc", [128, 2, TL])
                qTa = msb("qTa", [128, 2, TL], BF16)
                kTa = msb("kTa", [128, 2, 2 * TL], BF16)
                vat = msb("vat", [128, 8, 256], BF16)
                tS2 = msb("tS", [128, 2, 5, 128])
                PT2 = msb("PT", [128, 2, 5, 128], BF16)
                rrec = msb("rrec", [128, 128])
                WOUTK = [f"woutb{k}" for k in range(8)]

                WBLK = [(512, 1024), (2048, 2560), (0, 512), (2560, 3328)]

                def wsrc(c0, c1_):
                    if 1024 <= c0 and c1_ <= 2048:
                        return wpre, c0 - 1024, WPK
                    loc = c0 if c0 < 1024 else c0 - 1024
                    keys = [f"winb{k}_{b}" for k in range(8) for b, (a0, a1) in enumerate(WBLK) if a0 < c1_ and c0 < a1]
                    return winb, loc, keys
                for b, (c0, c1_) in enumerate(WBLK):
                    loc = c0 if c0 < 1024 else c0 - 1024
                    for k in range(8):
                        P.dma("pool", f"winb{b}", I(G.dma_start, out=winb[:, k, loc:loc + (c1_ - c0)], in_=win_d[l][k * 128:(k + 1) * 128, c0:c1_]), writes=[f"winb{k}_{b}"])
                for k in range(8):
                    P.dma("pool", "woutb", I(G.dma_start, out=woutb[:, k, :], in_=wout_d[l][k * 128:(k + 1) * 128, :]), writes=[f"woutb{k}"])
                P.dma("sp", "biasb", I(nc.sync.dma_start, out=biasb[:].rearrange("p h r j -> p (h r j)"), in_=bias_d[l]), writes=["biasb"])
                for cj in range(62):
                    P.emit("pool", I(G.tensor_scalar, out=diag[:, cj, :], in0=cstt[:, 0:128], scalar1=convw[:, cj:cj + 1], scalar2=None, op0=ALU.mult),
                           reads=["cstt", "part"], writes=["diag"])
                P.emit("dve", I(V.memset, Sst[:], 0.0), writes=[f"S{h}_{c}" for h in range(4) for c in range(2)])
                P.emit("dve", I(V.memset, ubuf[:], 0.0), writes=["ubuf"])
                P.emit("dve", I(V.memset, kTa[:], 0.0), writes=["kTa"])
                P.emit("dve", I(V.memset, vat[:], 0.0), writes=["vat"])
                scur = [0, 0, 0, 0]
                ck("wload")

                def proj_fm(col):
                    bank, bkey = gbank()
                    wt_, lc_, wk_ = wsrc(col, col + 128)
                    P.emit("pe", [I(MM, bank[:], lhsT=wt_[:, k, lc_:lc_ + 128], rhs=hbf[:, k, :], start=(k == 0), stop=(k == 7)) for k in range(8)],
                           reads=wk_ + ["hbf"], writes=[bkey])
                    return bank, bkey

                TB = [[(W[i], f"W{i}") for i in range(0, 4)], [(W[i], f"W{i}") for i in range(4, 8)]]
                tsf = tS2[:].rearrange("p a r q -> p (a r q)")
                phys = [(W[i], f"W{i}") for i in range(8)] + [(rstdT, "rstdT"), (nmrT, "nmrT"), (tsf[:, 0:512], "tS0"), (tsf[:, 640:1152], "tS1")]
                TBA = [[phys[3 * i], phys[3 * i + 1], phys[3 * i + 2], phys[3 * i + 2]] for i in range(4)]

                def load_tile(t):
                    tsl = slice(t * TL, (t + 1) * TL)
                    P.dma("sp", "xld", I(nc.sync.dma_start, out=xt[:], in_=fm(xsrc[:, tsl])), reads=XSK, writes=["xt"])

                def prefetch_h(t):
                    for k in range(8):
                        P.dma("sp", f"xpf{k % 2}", I(nc.sync.dma_start, out=uc[:, k % 2, :], in_=xsrc[k * 128:(k + 1) * 128, t * TL:(t + 1) * TL]),
                              reads=XSK, writes=[f"uc{k % 2}"])
                        P.emit("act", I(ACT, out=hbf[:, k, :], in_=uc[:, k % 2, :], func=AF.Identity, scale=A1[:, k:k + 1], bias=B1[:, k:k + 1]),
                               reads=[f"uc{k % 2}", f"vA1_{pm}", f"vB1_{pm}"], writes=["hbf"])
                        yield

                def rec_v(t):
                    for s in range(4):
                        bank, bkey = gbank()
                        P.emit("pe", [I(MM, bank[:], lhsT=hbf[:, k, s * 128:(s + 1) * 128], rhs=wpre[:, k, 512:1024], start=(k == 0), stop=(k == 7)) for k in range(8)],
                               reads=WPK + ["hbf"], writes=[bkey])
                        P.emit("act", I(A.copy, out=v_tm[:, s, :], in_=bank[:]), reads=[bkey], writes=["v_tm"])
                    P.dma("sp", "sVt", I(nc.sync.dma_start, out=sVt[t], in_=v_tm[:].rearrange("p s n -> p (s n)")), reads=["v_tm"], writes=[f"sVt{t}"])

                def rec_pre(hd, th, t):
                    (B0, B0k), (B1, B1k), _, (B3, B3k) = TBA[th]
                    pnA, pnB = (5, 6) if th % 2 == 0 else (3, 4)
                    hs = slice(hd * 128, (hd + 1) * 128)
                    h5 = slice(hd * 512, (hd + 1) * 512)
                    bank, bkey = proj_fm(1024 + hd * 128)
                    sg, sgk = B0, B0k
                    P.emit("act", I(ACT, out=sg[:], in_=bank[:], func=AF.Sigmoid, scale=-1.0), reads=[bkey], writes=[sgk])
                    yield
                    kk, kkk = B1, B1k
                    P.emit("dve", I(V.tensor_scalar, out=kk[:], in0=sg[:], scalar1=omlb[:, l, hd:hd + 1], scalar2=None, op0=ALU.mult),
                           reads=[sgk, "omlb"], writes=[kkk])
                    yield
                    lf, lfk = B3, B3k
                    P.emit("act", I(ACT, out=lf[:], in_=sg[:], func=AF.Ln, scale=nomlb[:, l, hd:hd + 1], bias=epsT[:, 2:3]), reads=[sgk, "nomlb", "epsT"], writes=[lfk])
                    yield
                    bcum, bck = B0, B0k
                    P.emit("dve", I(V.tensor_tensor_scan, out=bcum[:], data0=resetm, data1=lf[:], initial=0.0, op0=ALU.mult, op1=ALU.add),
                           reads=[lfk, "cstt"], writes=[bck])
                    yield
                    bc, bcK = B3, B3k
                    P.emit("dve", I(TT, out=b3(bc), in0=b3(bcum), in1=b3(bcum)[:, :, 63:64].broadcast_to([128, 8, 64]), op=ALU.subtract),
                           reads=[bck], writes=[bcK])
                    yield
                    P.emit("act", I(ACT, out=adec[:, hd, :], in_=b3(bcum)[:, :, 63], func=AF.Exp), reads=[bck], writes=[f"adec{hd}"])
                    P.dma("sp", f"sAd{hd}", I(nc.sync.dma_start, out=sAd[t][:, hd * 8:(hd + 1) * 8], in_=adec[:, hd, :]), reads=[f"adec{hd}"], writes=[f"sAd{t}_{hd}"])
                    yield
                    Ep, Epk = B0, B0k
                    P.emit("act", I(ACT, out=Ep[:], in_=bc[:], func=AF.Exp), reads=[bcK, bck], writes=[Epk])
                    P.dma("sp", f"sEp{th}", I(nc.sync.dma_start, out=sEp[t][:, h5], in_=Ep[:]), reads=[Epk], writes=[f"sEp{t}_{hd}"])
                    yield
                    Em, Emk = B3, B3k
                    P.emit("act", I(ACT, out=Em[:], in_=bc[:], func=AF.Exp, scale=-1.0), reads=[bcK], writes=[Emk])
                    yield
                    P.emit("dve", I(TT, out=kh_bf[:, hd, :], in0=kk[:], in1=Em[:], op=ALU.mult), reads=[kkk, Emk], writes=[f"kh{hd}"])
                    P.dma("sp", f"sKh{hd}", I(nc.sync.dma_start, out=sKh[t][:, h5], in_=kh_bf[:, hd, :]), reads=[f"kh{hd}"], writes=[f"sKh{t}_{hd}"])
                    yield
                    P.emit("pe", [I(T.transpose, pT[:, s * 128:(s + 1) * 128], kh_bf[:, hd, s * 128:(s + 1) * 128], ident[:]) for s in range(4)],
                           reads=[f"kh{hd}", "ident"], writes=["pT"])
                    P.emit("act", I(A.copy, out=kh_tm[:, :, hs], in_=pT[:, 0:512].rearrange("p (s d) -> p s d", d=128)),
                           reads=["pT"], writes=[f"khtm{hd}"])
                    P.dma("sp", f"sKt{hd}", I(nc.sync.dma_start, out=sKt[t].rearrange("p (s d) -> p s d", d=512)[:, :, hs], in_=kh_tm[:, :, hs]),
                          reads=[f"khtm{hd}"], writes=[f"sKt{t}_{hd}"])
                    yield
                    ops = []
                    for n in range(8):
                        pr, base = n // 2, (n % 2) * 64
                        bankp = pb[pnA if n % 2 == 0 else pnB]
                        ops.append(I(MM, bankp[:, pr * 128:(pr + 1) * 128], lhsT=kh_tm[base:base + 64, pr, hs],
                                     rhs=v_tm[base:base + 64, pr, hs], start=True, stop=True))
                    P.emit("pe", ops, reads=[f"khtm{hd}", "v_tm"], writes=[f"pb{pnA}", f"pb{pnB}"])
                    for n in range(8):
                        cur = scur[hd]
                        bankp = pb[pnA if n % 2 == 0 else pnB]
                        P.emit("dve", I(V.scalar_tensor_tensor, out=Sst[:, hd, 1 - cur, :], in0=Sst[:, hd, cur, :], scalar=adec[:, hd, n:n + 1],
                                        in1=bankp[:, (n // 2) * 128:(n // 2 + 1) * 128], op0=ALU.mult, op1=ALU.add),
                               reads=[f"S{hd}_{cur}", f"adec{hd}", f"pb{pnA}", f"pb{pnB}"], writes=[f"S{hd}_{1 - cur}"])
                        scur[hd] = 1 - cur
                    yield

                def load_rec(t):
                    P.dma("sp", "lKh", I(nc.sync.dma_start, out=kh_bf[:].rearrange("p h n -> p (h n)"), in_=sKh[t]),
                          reads=[f"sKh{t}_{h}" for h in range(4)], writes=[f"kh{h}" for h in range(4)])
                    P.dma("sp", "lKt", I(nc.sync.dma_start, out=kh_tm[:].rearrange("p s n -> p (s n)"), in_=sKt[t]),
                          reads=[f"sKt{t}_{h}" for h in range(4)], writes=[f"khtm{h}" for h in range(4)])
                    P.dma("sp", "lVt", I(nc.sync.dma_start, out=v_tm[:].rearrange("p s n -> p (s n)"), in_=sVt[t]), reads=[f"sVt{t}"], writes=["v_tm"])
                    P.dma("sp", "lAd", I(nc.sync.dma_start, out=adec[:].rearrange("p h n -> p (h n)"), in_=sAd[t]),
                          reads=[f"sAd{t}_{h}" for h in range(4)], writes=[f"adec{h}" for h in range(4)])

                def rec_main(hd, th, t):
                    (B0, B0k), (B1, B1k), (B2, B2k), (B3, B3k) = TB[th]
                    pO, pOk = (pb[4], "pb4") if th == 0 else (pb[2], "pb2")
                    aT, aTk = attnT[:, th, :], f"attnT{th}"
                    hs = slice(hd * 128, (hd + 1) * 128)
                    Ep, Epk = B2, B2k
                    P.dma("sp", f"lEp{th}", I(nc.sync.dma_start, out=Ep[:], in_=sEp[t][:, hd * 512:(hd + 1) * 512]), reads=[f"sEp{t}_{hd}"], writes=[Epk])
                    yield
                    bankq, bqk = proj_fm(512 + hd * 128)
                    P.emit("dve", I(TT, out=qt_bf[:, hd, :], in0=bankq[:], in1=Ep[:], op=ALU.mult), reads=[bqk, Epk], writes=[f"qt{hd}"])
                    yield
                    P.emit("pe", [I(MM, pb[3][:, pr * 128:(pr + 1) * 128], lhsT=kh_bf[:, hd, pr * 128:(pr + 1) * 128],
                                    rhs=qt_bf[:, hd, pr * 128:(pr + 1) * 128], start=True, stop=True) for pr in range(4)],
                           reads=[f"kh{hd}", f"qt{hd}"], writes=["pb3"])
                    P.emit("dve", I(TT, out=aT, in0=pb[3][:], in1=recmask, op=ALU.mult), reads=["pb3", "cstt"], writes=[aTk])
                    yield
                    ops = []
                    for n in range(8):
                        pr, base = n // 2, (n % 2) * 64
                        bankp = pb[5 + n % 2]
                        ops.append(I(MM, bankp[:, pr * 128:(pr + 1) * 128], lhsT=kh_tm[base:base + 64, pr, hs],
                                     rhs=v_tm[base:base + 64, pr, hs], start=True, stop=True))
                    P.emit("pe", ops, reads=[f"khtm{hd}", "v_tm"], writes=["pb5", "pb6"])
                    for n in range(8):
                        cur = scur[hd]
                        bankp = pb[5 + n % 2]
                        P.emit("act", I(ACT, out=Sbf[:, th, n, :], in_=Sst[:, hd, cur, :], func=AF.Identity, scale=adec[:, hd, n:n + 1]),
                               reads=[f"S{hd}_{cur}", f"adec{hd}"], writes=[f"Sbf{th}_{n}"])
                        P.emit("dve", I(V.scalar_tensor_tensor, out=Sst[:, hd, 1 - cur, :], in0=Sst[:, hd, cur, :], scalar=adec[:, hd, n:n + 1],
                                        in1=bankp[:, (n // 2) * 128:(n // 2 + 1) * 128], op0=ALU.mult, op1=ALU.add),
                               reads=[f"S{hd}_{cur}", f"adec{hd}", "pb5", "pb6"], writes=[f"S{hd}_{1 - cur}"])
                        scur[hd] = 1 - cur
                    yield
                    ops = []
                    for pr in range(4):
                        ops.append(I(MM, pO[:, pr * 128:(pr + 1) * 128], lhsT=v_tm[:, pr, hs], rhs=aT[:, pr * 128:(pr + 1) * 128], start=True, stop=False))
                        for n in (2 * pr, 2 * pr + 1):
                            ops.append(I(MM, pO[:, n * 64:(n + 1) * 64], lhsT=Sbf[:, th, n, :], rhs=qt_bf[:, hd, n * 64:(n + 1) * 64],
                                         start=False, stop=(n % 2 == 1)))
                    P.emit("pe", ops, reads=["v_tm", aTk, f"qt{hd}"] + [f"Sbf{th}_{n}" for n in range(8)], writes=[pOk])
                    yield
                    osq, osqk = B2, B2k
                    P.emit("act", I(ACT, out=osq[:], in_=pO[:], func=AF.Square), reads=[pOk], writes=[osqk])
                    yield
                    P.emit("pe", I(MM, pb[3][:], lhsT=ones_r[:], rhs=osq[:], start=True, stop=True), reads=[osqk, "ones_r"], writes=["pb3"])
                    sd, sdk = B0, B0k
                    P.emit("act", I(ACT, out=sd[:], in_=pb[3][:], func=AF.Ln, bias=epsT[:, 0:1]), reads=["pb3", "epsT"], writes=[sdk])
                    yield
                    rstd, rsk = B0, B0k
                    P.emit("act", I(ACT, out=rstd[:], in_=sd[:], func=AF.Exp, scale=-0.5), reads=[sdk], writes=[rsk])
                    yield
                    t1, t1k = B3, B3k
                    P.emit("dve", I(TT, out=t1[:], in0=pO[:], in1=rstd[:], op=ALU.mult), reads=[pOk, rsk], writes=[t1k])
                    yield
                    bankg, bgk = proj_fm(2048 + hd * 128)
                    sgt, sgtk = B1, B1k
                    P.emit("act", I(ACT, out=sgt[:], in_=bankg[:], func=AF.Silu), reads=[bgk], writes=[sgtk])
                    yield
                    P.emit("dve", I(V.scalar_tensor_tensor, out=ymix[:, 2 + hd, :], in0=t1[:], scalar=normg, in1=sgt[:], op0=ALU.mult, op1=ALU.mult),
                           reads=[t1k, sgtk, "part"], writes=[f"ymix{2 + hd}"])
                    yield

                def conv_glu():
                    for c in range(2):
                        bankg, bgk = proj_fm(256 + c * 128)
                        sg, sgk = wtmp()
                        P.emit("act", I(ACT, out=sg[:], in_=bankg[:], func=AF.Sigmoid), reads=[bgk], writes=[sgk])
                        bankv, bvk = proj_fm(c * 128)
                        P.emit("dve", I(TT, out=ubuf[:, c, 30:30 + TL], in0=bankv[:], in1=sg[:], op=ALU.mult), reads=[bvk, sgk, "ubuf"], writes=["ubuf"])

                def conv_rest():
                    usq = []
                    for c in range(2):
                        P.emit("pe", [I(MM, pb[5 + c][:], lhsT=diag[:, c * 31 + j, :], rhs=ubuf[:, c, j:j + TL], start=(j == 0), stop=(j == 30)) for j in range(31)],
                               reads=["diag", "ubuf"], writes=[f"pb{5 + c}"])
                        P.emit("act", I(ACT, out=uc[:, c, :], in_=pb[5 + c][:], func=AF.Identity, bias=convb[:, c:c + 1]), reads=[f"pb{5 + c}", "part"], writes=[f"uc{c}"])
                        q_, qk_ = wtmp()
                        P.emit("act", I(ACT, out=q_[:], in_=uc[:, c, :], func=AF.Square), reads=[f"uc{c}"], writes=[qk_])
                        usq.append((q_, qk_))
                    P.emit("pe", [I(MM, pb[3][:], lhsT=ones_c[:], rhs=uc[:, c, :], start=(c == 0), stop=(c == 1)) for c in range(2)]
                           + [I(MM, pb[4][:], lhsT=ones_c[:], rhs=usq[c][0][:], start=(c == 0), stop=(c == 1)) for c in range(2)],
                           reads=["uc0", "uc1", usq[0][1], usq[1][1], "ones_c"], writes=["pb3", "pb4"])
                    rstd, rsk, nmr, nmk = ln_stats(0)
                    for c in range(2):
                        ta, tak = wtmp()
                        P.emit("dve", I(TT, out=ta[:], in0=uc[:, c, :], in1=rstd[:], op=ALU.mult), reads=[f"uc{c}", rsk], writes=[tak])
                        tb, tbk = wtmp()
                        P.emit("pool", I(G.tensor_tensor, out=tb[:], in0=ta[:], in1=nmr[:], op=ALU.add), reads=[tak, nmk], writes=[tbk])
                        P.emit("act", I(ACT, out=ymix[:, c, :], in_=tb[:], func=AF.Silu, scale=convg[:, c:c + 1], bias=convlb[:, c:c + 1]),
                               reads=[tbk, "part"], writes=[f"ymix{c}"])
                    P.emit("pool", I(G.tensor_copy, out=ubuf[:, :, 0:30], in_=ubuf[:, :, TL:TL + 30]), reads=["ubuf"], writes=["ubuf"])


                def att_proj():
                    for c in range(2):
                        bq, bqk = proj_fm(2560 + c * 128)
                        P.emit("act", I(ACT, out=qTa[:, c, :], in_=bq[:], func=AF.Copy, scale=0.125), reads=[bqk], writes=["qTa"])
                        bk_, bkk = proj_fm(2816 + c * 128)
                        P.emit("dve", I(V.tensor_copy, out=kTa[:, c, TL:2 * TL], in_=bk_[:]), reads=[bkk, "kTa"], writes=["kTa"])
                    for s2 in range(2):
                        bank, bkey = gbank()
                        ops = []
                        for ss in range(2):
                            s = s2 * 2 + ss
                            for k in range(8):
                                ops.append(I(MM, bank[:, ss * 256:(ss + 1) * 256], lhsT=hbf[:, k, s * 128:(s + 1) * 128], rhs=winb[:, k, 2048:2304],
                                             start=(k == 0), stop=(k == 7)))
                        P.emit("pe", ops, reads=wsrc(3072, 3328)[2] + ["hbf"], writes=[bkey])
                        P.emit("act", I(A.copy, out=vat[:, 4 + 2 * s2:6 + 2 * s2, :], in_=bank[:].rearrange("p (s d) -> p s d", d=256)),
                               reads=[bkey, "vat"], writes=["vat"])

                def att_main(t):
                    obanks = {}

                    def stage1(j, c, hh2):
                        J = 4 * t + j
                        nh = max(0, 4 - J)
                        hh = 2 * c + hh2
                        base = hh2 * 64
                        bA, bB = (5, 6) if hh2 == 0 else (3, 4)
                        tS, PT, tSk, PTk = tS2[:, hh2], PT2[:, hh2], f"tS{hh2}", f"PT{hh2}"
                        ops = []
                        for r in range(5):
                            bankS = pb[bA] if r < 4 else pb[bB]
                            ops.append(I(MM, bankS[:, (r % 4) * 128:(r % 4 + 1) * 128], lhsT=kTa[base:base + 64, c, (j + r) * 128:(j + r + 1) * 128],
                                         rhs=qTa[base:base + 64, c, j * 128:(j + 1) * 128], start=True, stop=True))
                        P.emit("pe", ops, reads=["kTa", "qTa"], writes=[f"pb{bA}", f"pb{bB}"])
                        P.emit("dve", I(TT, out=tS[:, 0:4, :], in0=pb[bA][:].rearrange("p (r q) -> p r q", q=128),
                                        in1=biasb[:, hh, 0:4, :], op=ALU.add), reads=[f"pb{bA}", "biasb", tSk], writes=[tSk])
                        P.emit("dve", I(TT, out=tS[:, 4, :], in0=pb[bB][:, 0:128], in1=biasb[:, hh, 4, :], op=ALU.add),
                               reads=[f"pb{bB}", "biasb", tSk], writes=[tSk])
                        if nh > 0:
                            P.emit("act", I(ACT, out=PT[:, 0:nh, :], in_=tS[:, 0:nh, :], func=AF.Exp, bias=hbias), reads=[tSk, "flg", PTk], writes=[PTk])
                        P.emit("act", I(ACT, out=PT[:, nh:5, :], in_=tS[:, nh:5, :], func=AF.Exp), reads=[tSk, PTk], writes=[PTk])

                    def stage2(j, c, hh2):
                        hh = 2 * c + hh2
                        base = hh2 * 64
                        PT, PTk = PT2[:, hh2], f"PT{hh2}"
                        if (j, c) not in obanks:
                            obanks[(j, c)] = gbank()
                        obank, obk = obanks[(j, c)]
                        ops = []
                        for r in range(5):
                            ops.append(I(MM, obank[base:base + 64, 0:128], lhsT=vat[:, j + r, hh * 64:(hh + 1) * 64], rhs=PT[:, r, :],
                                         start=(r == 0), stop=(r == 4)))
                        for r in range(5):
                            ops.append(I(MM, obank[base:base + 64, 128:256], lhsT=ones_b[:], rhs=PT[:, r, :], start=(r == 0), stop=(r == 4)))
                        P.emit("pe", ops, reads=["vat", PTk, "ones_b"], writes=[obk])
                        if hh2 == 1:
                            P.emit("dve", I(V.reciprocal, out=rrec[:], in_=obank[:, 128:256]), reads=[obk], writes=["rrec"])
                            P.emit("dve", I(TT, out=ymix[:, 6 + c, j * 128:(j + 1) * 128], in0=obank[:, 0:128], in1=rrec[:], op=ALU.mult),
                                   reads=[obk, "rrec", f"ymix{6 + c}"], writes=[f"ymix{6 + c}"])

                    prev = None
                    for it in [(j, c, hh2) for j in range(4) for c in range(2) for hh2 in range(2)]:
                        stage1(*it)
                        yield
                        if prev is not None:
                            stage2(*prev)
                            yield
                        prev = it
                    stage2(*prev)
                    yield

                def att_shift():
                    P.emit("pool", I(G.tensor_copy, out=kTa[:, :, 0:TL], in_=kTa[:, :, TL:2 * TL]), reads=["kTa"], writes=["kTa"])
                    P.emit("pool", I(G.tensor_copy, out=vat[:, 0:4, :], in_=vat[:, 4:8, :]), reads=["vat"], writes=["vat"])


                def finish_tile(t):
                    tsl = slice(t * TL, (t + 1) * TL)
                    ymk = [f"ymix{i}" for i in range(8)]
                    if t == 0 and l == 0:
                        tap("ymix", ymix[:], ymk)
                    for m in range(8):
                        bank, bkey = gbank()
                        P.emit("pe", [I(MM, bank[:], lhsT=woutb[:, k, m * 128:(m + 1) * 128], rhs=ymix[:, k, :], start=(k == 0), stop=(k == 7)) for k in range(8)],
                               reads=WOUTK + ymk, writes=[bkey])
                        P.emit("dve", I(V.scalar_tensor_tensor, out=xt[:, m, :], in0=bank[:], scalar=G1[:, m:m + 1], in1=xt[:, m, :], op0=ALU.mult, op1=ALU.add),
                               reads=[bkey, f"vG1_{pm}", "xt"], writes=["xt"])
                    return ln_stats_part(xt, "xt")

                def ln1_tail(t, st, rot=False):
                    tsl = slice(t * TL, (t + 1) * TL)
                    yield from ln_apply_gen(xt, "xt", ln1g, ln1b, st, bufs=None if rot else [(uc[:, 0, :], "uc0"), (uc[:, 1, :], "uc1")])
                    if t == 0 and l == 0:
                        tap("x1", xt[:], ["xt"])
                    P.dma("sp", "xst", I(nc.sync.dma_start, out=fm(scrA[:, tsl]), in_=xt[:]), reads=["xt"], writes=["scrA"])
                    yield


                pay = xt[:].rearrange("p k n -> p (k n)")[:, 0:PAYW]
                interleave([prefetch_h(0)])
                for t in range(NTILE):
                    rec_v(t)
                    if t == NTILE - 1:
                        conv_glu()
                        att_proj()
                        interleave([rec_pre(h, h, t) for h in range(4)])
                    else:
                        interleave([rec_pre(h, h, t) for h in range(4)] + [prefetch_h(t + 1)])
                for hd in range(4):
                    P.emit("dve", I(V.tensor_scalar, out=pay[:, hd * 128:(hd + 1) * 128], in0=Sst[:, hd, scur[hd], :], scalar1=isA, scalar2=None, op0=ALU.mult),
                           reads=[f"S{hd}_{scur[hd]}", "flg"], writes=["xt"])
                P.emit("dve", I(V.tensor_scalar, out=pay[:, 512:1536].rearrange("p (c n) -> p c n", c=2), in0=kTa[:, :, TL:2 * TL], scalar1=isA, scalar2=None, op0=ALU.mult),
                       reads=["kTa", "flg", "xt"], writes=["xt"])
                P.emit("dve", I(V.tensor_scalar, out=pay[:, 1536:2560].rearrange("p (s n) -> p s n", s=4), in0=vat[:, 4:8, :], scalar1=isA, scalar2=None, op0=ALU.mult),
                       reads=["vat", "flg", "xt"], writes=["xt"])
                P.emit("dve", I(V.tensor_scalar, out=pay[:, 2560:2620].rearrange("p (c n) -> p c n", c=2), in0=ubuf[:, :, TL:TL + 30], scalar1=isA, scalar2=None, op0=ALU.mult),
                       reads=["ubuf", "flg", "xt"], writes=["xt"])
                P.dma("pool", "bnc", I(G.dma_start, out=bounce[l].ap(), in_=pay), reads=["xt"], writes=["bounce"])
                P.dma("pool", "cc", I(G.collective_compute, "AllReduce", ALU.add, replica_groups=[[2 * i, 2 * i + 1] for i in range(n_cores // 2)],
                                      ins=[bounce[l].ap().opt()], outs=[gath[l].ap().opt()]), reads=["bounce"], writes=["gath"], inc=1)
                P.dma("pool", "gth", I(G.dma_start, out=pay, in_=gath[l].ap()), reads=["gath", "xt"], writes=["xt"])
                for hd in range(4):
                    P.emit("pool", I(G.tensor_scalar, out=Sst[:, hd, 0, :], in0=pay[:, hd * 128:(hd + 1) * 128], scalar1=isB, scalar2=None, op0=ALU.mult),
                           reads=["xt", "flg", f"S{hd}_0", f"S{hd}_1"], writes=[f"S{hd}_0"])
                    scur[hd] = 0
                P.emit("pool", I(G.tensor_scalar, out=kTa[:, :, 0:TL], in0=pay[:, 512:1536].rearrange("p (c n) -> p c n", c=2), scalar1=isB, scalar2=None, op0=ALU.mult),
                       reads=["xt", "flg", "kTa"], writes=["kTa"])
                P.emit("pool", I(G.tensor_scalar, out=vat[:, 0:4, :], in0=pay[:, 1536:2560].rearrange("p (s n) -> p s n", s=4), scalar1=isB, scalar2=None, op0=ALU.mult),
                       reads=["xt", "flg", "vat"], writes=["vat"])
                P.emit("pool", I(G.tensor_scalar, out=ubuf[:, :, 0:30], in0=pay[:, 2560:2620].rearrange("p (c n) -> p c n", c=2), scalar1=isB, scalar2=None, op0=ALU.mult),
                       reads=["xt", "flg", "ubuf"], writes=["ubuf"])
                interleave([prefetch_h(0)])
                st_prev = None
                for t in range(NTILE):
                    if t == 0:
                        load_rec(0)
                    if t > 0:
                        interleave([rec_main(0, 0, t), rec_main(1, 1, t), ln1_tail(t - 1, st_prev)])
                    else:
                        interleave([rec_main(0, 0, t), rec_main(1, 1, t)])
                    interleave([rec_main(2, 0, t), rec_main(3, 1, t)])
                    load_tile(t)
                    if t + 1 < NTILE:
                        load_rec(t + 1)
                    conv_glu()
                    conv_rest()
                    att_proj()
                    if t + 1 < NTILE:
                        interleave([att_main(t), prefetch_h(t + 1)])
                    else:
                        interleave([att_main(t)])
                    att_shift()
                    st_prev = finish_tile(t)
                interleave([ln1_tail(NTILE - 1, st_prev, rot=True)])
            P.barrier()
            ck("mixer")
            with contextlib.ExitStack() as fs:
                fsb = lambda n, s, d=F32: fs.enter_context(nc.sbuf_tensor(f"{n}_{l}", s, d))
                xb = fsb("xb", [128, 8, FB])
                h2 = fsb("h2", [128, 8, FB], BF16)
                hid = fsb("hid", [128, 5, FB], BF16)
                wg = fsb("wg", [128, 8, 640], BF16)
                wu = fsb("wu", [128, 8, 640], BF16)
                w2 = fsb("w2", [128, 5, D], BF16)
                groups = [(0, 5), (5, 5), (10, 4), (14, 4), (18, 4)]
                if l + 1 < DEPTH:
                    load_wpre(l + 1)
                XBK = [f"xb{tt}" for tt in range(FT)]
                lnb2 = [fsb(f"lnb2_{i}", [128, 512]) for i in range(4)]
                lnrow = [lnb2[i][0:1, 0:256] for i in range(2)]
                modg = None
                if l + 1 < DEPTH:
                    stgF = ([fsb(f"stgF{i}", [128, 8, 256], BF16) for i in range(2)], lnrow)
                    modg = mod_gen(l + 1, stgF)

                def adv():
                    if modg is not None:
                        next(modg, None)
                WGK = [f"wg{k}" for k in range(8)]
                WUK = [f"wu{k}" for k in range(8)]
                for fb in range(NFB):
                    bsl = slice(fb * FB, (fb + 1) * FB)
                    for tt in range(FT):
                        csl = slice(tt * TL, (tt + 1) * TL)
                        P.dma("sp", f"xbl{tt}", I(nc.sync.dma_start, out=xb[:, :, csl], in_=fm(scrA[:, fb * FB + tt * TL:fb * FB + (tt + 1) * TL])),
                              reads=["scrA"], writes=[f"xb{tt}"])
                    for tt in range(FT):
                        csl = slice(tt * TL, (tt + 1) * TL)
                        for k in range(8):
                            P.emit("act", I(ACT, out=h2[:, k, csl], in_=xb[:, k, csl], func=AF.Identity, scale=A2[:, k:k + 1], bias=sh2[:, k:k + 1]),
                                   reads=[f"xb{tt}", f"vA2_{pm}", MK], writes=[f"h2_{tt}"])
                    for (f0, nf) in groups:
                        for k in range(8):
                            P.dma("pool", "wg", I(G.dma_start, out=wg[:, k, 0:nf * 128], in_=wf1_d[l][k * 128:(k + 1) * 128, f0 * 128:(f0 + nf) * 128]), writes=[f"wg{k}"])
                            P.dma("pool", "wu", I(G.dma_start, out=wu[:, k, 0:nf * 128], in_=wf1_d[l][k * 128:(k + 1) * 128, DFF + f0 * 128:DFF + (f0 + nf) * 128]), writes=[f"wu{k}"])
                        for fi in range(nf):
                            P.dma("pool", "w2", I(G.dma_start, out=w2[:, fi, :], in_=wf2_d[l][(f0 + fi) * 128:(f0 + fi + 1) * 128, :]), writes=[f"w2{fi}"])
                        W2K = [f"w2{fi}" for fi in range(nf)]
                        for fi in range(nf):
                            for tt in range(FT):
                                csl = slice(tt * TL, (tt + 1) * TL)
                                bg, bgk = gbank(5)
                                P.emit("pe", [I(MM, bg[:], lhsT=wg[:, k, fi * 128:(fi + 1) * 128], rhs=h2[:, k, csl], start=(k == 0), stop=(k == 7)) for k in range(8)],
                                       reads=WGK + [f"h2_{tt}"], writes=[bgk])
                                bu, buk = gbank(5)
                                P.emit("pe", [I(MM, bu[:], lhsT=wu[:, k, fi * 128:(fi + 1) * 128], rhs=h2[:, k, csl], start=(k == 0), stop=(k == 7)) for k in range(8)],
                                       reads=WUK + [f"h2_{tt}"], writes=[buk])
                                sg, sgk = wtmp()
                                P.emit("act", I(ACT, out=sg[:], in_=bg[:], func=AF.Silu), reads=[bgk], writes=[sgk])
                                P.emit("dve", I(TT, out=hid[:, fi, csl], in0=bu[:], in1=sg[:], op=ALU.mult), reads=[buk, sgk, "hid"], writes=["hid"])
                                if fb == 0:
                                    adv()
                        if l == 0 and fb == 0:
                            tap(f"hid{f0}", hid[:, :, 0:512], ["hid"])
                        for m in range(8):
                            for tt in range(FT):
                                csl = slice(tt * TL, (tt + 1) * TL)
                                bo, bok = gbank(5)
                                P.emit("pe", [I(MM, bo[:], lhsT=w2[:, fi, m * 128:(m + 1) * 128], rhs=hid[:, fi, csl], start=(fi == 0), stop=(fi == nf - 1)) for fi in range(nf)],
                                       reads=W2K + ["hid"], writes=[bok])
                                P.emit("dve", I(V.scalar_tensor_tensor, out=xb[:, m, csl], in0=bo[:], scalar=G2[:, m:m + 1], in1=xb[:, m, csl], op0=ALU.mult, op1=ALU.add),
                                       reads=[bok, f"vG2_{pm}", f"xb{tt}"], writes=[f"xb{tt}"])
                    if l == 0 and fb == 0:
                        tap("z2", xb[:], XBK)
                    if modg is not None:
                        for _ in modg:
                            pass
                    for t0 in range(0, FT, 2):
                        gens = []
                        for i, tt in enumerate(range(t0, min(FT, t0 + 2))):
                            csl = slice(tt * TL, (tt + 1) * TL)
                            st = ln_stats_part(xb, f"xb{tt}", csl, bk=((3, 4), (5, 6))[i],
                                               outs=((lnb2[2 * i], f"lnb2_{2 * i}"), (lnb2[2 * i + 1], f"lnb2_{2 * i + 1}")))
                            gens.append(ln_apply_gen(xb, f"xb{tt}", ln2g, ln2b, st, csl,
                                                     bufs=[(W[4 + 2 * i], f"W{4 + 2 * i}"), (W[5 + 2 * i], f"W{5 + 2 * i}")]))
                        interleave(gens)
                        t1_ = min(FT, t0 + 2)
                        P.dma("sp", "xbs", I(nc.sync.dma_start, out=fm(xdst[:, fb * FB + t0 * TL:fb * FB + t1_ * TL]), in_=xb[:, :, t0 * TL:t1_ * TL]),
                              reads=[f"xb{tt}" for tt in range(t0, t1_)], writes=[f"{xdk}_{fb}_{t0}"])
                    if l == 0 and fb == 0:
                        tap("x_l0", xb[:], XBK)
                if modg is not None:
                    for _ in modg:
                        pass
            P.barrier()

        try:
            for l in range(DEPTH):
                layer(l)
        except _Stop:
            pass
        P.final_wait("sp", [k for k in list(P.res.keys()) if str(k).startswith("outT")] + ["dbgo_" + n for n in dbg_out])
        P.run()
    return nc


def _consts():
    c = np.zeros((128, 1152), np.float32)
    c[:, 0:128] = np.eye(128, dtype=np.float32)
    s = np.arange(128)[:, None]
    t = np.arange(128)[None, :]
    m = ((s // 64 == t // 64) & (s <= t)).astype(np.float32)
    c[:, 128:640] = np.tile(m, (1, 4))
    r = np.ones((128, 512), np.float32)
    r[:, 0::64] = 0.0
    c[:, 640:1152] = r
    return c


def _pack_params(inp, depth):
    par = np.zeros((128, NPAR), np.float32)
    ch = lambda v: np.ascontiguousarray(np.asarray(v, np.float32).reshape(-1, 128).T)
    for l in range(depth):
        po = l * PL
        par[:, po:po + 48] = ch(inp["b_ada"][l])
        par[:, po + 48:po + 56] = ch(inp["ln1_g"][l])
        par[:, po + 56:po + 64] = ch(inp["ln1_b"][l])
        par[:, po + 64:po + 72] = ch(inp["ln2_g"][l])
        par[:, po + 72:po + 80] = ch(inp["ln2_b"][l])
        cw = np.asarray(inp["conv_w"][l], np.float32)
        par[:, po + 80:po + 142] = cw.reshape(31, 2, 128).transpose(2, 1, 0).reshape(128, 62)
        par[:, po + 142:po + 144] = ch(inp["conv_b"][l])
        par[:, po + 144:po + 146] = ch(inp["conv_ln_g"][l])
        par[:, po + 146:po + 148] = ch(inp["conv_ln_b"][l])
        par[:, po + 148:po + 149] = np.asarray(inp["rec_norm_g"][l], np.float32).reshape(128, 1)
    rl = np.asarray(inp["rec_lower_bound"], np.float32)
    par[:, 2 * PL:2 * PL + 8] = rl.reshape(2, 4, 128).transpose(2, 0, 1).reshape(128, 8)
    return par


def _bias_tiles(rel_bias_l):
    r = np.arange(5)[:, None, None]
    i = np.arange(128)[None, :, None]
    j = np.arange(128)[None, None, :]
    idx = np.clip((r - 4) * 128 + i - j, -128, 128) + 128
    valid = ~(((r == 0) & (i < 64) & (j >= 64)) | ((r == 4) & (i >= 64) & (j < 64)))
    tb = np.asarray(rel_bias_l, np.float32)
    g = tb[:, idx]
    g = np.where(valid[None], g, np.float32(NEG_BIG)).astype(np.float32)
    return np.ascontiguousarray(g.transpose(2, 0, 1, 3).reshape(128, 4 * 5 * 128))


def make_in_maps(inp, NT, depth, n_cores=8):
    inp = {k: np.asarray(v) for k, v in inp.items()}
    cst = _consts()
    par = _pack_params(inp, depth)
    shared = {"par": par, "cst": cst}
    for l in range(depth):
        shared[f"wada{l}"] = np.ascontiguousarray(inp["w_ada"][l], np.float32)
        shared[f"win{l}"] = np.ascontiguousarray(inp["w_in"][l], np.float32)
        shared[f"wout{l}"] = np.ascontiguousarray(inp["w_out"][l], np.float32)
        shared[f"wf1{l}"] = np.ascontiguousarray(inp["w_ffn_in"][l], np.float32)
        shared[f"wf2{l}"] = np.ascontiguousarray(inp["w_ffn_out"][l], np.float32)
        shared[f"bias{l}"] = _bias_tiles(inp["rel_bias"][l])
    maps = []
    for c in range(n_cores):
        b, half = c // 2, c % 2
        m = dict(shared)
        m["xT"] = np.ascontiguousarray(inp["x"][b, half * NT:(half + 1) * NT].T.astype(np.float32))
        m["cT"] = np.ascontiguousarray(inp["c"][b].astype(np.float32).reshape(8, 128).T)
        flg = np.zeros((128, 4), np.float32)
        flg[:, 0] = 1.0 if half == 0 else 0.0
        flg[:, 1] = 1.0 if half == 1 else 0.0
        flg[:, 2] = 0.0 if half == 1 else NEG_BIG
        m["flg"] = flg
        maps.append(m)
    return maps


_NC_CACHE = {}


def kernel(**inputs):
    T = inputs["x"].shape[1]
    B = inputs["x"].shape[0]
    NT = T // 2
    key = (NT, 2)
    if key not in _NC_CACHE:
        _NC_CACHE[key] = build(NT, 2)
    nc = _NC_CACHE[key]
    maps = make_in_maps(inputs, NT, 2)
    res = run_bass_kernel_spmd(nc, maps, core_ids=list(range(8)))
    out = np.empty((B, T, D), np.float32)
    for c in range(2 * B):
        b, half = c // 2, c % 2
        out[b, half * NT:(half + 1) * NT] = res.results[c]["outT"].T
    return out
```

```python
import contextlib
import numpy as np
import concourse.bass as bass
import concourse.mybir as mybir
from concourse.bass_utils import run_bass_kernel_spmd

F32 = mybir.dt.float32
BF16 = mybir.dt.bfloat16
AF = mybir.ActivationFunctionType
ALU = mybir.AluOpType

D = 1024
DIN = 3328
DFF = 2816
DEPTH_FULL = 2
ALPHA = (2 * DEPTH_FULL) ** 0.25
LN_EPS = 1e-5
NEG_BIG = -1e30
TL = 512
PL = 149
NPAR = 2 * PL + 8
SEM_CAP = 30000


class Prog:
    ENGS = ("pe", "act", "dve", "pool", "sp")

    def __init__(self, nc, same_engine_sync=True):
        self.nc = nc
        self.eng = {"pe": nc.tensor, "act": nc.scalar, "dve": nc.vector,
                    "pool": nc.gpsimd, "sp": nc.sync}
        self.plan = {e: [] for e in self.ENGS}
        self.seq = {e: 0 for e in self.ENGS}
        self.sems = {}
        self.waited = {e: {} for e in self.ENGS}
        self.res = {}
        self.same = same_engine_sync
        self.ctx = []
        self.dma_sems = {}
        self.ninst = 0
        self.stopped = False

    def _sem(self, name):
        cm = self.nc.semaphore(name)
        s = cm.__enter__()
        self.ctx.append(cm)
        return s

    def eng_sem(self, e, epoch):
        k = (e, epoch)
        if k not in self.sems:
            self.sems[k] = self._sem(f"s_{e}_{epoch}")
        return self.sems[k]

    def _need_wait(self, E, tok):
        if tok is None:
            return None
        if tok[0] == "eng":
            _, F, s = tok
            if F == E and (E in ("pe", "sp") or not self.same):
                return None
            key = ("eng", F)
            if self.waited[E].get(key, 0) >= s:
                return None
            self.waited[E][key] = s
            epoch, val = (s - 1) // SEM_CAP, (s - 1) % SEM_CAP + 1
            return (self.eng_sem(F, epoch), val)
        _, name, val = tok
        key = ("dma", name)
        if self.waited[E].get(key, 0) >= val:
            return None
        self.waited[E][key] = val
        return (self.dma_sems[name][0], val)

    def _deps(self, E, reads, writes, own_dma=None):
        toks = []
        for k in reads:
            r = self.res.get(k)
            if r and r["w"] is not None:
                toks.append(r["w"])
        for k in writes:
            r = self.res.get(k)
            if r:
                if r["w"] is not None:
                    toks.append(r["w"])
                toks.extend(r["r"])
        best = {}
        for t in toks:
            if t[0] == "dma" and own_dma is not None and t[1] == own_dma:
                continue
            key = (t[0], t[1])
            if key not in best or t[2] > best[key][2]:
                best[key] = t
        waits = []
        for t in best.values():
            w = self._need_wait(E, t)
            if w:
                waits.append(w)
        return waits

    def _update(self, tok, reads, writes):
        for k in reads:
            r = self.res.setdefault(k, {"w": None, "r": []})
            r["r"] = [t for t in r["r"] if not (t[0] == tok[0] and t[1] == tok[1])] + [tok]
        for k in writes:
            self.res[k] = {"w": tok, "r": []}

    def emit(self, E, fn, reads=(), writes=()):
        if self.stopped:
            return
        waits = self._deps(E, reads, writes)
        self.seq[E] += 1
        s = self.seq[E]
        sem = self.eng_sem(E, (s - 1) // SEM_CAP)
        eng = self.eng[E]

        ops = fn if isinstance(fn, list) else [fn]

        def run(waits=waits, ops=ops, sem=sem, eng=eng):
            for (ws, wv) in waits:
                eng.wait_ge(ws, wv)
            for (m_, a_, k_) in ops:
                inst = m_(*a_, **k_)
            inst.then_inc(sem, 1)
        self.plan[E].append(run)
        self._update(("eng", E, s), reads, writes)
        self.ninst += 1

    def dma(self, Q, name, fn, reads=(), writes=(), inc=16):
        if self.stopped:
            return
        if name not in self.dma_sems:
            self.dma_sems[name] = [self._sem(f"d_{name}"), 0]
        waits = self._deps(Q, reads, writes, own_dma=name)
        self.dma_sems[name][1] += inc
        val = self.dma_sems[name][1]
        sem = self.dma_sems[name][0]
        eng = self.eng[Q]

        def run(waits=waits, fn=fn, sem=sem, eng=eng, inc=inc):
            for (ws, wv) in waits:
                eng.wait_ge(ws, wv)
            m_, a_, k_ = fn
            if inc == 16:
                m_(*a_, **k_).then_inc(sem, 16)
            else:
                m_(*a_, **k_).then_inc(sem)
        self.plan[Q].append(run)
        self._update(("dma", name, val), reads, writes)
        self.ninst += 1

    def barrier(self):
        if self.stopped:
            return
        for E in self.ENGS:
            waits = []
            for F in self.ENGS:
                if F == E or F == "sp" or self.seq[F] == 0:
                    continue
                w = self._need_wait(E, ("eng", F, self.seq[F]))
                if w:
                    waits.append(w)
            for name, (sem, val) in self.dma_sems.items():
                if val > 0:
                    w = self._need_wait(E, ("dma", name, val))
                    if w:
                        waits.append(w)
            eng = self.eng[E]

            def run(waits=waits, eng=eng):
                for (ws, wv) in waits:
                    eng.wait_ge(ws, wv)
            self.plan[E].append(run)

    def final_wait(self, Q, keys):
        waits = self._deps(Q, keys, keys)
        eng = self.eng[Q]

        def run(waits=waits, eng=eng):
            for (ws, wv) in waits:
                eng.wait_ge(ws, wv)
        self.plan[Q].append(run)

    def run(self):
        nc = self.nc
        with nc.Block() as block:
            @block.tensor
            def _(e):
                for f in self.plan["pe"]:
                    f()

            @block.scalar
            def _(e):
                for f in self.plan["act"]:
                    f()

            @block.vector
            def _(e):
                for f in self.plan["dve"]:
                    f()

            @block.gpsimd
            def _(e):
                for f in self.plan["pool"]:
                    f()

            @block.sync
            def _(e):
                for f in self.plan["sp"]:
                    f()
        for cm in reversed(self.ctx):
            cm.__exit__(None, None, None)


def I(m, *a, **k):
    return (m, a, k)


class _Stop(Exception):
    pass


def build(NT, DEPTH=2, dbg=None, same_engine_sync=True, stop=None, n_cores=8):
    assert NT % TL == 0
    NTILE = NT // TL
    FB = min(NT, 2048)
    NFB = NT // FB
    FT = FB // TL
    nc = bass.Bass("TRN2", target_bir_lowering=False)
    dr = lambda n, s, kind="ExternalInput", d=F32: nc.dram_tensor(n, s, d, kind=kind).ap()
    xT = dr("xT", [D, NT])
    cT = dr("cT", [128, 8])
    par = dr("par", [128, NPAR])
    cst = dr("cst", [128, 128 + 512 + 512])
    wada = [dr(f"wada{l}", [D, 6 * D]) for l in range(DEPTH)]
    win_d = [dr(f"win{l}", [D, DIN]) for l in range(DEPTH)]
    wout_d = [dr(f"wout{l}", [D, D]) for l in range(DEPTH)]
    wf1_d = [dr(f"wf1{l}", [D, 2 * DFF]) for l in range(DEPTH)]
    wf2_d = [dr(f"wf2{l}", [DFF, D]) for l in range(DEPTH)]
    bias_d = [dr(f"bias{l}", [128, 4 * 5 * 128]) for l in range(DEPTH)]
    flg_d = dr("flg", [128, 4])
    PAYW = 512 + 1024 + 1024 + 60
    bounce = [nc.dram_tensor(f"bounce{l}", [128, PAYW], F32, kind="Internal") for l in range(DEPTH)]
    sEp = [dr(f"sEp{t}", [128, 2048], kind="Internal") for t in range(NT // TL)]
    sKh = [dr(f"sKh{t}", [128, 2048], kind="Internal", d=BF16) for t in range(NT // TL)]
    sKt = [dr(f"sKt{t}", [128, 2048], kind="Internal", d=BF16) for t in range(NT // TL)]
    sAd = [dr(f"sAd{t}", [128, 32], kind="Internal") for t in range(NT // TL)]
    sVt = [dr(f"sVt{t}", [128, 2048], kind="Internal", d=BF16) for t in range(NT // TL)]
    gath = [nc.dram_tensor(f"gath{l}", [128, PAYW], F32, kind="Internal") for l in range(DEPTH)]
    outT = dr("outT", [D, NT], kind="ExternalOutput")
    scrA = dr("scrA", [D, NT], kind="Internal")
    scrB = dr("scrB", [D, NT], kind="Internal")
    dbg_out = {}
    if dbg:
        for n, s in dbg.items():
            dbg_out[n] = dr("dbg_" + n, list(s), kind="ExternalOutput")

    P = Prog(nc, same_engine_sync)
    es = contextlib.ExitStack()
    sb = lambda n, s, d=F32: es.enter_context(nc.sbuf_tensor(n, s, d))
    ps = lambda n, s, d=F32: es.enter_context(nc.psum_tensor(n, s, d))
    fm = lambda ap: ap.rearrange("(kc p) n -> p kc n", p=128)
    V, A, G, T = nc.vector, nc.scalar, nc.gpsimd, nc.tensor
    MM = T.matmul
    ACT = A.activation

    with es:
        pb = [ps(f"pb{i}", [128, 512]) for i in range(7)]
        pT = ps("pT", [128, 1024], BF16)
        part = sb("part", [128, NPAR])
        cstt = sb("cstt", [128, 1152])
        ident = sb("ident", [128, 128], BF16)
        ones_d = sb("ones_d", [128, 128])
        ones_c = sb("ones_c", [128, 128])
        ones_r = sb("ones_r", [128, 128])
        ones_b = sb("ones_b", [128, 64], BF16)
        epsT = sb("epsT", [128, 4])
        cact = sb("cact", [128, 8], BF16)
        cf = sb("cf", [128, 8])
        modv2 = sb("modv", [128, 2, 48])
        cactf = sb("cactf", [128, 8])
        vec2 = sb("vec", [128, 2, 8, 8])
        lbt = sb("lbt", [128, 8, 4])
        omlb = sb("omlb", [128, 2, 4])
        W = [sb(f"W{i}", [128, 512]) for i in range(8)]
        wpre = sb("wpre", [128, 8, 1024], BF16)

        def load_wpre(lw):
            for k in range(8):
                P.dma("pool", "wpre", I(G.dma_start, out=wpre[:, k, :], in_=win_d[lw][k * 128:(k + 1) * 128, 1024:2048]), writes=[f"wpre{k}"])
        WPK = [f"wpre{k}" for k in range(8)]
        rstdT = sb("rstdT", [128, 512])
        nmrT = sb("nmrT", [128, 512])
        wctr = [0]

        def wtmp():
            i = wctr[0] % len(W)
            wctr[0] += 1
            return W[i], f"W{i}"

        gctr = [0]

        def gbank(n=2):
            i = gctr[0] % n
            gctr[0] += 1
            return pb[i], f"pb{i}"

        recmask = cstt[:, 128:640]
        resetm = cstt[:, 640:1152]
        b3 = lambda a: a[:].rearrange("p (c t) -> p c t", t=64)

        P.dma("sp", "ld0", I(nc.sync.dma_start, out=part[:], in_=par), writes=["part"])
        P.dma("sp", "ld1", I(nc.sync.dma_start, out=cstt[:], in_=cst), writes=["cstt"])
        P.dma("sp", "ld2", I(nc.sync.dma_start, out=cf[:], in_=cT), writes=["cf"])
        flg = sb("flgs", [128, 4])
        P.dma("sp", "ld3", I(nc.sync.dma_start, out=flg[:], in_=flg_d), writes=["flg"])
        isA, isB, hbias = flg[:, 0:1], flg[:, 1:2], flg[:, 2:3]
        P.emit("pool", I(G.memset, ones_d[:], 1.0 / D), writes=["ones_d"])
        P.emit("pool", I(G.memset, ones_c[:], 1.0 / 256), writes=["ones_c"])
        P.emit("pool", I(G.memset, ones_r[:], 1.0 / 128), writes=["ones_r"])
        P.emit("pool", I(G.memset, ones_b[:], 1.0), writes=["ones_b"])
        P.emit("pool", I(G.memset, epsT[:, 0:1], LN_EPS), writes=["epsT"])
        P.emit("pool", I(G.memset, epsT[:, 1:2], LN_EPS / (ALPHA * ALPHA)), writes=["epsT"])
        P.emit("pool", I(G.memset, epsT[:, 2:3], 1.0), writes=["epsT"])
        P.emit("dve", I(V.tensor_copy, out=ident[:], in_=cstt[:, 0:128]), reads=["cstt"], writes=["ident"])
        P.emit("act", I(ACT, out=cact[:], in_=cf[:], func=AF.Silu), reads=["cf"], writes=["cact"])
        P.emit("act", I(ACT, out=cactf[:], in_=cf[:], func=AF.Silu), reads=["cf"], writes=["cactf"])
        rl = part[:, 2 * PL:2 * PL + 8].rearrange("p (l c) -> p l c", c=4)
        r0, r1 = rl[:, 0, :], rl[:, 1, :]
        mx, e0, e1, ssum, rs, s0, s1, c1 = [lbt[:, i, :] for i in range(8)]
        TT = V.tensor_tensor
        P.emit("dve", I(V.tensor_max, out=mx, in0=r0, in1=r1), reads=["part"], writes=["lb_mx"])
        P.emit("dve", I(TT, out=e0, in0=r0, in1=mx, op=ALU.subtract), reads=["part", "lb_mx"], writes=["lb_e0"])
        P.emit("dve", I(TT, out=e1, in0=r1, in1=mx, op=ALU.subtract), reads=["part", "lb_mx"], writes=["lb_e1"])
        P.emit("act", I(ACT, out=e0, in_=e0, func=AF.Exp), reads=["lb_e0"], writes=["lb_e0"])
        P.emit("act", I(ACT, out=e1, in_=e1, func=AF.Exp), reads=["lb_e1"], writes=["lb_e1"])
        P.emit("dve", I(TT, out=ssum, in0=e0, in1=e1, op=ALU.add), reads=["lb_e0", "lb_e1"], writes=["lb_s"])
        P.emit("dve", I(V.reciprocal, out=rs, in_=ssum), reads=["lb_s"], writes=["lb_rs"])
        P.emit("dve", I(TT, out=s0, in0=e0, in1=rs, op=ALU.mult), reads=["lb_e0", "lb_rs"], writes=["lb_s0"])
        P.emit("dve", I(TT, out=s1, in0=e1, in1=rs, op=ALU.mult), reads=["lb_e1", "lb_rs"], writes=["lb_s1"])
        P.emit("dve", I(TT, out=c1, in0=s0, in1=s1, op=ALU.add), reads=["lb_s0", "lb_s1"], writes=["lb_c1"])
        P.emit("dve", I(TT, out=mx, in0=s0, in1=s0, op=ALU.subtract), reads=["lb_s0"], writes=["lb_mx"])
        P.emit("dve", I(TT, out=c1, in0=c1, in1=s0, op=ALU.subtract), reads=["lb_c1", "lb_s0"], writes=["lb_c1"])
        P.emit("dve", I(V.tensor_scalar, out=omlb[:, 0, :], in0=mx, scalar1=-1.0, scalar2=1.0, op0=ALU.mult, op1=ALU.add),
               reads=["lb_mx"], writes=["omlb"])
        P.emit("dve", I(V.tensor_scalar, out=omlb[:, 1, :], in0=c1, scalar1=-1.0, scalar2=1.0, op0=ALU.mult, op1=ALU.add),
               reads=["lb_c1", "omlb"], writes=["omlb"])
        nomlb = sb("nomlb", [128, 2, 4])
        P.emit("dve", I(V.tensor_scalar, out=nomlb[:], in0=omlb[:], scalar1=-1.0, scalar2=None, op0=ALU.mult), reads=["omlb"], writes=["nomlb"])

        def tap(name, src_ap, keys):
            if name in dbg_out:
                P.dma("pool", "dbg_" + name, I(G.dma_start, out=dbg_out[name], in_=src_ap), reads=keys, writes=["dbgo_" + name])

        def ln_stats(epscol, bk=(3, 4), outs=None):
            pm_, pq_ = pb[bk[0]], pb[bk[1]]
            km_, kq_ = f"pb{bk[0]}", f"pb{bk[1]}"
            m2, m2k = wtmp()
            P.emit("act", I(ACT, out=m2[:], in_=pm_[:], func=AF.Square), reads=[km_], writes=[m2k])
            var, vark = wtmp()
            P.emit("dve", I(TT, out=var[:], in0=pq_[:], in1=m2[:], op=ALU.subtract), reads=[kq_, m2k], writes=[vark])
            sd, sdk = wtmp()
            P.emit("act", I(ACT, out=sd[:], in_=var[:], func=AF.Ln, bias=epsT[:, epscol:epscol + 1]), reads=[vark, "epsT"], writes=[sdk])
            (rstd, rsk), (nmr, nmk) = outs if outs else ((rstdT, "rstdT"), (nmrT, "nmrT"))
            P.emit("act", I(ACT, out=rstd[:], in_=sd[:], func=AF.Exp, scale=-0.5), reads=[sdk], writes=[rsk])
            P.emit("dve", I(V.scalar_tensor_tensor, out=nmr[:], in0=pm_[:], scalar=-1.0, in1=rstd[:], op0=ALU.mult, op1=ALU.mult),
                   reads=[km_, rsk], writes=[nmk])
            return rstd, rsk, nmr, nmk

        def ln_stats_part(xt, xkey, csl=slice(None), bk=(3, 4), outs=None):
            for m in range(8):
                q_, qk_ = wtmp()
                P.emit("act", I(ACT, out=q_[:], in_=xt[:, m, csl], func=AF.Square), reads=[xkey], writes=[qk_])
                P.emit("pe", [I(MM, pb[bk[0]][:], lhsT=ones_d[:], rhs=xt[:, m, csl], start=(m == 0), stop=(m == 7)),
                              I(MM, pb[bk[1]][:], lhsT=ones_d[:], rhs=q_[:], start=(m == 0), stop=(m == 7))],
                       reads=[xkey, qk_, "ones_d"], writes=[f"pb{bk[0]}", f"pb{bk[1]}"])
            return ln_stats(1, bk, outs)

        def ln_apply_gen(xt, xkey, lng, lnb, st, csl=slice(None), bufs=None):
            rstd, rsk, nmr, nmk = st
            for m in range(8):
                ta, tak = bufs[0] if bufs else wtmp()
                P.emit("dve", I(TT, out=ta[:], in0=xt[:, m, csl], in1=rstd[:], op=ALU.mult), reads=[xkey, rsk], writes=[tak])
                yield
                tb, tbk = bufs[1] if bufs else wtmp()
                if m % 2 == 0:
                    P.emit("pool", I(G.tensor_tensor, out=tb[:], in0=ta[:], in1=nmr[:], op=ALU.add), reads=[tak, nmk], writes=[tbk])
                else:
                    P.emit("dve", I(TT, out=tb[:], in0=ta[:], in1=nmr[:], op=ALU.add), reads=[tak, nmk], writes=[tbk])
                yield
                P.emit("act", I(ACT, out=xt[:, m, csl], in_=tb[:], func=AF.Identity, scale=lng[:, m:m + 1], bias=lnb[:, m:m + 1]),
                       reads=[tbk, "part", xkey], writes=[xkey])
                yield

        def ln_apply(xt, xkey, lng, lnb, csl=slice(None)):
            st = ln_stats_part(xt, xkey, csl)
            for _ in ln_apply_gen(xt, xkey, lng, lnb, st, csl):
                pass

        def ck(name):
            if stop == name:
                P.stopped = True

        def interleave(gens):
            gens = list(gens)
            while gens:
                for g in list(gens):
                    try:
                        next(g)
                    except StopIteration:
                        gens.remove(g)

        def interleave_primary(primary, extra):
            primary = list(primary)
            extra_alive = True
            while primary:
                for g in list(primary):
                    try:
                        next(g)
                    except StopIteration:
                        primary.remove(g)
                if extra_alive and primary:
                    try:
                        next(extra)
                    except StopIteration:
                        extra_alive = False

        def mod_gen(lm, stg):
            pm = lm % 2
            po_ = lm * PL
            NG = 24
            stgw, rowb = stg
            def ld(g):
                P.dma("pool", f"wa{g % 2}", I(G.dma_start, out=stgw[g % 2][:], in_=fm(wada[lm][:, g * 256:(g + 1) * 256])), writes=[f"stg{g % 2}"])
            ld(0)
            for g in range(NG):
                if g + 1 < NG:
                    ld(g + 1)
                yield
                P.emit("pe", [I(MM, pb[6][0:1, 0:256], lhsT=cact[:, k:k + 1], rhs=stgw[g % 2][:, k, :], start=(k == 0), stop=(k == 7)) for k in range(8)],
                       reads=[f"stg{g % 2}", "cact"], writes=["pb6"])
                P.emit("act", I(A.copy, out=rowb[g % 2], in_=pb[6][0:1, 0:256]), reads=["pb6"], writes=[f"rowb{g % 2}", f"lnb2_{g % 2}"])
                yield
                P.emit("pe", [I(MM, pb[5][:, g * 2 + j:g * 2 + j + 1], lhsT=rowb[g % 2][:, j * 128:(j + 1) * 128], rhs=epsT[0:1, 2:3], start=True, stop=True)
                              for j in range(2)], reads=[f"rowb{g % 2}", f"lnb2_{g % 2}", "epsT"], writes=["pb5"])
                yield
            mv = modv2[:, pm, :]
            mk = f"modv{pm}"
            P.emit("dve", I(TT, out=mv, in0=pb[5][:, 0:48], in1=part[:, po_:po_ + 48], op=ALU.add), reads=["pb5", "part"], writes=[mk])
            sh1, sc1, g1, sh2, sc2, g2 = [modv2[:, pm, i * 8:(i + 1) * 8] for i in range(6)]
            A1_, B1_, G1_, G2_, A2_ = [vec2[:, pm, i, :] for i in range(5)]
            sfx = f"_{pm}"
            P.emit("dve", I(V.tensor_scalar, out=A1_, in0=sc1, scalar1=1.0, scalar2=None, op0=ALU.add), reads=[mk], writes=["vA1" + sfx])
            P.emit("dve", I(V.tensor_copy, out=B1_, in_=sh1), reads=[mk], writes=["vB1" + sfx])
            P.emit("dve", I(V.tensor_scalar, out=G1_, in0=g1, scalar1=1.0, scalar2=1.0 / ALPHA, op0=ALU.add, op1=ALU.mult), reads=[mk], writes=["vG1" + sfx])
            P.emit("dve", I(V.tensor_scalar, out=G2_, in0=g2, scalar1=1.0, scalar2=1.0 / ALPHA, op0=ALU.add, op1=ALU.mult), reads=[mk], writes=["vG2" + sfx])
            P.emit("dve", I(V.tensor_scalar, out=A2_, in0=sc2, scalar1=1.0, scalar2=None, op0=ALU.add), reads=[mk], writes=["vA2" + sfx])
            yield

        def layer(l):
            po = l * PL
            bada = part[:, po:po + 48]
            ln1g, ln1b = part[:, po + 48:po + 56], part[:, po + 56:po + 64]
            ln2g, ln2b = part[:, po + 64:po + 72], part[:, po + 72:po + 80]
            convw = part[:, po + 80:po + 142]
            convb, convg, convlb = part[:, po + 142:po + 144], part[:, po + 144:po + 146], part[:, po + 146:po + 148]
            normg = part[:, po + 148:po + 149]
            xsrc, xsk = (xT, "xT") if l == 0 else (scrB, "scrB")
            xdst, xdk = (outT, "outT") if l == DEPTH - 1 else (scrB, "scrB")
            XSK = [xsk] + [k_ for k_ in list(P.res.keys()) if str(k_).startswith(xsk + "_")]

            pm = l % 2
            if l == 0:
                with contextlib.ExitStack() as ls:
                    stg = ([ls.enter_context(nc.sbuf_tensor(f"stgA{i}", [128, 8, 256], BF16)) for i in range(2)],
                           [ls.enter_context(nc.sbuf_tensor(f"rowA{i}", [1, 256], F32))[0:1, :] for i in range(2)])
                    g0 = mod_gen(0, stg)
                    for i_, _ in enumerate(g0):
                        if i_ == 40:
                            load_wpre(0)
                P.barrier()
            modv = modv2[:, pm, :]
            MK = f"modv{pm}"
            sh1, sc1, g1, sh2, sc2, g2 = [modv2[:, pm, i * 8:(i + 1) * 8] for i in range(6)]
            A1, B1, G1, G2, A2 = [vec2[:, pm, i, :] for i in range(5)]
            if l == 0:
                tap("modv", modv, [MK])
            ck("mod")
            with contextlib.ExitStack() as ms:
                msb = lambda n, s, d=F32: ms.enter_context(nc.sbuf_tensor(f"{n}_{l}", s, d))
                winb = msb("winb", [128, 8, DIN - 1024], BF16)
                woutb = msb("woutb", [128, 8, D], BF16)
                biasb = msb("biasb", [128, 4, 5, 128])
                diag = msb("diag", [128, 62, 128], BF16)
                xt = msb("xt", [128, 8, TL])
                hbf = msb("hbf", [128, 8, TL], BF16)
                qt_bf = msb("qt_bf", [128, 4, TL], BF16)
                kh_bf = msb("kh_bf", [128, 4, TL], BF16)
                kh_tm = msb("kh_tm", [128, 4, 512], BF16)
                v_tm = msb("v_tm", [128, 4, 512], BF16)
                attnT = msb("attnT", [128, 2, 512], BF16)
                adec = msb("adec", [128, 4, 8])
                Sst = msb("Sst", [128, 4, 2, 128])
                Sbf = msb("Sbf", [128, 2, 8, 128], BF16)
                ymix = msb("ymix", [128, 8, TL], BF16)
                ubuf = msb("ubuf", [128, 2, 30 + TL], BF16)
                uc = msb("uc", [128, 2, TL])
                qTa = msb("qTa", [128, 2, TL], BF16)
                kTa = msb("kTa", [128, 2, 2 * TL], BF16)
                vat = msb("vat", [128, 8, 256], BF16)
                tS2 = msb("tS", [128, 2, 5, 128])
                PT2 = msb("PT", [128, 2, 5, 128], BF16)
                rrec = msb("rrec", [128, 128])
                WOUTK = [f"woutb{k}" for k in range(8)]

                WBLK = [(512, 1024), (2048, 2560), (0, 512), (2560, 3328)]

                def wsrc(c0, c1_):
                    if 1024 <= c0 and c1_ <= 2048:
                        return wpre, c0 - 1024, WPK
                    loc = c0 if c0 < 1024 else c0 - 1024
                    keys = [f"winb{k}_{b}" for k in range(8) for b, (a0, a1) in enumerate(WBLK) if a0 < c1_ and c0 < a1]
                    return winb, loc, keys
                for b, (c0, c1_) in enumerate(WBLK):
                    loc = c0 if c0 < 1024 else c0 - 1024
                    for k in range(8):
                        P.dma("pool", f"winb{b}", I(G.dma_start, out=winb[:, k, loc:loc + (c1_ - c0)], in_=win_d[l][k * 128:(k + 1) * 128, c0:c1_]), writes=[f"winb{k}_{b}"])
                for k in range(8):
                    P.dma("pool", "woutb", I(G.dma_start, out=woutb[:, k, :], in_=wout_d[l][k * 128:(k + 1) * 128, :]), writes=[f"woutb{k}"])
                P.dma("sp", "biasb", I(nc.sync.dma_start, out=biasb[:].rearrange("p h r j -> p (h r j)"), in_=bias_d[l]), writes=["biasb"])
                for cj in range(62):
                    P.emit("pool", I(G.tensor_scalar, out=diag[:, cj, :], in0=cstt[:, 0:128], scalar1=convw[:, cj:cj + 1], scalar2=None, op0=ALU.mult),
                           reads=["cstt", "part"], writes=["diag"])
                P.emit("dve", I(V.memset, Sst[:], 0.0), writes=[f"S{h}_{c}" for h in range(4) for c in range(2)])
                P.emit("dve", I(V.memset, ubuf[:], 0.0), writes=["ubuf"])
                P.emit("dve", I(V.memset, kTa[:], 0.0), writes=["kTa"])
                P.emit("dve", I(V.memset, vat[:], 0.0), writes=["vat"])
                scur = [0, 0, 0, 0]
                ck("wload")

                def proj_fm(col):
                    bank, bkey = gbank()
                    wt_, lc_, wk_ = wsrc(col, col + 128)
                    P.emit("pe", [I(MM, bank[:], lhsT=wt_[:, k, lc_:lc_ + 128], rhs=hbf[:, k, :], start=(k == 0), stop=(k == 7)) for k in range(8)],
                           reads=wk_ + ["hbf"], writes=[bkey])
                    return bank, bkey

                TB = [[(W[i], f"W{i}") for i in range(0, 4)], [(W[i], f"W{i}") for i in range(4, 8)]]
                tsf = tS2[:].rearrange("p a r q -> p (a r q)")
                phys = [(W[i], f"W{i}") for i in range(8)] + [(rstdT, "rstdT"), (nmrT, "nmrT"), (tsf[:, 0:512], "tS0"), (tsf[:, 640:1152], "tS1")]
                TBA = [[phys[3 * i], phys[3 * i + 1], phys[3 * i + 2], phys[3 * i + 2]] for i in range(4)]

                def load_tile(t):
                    tsl = slice(t * TL, (t + 1) * TL)
                    P.dma("sp", "xld", I(nc.sync.dma_start, out=xt[:], in_=fm(xsrc[:, tsl])), reads=XSK, writes=["xt"])

                def prefetch_h(t):
                    for k in range(8):
                        P.dma("sp", f"xpf{k % 2}", I(nc.sync.dma_start, out=uc[:, k % 2, :], in_=xsrc[k * 128:(k + 1) * 128, t * TL:(t + 1) * TL]),
                              reads=XSK, writes=[f"uc{k % 2}"])
                        P.emit("act", I(ACT, out=hbf[:, k, :], in_=uc[:, k % 2, :], func=AF.Identity, scale=A1[:, k:k + 1], bias=B1[:, k:k + 1]),
                               reads=[f"uc{k % 2}", f"vA1_{pm}", f"vB1_{pm}"], writes=["hbf"])
                        yield

                def rec_v(t):
                    for s in range(4):
                        bank, bkey = gbank()
                        P.emit("pe", [I(MM, bank[:], lhsT=hbf[:, k, s * 128:(s + 1) * 128], rhs=wpre[:, k, 512:1024], start=(k == 0), stop=(k == 7)) for k in range(8)],
                               reads=WPK + ["hbf"], writes=[bkey])
                        P.emit("act", I(A.copy, out=v_tm[:, s, :], in_=bank[:]), reads=[bkey], writes=["v_tm"])
                    P.dma("sp", "sVt", I(nc.sync.dma_start, out=sVt[t], in_=v_tm[:].rearrange("p s n -> p (s n)")), reads=["v_tm"], writes=[f"sVt{t}"])

                def rec_pre(hd, th, t):
                    (B0, B0k), (B1, B1k), _, (B3, B3k) = TBA[th]
                    pnA, pnB = (5, 6) if th % 2 == 0 else (3, 4)
                    hs = slice(hd * 128, (hd + 1) * 128)
                    h5 = slice(hd * 512, (hd + 1) * 512)
                    bank, bkey = proj_fm(1024 + hd * 128)
                    sg, sgk = B0, B0k
                    P.emit("act", I(ACT, out=sg[:], in_=bank[:], func=AF.Sigmoid, scale=-1.0), reads=[bkey], writes=[sgk])
                    yield
                    kk, kkk = B1, B1k
                    P.emit("dve", I(V.tensor_scalar, out=kk[:], in0=sg[:], scalar1=omlb[:, l, hd:hd + 1], scalar2=None, op0=ALU.mult),
                           reads=[sgk, "omlb"], writes=[kkk])
                    yield
                    lf, lfk = B3, B3k
                    P.emit("act", I(ACT, out=lf[:], in_=sg[:], func=AF.Ln, scale=nomlb[:, l, hd:hd + 1], bias=epsT[:, 2:3]), reads=[sgk, "nomlb", "epsT"], writes=[lfk])
                    yield
                    bcum, bck = B0, B0k
                    P.emit("dve", I(V.tensor_tensor_scan, out=bcum[:], data0=resetm, data1=lf[:], initial=0.0, op0=ALU.mult, op1=ALU.add),
                           reads=[lfk, "cstt"], writes=[bck])
                    yield
                    bc, bcK = B3, B3k
                    P.emit("dve", I(TT, out=b3(bc), in0=b3(bcum), in1=b3(bcum)[:, :, 63:64].broadcast_to([128, 8, 64]), op=ALU.subtract),
                           reads=[bck], writes=[bcK])
                    yield
                    P.emit("act", I(ACT, out=adec[:, hd, :], in_=b3(bcum)[:, :, 63], func=AF.Exp), reads=[bck], writes=[f"adec{hd}"])
                    P.dma("sp", f"sAd{hd}", I(nc.sync.dma_start, out=sAd[t][:, hd * 8:(hd + 1) * 8], in_=adec[:, hd, :]), reads=[f"adec{hd}"], writes=[f"sAd{t}_{hd}"])
                    yield
                    Ep, Epk = B0, B0k
                    P.emit("act", I(ACT, out=Ep[:], in_=bc[:], func=AF.Exp), reads=[bcK, bck], writes=[Epk])
                    P.dma("sp", f"sEp{th}", I(nc.sync.dma_start, out=sEp[t][:, h5], in_=Ep[:]), reads=[Epk], writes=[f"sEp{t}_{hd}"])
                    yield
                    Em, Emk = B3, B3k
                    P.emit("act", I(ACT, out=Em[:], in_=bc[:], func=AF.Exp, scale=-1.0), reads=[bcK], writes=[Emk])
                    yield
                    P.emit("dve", I(TT, out=kh_bf[:, hd, :], in0=kk[:], in1=Em[:], op=ALU.mult), reads=[kkk, Emk], writes=[f"kh{hd}"])
                    P.dma("sp", f"sKh{hd}", I(nc.sync.dma_start, out=sKh[t][:, h5], in_=kh_bf[:, hd, :]), reads=[f"kh{hd}"], writes=[f"sKh{t}_{hd}"])
                    yield
                    P.emit("pe", [I(T.transpose, pT[:, s * 128:(s + 1) * 128], kh_bf[:, hd, s * 128:(s + 1) * 128], ident[:]) for s in range(4)],
                           reads=[f"kh{hd}", "ident"], writes=["pT"])
                    P.emit("act", I(A.copy, out=kh_tm[:, :, hs], in_=pT[:, 0:512].rearrange("p (s d) -> p s d", d=128)),
                           reads=["pT"], writes=[f"khtm{hd}"])
                    P.dma("sp", f"sKt{hd}", I(nc.sync.dma_start, out=sKt[t].rearrange("p (s d) -> p s d", d=512)[:, :, hs], in_=kh_tm[:, :, hs]),
                          reads=[f"khtm{hd}"], writes=[f"sKt{t}_{hd}"])
                    yield
                    ops = []
                    for n in range(8):
                        pr, base = n // 2, (n % 2) * 64
                        bankp = pb[pnA if n % 2 == 0 else pnB]
                        ops.append(I(MM, bankp[:, pr * 128:(pr + 1) * 128], lhsT=kh_tm[base:base + 64, pr, hs],
                                     rhs=v_tm[base:base + 64, pr, hs], start=True, stop=True))
                    P.emit("pe", ops, reads=[f"khtm{hd}", "v_tm"], writes=[f"pb{pnA}", f"pb{pnB}"])
                    for n in range(8):
                        cur = scur[hd]
                        bankp = pb[pnA if n % 2 == 0 else pnB]
                        P.emit("dve", I(V.scalar_tensor_tensor, out=Sst[:, hd, 1 - cur, :], in0=Sst[:, hd, cur, :], scalar=adec[:, hd, n:n + 1],
                                        in1=bankp[:, (n // 2) * 128:(n // 2 + 1) * 128], op0=ALU.mult, op1=ALU.add),
                               reads=[f"S{hd}_{cur}", f"adec{hd}", f"pb{pnA}", f"pb{pnB}"], writes=[f"S{hd}_{1 - cur}"])
                        scur[hd] = 1 - cur
                    yield

                def load_rec(t):
                    P.dma("sp", "lKh", I(nc.sync.dma_start, out=kh_bf[:].rearrange("p h n -> p (h n)"), in_=sKh[t]),
                          reads=[f"sKh{t}_{h}" for h in range(4)], writes=[f"kh{h}" for h in range(4)])
                    P.dma("sp", "lKt", I(nc.sync.dma_start, out=kh_tm[:].rearrange("p s n -> p (s n)"), in_=sKt[t]),
                          reads=[f"sKt{t}_{h}" for h in range(4)], writes=[f"khtm{h}" for h in range(4)])
                    P.dma("sp", "lVt", I(nc.sync.dma_start, out=v_tm[:].rearrange("p s n -> p (s n)"), in_=sVt[t]), reads=[f"sVt{t}"], writes=["v_tm"])
                    P.dma("sp", "lAd", I(nc.sync.dma_start, out=adec[:].rearrange("p h n -> p (h n)"), in_=sAd[t]),
                          reads=[f"sAd{t}_{h}" for h in range(4)], writes=[f"adec{h}" for h in range(4)])

                def rec_main(hd, th, t):
                    (B0, B0k), (B1, B1k), (B2, B2k), (B3, B3k) = TB[th]
                    pO, pOk = (pb[4], "pb4") if th == 0 else (pb[2], "pb2")
                    aT, aTk = attnT[:, th, :], f"attnT{th}"
                    hs = slice(hd * 128, (hd + 1) * 128)
                    Ep, Epk = B2, B2k
                    P.dma("sp", f"lEp{th}", I(nc.sync.dma_start, out=Ep[:], in_=sEp[t][:, hd * 512:(hd + 1) * 512]), reads=[f"sEp{t}_{hd}"], writes=[Epk])
                    yield
                    bankq, bqk = proj_fm(512 + hd * 128)
                    P.emit("dve", I(TT, out=qt_bf[:, hd, :], in0=bankq[:], in1=Ep[:], op=ALU.mult), reads=[bqk, Epk], writes=[f"qt{hd}"])
                    yield
                    P.emit("pe", [I(MM, pb[3][:, pr * 128:(pr + 1) * 128], lhsT=kh_bf[:, hd, pr * 128:(pr + 1) * 128],
                                    rhs=qt_bf[:, hd, pr * 128:(pr + 1) * 128], start=True, stop=True) for pr in range(4)],
                           reads=[f"kh{hd}", f"qt{hd}"], writes=["pb3"])
                    P.emit("dve", I(TT, out=aT, in0=pb[3][:], in1=recmask, op=ALU.mult), reads=["pb3", "cstt"], writes=[aTk])
                    yield
                    ops = []
                    for n in range(8):
                        pr, base = n // 2, (n % 2) * 64
                        bankp = pb[5 + n % 2]
                        ops.append(I(MM, bankp[:, pr * 128:(pr + 1) * 128], lhsT=kh_tm[base:base + 64, pr, hs],
                                     rhs=v_tm[base:base + 64, pr, hs], start=True, stop=True))
                    P.emit("pe", ops, reads=[f"khtm{hd}", "v_tm"], writes=["pb5", "pb6"])
                    for n in range(8):
                        cur = scur[hd]
                        bankp = pb[5 + n % 2]
                        P.emit("act", I(ACT, out=Sbf[:, th, n, :], in_=Sst[:, hd, cur, :], func=AF.Identity, scale=adec[:, hd, n:n + 1]),
                               reads=[f"S{hd}_{cur}", f"adec{hd}"], writes=[f"Sbf{th}_{n}"])
                        P.emit("dve", I(V.scalar_tensor_tensor, out=Sst[:, hd, 1 - cur, :], in0=Sst[:, hd, cur, :], scalar=adec[:, hd, n:n + 1],
                                        in1=bankp[:, (n // 2) * 128:(n // 2 + 1) * 128], op0=ALU.mult, op1=ALU.add),
                               reads=[f"S{hd}_{cur}", f"adec{hd}", "pb5", "pb6"], writes=[f"S{hd}_{1 - cur}"])
                        scur[hd] = 1 - cur
                    yield
                    ops = []
                    for pr in range(4):
                        ops.append(I(MM, pO[:, pr * 128:(pr + 1) * 128], lhsT=v_tm[:, pr, hs], rhs=aT[:, pr * 128:(pr + 1) * 128], start=True, stop=False))
                        for n in (2 * pr, 2 * pr + 1):
                            ops.append(I(MM, pO[:, n * 64:(n + 1) * 64], lhsT=Sbf[:, th, n, :], rhs=qt_bf[:, hd, n * 64:(n + 1) * 64],
                                         start=False, stop=(n % 2 == 1)))
                    P.emit("pe", ops, reads=["v_tm", aTk, f"qt{hd}"] + [f"Sbf{th}_{n}" for n in range(8)], writes=[pOk])
                    yield
                    osq, osqk = B2, B2k
                    P.emit("act", I(ACT, out=osq[:], in_=pO[:], func=AF.Square), reads=[pOk], writes=[osqk])
                    yield
                    P.emit("pe", I(MM, pb[3][:], lhsT=ones_r[:], rhs=osq[:], start=True, stop=True), reads=[osqk, "ones_r"], writes=["pb3"])
                    sd, sdk = B0, B0k
                    P.emit("act", I(ACT, out=sd[:], in_=pb[3][:], func=AF.Ln, bias=epsT[:, 0:1]), reads=["pb3", "epsT"], writes=[sdk])
                    yield
                    rstd, rsk = B0, B0k
                    P.emit("act", I(ACT, out=rstd[:], in_=sd[:], func=AF.Exp, scale=-0.5), reads=[sdk], writes=[rsk])
                    yield
                    t1, t1k = B3, B3k
                    P.emit("dve", I(TT, out=t1[:], in0=pO[:], in1=rstd[:], op=ALU.mult), reads=[pOk, rsk], writes=[t1k])
                    yield
                    bankg, bgk = proj_fm(2048 + hd * 128)
                    sgt, sgtk = B1, B1k
                    P.emit("act", I(ACT, out=sgt[:], in_=bankg[:], func=AF.Silu), reads=[bgk], writes=[sgtk])
                    yield
                    P.emit("dve", I(V.scalar_tensor_tensor, out=ymix[:, 2 + hd, :], in0=t1[:], scalar=normg, in1=sgt[:], op0=ALU.mult, op1=ALU.mult),
                           reads=[t1k, sgtk, "part"], writes=[f"ymix{2 + hd}"])
                    yield

                def conv_glu():
                    for c in range(2):
                        bankg, bgk = proj_fm(256 + c * 128)
                        sg, sgk = wtmp()
                        P.emit("act", I(ACT, out=sg[:], in_=bankg[:], func=AF.Sigmoid), reads=[bgk], writes=[sgk])
                        bankv, bvk = proj_fm(c * 128)
                        P.emit("dve", I(TT, out=ubuf[:, c, 30:30 + TL], in0=bankv[:], in1=sg[:], op=ALU.mult), reads=[bvk, sgk, "ubuf"], writes=["ubuf"])

                def conv_rest():
                    usq = []
                    for c in range(2):
                        P.emit("pe", [I(MM, pb[5 + c][:], lhsT=diag[:, c * 31 + j, :], rhs=ubuf[:, c, j:j + TL], start=(j == 0), stop=(j == 30)) for j in range(31)],
                               reads=["diag", "ubuf"], writes=[f"pb{5 + c}"])
                        P.emit("act", I(ACT, out=uc[:, c, :], in_=pb[5 + c][:], func=AF.Identity, bias=convb[:, c:c + 1]), reads=[f"pb{5 + c}", "part"], writes=[f"uc{c}"])
                        q_, qk_ = wtmp()
                        P.emit("act", I(ACT, out=q_[:], in_=uc[:, c, :], func=AF.Square), reads=[f"uc{c}"], writes=[qk_])
                        usq.append((q_, qk_))
                    P.emit("pe", [I(MM, pb[3][:], lhsT=ones_c[:], rhs=uc[:, c, :], start=(c == 0), stop=(c == 1)) for c in range(2)]
                           + [I(MM, pb[4][:], lhsT=ones_c[:], rhs=usq[c][0][:], start=(c == 0), stop=(c == 1)) for c in range(2)],
                           reads=["uc0", "uc1", usq[0][1], usq[1][1], "ones_c"], writes=["pb3", "pb4"])
                    rstd, rsk, nmr, nmk = ln_stats(0)
                    for c in range(2):
                        ta, tak = wtmp()
                        P.emit("dve", I(TT, out=ta[:], in0=uc[:, c, :], in1=rstd[:], op=ALU.mult), reads=[f"uc{c}", rsk], writes=[tak])
                        tb, tbk = wtmp()
                        P.emit("pool", I(G.tensor_tensor, out=tb[:], in0=ta[:], in1=nmr[:], op=ALU.add), reads=[tak, nmk], writes=[tbk])
                        P.emit("act", I(ACT, out=ymix[:, c, :], in_=tb[:], func=AF.Silu, scale=convg[:, c:c + 1], bias=convlb[:, c:c + 1]),
                               reads=[tbk, "part"], writes=[f"ymix{c}"])
                    P.emit("pool", I(G.tensor_copy, out=ubuf[:, :, 0:30], in_=ubuf[:, :, TL:TL + 30]), reads=["ubuf"], writes=["ubuf"])


                def att_proj():
                    for c in range(2):
                        bq, bqk = proj_fm(2560 + c * 128)
                        P.emit("act", I(ACT, out=qTa[:, c, :], in_=bq[:], func=AF.Copy, scale=0.125), reads=[bqk], writes=["qTa"])
                        bk_, bkk = proj_fm(2816 + c * 128)
                        P.emit("dve", I(V.tensor_copy, out=kTa[:, c, TL:2 * TL], in_=bk_[:]), reads=[bkk, "kTa"], writes=["kTa"])
                    for s2 in range(2):
                        bank, bkey = gbank()
                        ops = []
                        for ss in range(2):
                            s = s2 * 2 + ss
                            for k in range(8):
                                ops.append(I(MM, bank[:, ss * 256:(ss + 1) * 256], lhsT=hbf[:, k, s * 128:(s + 1) * 128], rhs=winb[:, k, 2048:2304],
                                             start=(k == 0), stop=(k == 7)))
                        P.emit("pe", ops, reads=wsrc(3072, 3328)[2] + ["hbf"], writes=[bkey])
                        P.emit("act", I(A.copy, out=vat[:, 4 + 2 * s2:6 + 2 * s2, :], in_=bank[:].rearrange("p (s d) -> p s d", d=256)),
                               reads=[bkey, "vat"], writes=["vat"])

                def att_main(t):
                    obanks = {}

                    def stage1(j, c, hh2):
                        J = 4 * t + j
                        nh = max(0, 4 - J)
                        hh = 2 * c + hh2
                        base = hh2 * 64
                        bA, bB = (5, 6) if hh2 == 0 else (3, 4)
                        tS, PT, tSk, PTk = tS2[:, hh2], PT2[:, hh2], f"tS{hh2}", f"PT{hh2}"
                        ops = []
                        for r in range(5):
                            bankS = pb[bA] if r < 4 else pb[bB]
                            ops.append(I(MM, bankS[:, (r % 4) * 128:(r % 4 + 1) * 128], lhsT=kTa[base:base + 64, c, (j + r) * 128:(j + r + 1) * 128],
                                         rhs=qTa[base:base + 64, c, j * 128:(j + 1) * 128], start=True, stop=True))
                        P.emit("pe", ops, reads=["kTa", "qTa"], writes=[f"pb{bA}", f"pb{bB}"])
                        P.emit("dve", I(TT, out=tS[:, 0:4, :], in0=pb[bA][:].rearrange("p (r q) -> p r q", q=128),
                                        in1=biasb[:, hh, 0:4, :], op=ALU.add), reads=[f"pb{bA}", "biasb", tSk], writes=[tSk])
                        P.emit("dve", I(TT, out=tS[:, 4, :], in0=pb[bB][:, 0:128], in1=biasb[:, hh, 4, :], op=ALU.add),
                               reads=[f"pb{bB}", "biasb", tSk], writes=[tSk])
                        if nh > 0:
                            P.emit("act", I(ACT, out=PT[:, 0:nh, :], in_=tS[:, 0:nh, :], func=AF.Exp, bias=hbias), reads=[tSk, "flg", PTk], writes=[PTk])
                        P.emit("act", I(ACT, out=PT[:, nh:5, :], in_=tS[:, nh:5, :], func=AF.Exp), reads=[tSk, PTk], writes=[PTk])

                    def stage2(j, c, hh2):
                        hh = 2 * c + hh2
                        base = hh2 * 64
                        PT, PTk = PT2[:, hh2], f"PT{hh2}"
                        if (j, c) not in obanks:
                            obanks[(j, c)] = gbank()
                        obank, obk = obanks[(j, c)]
                        ops = []
                        for r in range(5):
                            ops.append(I(MM, obank[base:base + 64, 0:128], lhsT=vat[:, j + r, hh * 64:(hh + 1) * 64], rhs=PT[:, r, :],
                                         start=(r == 0), stop=(r == 4)))
                        for r in range(5):
                            ops.append(I(MM, obank[base:base + 64, 128:256], lhsT=ones_b[:], rhs=PT[:, r, :], start=(r == 0), stop=(r == 4)))
                        P.emit("pe", ops, reads=["vat", PTk, "ones_b"], writes=[obk])
                        if hh2 == 1:
                            P.emit("dve", I(V.reciprocal, out=rrec[:], in_=obank[:, 128:256]), reads=[obk], writes=["rrec"])
                            P.emit("dve", I(TT, out=ymix[:, 6 + c, j * 128:(j + 1) * 128], in0=obank[:, 0:128], in1=rrec[:], op=ALU.mult),
                                   reads=[obk, "rrec", f"ymix{6 + c}"], writes=[f"ymix{6 + c}"])

                    prev = None
                    for it in [(j, c, hh2) for j in range(4) for c in range(2) for hh2 in range(2)]:
                        stage1(*it)
                        yield
                        if prev is not None:
                            stage2(*prev)
                            yield
                        prev = it
                    stage2(*prev)
                    yield

                def att_shift():
                    P.emit("pool", I(G.tensor_copy, out=kTa[:, :, 0:TL], in_=kTa[:, :, TL:2 * TL]), reads=["kTa"], writes=["kTa"])
                    P.emit("pool", I(G.tensor_copy, out=vat[:, 0:4, :], in_=vat[:, 4:8, :]), reads=["vat"], writes=["vat"])


                def finish_tile(t):
                    tsl = slice(t * TL, (t + 1) * TL)
                    ymk = [f"ymix{i}" for i in range(8)]
                    if t == 0 and l == 0:
                        tap("ymix", ymix[:], ymk)
                    for m in range(8):
                        bank, bkey = gbank()
                        P.emit("pe", [I(MM, bank[:], lhsT=woutb[:, k, m * 128:(m + 1) * 128], rhs=ymix[:, k, :], start=(k == 0), stop=(k == 7)) for k in range(8)],
                               reads=WOUTK + ymk, writes=[bkey])
                        P.emit("dve", I(V.scalar_tensor_tensor, out=xt[:, m, :], in0=bank[:], scalar=G1[:, m:m + 1], in1=xt[:, m, :], op0=ALU.mult, op1=ALU.add),
                               reads=[bkey, f"vG1_{pm}", "xt"], writes=["xt"])
                    return ln_stats_part(xt, "xt")

                def ln1_tail(t, st, rot=False):
                    tsl = slice(t * TL, (t + 1) * TL)
                    yield from ln_apply_gen(xt, "xt", ln1g, ln1b, st, bufs=None if rot else [(uc[:, 0, :], "uc0"), (uc[:, 1, :], "uc1")])
                    if t == 0 and l == 0:
                        tap("x1", xt[:], ["xt"])
                    P.dma("sp", "xst", I(nc.sync.dma_start, out=fm(scrA[:, tsl]), in_=xt[:]), reads=["xt"], writes=["scrA"])
                    yield


                pay = xt[:].rearrange("p k n -> p (k n)")[:, 0:PAYW]
                interleave([prefetch_h(0)])
                for t in range(NTILE):
                    rec_v(t)
                    if t == NTILE - 1:
                        conv_glu()
                        att_proj()
                        interleave([rec_pre(h, h, t) for h in range(4)])
                    else:
                        interleave([rec_pre(h, h, t) for h in range(4)] + [prefetch_h(t + 1)])
                for hd in range(4):
                    P.emit("dve", I(V.tensor_scalar, out=pay[:, hd * 128:(hd + 1) * 128], in0=Sst[:, hd, scur[hd], :], scalar1=isA, scalar2=None, op0=ALU.mult),
                           reads=[f"S{hd}_{scur[hd]}", "flg"], writes=["xt"])
                P.emit("dve", I(V.tensor_scalar, out=pay[:, 512:1536].rearrange("p (c n) -> p c n", c=2), in0=kTa[:, :, TL:2 * TL], scalar1=isA, scalar2=None, op0=ALU.mult),
                       reads=["kTa", "flg", "xt"], writes=["xt"])
                P.emit("dve", I(V.tensor_scalar, out=pay[:, 1536:2560].rearrange("p (s n) -> p s n", s=4), in0=vat[:, 4:8, :], scalar1=isA, scalar2=None, op0=ALU.mult),
                       reads=["vat", "flg", "xt"], writes=["xt"])
                P.emit("dve", I(V.tensor_scalar, out=pay[:, 2560:2620].rearrange("p (c n) -> p c n", c=2), in0=ubuf[:, :, TL:TL + 30], scalar1=isA, scalar2=None, op0=ALU.mult),
                       reads=["ubuf", "flg", "xt"], writes=["xt"])
                P.dma("pool", "bnc", I(G.dma_start, out=bounce[l].ap(), in_=pay), reads=["xt"], writes=["bounce"])
                P.dma("pool", "cc", I(G.collective_compute, "AllReduce", ALU.add, replica_groups=[[2 * i, 2 * i + 1] for i in range(n_cores // 2)],
                                      ins=[bounce[l].ap().opt()], outs=[gath[l].ap().opt()]), reads=["bounce"], writes=["gath"], inc=1)
                P.dma("pool", "gth", I(G.dma_start, out=pay, in_=gath[l].ap()), reads=["gath", "xt"], writes=["xt"])
                for hd in range(4):
                    P.emit("pool", I(G.tensor_scalar, out=Sst[:, hd, 0, :], in0=pay[:, hd * 128:(hd + 1) * 128], scalar1=isB, scalar2=None, op0=ALU.mult),
                           reads=["xt", "flg", f"S{hd}_0", f"S{hd}_1"], writes=[f"S{hd}_0"])
                    scur[hd] = 0
                P.emit("pool", I(G.tensor_scalar, out=kTa[:, :, 0:TL], in0=pay[:, 512:1536].rearrange("p (c n) -> p c n", c=2), scalar1=isB, scalar2=None, op0=ALU.mult),
                       reads=["xt", "flg", "kTa"], writes=["kTa"])
                P.emit("pool", I(G.tensor_scalar, out=vat[:, 0:4, :], in0=pay[:, 1536:2560].rearrange("p (s n) -> p s n", s=4), scalar1=isB, scalar2=None, op0=ALU.mult),
                       reads=["xt", "flg", "vat"], writes=["vat"])
                P.emit("pool", I(G.tensor_scalar, out=ubuf[:, :, 0:30], in0=pay[:, 2560:2620].rearrange("p (c n) -> p c n", c=2), scalar1=isB, scalar2=None, op0=ALU.mult),
                       reads=["xt", "flg", "ubuf"], writes=["ubuf"])
                interleave([prefetch_h(0)])
                st_prev = None
                for t in range(NTILE):
                    if t == 0:
                        load_rec(0)
                    if t > 0:
                        tail = ln1_tail(t - 1, st_prev)
                        interleave_primary([rec_main(0, 0, t), rec_main(1, 1, t)], tail)
                        interleave([rec_main(2, 0, t), rec_main(3, 1, t), tail])
                    else:
                        interleave([rec_main(0, 0, t), rec_main(1, 1, t)])
                        interleave([rec_main(2, 0, t), rec_main(3, 1, t)])
                    load_tile(t)
                    if t + 1 < NTILE:
                        load_rec(t + 1)
                    conv_glu()
                    conv_rest()
                    att_proj()
                    if t + 1 < NTILE:
                        interleave([att_main(t), prefetch_h(t + 1)])
                    else:
                        interleave([att_main(t)])
                    att_shift()
                    st_prev = finish_tile(t)
                interleave([ln1_tail(NTILE - 1, st_prev, rot=True)])
            P.barrier()
            ck("mixer")
            with contextlib.ExitStack() as fs:
                fsb = lambda n, s, d=F32: fs.enter_context(nc.sbuf_tensor(f"{n}_{l}", s, d))
                xb = fsb("xb", [128, 8, FB])
                h2 = fsb("h2", [128, 8, FB], BF16)
                hid = fsb("hid", [128, 5, FB], BF16)
                wg = fsb("wg", [128, 8, 640], BF16)
                wu = fsb("wu", [128, 8, 640], BF16)
                w2 = fsb("w2", [128, 5, D], BF16)
                groups = [(0, 5), (5, 5), (10, 4), (14, 4), (18, 4)]
                XBK = [f"xb{tt}" for tt in range(FT)]
                lnb2 = [fsb(f"lnb2_{i}", [128, 512]) for i in range(4)]
                lnrow = [lnb2[i][0:1, 0:256] for i in range(2)]
                modg = None
                if l + 1 < DEPTH:
                    stgF = ([fsb(f"stgF{i}", [128, 8, 256], BF16) for i in range(2)], lnrow)
                    modg = mod_gen(l + 1, stgF)

                def adv():
                    if modg is not None:
                        next(modg, None)
                WGK = [f"wg{k}" for k in range(8)]
                WUK = [f"wu{k}" for k in range(8)]
                for fb in range(NFB):
                    bsl = slice(fb * FB, (fb + 1) * FB)
                    for tt in range(FT):
                        csl = slice(tt * TL, (tt + 1) * TL)
                        P.dma("sp", f"xbl{tt}", I(nc.sync.dma_start, out=xb[:, :, csl], in_=fm(scrA[:, fb * FB + tt * TL:fb * FB + (tt + 1) * TL])),
                              reads=["scrA"], writes=[f"xb{tt}"])
                    for tt in range(FT):
                        csl = slice(tt * TL, (tt + 1) * TL)
                        for k in range(8):
                            P.emit("act", I(ACT, out=h2[:, k, csl], in_=xb[:, k, csl], func=AF.Identity, scale=A2[:, k:k + 1], bias=sh2[:, k:k + 1]),
                                   reads=[f"xb{tt}", f"vA2_{pm}", MK], writes=[f"h2_{tt}"])
                    for (f0, nf) in groups:
                        for k in range(8):
                            P.dma("pool", "wg", I(G.dma_start, out=wg[:, k, 0:nf * 128], in_=wf1_d[l][k * 128:(k + 1) * 128, f0 * 128:(f0 + nf) * 128]), writes=[f"wg{k}"])
                        for k in range(8):
                            P.dma("pool", "wu", I(G.dma_start, out=wu[:, k, 0:nf * 128], in_=wf1_d[l][k * 128:(k + 1) * 128, DFF + f0 * 128:DFF + (f0 + nf) * 128]), writes=[f"wu{k}"])
                        for fi in range(nf):
                            P.dma("pool", "w2", I(G.dma_start, out=w2[:, fi, :], in_=wf2_d[l][(f0 + fi) * 128:(f0 + fi + 1) * 128, :]), writes=[f"w2{fi}"])
                        W2K = [f"w2{fi}" for fi in range(nf)]
                        if l + 1 < DEPTH and fb == 0 and f0 == 0:
                            load_wpre(l + 1)
                        for fi in range(nf):
                            for tt in range(FT):
                                csl = slice(tt * TL, (tt + 1) * TL)
                                bg, bgk = gbank(5)
                                P.emit("pe", [I(MM, bg[:], lhsT=wg[:, k, fi * 128:(fi + 1) * 128], rhs=h2[:, k, csl], start=(k == 0), stop=(k == 7)) for k in range(8)],
                                       reads=WGK + [f"h2_{tt}"], writes=[bgk])
                                bu, buk = gbank(5)
                                P.emit("pe", [I(MM, bu[:], lhsT=wu[:, k, fi * 128:(fi + 1) * 128], rhs=h2[:, k, csl], start=(k == 0), stop=(k == 7)) for k in range(8)],
                                       reads=WUK + [f"h2_{tt}"], writes=[buk])
                                sg, sgk = wtmp()
                                P.emit("act", I(ACT, out=sg[:], in_=bg[:], func=AF.Silu), reads=[bgk], writes=[sgk])
                                P.emit("dve", I(TT, out=hid[:, fi, csl], in0=bu[:], in1=sg[:], op=ALU.mult), reads=[buk, sgk, "hid"], writes=["hid"])
                                if fb == 0:
                                    adv()
                        if l == 0 and fb == 0:
                            tap(f"hid{f0}", hid[:, :, 0:512], ["hid"])
                        for m in range(8):
                            for tt in range(FT):
                                csl = slice(tt * TL, (tt + 1) * TL)
                                bo, bok = gbank(5)
                                P.emit("pe", [I(MM, bo[:], lhsT=w2[:, fi, m * 128:(m + 1) * 128], rhs=hid[:, fi, csl], start=(fi == 0), stop=(fi == nf - 1)) for fi in range(nf)],
                                       reads=W2K + ["hid"], writes=[bok])
                                P.emit("dve", I(V.scalar_tensor_tensor, out=xb[:, m, csl], in0=bo[:], scalar=G2[:, m:m + 1], in1=xb[:, m, csl], op0=ALU.mult, op1=ALU.add),
                                       reads=[bok, f"vG2_{pm}", f"xb{tt}"], writes=[f"xb{tt}"])
                    if l == 0 and fb == 0:
                        tap("z2", xb[:], XBK)
                    if modg is not None:
                        for _ in modg:
                            pass
                    for t0 in range(0, FT, 2):
                        gens = []
                        for i, tt in enumerate(range(t0, min(FT, t0 + 2))):
                            csl = slice(tt * TL, (tt + 1) * TL)
                            st = ln_stats_part(xb, f"xb{tt}", csl, bk=((3, 4), (5, 6))[i],
                                               outs=((lnb2[2 * i], f"lnb2_{2 * i}"), (lnb2[2 * i + 1], f"lnb2_{2 * i + 1}")))
                            gens.append(ln_apply_gen(xb, f"xb{tt}", ln2g, ln2b, st, csl,
                                                     bufs=[(W[4 + 2 * i], f"W{4 + 2 * i}"), (W[5 + 2 * i], f"W{5 + 2 * i}")]))
                        interleave(gens)
                        t1_ = min(FT, t0 + 2)
                        P.dma("sp", "xbs", I(nc.sync.dma_start, out=fm(xdst[:, fb * FB + t0 * TL:fb * FB + t1_ * TL]), in_=xb[:, :, t0 * TL:t1_ * TL]),
                              reads=[f"xb{tt}" for tt in range(t0, t1_)], writes=[f"{xdk}_{fb}_{t0}"])
                    if l == 0 and fb == 0:
                        tap("x_l0", xb[:], XBK)
                if modg is not None:
                    for _ in modg:
                        pass
            P.barrier()

        try:
            for l in range(DEPTH):
                layer(l)
        except _Stop:
            pass
        P.final_wait("sp", [k for k in list(P.res.keys()) if str(k).startswith("outT")] + ["dbgo_" + n for n in dbg_out])
        P.run()
    return nc


def _consts():
    c = np.zeros((128, 1152), np.float32)
    c[:, 0:128] = np.eye(128, dtype=np.float32)
    s = np.arange(128)[:, None]
    t = np.arange(128)[None, :]
    m = ((s // 64 == t // 64) & (s <= t)).astype(np.float32)
    c[:, 128:640] = np.tile(m, (1, 4))
    r = np.ones((128, 512), np.float32)
    r[:, 0::64] = 0.0
    c[:, 640:1152] = r
    return c


def _pack_params(inp, depth):
    par = np.zeros((128, NPAR), np.float32)
    ch = lambda v: np.ascontiguousarray(np.asarray(v, np.float32).reshape(-1, 128).T)
    for l in range(depth):
        po = l * PL
        par[:, po:po + 48] = ch(inp["b_ada"][l])
        par[:, po + 48:po + 56] = ch(inp["ln1_g"][l])
        par[:, po + 56:po + 64] = ch(inp["ln1_b"][l])
        par[:, po + 64:po + 72] = ch(inp["ln2_g"][l])
        par[:, po + 72:po + 80] = ch(inp["ln2_b"][l])
        cw = np.asarray(inp["conv_w"][l], np.float32)
        par[:, po + 80:po + 142] = cw.reshape(31, 2, 128).transpose(2, 1, 0).reshape(128, 62)
        par[:, po + 142:po + 144] = ch(inp["conv_b"][l])
        par[:, po + 144:po + 146] = ch(inp["conv_ln_g"][l])
        par[:, po + 146:po + 148] = ch(inp["conv_ln_b"][l])
        par[:, po + 148:po + 149] = np.asarray(inp["rec_norm_g"][l], np.float32).reshape(128, 1)
    rl = np.asarray(inp["rec_lower_bound"], np.float32)
    par[:, 2 * PL:2 * PL + 8] = rl.reshape(2, 4, 128).transpose(2, 0, 1).reshape(128, 8)
    return par


def _bias_tiles(rel_bias_l):
    r = np.arange(5)[:, None, None]
    i = np.arange(128)[None, :, None]
    j = np.arange(128)[None, None, :]
    idx = np.clip((r - 4) * 128 + i - j, -128, 128) + 128
    valid = ~(((r == 0) & (i < 64) & (j >= 64)) | ((r == 4) & (i >= 64) & (j < 64)))
    tb = np.asarray(rel_bias_l, np.float32)
    g = tb[:, idx]
    g = np.where(valid[None], g, np.float32(NEG_BIG)).astype(np.float32)
    return np.ascontiguousarray(g.transpose(2, 0, 1, 3).reshape(128, 4 * 5 * 128))


def make_in_maps(inp, NT, depth, n_cores=8):
    inp = {k: np.asarray(v) for k, v in inp.items()}
    cst = _consts()
    par = _pack_params(inp, depth)
    shared = {"par": par, "cst": cst}
    for l in range(depth):
        shared[f"wada{l}"] = np.ascontiguousarray(inp["w_ada"][l], np.float32)
        shared[f"win{l}"] = np.ascontiguousarray(inp["w_in"][l], np.float32)
        shared[f"wout{l}"] = np.ascontiguousarray(inp["w_out"][l], np.float32)
        shared[f"wf1{l}"] = np.ascontiguousarray(inp["w_ffn_in"][l], np.float32)
        shared[f"wf2{l}"] = np.ascontiguousarray(inp["w_ffn_out"][l], np.float32)
        shared[f"bias{l}"] = _bias_tiles(inp["rel_bias"][l])
    maps = []
    for c in range(n_cores):
        b, half = c // 2, c % 2
        m = dict(shared)
        m["xT"] = np.ascontiguousarray(inp["x"][b, half * NT:(half + 1) * NT].T.astype(np.float32))
        m["cT"] = np.ascontiguousarray(inp["c"][b].astype(np.float32).reshape(8, 128).T)
        flg = np.zeros((128, 4), np.float32)
        flg[:, 0] = 1.0 if half == 0 else 0.0
        flg[:, 1] = 1.0 if half == 1 else 0.0
        flg[:, 2] = 0.0 if half == 1 else NEG_BIG
        m["flg"] = flg
        maps.append(m)
    return maps


_NC_CACHE = {}


def kernel(**inputs):
    T = inputs["x"].shape[1]
    B = inputs["x"].shape[0]
    NT = T // 2
    key = (NT, 2)
    if key not in _NC_CACHE:
        _NC_CACHE[key] = build(NT, 2)
    nc = _NC_CACHE[key]
    maps = make_in_maps(inputs, NT, 2)
    res = run_bass_kernel_spmd(nc, maps, core_ids=list(range(8)))
    out = np.empty((B, T, D), np.float32)
    for c in range(2 * B):
        b, half = c // 2, c % 2
        out[b, half * NT:(half + 1) * NT] = res.results[c]["outT"].T
    return out
```

```python
import contextlib
import numpy as np
import concourse.bass as bass
import concourse.mybir as mybir
from concourse.bass_utils import run_bass_kernel_spmd

F32 = mybir.dt.float32
BF16 = mybir.dt.bfloat16
AF = mybir.ActivationFunctionType
ALU = mybir.AluOpType

D = 1024
DIN = 3328
DFF = 2816
DEPTH_FULL = 2
ALPHA = (2 * DEPTH_FULL) ** 0.25
LN_EPS = 1e-5
NEG_BIG = -1e30
TL = 512
PL = 149
NPAR = 2 * PL + 8
SEM_CAP = 30000


class Prog:
    ENGS = ("pe", "act", "dve", "pool", "sp")

    def __init__(self, nc, same_engine_sync=True):
        self.nc = nc
        self.eng = {"pe": nc.tensor, "act": nc.scalar, "dve": nc.vector,
                    "pool": nc.gpsimd, "sp": nc.sync}
        self.plan = {e: [] for e in self.ENGS}
        self.seq = {e: 0 for e in self.ENGS}
        self.sems = {}
        self.waited = {e: {} for e in self.ENGS}
        self.res = {}
        self.same = same_engine_sync
        self.ctx = []
        self.dma_sems = {}
        self.ninst = 0
        self.stopped = False

    def _sem(self, name):
        cm = self.nc.semaphore(name)
        s = cm.__enter__()
        self.ctx.append(cm)
        return s

    def eng_sem(self, e, epoch):
        k = (e, epoch)
        if k not in self.sems:
            self.sems[k] = self._sem(f"s_{e}_{epoch}")
        return self.sems[k]

    def _need_wait(self, E, tok):
        if tok is None:
            return None
        if tok[0] == "eng":
            _, F, s = tok
            if F == E and (E in ("pe", "sp") or not self.same):
                return None
            key = ("eng", F)
            if self.waited[E].get(key, 0) >= s:
                return None
            self.waited[E][key] = s
            epoch, val = (s - 1) // SEM_CAP, (s - 1) % SEM_CAP + 1
            return (self.eng_sem(F, epoch), val)
        _, name, val = tok
        key = ("dma", name)
        if self.waited[E].get(key, 0) >= val:
            return None
        self.waited[E][key] = val
        return (self.dma_sems[name][0], val)

    def _deps(self, E, reads, writes, own_dma=None):
        toks = []
        for k in reads:
            r = self.res.get(k)
            if r and r["w"] is not None:
                toks.append(r["w"])
        for k in writes:
            r = self.res.get(k)
            if r:
                if r["w"] is not None:
                    toks.append(r["w"])
                toks.extend(r["r"])
        best = {}
        for t in toks:
            if t[0] == "dma" and own_dma is not None and t[1] == own_dma:
                continue
            key = (t[0], t[1])
            if key not in best or t[2] > best[key][2]:
                best[key] = t
        waits = []
        for t in best.values():
            w = self._need_wait(E, t)
            if w:
                waits.append(w)
        return waits

    def _update(self, tok, reads, writes):
        for k in reads:
            r = self.res.setdefault(k, {"w": None, "r": []})
            r["r"] = [t for t in r["r"] if not (t[0] == tok[0] and t[1] == tok[1])] + [tok]
        for k in writes:
            self.res[k] = {"w": tok, "r": []}

    def emit(self, E, fn, reads=(), writes=()):
        if self.stopped:
            return
        waits = self._deps(E, reads, writes)
        self.seq[E] += 1
        s = self.seq[E]
        sem = self.eng_sem(E, (s - 1) // SEM_CAP)
        eng = self.eng[E]

        ops = fn if isinstance(fn, list) else [fn]

        def run(waits=waits, ops=ops, sem=sem, eng=eng):
            for (ws, wv) in waits:
                eng.wait_ge(ws, wv)
            for (m_, a_, k_) in ops:
                inst = m_(*a_, **k_)
            inst.then_inc(sem, 1)
        self.plan[E].append(run)
        self._update(("eng", E, s), reads, writes)
        self.ninst += 1

    def dma(self, Q, name, fn, reads=(), writes=(), inc=16):
        if self.stopped:
            return
        if name not in self.dma_sems:
            self.dma_sems[name] = [self._sem(f"d_{name}"), 0]
        waits = self._deps(Q, reads, writes, own_dma=name)
        self.dma_sems[name][1] += inc
        val = self.dma_sems[name][1]
        sem = self.dma_sems[name][0]
        eng = self.eng[Q]

        def run(waits=waits, fn=fn, sem=sem, eng=eng, inc=inc):
            for (ws, wv) in waits:
                eng.wait_ge(ws, wv)
            m_, a_, k_ = fn
            if inc == 16:
                m_(*a_, **k_).then_inc(sem, 16)
            else:
                m_(*a_, **k_).then_inc(sem)
        self.plan[Q].append(run)
        self._update(("dma", name, val), reads, writes)
        self.ninst += 1

    def barrier(self):
        if self.stopped:
            return
        for E in self.ENGS:
            waits = []
            for F in self.ENGS:
                if F == E or F == "sp" or self.seq[F] == 0:
                    continue
                w = self._need_wait(E, ("eng", F, self.seq[F]))
                if w:
                    waits.append(w)
            for name, (sem, val) in self.dma_sems.items():
                if val > 0:
                    w = self._need_wait(E, ("dma", name, val))
                    if w:
                        waits.append(w)
            eng = self.eng[E]

            def run(waits=waits, eng=eng):
                for (ws, wv) in waits:
                    eng.wait_ge(ws, wv)
            self.plan[E].append(run)

    def final_wait(self, Q, keys):
        waits = self._deps(Q, keys, keys)
        eng = self.eng[Q]

        def run(waits=waits, eng=eng):
            for (ws, wv) in waits:
                eng.wait_ge(ws, wv)
        self.plan[Q].append(run)

    def run(self):
        nc = self.nc
        with nc.Block() as block:
            @block.tensor
            def _(e):
                for f in self.plan["pe"]:
                    f()

            @block.scalar
            def _(e):
                for f in self.plan["act"]:
                    f()

            @block.vector
            def _(e):
                for f in self.plan["dve"]:
                    f()

            @block.gpsimd
            def _(e):
                for f in self.plan["pool"]:
                    f()

            @block.sync
            def _(e):
                for f in self.plan["sp"]:
                    f()
        for cm in reversed(self.ctx):
            cm.__exit__(None, None, None)


def I(m, *a, **k):
    return (m, a, k)


class _Stop(Exception):
    pass


def build(NT, DEPTH=2, dbg=None, same_engine_sync=True, stop=None, n_cores=8):
    assert NT % TL == 0
    NTILE = NT // TL
    FB = min(NT, 2048)
    NFB = NT // FB
    FT = FB // TL
    nc = bass.Bass("TRN2", target_bir_lowering=False)
    dr = lambda n, s, kind="ExternalInput", d=F32: nc.dram_tensor(n, s, d, kind=kind).ap()
    xT = dr("xT", [D, NT])
    cT = dr("cT", [128, 8])
    par = dr("par", [128, NPAR])
    cst = dr("cst", [128, 128 + 512 + 512])
    wada = [dr(f"wada{l}", [D, 6 * D]) for l in range(DEPTH)]
    win_d = [dr(f"win{l}", [D, DIN]) for l in range(DEPTH)]
    wout_d = [dr(f"wout{l}", [D, D]) for l in range(DEPTH)]
    wf1_d = [dr(f"wf1{l}", [D, 2 * DFF]) for l in range(DEPTH)]
    wf2_d = [dr(f"wf2{l}", [DFF, D]) for l in range(DEPTH)]
    bias_d = [dr(f"bias{l}", [128, 4 * 5 * 128]) for l in range(DEPTH)]
    flg_d = dr("flg", [128, 4])
    PAYW = 512 + 1024 + 1024 + 60
    bounce = [nc.dram_tensor(f"bounce{l}", [128, PAYW], F32, kind="Internal") for l in range(DEPTH)]
    sEp = [dr(f"sEp{t}", [128, 2048], kind="Internal") for t in range(NT // TL)]
    sKh = [dr(f"sKh{t}", [128, 2048], kind="Internal", d=BF16) for t in range(NT // TL)]
    sKt = [dr(f"sKt{t}", [128, 2048], kind="Internal", d=BF16) for t in range(NT // TL)]
    sAd = [dr(f"sAd{t}", [128, 32], kind="Internal") for t in range(NT // TL)]
    sVt = [dr(f"sVt{t}", [128, 2048], kind="Internal", d=BF16) for t in range(NT // TL)]
    gath = [nc.dram_tensor(f"gath{l}", [128, PAYW], F32, kind="Internal") for l in range(DEPTH)]
    outT = dr("outT", [D, NT], kind="ExternalOutput")
    scrA = dr("scrA", [D, NT], kind="Internal")
    scrB = dr("scrB", [D, NT], kind="Internal")
    dbg_out = {}
    if dbg:
        for n, s in dbg.items():
            dbg_out[n] = dr("dbg_" + n, list(s), kind="ExternalOutput")

    P = Prog(nc, same_engine_sync)
    es = contextlib.ExitStack()
    sb = lambda n, s, d=F32: es.enter_context(nc.sbuf_tensor(n, s, d))
    ps = lambda n, s, d=F32: es.enter_context(nc.psum_tensor(n, s, d))
    fm = lambda ap: ap.rearrange("(kc p) n -> p kc n", p=128)
    V, A, G, T = nc.vector, nc.scalar, nc.gpsimd, nc.tensor
    MM = T.matmul
    ACT = A.activation

    with es:
        pb = [ps(f"pb{i}", [128, 512]) for i in range(7)]
        pT = ps("pT", [128, 1024], BF16)
        part = sb("part", [128, NPAR])
        cstt = sb("cstt", [128, 1152])
        ident = sb("ident", [128, 128], BF16)
        ones_d = sb("ones_d", [128, 128])
        ones_c = sb("ones_c", [128, 128])
        ones_r = sb("ones_r", [128, 128])
        ones_b = sb("ones_b", [128, 64], BF16)
        epsT = sb("epsT", [128, 4])
        cact = sb("cact", [128, 8], BF16)
        cf = sb("cf", [128, 8])
        modv2 = sb("modv", [128, 2, 48])
        cactf = sb("cactf", [128, 8])
        vec2 = sb("vec", [128, 2, 8, 8])
        lbt = sb("lbt", [128, 8, 4])
        omlb = sb("omlb", [128, 2, 4])
        W = [sb(f"W{i}", [128, 512]) for i in range(8)]
        wpre = sb("wpre", [128, 8, 1024], BF16)

        def load_wpre(lw):
            for k in range(8):
                P.dma("pool", "wpre", I(G.dma_start, out=wpre[:, k, :], in_=win_d[lw][k * 128:(k + 1) * 128, 1024:2048]), writes=[f"wpre{k}"])
        WPK = [f"wpre{k}" for k in range(8)]
        rstdT = sb("rstdT", [128, 512])
        nmrT = sb("nmrT", [128, 512])
        wctr = [0]

        def wtmp():
            i = wctr[0] % len(W)
            wctr[0] += 1
            return W[i], f"W{i}"

        gctr = [0]

        def gbank(n=2):
            i = gctr[0] % n
            gctr[0] += 1
            return pb[i], f"pb{i}"

        recmask = cstt[:, 128:640]
        resetm = cstt[:, 640:1152]
        b3 = lambda a: a[:].rearrange("p (c t) -> p c t", t=64)

        P.dma("sp", "ld0", I(nc.sync.dma_start, out=part[:], in_=par), writes=["part"])
        P.dma("sp", "ld1", I(nc.sync.dma_start, out=cstt[:], in_=cst), writes=["cstt"])
        P.dma("sp", "ld2", I(nc.sync.dma_start, out=cf[:], in_=cT), writes=["cf"])
        flg = sb("flgs", [128, 4])
        P.dma("sp", "ld3", I(nc.sync.dma_start, out=flg[:], in_=flg_d), writes=["flg"])
        isA, isB, hbias = flg[:, 0:1], flg[:, 1:2], flg[:, 2:3]
        P.emit("pool", I(G.memset, ones_d[:], 1.0 / D), writes=["ones_d"])
        P.emit("pool", I(G.memset, ones_c[:], 1.0 / 256), writes=["ones_c"])
        P.emit("pool", I(G.memset, ones_r[:], 1.0 / 128), writes=["ones_r"])
        P.emit("pool", I(G.memset, ones_b[:], 1.0), writes=["ones_b"])
        P.emit("pool", I(G.memset, epsT[:, 0:1], LN_EPS), writes=["epsT"])
        P.emit("pool", I(G.memset, epsT[:, 1:2], LN_EPS / (ALPHA * ALPHA)), writes=["epsT"])
        P.emit("pool", I(G.memset, epsT[:, 2:3], 1.0), writes=["epsT"])
        P.emit("dve", I(V.tensor_copy, out=ident[:], in_=cstt[:, 0:128]), reads=["cstt"], writes=["ident"])
        P.emit("act", I(ACT, out=cact[:], in_=cf[:], func=AF.Silu), reads=["cf"], writes=["cact"])
        P.emit("act", I(ACT, out=cactf[:], in_=cf[:], func=AF.Silu), reads=["cf"], writes=["cactf"])
        rl = part[:, 2 * PL:2 * PL + 8].rearrange("p (l c) -> p l c", c=4)
        r0, r1 = rl[:, 0, :], rl[:, 1, :]
        mx, e0, e1, ssum, rs, s0, s1, c1 = [lbt[:, i, :] for i in range(8)]
        TT = V.tensor_tensor
        P.emit("dve", I(V.tensor_max, out=mx, in0=r0, in1=r1), reads=["part"], writes=["lb_mx"])
        P.emit("dve", I(TT, out=e0, in0=r0, in1=mx, op=ALU.subtract), reads=["part", "lb_mx"], writes=["lb_e0"])
        P.emit("dve", I(TT, out=e1, in0=r1, in1=mx, op=ALU.subtract), reads=["part", "lb_mx"], writes=["lb_e1"])
        P.emit("act", I(ACT, out=e0, in_=e0, func=AF.Exp), reads=["lb_e0"], writes=["lb_e0"])
        P.emit("act", I(ACT, out=e1, in_=e1, func=AF.Exp), reads=["lb_e1"], writes=["lb_e1"])
        P.emit("dve", I(TT, out=ssum, in0=e0, in1=e1, op=ALU.add), reads=["lb_e0", "lb_e1"], writes=["lb_s"])
        P.emit("dve", I(V.reciprocal, out=rs, in_=ssum), reads=["lb_s"], writes=["lb_rs"])
        P.emit("dve", I(TT, out=s0, in0=e0, in1=rs, op=ALU.mult), reads=["lb_e0", "lb_rs"], writes=["lb_s0"])
        P.emit("dve", I(TT, out=s1, in0=e1, in1=rs, op=ALU.mult), reads=["lb_e1", "lb_rs"], writes=["lb_s1"])
        P.emit("dve", I(TT, out=c1, in0=s0, in1=s1, op=ALU.add), reads=["lb_s0", "lb_s1"], writes=["lb_c1"])
        P.emit("dve", I(TT, out=mx, in0=s0, in1=s0, op=ALU.subtract), reads=["lb_s0"], writes=["lb_mx"])
        P.emit("dve", I(TT, out=c1, in0=c1, in1=s0, op=ALU.subtract), reads=["lb_c1", "lb_s0"], writes=["lb_c1"])
        P.emit("dve", I(V.tensor_scalar, out=omlb[:, 0, :], in0=mx, scalar1=-1.0, scalar2=1.0, op0=ALU.mult, op1=ALU.add),
               reads=["lb_mx"], writes=["omlb"])
        P.emit("dve", I(V.tensor_scalar, out=omlb[:, 1, :], in0=c1, scalar1=-1.0, scalar2=1.0, op0=ALU.mult, op1=ALU.add),
               reads=["lb_c1", "omlb"], writes=["omlb"])
        nomlb = sb("nomlb", [128, 2, 4])
        P.emit("dve", I(V.tensor_scalar, out=nomlb[:], in0=omlb[:], scalar1=-1.0, scalar2=None, op0=ALU.mult), reads=["omlb"], writes=["nomlb"])

        def tap(name, src_ap, keys):
            if name in dbg_out:
                P.dma("pool", "dbg_" + name, I(G.dma_start, out=dbg_out[name], in_=src_ap), reads=keys, writes=["dbgo_" + name])

        def ln_stats(epscol, bk=(3, 4), outs=None):
            pm_, pq_ = pb[bk[0]], pb[bk[1]]
            km_, kq_ = f"pb{bk[0]}", f"pb{bk[1]}"
            m2, m2k = wtmp()
            P.emit("act", I(ACT, out=m2[:], in_=pm_[:], func=AF.Square), reads=[km_], writes=[m2k])
            var, vark = wtmp()
            P.emit("dve", I(TT, out=var[:], in0=pq_[:], in1=m2[:], op=ALU.subtract), reads=[kq_, m2k], writes=[vark])
            sd, sdk = wtmp()
            P.emit("act", I(ACT, out=sd[:], in_=var[:], func=AF.Ln, bias=epsT[:, epscol:epscol + 1]), reads=[vark, "epsT"], writes=[sdk])
            (rstd, rsk), (nmr, nmk) = outs if outs else ((rstdT, "rstdT"), (nmrT, "nmrT"))
            P.emit("act", I(ACT, out=rstd[:], in_=sd[:], func=AF.Exp, scale=-0.5), reads=[sdk], writes=[rsk])
            P.emit("dve", I(V.scalar_tensor_tensor, out=nmr[:], in0=pm_[:], scalar=-1.0, in1=rstd[:], op0=ALU.mult, op1=ALU.mult),
                   reads=[km_, rsk], writes=[nmk])
            return rstd, rsk, nmr, nmk

        def ln_stats_part(xt, xkey, csl=slice(None), bk=(3, 4), outs=None):
            for m in range(8):
                q_, qk_ = wtmp()
                P.emit("act", I(ACT, out=q_[:], in_=xt[:, m, csl], func=AF.Square), reads=[xkey], writes=[qk_])
                P.emit("pe", [I(MM, pb[bk[0]][:], lhsT=ones_d[:], rhs=xt[:, m, csl], start=(m == 0), stop=(m == 7)),
                              I(MM, pb[bk[1]][:], lhsT=ones_d[:], rhs=q_[:], start=(m == 0), stop=(m == 7))],
                       reads=[xkey, qk_, "ones_d"], writes=[f"pb{bk[0]}", f"pb{bk[1]}"])
            return ln_stats(1, bk, outs)

        def ln_apply_gen(xt, xkey, lng, lnb, st, csl=slice(None), bufs=None):
            rstd, rsk, nmr, nmk = st
            for m in range(8):
                ta, tak = bufs[0] if bufs else wtmp()
                P.emit("dve", I(TT, out=ta[:], in0=xt[:, m, csl], in1=rstd[:], op=ALU.mult), reads=[xkey, rsk], writes=[tak])
                yield
                tb, tbk = bufs[1] if bufs else wtmp()
                if m % 2 == 0:
                    P.emit("pool", I(G.tensor_tensor, out=tb[:], in0=ta[:], in1=nmr[:], op=ALU.add), reads=[tak, nmk], writes=[tbk])
                else:
                    P.emit("dve", I(TT, out=tb[:], in0=ta[:], in1=nmr[:], op=ALU.add), reads=[tak, nmk], writes=[tbk])
                yield
                P.emit("act", I(ACT, out=xt[:, m, csl], in_=tb[:], func=AF.Identity, scale=lng[:, m:m + 1], bias=lnb[:, m:m + 1]),
                       reads=[tbk, "part", xkey], writes=[xkey])
                yield

        def ln_apply(xt, xkey, lng, lnb, csl=slice(None)):
            st = ln_stats_part(xt, xkey, csl)
            for _ in ln_apply_gen(xt, xkey, lng, lnb, st, csl):
                pass

        def ck(name):
            if stop == name:
                P.stopped = True

        def interleave(gens):
            gens = list(gens)
            while gens:
                for g in list(gens):
                    try:
                        next(g)
                    except StopIteration:
                        gens.remove(g)

        def interleave_primary(primary, extra):
            primary = list(primary)
            extra_alive = True
            while primary:
                for g in list(primary):
                    try:
                        next(g)
                    except StopIteration:
                        primary.remove(g)
                if extra_alive and primary:
                    try:
                        next(extra)
                    except StopIteration:
                        extra_alive = False

        def mod_gen(lm, stg):
            pm = lm % 2
            po_ = lm * PL
            NG = 24
            stgw, rowb = stg
            def ld(g):
                P.dma("pool", f"wa{g % 2}", I(G.dma_start, out=stgw[g % 2][:], in_=fm(wada[lm][:, g * 256:(g + 1) * 256])), writes=[f"stg{g % 2}"])
            ld(0)
            for g in range(NG):
                if g + 1 < NG:
                    ld(g + 1)
                yield
                P.emit("pe", [I(MM, pb[6][0:1, 0:256], lhsT=cact[:, k:k + 1], rhs=stgw[g % 2][:, k, :], start=(k == 0), stop=(k == 7)) for k in range(8)],
                       reads=[f"stg{g % 2}", "cact"], writes=["pb6"])
                P.emit("act", I(A.copy, out=rowb[g % 2], in_=pb[6][0:1, 0:256]), reads=["pb6"], writes=[f"rowb{g % 2}", f"lnb2_{g % 2}"])
                yield
                P.emit("pe", [I(MM, pb[5][:, g * 2 + j:g * 2 + j + 1], lhsT=rowb[g % 2][:, j * 128:(j + 1) * 128], rhs=epsT[0:1, 2:3], start=True, stop=True)
                              for j in range(2)], reads=[f"rowb{g % 2}", f"lnb2_{g % 2}", "epsT"], writes=["pb5"])
                yield
            mv = modv2[:, pm, :]
            mk = f"modv{pm}"
            P.emit("dve", I(TT, out=mv, in0=pb[5][:, 0:48], in1=part[:, po_:po_ + 48], op=ALU.add), reads=["pb5", "part"], writes=[mk])
            sh1, sc1, g1, sh2, sc2, g2 = [modv2[:, pm, i * 8:(i + 1) * 8] for i in range(6)]
            A1_, B1_, G1_, G2_, A2_ = [vec2[:, pm, i, :] for i in range(5)]
            sfx = f"_{pm}"
            P.emit("dve", I(V.tensor_scalar, out=A1_, in0=sc1, scalar1=1.0, scalar2=None, op0=ALU.add), reads=[mk], writes=["vA1" + sfx])
            P.emit("dve", I(V.tensor_copy, out=B1_, in_=sh1), reads=[mk], writes=["vB1" + sfx])
            P.emit("dve", I(V.tensor_scalar, out=G1_, in0=g1, scalar1=1.0, scalar2=1.0 / ALPHA, op0=ALU.add, op1=ALU.mult), reads=[mk], writes=["vG1" + sfx])
            P.emit("dve", I(V.tensor_scalar, out=G2_, in0=g2, scalar1=1.0, scalar2=1.0 / ALPHA, op0=ALU.add, op1=ALU.mult), reads=[mk], writes=["vG2" + sfx])
            P.emit("dve", I(V.tensor_scalar, out=A2_, in0=sc2, scalar1=1.0, scalar2=None, op0=ALU.add), reads=[mk], writes=["vA2" + sfx])
            yield

        def layer(l):
            po = l * PL
            bada = part[:, po:po + 48]
            ln1g, ln1b = part[:, po + 48:po + 56], part[:, po + 56:po + 64]
            ln2g, ln2b = part[:, po + 64:po + 72], part[:, po + 72:po + 80]
            convw = part[:, po + 80:po + 142]
            convb, convg, convlb = part[:, po + 142:po + 144], part[:, po + 144:po + 146], part[:, po + 146:po + 148]
            normg = part[:, po + 148:po + 149]
            xsrc, xsk = (xT, "xT") if l == 0 else (scrB, "scrB")
            xdst, xdk = (outT, "outT") if l == DEPTH - 1 else (scrB, "scrB")
            XSK = [xsk] + [k_ for k_ in list(P.res.keys()) if str(k_).startswith(xsk + "_")]

            pm = l % 2
            if l == 0:
                with contextlib.ExitStack() as ls:
                    stg = ([ls.enter_context(nc.sbuf_tensor(f"stgA{i}", [128, 8, 256], BF16)) for i in range(2)],
                           [ls.enter_context(nc.sbuf_tensor(f"rowA{i}", [1, 256], F32))[0:1, :] for i in range(2)])
                    g0 = mod_gen(0, stg)
                    for i_, _ in enumerate(g0):
                        if i_ == 40:
                            load_wpre(0)
                P.barrier()
            modv = modv2[:, pm, :]
            MK = f"modv{pm}"
            sh1, sc1, g1, sh2, sc2, g2 = [modv2[:, pm, i * 8:(i + 1) * 8] for i in range(6)]
            A1, B1, G1, G2, A2 = [vec2[:, pm, i, :] for i in range(5)]
            if l == 0:
                tap("modv", modv, [MK])
            ck("mod")
            with contextlib.ExitStack() as ms:
                msb = lambda n, s, d=F32: ms.enter_context(nc.sbuf_tensor(f"{n}_{l}", s, d))
                winb = msb("winb", [128, 8, DIN - 1024], BF16)
                woutb = msb("woutb", [128, 8, D], BF16)
                biasb = msb("biasb", [128, 4, 5, 128])
                diag = msb("diag", [128, 62, 128], BF16)
                xt = msb("xt", [128, 8, TL])
                hbf = msb("hbf", [128, 8, TL], BF16)
                qt_bf = msb("qt_bf", [128, 4, TL], BF16)
                kh_bf = msb("kh_bf", [128, 4, TL], BF16)
                kh_tm = msb("kh_tm", [128, 4, 512], BF16)
                v_tm = msb("v_tm", [128, 4, 512], BF16)
                attnT = msb("attnT", [128, 2, 512], BF16)
                adec = msb("adec", [128, 4, 8])
                Sst = msb("Sst", [128, 4, 2, 128])
                Sbf = msb("Sbf", [128, 2, 8, 128], BF16)
                ymix = msb("ymix", [128, 8, TL], BF16)
                ubuf = msb("ubuf", [128, 2, 30 + TL], BF16)
                uc = msb("uc", [128, 2, TL])
                qTa = msb("qTa", [128, 2, TL], BF16)
                kTa = msb("kTa", [128, 2, 2 * TL], BF16)
                vat = msb("vat", [128, 8, 256], BF16)
                tS2 = msb("tS", [128, 2, 5, 128])
                PT2 = msb("PT", [128, 2, 5, 128], BF16)
                rrec = msb("rrec", [128, 128])
                WOUTK = [f"woutb{k}" for k in range(8)]

                WBLK = [(512, 1024), (2048, 2560), (0, 512), (2560, 3328)]

                def wsrc(c0, c1_):
                    if 1024 <= c0 and c1_ <= 2048:
                        return wpre, c0 - 1024, WPK
                    loc = c0 if c0 < 1024 else c0 - 1024
                    keys = [f"winb{k}_{b}" for k in range(8) for b, (a0, a1) in enumerate(WBLK) if a0 < c1_ and c0 < a1]
                    return winb, loc, keys
                for b, (c0, c1_) in enumerate(WBLK):
                    loc = c0 if c0 < 1024 else c0 - 1024
                    for k in range(8):
                        P.dma("pool", f"winb{b}", I(G.dma_start, out=winb[:, k, loc:loc + (c1_ - c0)], in_=win_d[l][k * 128:(k + 1) * 128, c0:c1_]), writes=[f"winb{k}_{b}"])
                for k in range(8):
                    P.dma("pool", "woutb", I(G.dma_start, out=woutb[:, k, :], in_=wout_d[l][k * 128:(k + 1) * 128, :]), writes=[f"woutb{k}"])
                P.dma("sp", "biasb", I(nc.sync.dma_start, out=biasb[:].rearrange("p h r j -> p (h r j)"), in_=bias_d[l]), writes=["biasb"])
                for cj in range(62):
                    P.emit("pool", I(G.tensor_scalar, out=diag[:, cj, :], in0=cstt[:, 0:128], scalar1=convw[:, cj:cj + 1], scalar2=None, op0=ALU.mult),
                           reads=["cstt", "part"], writes=["diag"])
                P.emit("dve", I(V.memset, Sst[:], 0.0), writes=[f"S{h}_{c}" for h in range(4) for c in range(2)])
                P.emit("dve", I(V.memset, ubuf[:], 0.0), writes=["ubuf"])
                P.emit("dve", I(V.memset, kTa[:], 0.0), writes=["kTa"])
                P.emit("dve", I(V.memset, vat[:], 0.0), writes=["vat"])
                scur = [0, 0, 0, 0]
                ck("wload")

                def proj_fm(col):
                    bank, bkey = gbank()
                    wt_, lc_, wk_ = wsrc(col, col + 128)
                    P.emit("pe", [I(MM, bank[:], lhsT=wt_[:, k, lc_:lc_ + 128], rhs=hbf[:, k, :], start=(k == 0), stop=(k == 7)) for k in range(8)],
                           reads=wk_ + ["hbf"], writes=[bkey])
                    return bank, bkey

                TB = [[(W[i], f"W{i}") for i in range(0, 4)], [(W[i], f"W{i}") for i in range(4, 8)]]
                tsf = tS2[:].rearrange("p a r q -> p (a r q)")
                phys = [(W[i], f"W{i}") for i in range(8)] + [(rstdT, "rstdT"), (nmrT, "nmrT"), (tsf[:, 0:512], "tS0"), (tsf[:, 640:1152], "tS1")]
                TBA = [[phys[3 * i], phys[3 * i + 1], phys[3 * i + 2], phys[3 * i + 2]] for i in range(4)]

                def load_tile(t):
                    tsl = slice(t * TL, (t + 1) * TL)
                    P.dma("sp", "xld", I(nc.sync.dma_start, out=xt[:], in_=fm(xsrc[:, tsl])), reads=XSK, writes=["xt"])

                def prefetch_h(t):
                    for k in range(8):
                        P.dma("sp", f"xpf{k % 2}", I(nc.sync.dma_start, out=uc[:, k % 2, :], in_=xsrc[k * 128:(k + 1) * 128, t * TL:(t + 1) * TL]),
                              reads=XSK, writes=[f"uc{k % 2}"])
                        P.emit("act", I(ACT, out=hbf[:, k, :], in_=uc[:, k % 2, :], func=AF.Identity, scale=A1[:, k:k + 1], bias=B1[:, k:k + 1]),
                               reads=[f"uc{k % 2}", f"vA1_{pm}", f"vB1_{pm}"], writes=["hbf"])
                        yield

                def rec_v(t):
                    for s in range(4):
                        bank, bkey = gbank()
                        P.emit("pe", [I(MM, bank[:], lhsT=hbf[:, k, s * 128:(s + 1) * 128], rhs=wpre[:, k, 512:1024], start=(k == 0), stop=(k == 7)) for k in range(8)],
                               reads=WPK + ["hbf"], writes=[bkey])
                        P.emit("act", I(A.copy, out=v_tm[:, s, :], in_=bank[:]), reads=[bkey], writes=["v_tm"])
                    P.dma("sp", "sVt", I(nc.sync.dma_start, out=sVt[t], in_=v_tm[:].rearrange("p s n -> p (s n)")), reads=["v_tm"], writes=[f"sVt{t}"])

                def rec_pre(hd, th, t):
                    (B0, B0k), (B1, B1k), _, (B3, B3k) = TBA[th]
                    pnA, pnB = (5, 6) if th % 2 == 0 else (3, 4)
                    hs = slice(hd * 128, (hd + 1) * 128)
                    h5 = slice(hd * 512, (hd + 1) * 512)
                    bank, bkey = proj_fm(1024 + hd * 128)
                    sg, sgk = B0, B0k
                    P.emit("act", I(ACT, out=sg[:], in_=bank[:], func=AF.Sigmoid, scale=-1.0), reads=[bkey], writes=[sgk])
                    yield
                    kk, kkk = B1, B1k
                    P.emit("dve", I(V.tensor_scalar, out=kk[:], in0=sg[:], scalar1=omlb[:, l, hd:hd + 1], scalar2=None, op0=ALU.mult),
                           reads=[sgk, "omlb"], writes=[kkk])
                    yield
                    lf, lfk = B3, B3k
                    P.emit("act", I(ACT, out=lf[:], in_=sg[:], func=AF.Ln, scale=nomlb[:, l, hd:hd + 1], bias=epsT[:, 2:3]), reads=[sgk, "nomlb", "epsT"], writes=[lfk])
                    yield
                    bcum, bck = B0, B0k
                    P.emit("dve", I(V.tensor_tensor_scan, out=bcum[:], data0=resetm, data1=lf[:], initial=0.0, op0=ALU.mult, op1=ALU.add),
                           reads=[lfk, "cstt"], writes=[bck])
                    yield
                    bc, bcK = B3, B3k
                    P.emit("dve", I(TT, out=b3(bc), in0=b3(bcum), in1=b3(bcum)[:, :, 63:64].broadcast_to([128, 8, 64]), op=ALU.subtract),
                           reads=[bck], writes=[bcK])
                    yield
                    P.emit("act", I(ACT, out=adec[:, hd, :], in_=b3(bcum)[:, :, 63], func=AF.Exp), reads=[bck], writes=[f"adec{hd}"])
                    P.dma("sp", f"sAd{hd}", I(nc.sync.dma_start, out=sAd[t][:, hd * 8:(hd + 1) * 8], in_=adec[:, hd, :]), reads=[f"adec{hd}"], writes=[f"sAd{t}_{hd}"])
                    yield
                    Ep, Epk = B0, B0k
                    P.emit("act", I(ACT, out=Ep[:], in_=bc[:], func=AF.Exp), reads=[bcK, bck], writes=[Epk])
                    P.dma("sp", f"sEp{th}", I(nc.sync.dma_start, out=sEp[t][:, h5], in_=Ep[:]), reads=[Epk], writes=[f"sEp{t}_{hd}"])
                    yield
                    Em, Emk = B3, B3k
                    P.emit("act", I(ACT, out=Em[:], in_=bc[:], func=AF.Exp, scale=-1.0), reads=[bcK], writes=[Emk])
                    yield
                    P.emit("dve", I(TT, out=kh_bf[:, hd, :], in0=kk[:], in1=Em[:], op=ALU.mult), reads=[kkk, Emk], writes=[f"kh{hd}"])
                    P.dma("sp", f"sKh{hd}", I(nc.sync.dma_start, out=sKh[t][:, h5], in_=kh_bf[:, hd, :]), reads=[f"kh{hd}"], writes=[f"sKh{t}_{hd}"])
                    yield
                    P.emit("pe", [I(T.transpose, pT[:, s * 128:(s + 1) * 128], kh_bf[:, hd, s * 128:(s + 1) * 128], ident[:]) for s in range(4)],
                           reads=[f"kh{hd}", "ident"], writes=["pT"])
                    P.emit("act", I(A.copy, out=kh_tm[:, :, hs], in_=pT[:, 0:512].rearrange("p (s d) -> p s d", d=128)),
                           reads=["pT"], writes=[f"khtm{hd}"])
                    P.dma("sp", f"sKt{hd}", I(nc.sync.dma_start, out=sKt[t].rearrange("p (s d) -> p s d", d=512)[:, :, hs], in_=kh_tm[:, :, hs]),
                          reads=[f"khtm{hd}"], writes=[f"sKt{t}_{hd}"])
                    yield
                    ops = []
                    for n in range(8):
                        pr, base = n // 2, (n % 2) * 64
                        bankp = pb[pnA if n % 2 == 0 else pnB]
                        ops.append(I(MM, bankp[:, pr * 128:(pr + 1) * 128], lhsT=kh_tm[base:base + 64, pr, hs],
                                     rhs=v_tm[base:base + 64, pr, hs], start=True, stop=True))
                    P.emit("pe", ops, reads=[f"khtm{hd}", "v_tm"], writes=[f"pb{pnA}", f"pb{pnB}"])
                    for n in range(8):
                        cur = scur[hd]
                        bankp = pb[pnA if n % 2 == 0 else pnB]
                        P.emit("dve", I(V.scalar_tensor_tensor, out=Sst[:, hd, 1 - cur, :], in0=Sst[:, hd, cur, :], scalar=adec[:, hd, n:n + 1],
                                        in1=bankp[:, (n // 2) * 128:(n // 2 + 1) * 128], op0=ALU.mult, op1=ALU.add),
                               reads=[f"S{hd}_{cur}", f"adec{hd}", f"pb{pnA}", f"pb{pnB}"], writes=[f"S{hd}_{1 - cur}"])
                        scur[hd] = 1 - cur
                    yield

                def load_rec(t):
                    P.dma("sp", "lKh", I(nc.sync.dma_start, out=kh_bf[:].rearrange("p h n -> p (h n)"), in_=sKh[t]),
                          reads=[f"sKh{t}_{h}" for h in range(4)], writes=[f"kh{h}" for h in range(4)])
                    P.dma("sp", "lKt", I(nc.sync.dma_start, out=kh_tm[:].rearrange("p s n -> p (s n)"), in_=sKt[t]),
                          reads=[f"sKt{t}_{h}" for h in range(4)], writes=[f"khtm{h}" for h in range(4)])
                    P.dma("sp", "lVt", I(nc.sync.dma_start, out=v_tm[:].rearrange("p s n -> p (s n)"), in_=sVt[t]), reads=[f"sVt{t}"], writes=["v_tm"])
                    P.dma("sp", "lAd", I(nc.sync.dma_start, out=adec[:].rearrange("p h n -> p (h n)"), in_=sAd[t]),
                          reads=[f"sAd{t}_{h}" for h in range(4)], writes=[f"adec{h}" for h in range(4)])

                def rec_main(hd, th, t):
                    (B0, B0k), (B1, B1k), (B2, B2k), (B3, B3k) = TB[th]
                    pO, pOk = (pb[4], "pb4") if th == 0 else (pb[2], "pb2")
                    aT, aTk = attnT[:, th, :], f"attnT{th}"
                    hs = slice(hd * 128, (hd + 1) * 128)
                    Ep, Epk = B2, B2k
                    P.dma("sp", f"lEp{th}", I(nc.sync.dma_start, out=Ep[:], in_=sEp[t][:, hd * 512:(hd + 1) * 512]), reads=[f"sEp{t}_{hd}"], writes=[Epk])
                    yield
                    bankq, bqk = proj_fm(512 + hd * 128)
                    P.emit("dve", I(TT, out=qt_bf[:, hd, :], in0=bankq[:], in1=Ep[:], op=ALU.mult), reads=[bqk, Epk], writes=[f"qt{hd}"])
                    yield
                    P.emit("pe", [I(MM, pb[3][:, pr * 128:(pr + 1) * 128], lhsT=kh_bf[:, hd, pr * 128:(pr + 1) * 128],
                                    rhs=qt_bf[:, hd, pr * 128:(pr + 1) * 128], start=True, stop=True) for pr in range(4)],
                           reads=[f"kh{hd}", f"qt{hd}"], writes=["pb3"])
                    P.emit("dve", I(TT, out=aT, in0=pb[3][:], in1=recmask, op=ALU.mult), reads=["pb3", "cstt"], writes=[aTk])
                    yield
                    ops = []
                    for n in range(8):
                        pr, base = n // 2, (n % 2) * 64
                        bankp = pb[5 + n % 2]
                        ops.append(I(MM, bankp[:, pr * 128:(pr + 1) * 128], lhsT=kh_tm[base:base + 64, pr, hs],
                                     rhs=v_tm[base:base + 64, pr, hs], start=True, stop=True))
                    P.emit("pe", ops, reads=[f"khtm{hd}", "v_tm"], writes=["pb5", "pb6"])
                    for n in range(8):
                        cur = scur[hd]
                        bankp = pb[5 + n % 2]
                        P.emit("act", I(ACT, out=Sbf[:, th, n, :], in_=Sst[:, hd, cur, :], func=AF.Identity, scale=adec[:, hd, n:n + 1]),
                               reads=[f"S{hd}_{cur}", f"adec{hd}"], writes=[f"Sbf{th}_{n}"])
                        P.emit("dve", I(V.scalar_tensor_tensor, out=Sst[:, hd, 1 - cur, :], in0=Sst[:, hd, cur, :], scalar=adec[:, hd, n:n + 1],
                                        in1=bankp[:, (n // 2) * 128:(n // 2 + 1) * 128], op0=ALU.mult, op1=ALU.add),
                               reads=[f"S{hd}_{cur}", f"adec{hd}", "pb5", "pb6"], writes=[f"S{hd}_{1 - cur}"])
                        scur[hd] = 1 - cur
                    yield
                    ops = []
                    for pr in range(4):
                        ops.append(I(MM, pO[:, pr * 128:(pr + 1) * 128], lhsT=v_tm[:, pr, hs], rhs=aT[:, pr * 128:(pr + 1) * 128], start=True, stop=False))
                        for n in (2 * pr, 2 * pr + 1):
                            ops.append(I(MM, pO[:, n * 64:(n + 1) * 64], lhsT=Sbf[:, th, n, :], rhs=qt_bf[:, hd, n * 64:(n + 1) * 64],
                                         start=False, stop=(n % 2 == 1)))
                    P.emit("pe", ops, reads=["v_tm", aTk, f"qt{hd}"] + [f"Sbf{th}_{n}" for n in range(8)], writes=[pOk])
                    yield
                    osq, osqk = B2, B2k
                    P.emit("act", I(ACT, out=osq[:], in_=pO[:], func=AF.Square), reads=[pOk], writes=[osqk])
                    yield
                    P.emit("pe", I(MM, pb[3][:], lhsT=ones_r[:], rhs=osq[:], start=True, stop=True), reads=[osqk, "ones_r"], writes=["pb3"])
                    sd, sdk = B0, B0k
                    P.emit("act", I(ACT, out=sd[:], in_=pb[3][:], func=AF.Ln, bias=epsT[:, 0:1]), reads=["pb3", "epsT"], writes=[sdk])
                    yield
                    rstd, rsk = B0, B0k
                    P.emit("act", I(ACT, out=rstd[:], in_=sd[:], func=AF.Exp, scale=-0.5), reads=[sdk], writes=[rsk])
                    yield
                    t1, t1k = B3, B3k
                    P.emit("dve", I(TT, out=t1[:], in0=pO[:], in1=rstd[:], op=ALU.mult), reads=[pOk, rsk], writes=[t1k])
                    yield
                    bankg, bgk = proj_fm(2048 + hd * 128)
                    sgt, sgtk = B1, B1k
                    P.emit("act", I(ACT, out=sgt[:], in_=bankg[:], func=AF.Silu), reads=[bgk], writes=[sgtk])
                    yield
                    P.emit("dve", I(V.scalar_tensor_tensor, out=ymix[:, 2 + hd, :], in0=t1[:], scalar=normg, in1=sgt[:], op0=ALU.mult, op1=ALU.mult),
                           reads=[t1k, sgtk, "part"], writes=[f"ymix{2 + hd}"])
                    yield

                def conv_glu():
                    for c in range(2):
                        bankg, bgk = proj_fm(256 + c * 128)
                        sg, sgk = wtmp()
                        P.emit("act", I(ACT, out=sg[:], in_=bankg[:], func=AF.Sigmoid), reads=[bgk], writes=[sgk])
                        bankv, bvk = proj_fm(c * 128)
                        P.emit("dve", I(TT, out=ubuf[:, c, 30:30 + TL], in0=bankv[:], in1=sg[:], op=ALU.mult), reads=[bvk, sgk, "ubuf"], writes=["ubuf"])

                def conv_rest():
                    usq = []
                    for c in range(2):
                        P.emit("pe", [I(MM, pb[5 + c][:], lhsT=diag[:, c * 31 + j, :], rhs=ubuf[:, c, j:j + TL], start=(j == 0), stop=(j == 30)) for j in range(31)],
                               reads=["diag", "ubuf"], writes=[f"pb{5 + c}"])
                        P.emit("act", I(ACT, out=uc[:, c, :], in_=pb[5 + c][:], func=AF.Identity, bias=convb[:, c:c + 1]), reads=[f"pb{5 + c}", "part"], writes=[f"uc{c}"])
                        q_, qk_ = wtmp()
                        P.emit("act", I(ACT, out=q_[:], in_=uc[:, c, :], func=AF.Square), reads=[f"uc{c}"], writes=[qk_])
                        usq.append((q_, qk_))
                    P.emit("pe", [I(MM, pb[3][:], lhsT=ones_c[:], rhs=uc[:, c, :], start=(c == 0), stop=(c == 1)) for c in range(2)]
                           + [I(MM, pb[4][:], lhsT=ones_c[:], rhs=usq[c][0][:], start=(c == 0), stop=(c == 1)) for c in range(2)],
                           reads=["uc0", "uc1", usq[0][1], usq[1][1], "ones_c"], writes=["pb3", "pb4"])
                    rstd, rsk, nmr, nmk = ln_stats(0)
                    for c in range(2):
                        ta, tak = wtmp()
                        P.emit("dve", I(TT, out=ta[:], in0=uc[:, c, :], in1=rstd[:], op=ALU.mult), reads=[f"uc{c}", rsk], writes=[tak])
                        tb, tbk = wtmp()
                        P.emit("pool", I(G.tensor_tensor, out=tb[:], in0=ta[:], in1=nmr[:], op=ALU.add), reads=[tak, nmk], writes=[tbk])
                        P.emit("act", I(ACT, out=ymix[:, c, :], in_=tb[:], func=AF.Silu, scale=convg[:, c:c + 1], bias=convlb[:, c:c + 1]),
                               reads=[tbk, "part"], writes=[f"ymix{c}"])
                    P.emit("pool", I(G.tensor_copy, out=ubuf[:, :, 0:30], in_=ubuf[:, :, TL:TL + 30]), reads=["ubuf"], writes=["ubuf"])


                def att_proj():
                    for c in range(2):
                        bq, bqk = proj_fm(2560 + c * 128)
                        P.emit("act", I(ACT, out=qTa[:, c, :], in_=bq[:], func=AF.Copy, scale=0.125), reads=[bqk], writes=["qTa"])
                        bk_, bkk = proj_fm(2816 + c * 128)
                        P.emit("dve", I(V.tensor_copy, out=kTa[:, c, TL:2 * TL], in_=bk_[:]), reads=[bkk, "kTa"], writes=["kTa"])
                    for s2 in range(2):
                        bank, bkey = gbank()
                        ops = []
                        for ss in range(2):
                            s = s2 * 2 + ss
                            for k in range(8):
                                ops.append(I(MM, bank[:, ss * 256:(ss + 1) * 256], lhsT=hbf[:, k, s * 128:(s + 1) * 128], rhs=winb[:, k, 2048:2304],
                                             start=(k == 0), stop=(k == 7)))
                        P.emit("pe", ops, reads=wsrc(3072, 3328)[2] + ["hbf"], writes=[bkey])
                        P.emit("act", I(A.copy, out=vat[:, 4 + 2 * s2:6 + 2 * s2, :], in_=bank[:].rearrange("p (s d) -> p s d", d=256)),
                               reads=[bkey, "vat"], writes=["vat"])

                def att_main(t):
                    obanks = {}

                    def stage1(j, c, hh2):
                        J = 4 * t + j
                        nh = max(0, 4 - J)
                        hh = 2 * c + hh2
                        base = hh2 * 64
                        bA, bB = (5, 6) if hh2 == 0 else (3, 4)
                        tS, PT, tSk, PTk = tS2[:, hh2], PT2[:, hh2], f"tS{hh2}", f"PT{hh2}"
                        ops = []
                        for r in range(5):
                            bankS = pb[bA] if r < 4 else pb[bB]
                            ops.append(I(MM, bankS[:, (r % 4) * 128:(r % 4 + 1) * 128], lhsT=kTa[base:base + 64, c, (j + r) * 128:(j + r + 1) * 128],
                                         rhs=qTa[base:base + 64, c, j * 128:(j + 1) * 128], start=True, stop=True))
                        P.emit("pe", ops, reads=["kTa", "qTa"], writes=[f"pb{bA}", f"pb{bB}"])
                        P.emit("dve", I(TT, out=tS[:, 0:4, :], in0=pb[bA][:].rearrange("p (r q) -> p r q", q=128),
                                        in1=biasb[:, hh, 0:4, :], op=ALU.add), reads=[f"pb{bA}", "biasb", tSk], writes=[tSk])
                        P.emit("dve", I(TT, out=tS[:, 4, :], in0=pb[bB][:, 0:128], in1=biasb[:, hh, 4, :], op=ALU.add),
                               reads=[f"pb{bB}", "biasb", tSk], writes=[tSk])
                        if nh > 0:
                            P.emit("act", I(ACT, out=PT[:, 0:nh, :], in_=tS[:, 0:nh, :], func=AF.Exp, bias=hbias), reads=[tSk, "flg", PTk], writes=[PTk])
                        P.emit("act", I(ACT, out=PT[:, nh:5, :], in_=tS[:, nh:5, :], func=AF.Exp), reads=[tSk, PTk], writes=[PTk])

                    def stage2(j, c, hh2):
                        hh = 2 * c + hh2
                        base = hh2 * 64
                        PT, PTk = PT2[:, hh2], f"PT{hh2}"
                        if (j, c) not in obanks:
                            obanks[(j, c)] = gbank()
                        obank, obk = obanks[(j, c)]
                        ops = []
                        for r in range(5):
                            ops.append(I(MM, obank[base:base + 64, 0:128], lhsT=vat[:, j + r, hh * 64:(hh + 1) * 64], rhs=PT[:, r, :],
                                         start=(r == 0), stop=(r == 4)))
                        for r in range(5):
                            ops.append(I(MM, obank[base:base + 64, 128:256], lhsT=ones_b[:], rhs=PT[:, r, :], start=(r == 0), stop=(r == 4)))
                        P.emit("pe", ops, reads=["vat", PTk, "ones_b"], writes=[obk])
                        if hh2 == 1:
                            P.emit("dve", I(V.reciprocal, out=rrec[:], in_=obank[:, 128:256]), reads=[obk], writes=["rrec"])
                            P.emit("dve", I(TT, out=ymix[:, 6 + c, j * 128:(j + 1) * 128], in0=obank[:, 0:128], in1=rrec[:], op=ALU.mult),
                                   reads=[obk, "rrec", f"ymix{6 + c}"], writes=[f"ymix{6 + c}"])

                    prev = None
                    for it in [(j, c, hh2) for j in range(4) for c in range(2) for hh2 in range(2)]:
                        stage1(*it)
                        yield
                        if prev is not None:
                            stage2(*prev)
                            yield
                        prev = it
                    stage2(*prev)
                    yield

                def att_shift():
                    P.emit("pool", I(G.tensor_copy, out=kTa[:, :, 0:TL], in_=kTa[:, :, TL:2 * TL]), reads=["kTa"], writes=["kTa"])
                    P.emit("pool", I(G.tensor_copy, out=vat[:, 0:4, :], in_=vat[:, 4:8, :]), reads=["vat"], writes=["vat"])


                def finish_tile(t):
                    tsl = slice(t * TL, (t + 1) * TL)
                    ymk = [f"ymix{i}" for i in range(8)]
                    if t == 0 and l == 0:
                        tap("ymix", ymix[:], ymk)
                    for m in range(8):
                        bank, bkey = gbank()
                        P.emit("pe", [I(MM, bank[:], lhsT=woutb[:, k, m * 128:(m + 1) * 128], rhs=ymix[:, k, :], start=(k == 0), stop=(k == 7)) for k in range(8)],
                               reads=WOUTK + ymk, writes=[bkey])
                        P.emit("dve", I(V.scalar_tensor_tensor, out=xt[:, m, :], in0=bank[:], scalar=G1[:, m:m + 1], in1=xt[:, m, :], op0=ALU.mult, op1=ALU.add),
                               reads=[bkey, f"vG1_{pm}", "xt"], writes=["xt"])
                    return ln_stats_part(xt, "xt")

                def ln1_tail(t, st, rot=False):
                    tsl = slice(t * TL, (t + 1) * TL)
                    yield from ln_apply_gen(xt, "xt", ln1g, ln1b, st, bufs=None if rot else [(uc[:, 0, :], "uc0"), (uc[:, 1, :], "uc1")])
                    if t == 0 and l == 0:
                        tap("x1", xt[:], ["xt"])
                    P.dma("sp", "xst", I(nc.sync.dma_start, out=fm(scrA[:, tsl]), in_=xt[:]), reads=["xt"], writes=["scrA"])
                    yield


                pay = xt[:].rearrange("p k n -> p (k n)")[:, 0:PAYW]
                interleave([prefetch_h(0)])
                for t in range(NTILE):
                    rec_v(t)
                    if t == NTILE - 1:
                        conv_glu()
                        att_proj()
                        interleave([rec_pre(h, h, t) for h in range(4)])
                    else:
                        interleave([rec_pre(h, h, t) for h in range(4)] + [prefetch_h(t + 1)])
                for hd in range(4):
                    P.emit("dve", I(V.tensor_scalar, out=pay[:, hd * 128:(hd + 1) * 128], in0=Sst[:, hd, scur[hd], :], scalar1=isA, scalar2=None, op0=ALU.mult),
                           reads=[f"S{hd}_{scur[hd]}", "flg"], writes=["xt"])
                P.emit("dve", I(V.tensor_scalar, out=pay[:, 512:1536].rearrange("p (c n) -> p c n", c=2), in0=kTa[:, :, TL:2 * TL], scalar1=isA, scalar2=None, op0=ALU.mult),
                       reads=["kTa", "flg", "xt"], writes=["xt"])
                P.emit("dve", I(V.tensor_scalar, out=pay[:, 1536:2560].rearrange("p (s n) -> p s n", s=4), in0=vat[:, 4:8, :], scalar1=isA, scalar2=None, op0=ALU.mult),
                       reads=["vat", "flg", "xt"], writes=["xt"])
                P.emit("dve", I(V.tensor_scalar, out=pay[:, 2560:2620].rearrange("p (c n) -> p c n", c=2), in0=ubuf[:, :, TL:TL + 30], scalar1=isA, scalar2=None, op0=ALU.mult),
                       reads=["ubuf", "flg", "xt"], writes=["xt"])
                P.dma("pool", "bnc", I(G.dma_start, out=bounce[l].ap(), in_=pay), reads=["xt"], writes=["bounce"])
                P.dma("pool", "cc", I(G.collective_compute, "AllReduce", ALU.add, replica_groups=[[2 * i, 2 * i + 1] for i in range(n_cores // 2)],
                                      ins=[bounce[l].ap().opt()], outs=[gath[l].ap().opt()]), reads=["bounce"], writes=["gath"], inc=1)
                P.dma("pool", "gth", I(G.dma_start, out=pay, in_=gath[l].ap()), reads=["gath", "xt"], writes=["xt"])
                for hd in range(4):
                    P.emit("pool", I(G.tensor_scalar, out=Sst[:, hd, 0, :], in0=pay[:, hd * 128:(hd + 1) * 128], scalar1=isB, scalar2=None, op0=ALU.mult),
                           reads=["xt", "flg", f"S{hd}_0", f"S{hd}_1"], writes=[f"S{hd}_0"])
                    scur[hd] = 0
                P.emit("pool", I(G.tensor_scalar, out=kTa[:, :, 0:TL], in0=pay[:, 512:1536].rearrange("p (c n) -> p c n", c=2), scalar1=isB, scalar2=None, op0=ALU.mult),
                       reads=["xt", "flg", "kTa"], writes=["kTa"])
                P.emit("pool", I(G.tensor_scalar, out=vat[:, 0:4, :], in0=pay[:, 1536:2560].rearrange("p (s n) -> p s n", s=4), scalar1=isB, scalar2=None, op0=ALU.mult),
                       reads=["xt", "flg", "vat"], writes=["vat"])
                P.emit("pool", I(G.tensor_scalar, out=ubuf[:, :, 0:30], in0=pay[:, 2560:2620].rearrange("p (c n) -> p c n", c=2), scalar1=isB, scalar2=None, op0=ALU.mult),
                       reads=["xt", "flg", "ubuf"], writes=["ubuf"])
                interleave([prefetch_h(0)])
                st_prev = None
                for t in range(NTILE):
                    if t == 0:
                        load_rec(0)
                    if t > 0:
                        tail = ln1_tail(t - 1, st_prev)
                        interleave_primary([rec_main(0, 0, t), rec_main(1, 1, t)], tail)
                        interleave([rec_main(2, 0, t), rec_main(3, 1, t), tail])
                    else:
                        interleave([rec_main(0, 0, t), rec_main(1, 1, t)])
                        interleave([rec_main(2, 0, t), rec_main(3, 1, t)])
                    load_tile(t)
                    if t + 1 < NTILE:
                        load_rec(t + 1)
                    conv_glu()
                    conv_rest()
                    att_proj()
                    if t + 1 < NTILE:
                        interleave([att_main(t), prefetch_h(t + 1)])
                    else:
                        interleave([att_main(t)])
                    att_shift()
                    st_prev = finish_tile(t)
                interleave([ln1_tail(NTILE - 1, st_prev, rot=True)])
            P.barrier()
            ck("mixer")
            with contextlib.ExitStack() as fs:
                fsb = lambda n, s, d=F32: fs.enter_context(nc.sbuf_tensor(f"{n}_{l}", s, d))
                xb = fsb("xb", [128, 8, FB])
                h2 = fsb("h2", [128, 8, FB], BF16)
                hid = fsb("hid", [128, 5, FB], BF16)
                wg = fsb("wg", [128, 8, 640], BF16)
                wu = fsb("wu", [128, 8, 640], BF16)
                w2 = fsb("w2", [128, 5, D], BF16)
                groups = [(0, 5), (5, 5), (10, 4), (14, 4), (18, 4)]
                if l + 1 < DEPTH:
                    load_wpre(l + 1)
                XBK = [f"xb{tt}" for tt in range(FT)]
                lnb2 = [fsb(f"lnb2_{i}", [128, 512]) for i in range(4)]
                lnrow = [lnb2[i][0:1, 0:256] for i in range(2)]
                modg = None
                if l + 1 < DEPTH:
                    stgF = ([fsb(f"stgF{i}", [128, 8, 256], BF16) for i in range(2)], lnrow)
                    modg = mod_gen(l + 1, stgF)

                def adv():
                    if modg is not None:
                        next(modg, None)
                WGK = [f"wg{k}" for k in range(8)]
                WUK = [f"wu{k}" for k in range(8)]
                for fb in range(NFB):
                    bsl = slice(fb * FB, (fb + 1) * FB)
                    for tt in range(FT):
                        csl = slice(tt * TL, (tt + 1) * TL)
                        P.dma("sp", f"xbl{tt}", I(nc.sync.dma_start, out=xb[:, :, csl], in_=fm(scrA[:, fb * FB + tt * TL:fb * FB + (tt + 1) * TL])),
                              reads=["scrA"], writes=[f"xb{tt}"])
                    def emit_h2(tt):
                        csl = slice(tt * TL, (tt + 1) * TL)
                        for k in range(8):
                            P.emit("act", I(ACT, out=h2[:, k, csl], in_=xb[:, k, csl], func=AF.Identity, scale=A2[:, k:k + 1], bias=sh2[:, k:k + 1]),
                                   reads=[f"xb{tt}", f"vA2_{pm}", MK], writes=[f"h2_{tt}"])
                    emit_h2(0)
                    for (f0, nf) in groups:
                        for k in range(8):
                            P.dma("pool", "wg", I(G.dma_start, out=wg[:, k, 0:nf * 128], in_=wf1_d[l][k * 128:(k + 1) * 128, f0 * 128:(f0 + nf) * 128]), writes=[f"wg{k}"])
                        for k in range(8):
                            P.dma("pool", "wu", I(G.dma_start, out=wu[:, k, 0:nf * 128], in_=wf1_d[l][k * 128:(k + 1) * 128, DFF + f0 * 128:DFF + (f0 + nf) * 128]), writes=[f"wu{k}"])
                        for fi in range(nf):
                            P.dma("pool", "w2", I(G.dma_start, out=w2[:, fi, :], in_=wf2_d[l][(f0 + fi) * 128:(f0 + fi + 1) * 128, :]), writes=[f"w2{fi}"])
                        W2K = [f"w2{fi}" for fi in range(nf)]
                        for fi in range(nf):
                            for tt in range(FT):
                                csl = slice(tt * TL, (tt + 1) * TL)
                                bg, bgk = gbank(5)
                                P.emit("pe", [I(MM, bg[:], lhsT=wg[:, k, fi * 128:(fi + 1) * 128], rhs=h2[:, k, csl], start=(k == 0), stop=(k == 7)) for k in range(8)],
                                       reads=WGK + [f"h2_{tt}"], writes=[bgk])
                                bu, buk = gbank(5)
                                P.emit("pe", [I(MM, bu[:], lhsT=wu[:, k, fi * 128:(fi + 1) * 128], rhs=h2[:, k, csl], start=(k == 0), stop=(k == 7)) for k in range(8)],
                                       reads=WUK + [f"h2_{tt}"], writes=[buk])
                                sg, sgk = wtmp()
                                P.emit("act", I(ACT, out=sg[:], in_=bg[:], func=AF.Silu), reads=[bgk], writes=[sgk])
                                P.emit("dve", I(TT, out=hid[:, fi, csl], in0=bu[:], in1=sg[:], op=ALU.mult), reads=[buk, sgk, "hid"], writes=["hid"])
                                if f0 == 0 and fi == 0 and tt + 1 < FT:
                                    emit_h2(tt + 1)
                                if fb == 0:
                                    adv()
                        if l == 0 and fb == 0:
                            tap(f"hid{f0}", hid[:, :, 0:512], ["hid"])
                        for m in range(8):
                            for tt in range(FT):
                                csl = slice(tt * TL, (tt + 1) * TL)
                                bo, bok = gbank(5)
                                P.emit("pe", [I(MM, bo[:], lhsT=w2[:, fi, m * 128:(m + 1) * 128], rhs=hid[:, fi, csl], start=(fi == 0), stop=(fi == nf - 1)) for fi in range(nf)],
                                       reads=W2K + ["hid"], writes=[bok])
                                P.emit("dve", I(V.scalar_tensor_tensor, out=xb[:, m, csl], in0=bo[:], scalar=G2[:, m:m + 1], in1=xb[:, m, csl], op0=ALU.mult, op1=ALU.add),
                                       reads=[bok, f"vG2_{pm}", f"xb{tt}"], writes=[f"xb{tt}"])
                    if l == 0 and fb == 0:
                        tap("z2", xb[:], XBK)
                    if modg is not None:
                        for _ in modg:
                            pass
                    for t0 in range(0, FT, 2):
                        gens = []
                        for i, tt in enumerate(range(t0, min(FT, t0 + 2))):
                            csl = slice(tt * TL, (tt + 1) * TL)
                            st = ln_stats_part(xb, f"xb{tt}", csl, bk=((3, 4), (5, 6))[i],
                                               outs=((lnb2[2 * i], f"lnb2_{2 * i}"), (lnb2[2 * i + 1], f"lnb2_{2 * i + 1}")))
                            gens.append(ln_apply_gen(xb, f"xb{tt}", ln2g, ln2b, st, csl,
                                                     bufs=[(W[4 + 2 * i], f"W{4 + 2 * i}"), (W[5 + 2 * i], f"W{5 + 2 * i}")]))
                        interleave(gens)
                        t1_ = min(FT, t0 + 2)
                        P.dma("sp", "xbs", I(nc.sync.dma_start, out=fm(xdst[:, fb * FB + t0 * TL:fb * FB + t1_ * TL]), in_=xb[:, :, t0 * TL:t1_ * TL]),
                              reads=[f"xb{tt}" for tt in range(t0, t1_)], writes=[f"{xdk}_{fb}_{t0}"])
                    if l == 0 and fb == 0:
                        tap("x_l0", xb[:], XBK)
                if modg is not None:
                    for _ in modg:
                        pass
            P.barrier()

        try:
            for l in range(DEPTH):
                layer(l)
        except _Stop:
            pass
        P.final_wait("sp", [k for k in list(P.res.keys()) if str(k).startswith("outT")] + ["dbgo_" + n for n in dbg_out])
        P.run()
    return nc


def _consts():
    c = np.zeros((128, 1152), np.float32)
    c[:, 0:128] = np.eye(128, dtype=np.float32)
    s = np.arange(128)[:, None]
    t = np.arange(128)[None, :]
    m = ((s // 64 == t // 64) & (s <= t)).astype(np.float32)
    c[:, 128:640] = np.tile(m, (1, 4))
    r = np.ones((128, 512), np.float32)
    r[:, 0::64] = 0.0
    c[:, 640:1152] = r
    return c


def _pack_params(inp, depth):
    par = np.zeros((128, NPAR), np.float32)
    ch = lambda v: np.ascontiguousarray(np.asarray(v, np.float32).reshape(-1, 128).T)
    for l in range(depth):
        po = l * PL
        par[:, po:po + 48] = ch(inp["b_ada"][l])
        par[:, po + 48:po + 56] = ch(inp["ln1_g"][l])
        par[:, po + 56:po + 64] = ch(inp["ln1_b"][l])
        par[:, po + 64:po + 72] = ch(inp["ln2_g"][l])
        par[:, po + 72:po + 80] = ch(inp["ln2_b"][l])
        cw = np.asarray(inp["conv_w"][l], np.float32)
        par[:, po + 80:po + 142] = cw.reshape(31, 2, 128).transpose(2, 1, 0).reshape(128, 62)
        par[:, po + 142:po + 144] = ch(inp["conv_b"][l])
        par[:, po + 144:po + 146] = ch(inp["conv_ln_g"][l])
        par[:, po + 146:po + 148] = ch(inp["conv_ln_b"][l])
        par[:, po + 148:po + 149] = np.asarray(inp["rec_norm_g"][l], np.float32).reshape(128, 1)
    rl = np.asarray(inp["rec_lower_bound"], np.float32)
    par[:, 2 * PL:2 * PL + 8] = rl.reshape(2, 4, 128).transpose(2, 0, 1).reshape(128, 8)
    return par


def _bias_tiles(rel_bias_l):
    r = np.arange(5)[:, None, None]
    i = np.arange(128)[None, :, None]
    j = np.arange(128)[None, None, :]
    idx = np.clip((r - 4) * 128 + i - j, -128, 128) + 128
    valid = ~(((r == 0) & (i < 64) & (j >= 64)) | ((r == 4) & (i >= 64) & (j < 64)))
    tb = np.asarray(rel_bias_l, np.float32)
    g = tb[:, idx]
    g = np.where(valid[None], g, np.float32(NEG_BIG)).astype(np.float32)
    return np.ascontiguousarray(g.transpose(2, 0, 1, 3).reshape(128, 4 * 5 * 128))


def make_in_maps(inp, NT, depth, n_cores=8):
    inp = {k: np.asarray(v) for k, v in inp.items()}
    cst = _consts()
    par = _pack_params(inp, depth)
    shared = {"par": par, "cst": cst}
    for l in range(depth):
        shared[f"wada{l}"] = np.ascontiguousarray(inp["w_ada"][l], np.float32)
        shared[f"win{l}"] = np.ascontiguousarray(inp["w_in"][l], np.float32)
        shared[f"wout{l}"] = np.ascontiguousarray(inp["w_out"][l], np.float32)
        shared[f"wf1{l}"] = np.ascontiguousarray(inp["w_ffn_in"][l], np.float32)
        shared[f"wf2{l}"] = np.ascontiguousarray(inp["w_ffn_out"][l], np.float32)
        shared[f"bias{l}"] = _bias_tiles(inp["rel_bias"][l])
    maps = []
    for c in range(n_cores):
        b, half = c // 2, c % 2
        m = dict(shared)
        m["xT"] = np.ascontiguousarray(inp["x"][b, half * NT:(half + 1) * NT].T.astype(np.float32))
        m["cT"] = np.ascontiguousarray(inp["c"][b].astype(np.float32).reshape(8, 128).T)
        flg = np.zeros((128, 4), np.float32)
        flg[:, 0] = 1.0 if half == 0 else 0.0
        flg[:, 1] = 1.0 if half == 1 else 0.0
        flg[:, 2] = 0.0 if half == 1 else NEG_BIG
        m["flg"] = flg
        maps.append(m)
    return maps


_NC_CACHE = {}


def kernel(**inputs):
    T = inputs["x"].shape[1]
    B = inputs["x"].shape[0]
    NT = T // 2
    key = (NT, 2)
    if key not in _NC_CACHE:
        _NC_CACHE[key] = build(NT, 2)
    nc = _NC_CACHE[key]
    maps = make_in_maps(inputs, NT, 2)
    res = run_bass_kernel_spmd(nc, maps, core_ids=list(range(8)))
    out = np.empty((B, T, D), np.float32)
    for c in range(2 * B):
        b, half = c // 2, c % 2
        out[b, half * NT:(half + 1) * NT] = res.results[c]["outT"].T
    return out
```
